# Optimizing a Trainium2 kernel written in Bass

```python
import math
import jax
import jax.numpy as jnp
from jax import lax
import numpy as np

D_MODEL = 1024
BATCH = 8
SEQ = 4096
DEPTH = 2

EPS = 1e-6
PLE_DIM = 256
D_FF = 2816

CONV_CH = 1024
CONV_WIDTH = 31

SSM_HEADS = 16
SSM_HEAD_DIM = 64
SSM_INNER = SSM_HEADS * SSM_HEAD_DIM
SSM_GROUPS = 2
SSM_STATE = 128
SSM_CONV = 4
SSM_CHUNK = 128
SSM_XBC = SSM_INNER + 2 * SSM_GROUPS * SSM_STATE

HYB_IN = 2 * CONV_CH + SSM_INNER + SSM_XBC + SSM_HEADS
HYB_MIX = CONV_CH + SSM_INNER

ATT_HEADS = 16
ATT_KV_HEADS = 4
ATT_HEAD_DIM = 64
ATT_QKV = (ATT_HEADS + 2 * ATT_KV_HEADS) * ATT_HEAD_DIM
WINDOW = 128
ROPE_THETA = 10000.0

N_EVEN = (DEPTH + 1) // 2
N_ODD = DEPTH // 2

kernel_name = 'macaron_conv_ssd_swa_hybrid'


def rms_norm(x, g):
    xf = x.astype(jnp.float32)
    y = xf * lax.rsqrt(jnp.mean(xf * xf, axis=-1, keepdims=True) + EPS)
    return (y * g.astype(jnp.float32)).astype(x.dtype)


def layer_norm(x, g, b):
    xf = x.astype(jnp.float32)
    mu = jnp.mean(xf, axis=-1, keepdims=True)
    xc = xf - mu
    var = jnp.mean(xc * xc, axis=-1, keepdims=True)
    y = xc * lax.rsqrt(var + EPS) * g.astype(jnp.float32) + b.astype(jnp.float32)
    return y.astype(x.dtype)


def grouped_rms_norm(y, g):
    bsz, seqlen, ch = y.shape
    yg = y.reshape(bsz, seqlen, SSM_GROUPS, ch // SSM_GROUPS)
    yg = yg * lax.rsqrt(jnp.mean(yg * yg, axis=-1, keepdims=True) + EPS)
    return yg.reshape(bsz, seqlen, ch) * g.astype(jnp.float32)


def swiglu(x, w_in, w_out):
    gate, up = jnp.split(x @ w_in, 2, axis=-1)
    return (jax.nn.silu(gate) * up) @ w_out


def causal_depthwise_conv(x, w, b):
    width, ch = w.shape
    y = lax.conv_general_dilated(x, w[:, None, :], window_strides=(1,), padding=[(width - 1, 0)],
                                 dimension_numbers=('NWC', 'WIO', 'NWC'), feature_group_count=ch)
    return y + b


def segsum_exp(a):
    t = a.shape[-1]
    cs = jnp.cumsum(a, axis=-1)
    diff = cs[..., :, None] - cs[..., None, :]
    mask = jnp.tril(jnp.ones((t, t), dtype=bool))
    return jnp.exp(jnp.where(mask, diff, -jnp.inf))


def ssd_chunked(x, dt, a, b, c):
    bsz, seqlen = x.shape[:2]
    nc = seqlen // SSM_CHUNK
    r = SSM_HEADS // SSM_GROUPS
    ln = SSM_CHUNK
    xdt = (x * dt[..., None]).reshape(bsz, nc, ln, SSM_GROUPS, r, SSM_HEAD_DIM)
    adt = (dt * a).reshape(bsz, nc, ln, SSM_GROUPS, r).transpose(0, 3, 4, 1, 2)
    a_cs = jnp.cumsum(adt, axis=-1)
    bc = b.reshape(bsz, nc, ln, SSM_GROUPS, SSM_STATE)
    cc = c.reshape(bsz, nc, ln, SSM_GROUPS, SSM_STATE)
    cb = jnp.einsum('bclgn,bcsgn->bcgls', cc, bc)
    y_diag = jnp.einsum('bcgls,bgrcls,bcsgrp->bclgrp', cb, segsum_exp(adt), xdt)
    decay_to_end = jnp.exp(a_cs[..., -1:] - a_cs)
    states = jnp.einsum('bclgn,bgrcl,bclgrp->bcgrpn', bc, decay_to_end, xdt)
    chunk_decay = jnp.exp(a_cs[..., -1])

    def step(h, inp):
        s_c, d_c = inp
        return h * d_c[..., None, None] + s_c, h

    h0 = jnp.zeros((bsz, SSM_GROUPS, r, SSM_HEAD_DIM, SSM_STATE), jnp.float32)
    _, prev = lax.scan(step, h0, (jnp.moveaxis(states, 1, 0), jnp.moveaxis(chunk_decay, 3, 0)))
    y_off = jnp.einsum('bclgn,cbgrpn,bgrcl->bclgrp', cc, prev, jnp.exp(a_cs))
    return (y_diag + y_off).reshape(bsz, seqlen, SSM_HEADS, SSM_HEAD_DIM)


def conv_ssd_mixer(hn, w_in, cv_w, cv_b, cv_g, cv_beta, sc_w, sc_b, dt_bias, a_log, d_skip, ssm_norm, w_out):
    bsz, seqlen, _ = hn.shape
    f32 = jnp.float32
    cut = np.cumsum([CONV_CH, CONV_CH, SSM_INNER, SSM_XBC]).tolist()
    cv_val, cv_gate, z, xbc, dt_raw = jnp.split(hn @ w_in, cut, axis=-1)
    u = cv_val * jax.nn.sigmoid(cv_gate)
    u = causal_depthwise_conv(u, cv_w, cv_b)
    u = jax.nn.silu(layer_norm(u, cv_g, cv_beta))
    xbc = jax.nn.silu(causal_depthwise_conv(xbc, sc_w, sc_b))
    xs, bs, cs = jnp.split(xbc, [SSM_INNER, SSM_INNER + SSM_GROUPS * SSM_STATE], axis=-1)
    xs = xs.reshape(bsz, seqlen, SSM_HEADS, SSM_HEAD_DIM).astype(f32)
    dt = jax.nn.softplus(dt_raw.astype(f32) + dt_bias.astype(f32))
    a = -jnp.exp(a_log.astype(f32))
    y = ssd_chunked(xs, dt, a,
                    bs.reshape(bsz, seqlen, SSM_GROUPS, SSM_STATE).astype(f32),
                    cs.reshape(bsz, seqlen, SSM_GROUPS, SSM_STATE).astype(f32))
    y = y + d_skip.astype(f32)[:, None] * xs
    y = y.reshape(bsz, seqlen, SSM_INNER) * jax.nn.silu(z.astype(f32))
    y = grouped_rms_norm(y, ssm_norm).astype(hn.dtype)
    return jnp.concatenate([u, y], axis=-1) @ w_out


def rope_tables(seqlen):
    inv = ROPE_THETA ** (-jnp.arange(0, ATT_HEAD_DIM, 2, dtype=jnp.float32) / ATT_HEAD_DIM)
    ang = jnp.arange(seqlen, dtype=jnp.float32)[:, None] * inv[None, :]
    return jnp.cos(ang), jnp.sin(ang)


def apply_rope(t, cos, sin):
    half = t.shape[-1] // 2
    t1, t2 = t[..., :half], t[..., half:]
    c = cos[None, :, None, :].astype(t.dtype)
    s = sin[None, :, None, :].astype(t.dtype)
    return jnp.concatenate([t1 * c - t2 * s, t2 * c + t1 * s], axis=-1)


def swa_sink_attention(hn, w_qkv, b_qkv, sinks, w_o, b_o, cos, sin):
    bsz, seqlen, _ = hn.shape
    nb = seqlen // WINDOW
    grp = ATT_HEADS // ATT_KV_HEADS
    q, k, v = jnp.split(hn @ w_qkv + b_qkv,
                        [ATT_HEADS * ATT_HEAD_DIM, (ATT_HEADS + ATT_KV_HEADS) * ATT_HEAD_DIM], axis=-1)
    q = apply_rope(q.reshape(bsz, seqlen, ATT_HEADS, ATT_HEAD_DIM), cos, sin)
    k = apply_rope(k.reshape(bsz, seqlen, ATT_KV_HEADS, ATT_HEAD_DIM), cos, sin)
    v = v.reshape(bsz, seqlen, ATT_KV_HEADS, ATT_HEAD_DIM)
    qb = q.reshape(bsz, nb, WINDOW, ATT_KV_HEADS, grp, ATT_HEAD_DIM)

    def band(t):
        tp = jnp.pad(t, ((0, 0), (WINDOW, 0), (0, 0), (0, 0)))
        tp = tp.reshape(bsz, nb + 1, WINDOW, ATT_KV_HEADS, ATT_HEAD_DIM)
        return jnp.concatenate([tp[:, :-1], tp[:, 1:]], axis=2)

    kb, vb = band(k), band(v)
    logits = jnp.einsum('bnqkgd,bnskd->bnkgqs', qb, kb).astype(jnp.float32) * (ATT_HEAD_DIM ** -0.5)
    qpos = jnp.arange(nb)[:, None, None] * WINDOW + jnp.arange(WINDOW)[None, :, None]
    kpos = jnp.arange(nb)[:, None, None] * WINDOW - WINDOW + jnp.arange(2 * WINDOW)[None, None, :]
    rel = qpos - kpos
    valid = (kpos >= 0) & (rel >= 0) & (rel < WINDOW)
    logits = jnp.where(valid[None, :, None, None], logits, -jnp.inf)
    sink = sinks.astype(jnp.float32).reshape(ATT_KV_HEADS, grp)[None, None, :, :, None, None]
    m = jnp.maximum(jnp.max(logits, axis=-1, keepdims=True), sink)
    e = jnp.exp(logits - m)
    probs = e / (jnp.sum(e, axis=-1, keepdims=True) + jnp.exp(sink - m))
    o = jnp.einsum('bnkgqs,bnskd->bnqkgd', probs.astype(vb.dtype), vb)
    return o.reshape(bsz, seqlen, ATT_HEADS * ATT_HEAD_DIM) @ w_o + b_o


def setup_inputs(seed: int = 0) -> dict:
    key = jax.random.key(seed)
    ks = iter(jax.random.split(key, 40))
    D = D_MODEL

    def nrm(shape, scale):
        return scale * jax.random.normal(next(ks), shape, jnp.float32)

    def gain(shape):
        return 1.0 + nrm(shape, 0.02)

    x = jax.random.normal(next(ks), (BATCH, SEQ, D), jnp.float32)
    p = jax.random.normal(next(ks), (DEPTH, BATCH, SEQ, PLE_DIM), jnp.float32)
    dt0 = jnp.exp(jax.random.uniform(next(ks), (N_EVEN, SSM_HEADS), jnp.float32,
                                     minval=math.log(1e-3), maxval=math.log(1e-1)))
    a_init = jax.random.uniform(next(ks), (N_EVEN, SSM_HEADS), jnp.float32, minval=1.0, maxval=16.0)
    return {
        'x': x,
        'p': p,
        'norm_ffn1': gain((DEPTH, D)),
        'ffn1_w_in': nrm((DEPTH, D, 2 * D_FF), D ** -0.5),
        'ffn1_w_out': nrm((DEPTH, D_FF, D), D_FF ** -0.5),
        'norm_mix': gain((DEPTH, D)),
        'norm_ffn2': gain((DEPTH, D)),
        'ffn2_w_in': nrm((DEPTH, D, 2 * D_FF), D ** -0.5),
        'ffn2_w_out': nrm((DEPTH, D_FF, D), D_FF ** -0.5),
        'ple_norm': gain((DEPTH, D)),
        'ple_gate_w': nrm((DEPTH, D, D), D ** -0.5),
        'ple_proj_w': nrm((DEPTH, PLE_DIM, D), PLE_DIM ** -0.5),
        'hyb_w_in': nrm((N_EVEN, D, HYB_IN), D ** -0.5),
        'conv_dw_w': nrm((N_EVEN, CONV_WIDTH, CONV_CH), CONV_WIDTH ** -0.5),
        'conv_dw_b': nrm((N_EVEN, CONV_CH), 0.02),
        'conv_ln_g': gain((N_EVEN, CONV_CH)),
        'conv_ln_b': nrm((N_EVEN, CONV_CH), 0.02),
        'ssm_conv_w': nrm((N_EVEN, SSM_CONV, SSM_XBC), SSM_CONV ** -0.5),
        'ssm_conv_b': nrm((N_EVEN, SSM_XBC), 0.02),
        'ssm_dt_bias': dt0 + jnp.log(-jnp.expm1(-dt0)),
        'ssm_a_log': jnp.log(a_init),
        'ssm_d': gain((N_EVEN, SSM_HEADS)),
        'ssm_norm': gain((N_EVEN, SSM_INNER)),
        'hyb_w_out': nrm((N_EVEN, HYB_MIX, D), HYB_MIX ** -0.5),
        'att_w_qkv': nrm((N_ODD, D, ATT_QKV), D ** -0.5),
        'att_b_qkv': nrm((N_ODD, ATT_QKV), 0.02),
        'att_sinks': nrm((N_ODD, ATT_HEADS), 0.5),
        'att_w_o': nrm((N_ODD, ATT_HEADS * ATT_HEAD_DIM, D), (ATT_HEADS * ATT_HEAD_DIM) ** -0.5),
        'att_b_o': nrm((N_ODD, D), 0.02),
        'final_norm': gain((D,)),
    }


def reference(x, p, norm_ffn1, ffn1_w_in, ffn1_w_out, norm_mix, norm_ffn2, ffn2_w_in, ffn2_w_out,
              ple_norm, ple_gate_w, ple_proj_w, hyb_w_in, conv_dw_w, conv_dw_b, conv_ln_g, conv_ln_b,
              ssm_conv_w, ssm_conv_b, ssm_dt_bias, ssm_a_log, ssm_d, ssm_norm, hyb_w_out,
              att_w_qkv, att_b_qkv, att_sinks, att_w_o, att_b_o, final_norm):
    cos, sin = rope_tables(x.shape[1])
    h = x
    for i in range(DEPTH):
        j = i // 2
        h = h + 0.5 * swiglu(rms_norm(h, norm_ffn1[i]), ffn1_w_in[i], ffn1_w_out[i])
        hn = rms_norm(h, norm_mix[i])
        if i % 2 == 0:
            h = h + conv_ssd_mixer(hn, hyb_w_in[j], conv_dw_w[j], conv_dw_b[j], conv_ln_g[j], conv_ln_b[j],
                                   ssm_conv_w[j], ssm_conv_b[j], ssm_dt_bias[j], ssm_a_log[j], ssm_d[j],
                                   ssm_norm[j], hyb_w_out[j])
        else:
            h = h + swa_sink_attention(hn, att_w_qkv[j], att_b_qkv[j], att_sinks[j], att_w_o[j], att_b_o[j],
                                       cos, sin)
        h = h + 0.5 * swiglu(rms_norm(h, norm_ffn2[i]), ffn2_w_in[i], ffn2_w_out[i])
        gate = jax.nn.sigmoid(rms_norm(h, ple_norm[i]) @ ple_gate_w[i])
        h = h + gate * (p[i] @ ple_proj_w[i])
    return rms_norm(h, final_norm)
```

```python
from contextlib import ExitStack
import os
import numpy as np
import concourse.bass as bass
import concourse.mybir as mybir
from concourse.bass_utils import run_bass_kernel_spmd

F32 = mybir.dt.float32
BF16 = mybir.dt.bfloat16
AF = mybir.ActivationFunctionType
ALU = mybir.AluOpType
AX = mybir.AxisListType

D = 1024
DFF = 2816
NJ = DFF // 128
PLE = 256
SEQ = 4096
NCORES = 8
T = 256
NB = T // 128
EPS = 1e-6
CW = 31
HYB_IN = 4624
NEG = -30000.0
STG = int(os.environ.get('SSD_STAGE', '99'))
SUB = int(os.environ.get('SUB', '0'))


class Buf:
    __slots__ = ("name", "w", "r", "dsem", "excl")

    def __init__(self, name):
        self.name = name
        self.w = {}
        self.r = {}
        self.dsem = None
        self.excl = False


class Prog:
    ENGS = ("pe", "act", "dve", "pool", "sp")

    def __init__(self, nc, es):
        self.nc = nc
        self.es = es
        self.streams = {e: [] for e in self.ENGS}
        self.sems = {}
        self.cnt = {}
        self.waited = {e: {} for e in self.ENGS}
        self.nbuf = 0
        self.free_dsems = []
        self.phase_dsems = []
        self.ndsem = 0
        for e in ("pe", "act", "dve", "pool"):
            self._mksem("c_" + e)

    def _mksem(self, name):
        self.sems[name] = self.es.enter_context(self.nc.semaphore(name))
        self.cnt[name] = 0
        return name

    def _dsem(self):
        if self.free_dsems:
            s = self.free_dsems.pop()
        else:
            self.ndsem += 1
            s = self._mksem(f"d{self.ndsem}")
        self.phase_dsems.append(s)
        return s

    def buf(self, name=None):
        self.nbuf += 1
        return Buf(name or f"b{self.nbuf}")

    def bufs(self, n, name="b"):
        return [self.buf(f"{name}{i}") for i in range(n)]

    def _need(self, reads, writes):
        need = {}
        for b in reads:
            for s, v in b.w.items():
                if need.get(s, 0) < v:
                    need[s] = v
        for b in writes:
            for d in (b.w, b.r):
                for s, v in d.items():
                    if need.get(s, 0) < v:
                        need[s] = v
        return need

    def _emit_waits(self, eng, need, skip_own=False):
        wd = self.waited[eng]
        own = "c_" + eng
        for s, v in need.items():
            if skip_own and s == own:
                continue
            if wd.get(s, 0) < v:
                wd[s] = v
                h = self.sems[s]
                self.streams[eng].append(lambda e, h=h, v=v: e.wait_ge(h, v))

    def _mark(self, reads, writes, s, v):
        for b in reads:
            if b.r.get(s, 0) < v:
                b.r[s] = v
        for b in writes:
            b.w = {s: v}
            b.r = {}

    def op(self, eng, fns, reads=(), writes=(), skip_own=None):
        if callable(fns):
            fns = [fns]
        if skip_own is None:
            skip_own = (eng == "pe")
        ex = [b for b in reads if b.excl]
        if ex:
            writes = list(writes) + ex
            reads = [b for b in reads if not b.excl]
        self._emit_waits(eng, self._need(reads, writes), skip_own)
        s = "c_" + eng
        self.cnt[s] += 1
        v = self.cnt[s]
        h = self.sems[s]
        st = self.streams[eng]
        for f in fns[:-1]:
            st.append(f)
        last = fns[-1]
        st.append(lambda e, last=last, h=h: last(e).then_inc(h, 1))
        self._mark(reads, writes, s, v)

    def dma(self, q, out, in_, reads=(), writes=(), slot=None):
        self._emit_waits(q, self._need(reads, writes), False)
        if slot.dsem is None:
            slot.dsem = self._dsem()
        s = slot.dsem
        self.cnt[s] += 16
        v = self.cnt[s]
        h = self.sems[s]
        self.streams[q].append(lambda e, out=out, in_=in_, h=h: e.dma_start(out=out, in_=in_).then_inc(h, 16))
        self._mark(reads, writes, s, v)

    def barrier(self):
        need = {s: v for s, v in self.cnt.items() if v > 0}
        for e in self.ENGS:
            self._emit_waits(e, need, False)

    def end_phase(self):
        self.barrier()
        self.free_dsems.extend(self.phase_dsems)
        self.phase_dsems = []

    def emit(self):
        nc = self.nc
        st = self.streams
        with nc.Block() as block:
            @block.tensor
            def _(e):
                for f in st["pe"]:
                    f(e)

            @block.scalar
            def _(e):
                for f in st["act"]:
                    f(e)

            @block.vector
            def _(e):
                for f in st["dve"]:
                    f(e)

            @block.gpsimd
            def _(e):
                for f in st["pool"]:
                    f(e)

            @block.sync
            def _(e):
                for f in st["sp"]:
                    f(e)


class Ctx:
    pass


def sl(i, n=128):
    return slice(i * n, (i + 1) * n)


def build_program(S=SEQ, phases=("ffn1_0", "conv", "ssd", "ffn2_0", "ffn1_1", "att", "ffn2_1"), debug=False):
    nc = bass.Bass("TRN2", target_bir_lowering=False)
    nt = S // T
    dt_in = {}

    def din(name, shape):
        dt_in[name] = nc.dram_tensor(name, list(shape), F32, kind="ExternalInput").ap()
        return dt_in[name]

    x_d = din("x", [S, D])
    p_d = din("p", [2, S, PLE])
    ffn_win = din("ffn_win", [4, D, 2 * DFF])
    ffn_wout = din("ffn_wout", [4, DFF, D])
    vecs = din("vecs", [128, 20, 8])
    ple_wg = din("ple_wg", [2, D, D])
    ple_wp = din("ple_wp", [2, PLE, D])
    hyb_win = din("hyb_win", [D, HYB_IN])
    hyb_wout = din("hyb_wout", [2 * D, D])
    cw_d = din("cw", [128, 8, CW])
    scw_d = din("scw", [128, 12, 5])
    h16_d = din("h16", [3, 16])
    wqkv_d = din("wqkv", [D, 1536])
    bqk_d = din("bqk", [128, 10])
    bv_d = din("bv", [256])
    wo_d = din("wo", [D, D])
    rope_d = din("rope", [2, 128, S])
    cst_d = din("cst", [5, 128, 128])
    msk_d = din("msk", [2, 128, 256])
    if debug:
        out_d = nc.dram_tensor("H", [8, 128, S], F32, kind="ExternalOutput").ap()
        Hd = out_d
        yout_d = nc.dram_tensor("y", [S, D], F32, kind="ExternalOutput").ap()
    else:
        yout_d = nc.dram_tensor("y", [S, D], F32, kind="ExternalOutput").ap()
        Hd = nc.dram_tensor("Hs", [8, 128, S], F32).ap()
    Ud = nc.dram_tensor("Us", [8, 128, S], BF16).ap()
    Hv = Hd.rearrange("k p s -> p k s")
    Uv = Ud.rearrange("k p s -> p k s")

    with ExitStack() as es:
        P = Prog(nc, es)
        C = Ctx()
        gsb = lambda name, shape, dt: es.enter_context(nc.sbuf_tensor(name, shape, dt))
        pb = [es.enter_context(nc.psum_tensor(f"pb{i}", [128, 512], F32)) for i in range(8)]
        Bpb = P.bufs(8, "pb")
        for b_ in Bpb:
            b_.excl = True
        identf = gsb("identf", [128, 128], F32)
        identb = gsb("identb", [128, 128], BF16)
        triu = gsb("triu", [128, 128], F32)
        pswap = gsb("pswap", [128, 128], BF16)
        ones_f = gsb("ones_f", [128, 128], F32)
        ones1k = gsb("ones1k", [128, 128], BF16)
        ones512 = gsb("ones512", [128, 128], BF16)
        cols = gsb("cols", [128, 4], F32)
        vec = gsb("vec", [128, 20, 8], F32)
        Bc = P.buf("consts")
        P.dma("sp", identf[:], cst_d[0], writes=[Bc], slot=Bc)
        P.dma("sp", triu[:], cst_d[1], writes=[Bc], slot=Bc)
        P.dma("sp", vec[:], vecs, writes=[Bc], slot=Bc)
        P.dma("pool", identb[:], cst_d[0], writes=[Bc], slot=Bc)
        P.dma("pool", pswap[:], cst_d[2], writes=[Bc], slot=Bc)
        P.op("dve", lambda e: e.memset(ones_f[:], 1.0), writes=[Bc])
        P.op("dve", lambda e: e.memset(ones1k[:], 1.0 / 1024), writes=[Bc])
        P.op("dve", lambda e: e.memset(ones512[:], 1.0 / 512), writes=[Bc])
        P.op("dve", lambda e: e.memset(cols[:, 0:1], EPS), writes=[Bc])
        P.op("dve", lambda e: e.memset(cols[:, 1:2], 1.0), writes=[Bc])
        P.op("dve", lambda e: e.memset(cols[:, 2:3], 0.0), writes=[Bc])
        epsc = cols[:, 0:1]
        onec = cols[:, 1:2]
        B_H = P.bufs(nt, "H")
        B_U = P.bufs(nt, "U")
        B_Y = P.bufs(nt, "Y")
        P.end_phase()

        V_NF1, V_NMIX, V_NF2, V_PLE = 0, 2, 4, 6
        V_FIN, V_CVB, V_LNG, V_LNB, V_SSD, V_SSN, V_BO = 8, 9, 10, 11, 12, 13, 14

        def rms_rstd(xap, Bx, nk, ones_ap, sqt, Bsq, pbank, Bpbank, rstd, Brstd, k0=0):
            for k in range(nk):
                q = k % 2
                P.op("act", lambda e, k=k, q=q: e.activation(sqt[:, q, :], xap[:, k0 + k, :], AF.Square),
                     reads=[Bx], writes=[Bsq[q]])
                P.op("pe", lambda e, k=k, q=q: e.matmul(pbank[:, 0:T], ones_ap[:], sqt[:, q, :],
                                                       start=(k == 0), stop=(k == nk - 1)),
                     reads=[Bsq[q], Bc], writes=[Bpbank])
            P.op("act", lambda e: e.activation(rstd, pbank[:, 0:T], AF.Sqrt, bias=epsc, scale=1.0),
                 reads=[Bpbank, Bc], writes=[Brstd])
            P.op("dve", lambda e: e.reciprocal(rstd, rstd), reads=[Brstd], writes=[Brstd])

        def make_xn(ht_s, Bht, gidx, xn, Bxn, sqt, Bsq, rstd, Brstd, pbank, Bpbank):
            rms_rstd(ht_s, Bht, 8, ones1k, sqt, Bsq, pbank, Bpbank, rstd[:], Brstd)
            for k in range(8):
                P.op("dve", lambda e, k=k: e.scalar_tensor_tensor(
                    xn[:, k, :], ht_s[:, k, :], vec[:, gidx, k:k + 1], rstd[:], ALU.mult, ALU.mult),
                    reads=[Bht, Bc, Brstd], writes=[Bxn])

        ucnt = [0]

        def uniq(name):
            ucnt[0] += 1
            return f"s{ucnt[0]}_{name}"

        def load_w(dst, src_rows_ap, Bw, q="pool"):
            P.dma(q, dst, src_rows_ap, writes=[Bw], slot=Bw)

        def ffn_phase(fi, li, first, do_ple, final):
            with ExitStack() as pes:
                sb = lambda name, shape, dt: pes.enter_context(nc.sbuf_tensor(uniq(name), shape, dt))
                win = sb("win", [128, 8, 2 * DFF], BF16)
                wout = sb("wout", [128, NJ, D], BF16)
                ht = [sb(f"ht{i}", [128, 8, T], F32) for i in range(2)]
                xn = [sb(f"xn{i}", [128, 8, T], BF16) for i in range(2)]
                sqt = sb("sqt", [128, 2, T], BF16)
                rstd = [sb(f"rstd{i}", [128, T], F32) for i in range(2)]
                sg = [sb(f"sg{i}", [128, T], F32) for i in range(2)]
                hT = sb("hT", [128, NJ, T], BF16)
                B_win = P.bufs(1, "win")
                B_wout = P.bufs(2, "wout")
                B_ht = P.bufs(2, "ht")
                B_xn = P.bufs(2, "xn")
                B_sq = P.bufs(2, "sq")
                B_rstd = P.bufs(2, "rstd")
                B_sg = P.bufs(2, "sg")
                B_hT = P.buf("hT")
                if first:
                    xt = [sb("xt0", [128, D], F32)]
                    B_xt = P.bufs(1, "xt")
                if final:
                    yt = [sb(f"yt{i}", [128, 512], F32) for i in range(2)]
                    B_yt = P.bufs(2, "yt")
                if do_ple:
                    wg = sb("wg", [128, 8, D], BF16)
                    wp = sb("wp", [128, 2, D], BF16)
                    pt = [sb(f"pt{i}", [128, PLE], F32) for i in range(2)]
                    pT = sb("pT", [128, 2, T], BF16)
                    sg2 = [sb(f"sgp{i}", [128, T], F32) for i in range(2)]
                    B_wg = P.buf("wg")
                    B_wp = P.buf("wp")
                    B_pt = P.bufs(2, "pt")
                    B_pT = P.buf("pT")
                    B_sg2 = P.bufs(2, "sgp")
                for k in range(8):
                    load_w(win[:, k, :], ffn_win[fi, sl(k), :], B_win[0])
                wo_v = ffn_wout[fi].rearrange("(j p) m -> p j m", p=128)
                for hh in range(2):
                    load_w(wout[:, hh * 11:(hh + 1) * 11, :], wo_v[:, hh * 11:(hh + 1) * 11, :], B_wout[hh])
                if do_ple:
                    load_w(wg[:], ple_wg[li].rearrange("(k p) m -> p k m", p=128), B_wg)
                    load_w(wp[:], ple_wp[li].rearrange("(k p) m -> p k m", p=128), B_wp)
                gidx = (V_NF1 if not do_ple else V_NF2) + li

                def load_tile(it):
                    s = it % 2
                    t0 = it * T
                    hs = ht[s]
                    if first:
                        for b in range(NB):
                            P.dma("sp", xt[0][:], x_d[t0 + b * 128:t0 + (b + 1) * 128, :],
                                  writes=[B_xt[0]], slot=B_xt[0])
                            for hf in range(2):
                                P.op("pe", [lambda e, kk=kk, hf=hf: e.transpose(
                                    pb[7][:, sl(kk)], xt[0][:, sl(hf * 4 + kk)], identf[:]) for kk in range(4)],
                                    reads=[B_xt[0], Bc], writes=[Bpb[7]])
                                P.op("act", lambda e, hf=hf, b=b, hs=hs: e.copy(
                                    hs[:, hf * 4:(hf + 1) * 4, sl(b)], pb[7][:].rearrange("p (k t) -> p k t", k=4)),
                                    reads=[Bpb[7]], writes=[B_ht[s]])
                    else:
                        P.dma("sp", hs[:], Hv[:, :, t0:t0 + T], reads=[B_H[it]], writes=[B_ht[s]], slot=B_ht[s])

                def prologue(it):
                    s = it % 2
                    make_xn(ht[s], B_ht[s], gidx, xn[s], B_xn[s], sqt, B_sq, rstd[s], B_rstd[s], pb[7], Bpb[7])

                def sq_stat(hs_, s_, m_):
                    q_ = m_ % 2
                    P.op("act", lambda e: e.activation(sqt[:, q_, :], hs_[:, m_, :], AF.Square),
                         reads=[B_ht[s_]], writes=[B_sq[q_]])

                def pe_stat(m_):
                    q_ = m_ % 2
                    P.op("pe", lambda e: e.matmul(pb[6][:, 0:T], ones1k[:], sqt[:, q_, :],
                                                  start=(m_ == 0), stop=(m_ == 7)),
                         reads=[B_sq[q_], Bc], writes=[Bpb[6]])

                def finish_rstd(s_):
                    rs_ = rstd[s_]
                    P.op("act", lambda e: e.activation(rs_[:], pb[6][:, 0:T], AF.Sqrt, bias=epsc, scale=1.0),
                         reads=[Bpb[6], Bc], writes=[B_rstd[s_]])
                    P.op("dve", lambda e: e.reciprocal(rs_[:], rs_[:]), reads=[B_rstd[s_]], writes=[B_rstd[s_]])

                def p_dma(it):
                    t0 = it * T
                    for b in range(NB):
                        P.dma("sp", pt[b][:], p_d[li, t0 + b * 128:t0 + (b + 1) * 128, :],
                              writes=[B_pt[b]], slot=B_pt[b])

                def post_slots(it):
                    s = it % 2
                    t0 = it * T
                    hs = ht[s]
                    xs = xn[s]
                    sl_ = {}

                    def add(j, f):
                        sl_.setdefault(j, []).append(f)

                    def prep():
                        for b in range(NB):
                            P.op("pe", [lambda e, c=c, b=b: e.transpose(
                                pb[7][:, sl(c)], pt[b][:, sl(c)], identf[:]) for c in range(2)],
                                reads=[B_pt[b], Bc], writes=[Bpb[7]])
                            P.op("act", lambda e, b=b: e.copy(
                                pT[:, :, sl(b)], pb[7][:, 0:256].rearrange("p (k t) -> p k t", k=2)),
                                reads=[Bpb[7]], writes=[B_pT])
                        finish_rstd(s)
                        for k in range(8):
                            P.op("dve", lambda e, k=k: e.scalar_tensor_tensor(
                                xs[:, k, :], hs[:, k, :], vec[:, V_PLE + li, k:k + 1], rstd[s][:], ALU.mult, ALU.mult),
                                reads=[B_ht[s], Bc, B_rstd[s]], writes=[B_xn[s]])
                    add(0, prep)

                    def ple_group(m):
                        mb = m % 2
                        P.op("pe", [lambda e, k=k: e.matmul(
                            pb[4][:, 0:T], wg[:, k, sl(m)], xs[:, k, :], start=(k == 0), stop=(k == 7))
                            for k in range(8)], reads=[B_xn[s], B_wg], writes=[Bpb[4]])
                        P.op("pe", [lambda e, c=c: e.matmul(
                            pb[5][:, 0:T], wp[:, c, sl(m)], pT[:, c, :], start=(c == 0), stop=(c == 1))
                            for c in range(2)], reads=[B_pT, B_wp], writes=[Bpb[5]])
                        P.op("act", lambda e: e.activation(sg2[mb][:], pb[4][:, 0:T], AF.Sigmoid),
                             reads=[Bpb[4]], writes=[B_sg2[mb]])
                        P.op("dve", lambda e: e.tensor_tensor(sg2[mb][:], sg2[mb][:], pb[5][:, 0:T], ALU.mult),
                             reads=[B_sg2[mb], Bpb[5]], writes=[B_sg2[mb]])
                        P.op("dve", lambda e: e.tensor_tensor(hs[:, m, :], hs[:, m, :], sg2[mb][:], ALU.add),
                             reads=[B_sg2[mb], B_ht[s]], writes=[B_ht[s]])
                        if final:
                            sq_stat(hs, s, m)
                            if m > 0:
                                pe_stat(m - 1)
                    for m in range(8):
                        add(1 + m, lambda m=m: ple_group(m))

                    def fin_norm():
                        pe_stat(7)
                        finish_rstd(s)
                        for k in range(8):
                            P.op("dve", lambda e, k=k: e.scalar_tensor_tensor(
                                hs[:, k, :], hs[:, k, :], vec[:, V_FIN, k:k + 1], rstd[s][:], ALU.mult, ALU.mult),
                                reads=[B_ht[s], Bc, B_rstd[s]], writes=[B_ht[s]])

                    def out_block(b, hf):
                        P.op("pe", [lambda e, kk=kk: e.transpose(
                            pb[7][:, sl(kk)], hs[:, hf * 4 + kk, sl(b)], identf[:]) for kk in range(4)],
                            reads=[B_ht[s], Bc], writes=[Bpb[7]])
                        P.op("act", lambda e: e.copy(yt[hf][:], pb[7][:]), reads=[Bpb[7]], writes=[B_yt[hf]])
                        P.dma("sp", yout_d[t0 + b * 128:t0 + (b + 1) * 128, hf * 512:(hf + 1) * 512], yt[hf][:],
                              reads=[B_yt[hf]], writes=[B_Y[it]], slot=B_yt[hf])
                    if final:
                        add(9, fin_norm)
                        jj = 10
                        for b in range(NB):
                            for hf in range(2):
                                add(jj, lambda b=b, hf=hf: out_block(b, hf))
                                jj += 1
                    if not final or debug:
                        add(14, lambda: P.dma("sp", Hv[:, :, t0:t0 + T], hs[:], reads=[B_ht[s]],
                                              writes=[B_H[it]], slot=B_ht[s]))
                    return sl_

                load_tile(0)
                prologue(0)
                for it in range(nt):
                    s = it % 2
                    t0 = it * T
                    hs = ht[s]
                    xs = xn[s]
                    slots = post_slots(it - 1) if (do_ple and it > 0) else {}
                    if it + 1 < nt and not first and not do_ple:
                        load_tile(it + 1)
                    for j in range(NJ):
                        jb = j % 2
                        for f_ in slots.get(j, []):
                            f_()
                        if it + 1 < nt:
                            if first and j == 6:
                                load_tile(it + 1)
                            if do_ple and j == 15:
                                load_tile(it + 1)
                            if j == (18 if do_ple else 12):
                                prologue(it + 1)
                        P.op("pe", [lambda e, k=k, j=j, jb=jb, xs=xs: e.matmul(
                            pb[jb][:, 0:T], win[:, k, sl(j)], xs[:, k, :], start=(k == 0), stop=(k == 7))
                            for k in range(8)], reads=[B_xn[s], B_win[0]], writes=[Bpb[jb]])
                        P.op("pe", [lambda e, k=k, j=j, jb=jb, xs=xs: e.matmul(
                            pb[2 + jb][:, 0:T], win[:, k, DFF + j * 128:DFF + (j + 1) * 128], xs[:, k, :],
                            start=(k == 0), stop=(k == 7))
                            for k in range(8)], reads=[B_xn[s], B_win[0]], writes=[Bpb[2 + jb]])
                        P.op("act", lambda e, jb=jb: e.activation(sg[jb][:], pb[jb][:, 0:T], AF.Silu),
                             reads=[Bpb[jb]], writes=[B_sg[jb]])
                        P.op("dve", lambda e, j=j, jb=jb: e.tensor_tensor(
                            hT[:, j, :], sg[jb][:], pb[2 + jb][:, 0:T], ALU.mult),
                            reads=[B_sg[jb], Bpb[2 + jb]], writes=[B_hT])
                    if do_ple:
                        p_dma(it)
                    for m in range(8):
                        mb = 4 + m % 2
                        P.op("pe", [lambda e, j=j, m=m, mb=mb: e.matmul(
                            pb[mb][:, 0:T], wout[:, j, sl(m)], hT[:, j, :], start=(j == 0), stop=(j == NJ - 1))
                            for j in range(NJ)], reads=[B_hT] + B_wout, writes=[Bpb[mb]])
                        P.op("dve", lambda e, m=m, mb=mb, hs=hs: e.scalar_tensor_tensor(
                            hs[:, m, :], pb[mb][:, 0:T], 0.5, hs[:, m, :], ALU.mult, ALU.add),
                            reads=[Bpb[mb], B_ht[s]], writes=[B_ht[s]])
                        if do_ple:
                            sq_stat(hs, s, m)
                            if m > 0:
                                pe_stat(m - 1)
                    if do_ple:
                        pe_stat(7)
                    else:
                        P.dma("sp", Hv[:, :, t0:t0 + T], hs[:], reads=[B_ht[s]], writes=[B_H[it]], slot=B_ht[s])
                if do_ple:
                    last = post_slots(nt - 1)
                    for j in sorted(last):
                        for f_ in last[j]:
                            f_()
                P.end_phase()

        def conv_phase():
            with ExitStack() as pes:
                sb = lambda name, shape, dt: pes.enter_context(nc.sbuf_tensor(uniq(name), shape, dt))
                wcv = sb("wcv", [128, 8, 2048], BF16)
                dg = sb("dg", [128, 8 * CW, 128], BF16)
                cw = sb("cw", [128, 8, CW], F32)
                ht = [sb(f"ht{i}", [128, 8, T], F32) for i in range(2)]
                xn2 = [sb(f"xn{i}", [128, 8, T], BF16) for i in range(2)]
                sqt = sb("sqt", [128, 2, T], BF16)
                rstd2 = [sb(f"rstd{i}", [128, T], F32) for i in range(2)]
                sg = [sb(f"sg{i}", [128, T], F32) for i in range(2)]
                u0 = sb("u0", [128, 8, 30 + T], BF16)
                cv = sb("cv", [128, 8, T], F32)
                cvb = sb("cvb", [128, 2, T], BF16)
                mean = sb("mean", [128, T], F32)
                lrs = sb("lrs", [128, T], F32)
                tmp = [sb(f"tmp{i}", [128, T], F32) for i in range(2)]
                ub = [sb(f"ub{i}", [128, 8, T], BF16) for i in range(2)]
                B_w = P.bufs(8, "wcv")
                B_dg = P.buf("dg")
                B_cw = P.buf("cw")
                B_ht = P.bufs(2, "ht")
                B_xn2 = P.bufs(2, "xn")
                B_sq = P.bufs(2, "sq")
                B_rstd2 = P.bufs(2, "rstd")
                B_sg = P.bufs(2, "sg")
                B_u0 = P.buf("u0")
                B_cv = P.buf("cv")
                B_cvb = P.bufs(2, "cvb")
                B_mean = P.buf("mean")
                B_lrs = P.buf("lrs")
                B_tmp = P.bufs(2, "tmp")
                B_ub = P.bufs(2, "ub")
                for k in range(8):
                    load_w(wcv[:, k, :], hyb_win[sl(k), 0:2048], B_w[k])
                P.dma("sp", cw[:], cw_d, writes=[B_cw], slot=B_cw)
                for c in range(8):
                    P.op("pool", lambda e, c=c: e.tensor_tensor(
                        dg[:, c * CW:(c + 1) * CW, :],
                        identb[:].unsqueeze(1).broadcast_to([128, CW, 128]),
                        cw[:, c, :].unsqueeze(2).broadcast_to([128, CW, 128]), ALU.mult),
                        reads=[B_cw, Bc], writes=[B_dg])
                P.op("dve", lambda e: e.memset(u0[:, :, 0:30], 0.0), writes=[B_u0])
                def prologue(it_):
                    s_ = it_ % 2
                    P.dma("sp", ht[s_][:], Hv[:, :, it_ * T:(it_ + 1) * T], reads=[B_H[it_]], writes=[B_ht[s_]], slot=B_ht[s_])
                    make_xn(ht[s_], B_ht[s_], V_NMIX + 0, xn2[s_], B_xn2[s_], sqt, B_sq, rstd2[s_], B_rstd2[s_], pb[6], Bpb[6])

                prologue(0)
                for it in range(nt):
                    s = it % 2
                    t0 = it * T
                    hs = ht[s]
                    xn = xn2[s]
                    B_xn = B_xn2[s]
                    if it > 0:
                        P.op("dve", lambda e: e.tensor_copy(u0[:, :, 0:30], u0[:, :, T:T + 30]),
                             reads=[B_u0], writes=[B_u0])
                    for c in range(8):
                        jb = c % 2
                        P.op("pe", [lambda e, k=k, c=c, jb=jb, xn=xn: e.matmul(
                            pb[jb][:, 0:T], wcv[:, k, sl(c)], xn[:, k, :], start=(k == 0), stop=(k == 7))
                            for k in range(8)], reads=[B_xn] + B_w, writes=[Bpb[jb]])
                        P.op("pe", [lambda e, k=k, c=c, jb=jb, xn=xn: e.matmul(
                            pb[2 + jb][:, 0:T], wcv[:, k, 1024 + c * 128:1024 + (c + 1) * 128], xn[:, k, :],
                            start=(k == 0), stop=(k == 7))
                            for k in range(8)], reads=[B_xn] + B_w, writes=[Bpb[2 + jb]])
                        P.op("act", lambda e, jb=jb: e.activation(sg[jb][:], pb[2 + jb][:, 0:T], AF.Sigmoid),
                             reads=[Bpb[2 + jb]], writes=[B_sg[jb]])
                        P.op("dve", lambda e, c=c, jb=jb: e.tensor_tensor(
                            u0[:, c, 30:30 + T], sg[jb][:], pb[jb][:, 0:T], ALU.mult),
                            reads=[B_sg[jb], Bpb[jb]], writes=[B_u0])
                    if it + 1 < nt:
                        prologue(it + 1)
                    for c in range(8):
                        mb = 4 + c % 2
                        q = c % 2
                        P.op("pe", [lambda e, k=k, c=c, mb=mb: e.matmul(
                            pb[mb][:, 0:T], dg[:, c * CW + k, :], u0[:, c, k:k + T],
                            start=(k == 0), stop=(k == CW - 1))
                            for k in range(CW)], reads=[B_u0, B_dg], writes=[Bpb[mb]])
                        P.op("act", lambda e, c=c, mb=mb: e.activation(
                            cv[:, c, :], pb[mb][:, 0:T], AF.Identity, bias=vec[:, V_CVB, c:c + 1], scale=1.0),
                            reads=[Bpb[mb], Bc], writes=[B_cv])
                        P.op("dve", lambda e, c=c, q=q: e.tensor_copy(cvb[:, q, :], cv[:, c, :]),
                             reads=[B_cv], writes=[B_cvb[q]])
                        P.op("pe", lambda e, c=c, q=q: e.matmul(
                            pb[6][:, 0:T], ones1k[:], cvb[:, q, :], start=(c == 0), stop=(c == 7)),
                            reads=[B_cvb[q], Bc], writes=[Bpb[6]])
                    P.op("act", lambda e: e.copy(mean[:], pb[6][:, 0:T]), reads=[Bpb[6]], writes=[B_mean])
                    for c in range(8):
                        q = c % 2
                        P.op("dve", lambda e, c=c, q=q: e.tensor_tensor(tmp[q][:], cv[:, c, :], mean[:], ALU.subtract),
                             reads=[B_cv, B_mean], writes=[B_tmp[q]])
                        P.op("act", lambda e, q=q: e.activation(sqt[:, q, :], tmp[q][:], AF.Square),
                             reads=[B_tmp[q]], writes=[B_sq[q]])
                        P.op("pe", lambda e, c=c, q=q: e.matmul(
                            pb[7][:, 0:T], ones1k[:], sqt[:, q, :], start=(c == 0), stop=(c == 7)),
                            reads=[B_sq[q], Bc], writes=[Bpb[7]])
                    P.op("act", lambda e: e.activation(lrs[:], pb[7][:, 0:T], AF.Sqrt, bias=epsc, scale=1.0),
                         reads=[Bpb[7], Bc], writes=[B_lrs])
                    P.op("dve", lambda e: e.reciprocal(lrs[:], lrs[:]), reads=[B_lrs], writes=[B_lrs])
                    for c in range(8):
                        q = c % 2
                        P.op("dve", lambda e, c=c, q=q: e.tensor_tensor(tmp[q][:], cv[:, c, :], mean[:], ALU.subtract),
                             reads=[B_cv, B_mean], writes=[B_tmp[q]])
                        P.op("dve", lambda e, q=q: e.tensor_tensor(tmp[q][:], tmp[q][:], lrs[:], ALU.mult),
                             reads=[B_lrs, B_tmp[q]], writes=[B_tmp[q]])
                        P.op("act", lambda e, c=c, q=q, s=s: e.activation(
                            ub[s][:, c, :], tmp[q][:], AF.Silu, bias=vec[:, V_LNB, c:c + 1],
                            scale=vec[:, V_LNG, c:c + 1]),
                            reads=[B_tmp[q], Bc], writes=[B_ub[s]])
                    P.dma("sp", Uv[:, :, t0:t0 + T], ub[s][:], reads=[B_ub[s]], writes=[B_U[it]], slot=B_ub[s])
                P.end_phase()

        def ssd_phase():
            with ExitStack() as pes:
                sb = lambda name, shape, dt: pes.enter_context(nc.sbuf_tensor(uniq(name), shape, dt))
                NZ = HYB_IN - 2048
                wz = sb("wz", [128, 8, NZ], BF16)
                wo = sb("wo", [128, 16, D], BF16)
                scw = sb("scw", [128, 12, 5], F32)
                h16 = sb("h16", [128, 3, 16], F32)
                abc = sb("abc", [128, 16], F32)
                ht = [sb(f"ht{i}", [128, 8, T], F32) for i in range(2)]
                xn2 = [sb(f"xn{i}", [128, 8, T], BF16) for i in range(2)]
                sqt = sb("sqt", [128, 2, T], BF16)
                rstd2 = [sb(f"rstd{i}", [128, T], F32) for i in range(2)]
                sz = sb("sz", [128, 8, T], F32)
                xb = sb("xb", [128, 12, 3 + T], BF16)
                dg4 = sb("dg4", [128, 48, 128], BF16)
                xsf = sb("xsf", [128, 8, T], F32)
                xsb = sb("xsb", [128, 8, T], BF16)
                bcb = sb("bcb", [128, 4, T], BF16)
                dtt = sb("dtt", [128, 16], F32)
                adt = sb("adt", [128, 16], F32)
                acs = sb("acs", [128, 16], F32)
                ala = sb("ala", [128, 16], F32)
                cdec = sb("cdec", [128, 16], F32)
                coef = sb("coef", [128, 16], F32)
                xdt = sb("xdt", [128, 16, 64], BF16)
                xdd = sb("xdd", [128, 16, 64], BF16)
                btm = sb("btm", [128, 2, 128], BF16)
                R = sb("R", [128, 8, 128], F32)
                dif = sb("dif", [128, 8, 128], F32)
                erow = sb("erow", [128, 8, 128], F32)
                MT = sb("MT", [128, 8, 128], BF16)
                Cs = sb("Cs", [128, 8, 128], BF16)
                cbm = sb("cbm", [128, 2, 128], F32)
                prev = sb("prev", [128, 16, 64], F32)
                prevb = sb("prevb", [128, 16, 64], BF16)
                yg = sb("yg", [128, 8, T], F32)
                yn = sb("yn", [128, 8, T], BF16)
                grs = [sb(f"grs{i}", [128, T], F32) for i in range(2)]
                ut = sb("ut", [128, 8, T], BF16)
                B_wz = P.bufs(8, "wz")
                B_wo = P.bufs(2, "wo")
                B_sm = P.buf("small")
                B_ht = P.bufs(2, "ht")
                B_xn2 = P.bufs(2, "xn")
                B_sq = P.bufs(2, "sq")
                B_rstd2 = P.bufs(2, "rstd")
                B_sz = P.buf("sz")
                B_xb = P.buf("xb")
                B_dg4 = P.buf("dg4")
                B_xs = P.buf("xs")
                B_bc = P.buf("bcb")
                B_dt = P.buf("dt")
                B_co = P.buf("coefs")
                B_xdt = P.buf("xdt")
                B_btm = P.buf("btm")
                B_R = P.buf("R")
                B_dif = P.buf("dif")
                B_er = P.buf("erow")
                B_MT = P.buf("MT")
                B_Cs = P.buf("Cs")
                B_cbm = P.buf("cbm")
                B_prev = P.buf("prev")
                B_prevb = P.buf("prevb")
                B_yg = P.buf("yg")
                B_yn = P.buf("yn")
                B_grs = P.bufs(2, "grs")
                B_ut = P.buf("ut")
                for k in range(8):
                    load_w(wz[:, k, :], hyb_win[sl(k), 2048:HYB_IN], B_wz[k])
                wo_v = hyb_wout.rearrange("(j p) m -> p j m", p=128)
                for hh in range(2):
                    load_w(wo[:, hh * 8:(hh + 1) * 8, :], wo_v[:, hh * 8:(hh + 1) * 8, :], B_wo[hh])
                P.dma("sp", scw[:], scw_d, writes=[B_sm], slot=B_sm)
                for i in range(3):
                    P.dma("sp", h16[:, i, :], h16_d[i].partition_broadcast(128), writes=[B_sm], slot=B_sm)
                P.op("act", lambda e: e.activation(abc[:], h16[:, 1, :], AF.Exp), reads=[B_sm], writes=[B_sm])
                P.op("dve", lambda e: e.tensor_scalar_mul(abc[:], abc[:], -1.0), reads=[B_sm], writes=[B_sm])
                for c in range(12):
                    P.op("pool", lambda e, c=c: e.tensor_tensor(
                        dg4[:, c * 4:(c + 1) * 4, :],
                        identb[:].unsqueeze(1).broadcast_to([128, 4, 128]),
                        scw[:, c, 0:4].unsqueeze(2).broadcast_to([128, 4, 128]), ALU.mult),
                        reads=[B_sm, Bc], writes=[B_dg4])
                P.op("dve", lambda e: e.memset(xb[:, :, 0:3], 0.0), writes=[B_xb])
                P.op("dve", lambda e: e.memset(prev[:], 0.0), writes=[B_prev])
                P.op("dve", lambda e: e.memset(prevb[:], 0.0), writes=[B_prevb])
                def prologue(it_):
                    s_ = it_ % 2
                    P.dma("sp", ht[s_][:], Hv[:, :, it_ * T:(it_ + 1) * T], reads=[B_H[it_]], writes=[B_ht[s_]], slot=B_ht[s_])
                    make_xn(ht[s_], B_ht[s_], V_NMIX + 0, xn2[s_], B_xn2[s_], sqt, B_sq, rstd2[s_], B_rstd2[s_], pb[6], Bpb[6])

                prologue(0)
                for it in range(nt):
                    s = it % 2
                    t0 = it * T
                    hs = ht[s]
                    xn = xn2[s]
                    B_xn = B_xn2[s]
                    P.dma("sp", ut[:], Uv[:, :, t0:t0 + T], reads=[B_U[it]], writes=[B_ut], slot=B_ut)
                    if it > 0:
                        P.op("dve", lambda e: e.tensor_copy(xb[:, :, 0:3], xb[:, :, T:T + 3]),
                             reads=[B_xb], writes=[B_xb])
                    for c in range(8):
                        jb = c % 2
                        P.op("pe", [lambda e, k=k, c=c, jb=jb, xn=xn: e.matmul(
                            pb[jb][:, 0:T], wz[:, k, sl(c)], xn[:, k, :], start=(k == 0), stop=(k == 7))
                            for k in range(8)], reads=[B_xn] + B_wz, writes=[Bpb[jb]])
                        P.op("act", lambda e, c=c, jb=jb: e.activation(sz[:, c, :], pb[jb][:, 0:T], AF.Silu),
                             reads=[Bpb[jb]], writes=[B_sz])
                    for c in range(12):
                        jb = c % 2
                        P.op("pe", [lambda e, k=k, c=c, jb=jb, xn=xn: e.matmul(
                            pb[jb][:, 0:T], wz[:, k, 1024 + c * 128:1024 + (c + 1) * 128], xn[:, k, :],
                            start=(k == 0), stop=(k == 7))
                            for k in range(8)], reads=[B_xn] + B_wz, writes=[Bpb[jb]])
                        P.op("act", lambda e, c=c, jb=jb: e.copy(xb[:, c, 3:3 + T], pb[jb][:, 0:T]),
                             reads=[Bpb[jb]], writes=[B_xb])
                    for c in range(12):
                        mb = 2 + c % 2
                        P.op("pe", [lambda e, c=c, k=k, mb=mb: e.matmul(
                            pb[mb][:, 0:T], dg4[:, c * 4 + k, :], xb[:, c, k:k + T], start=(k == 0), stop=(k == 3))
                            for k in range(4)], reads=[B_xb, B_dg4], writes=[Bpb[mb]])
                        if c < 8:
                            P.op("act", lambda e, c=c, mb=mb: e.activation(
                                xsf[:, c, :], pb[mb][:, 0:T], AF.Silu, bias=scw[:, c, 4:5], scale=1.0),
                                reads=[Bpb[mb], B_sm], writes=[B_xs])
                            P.op("pool", lambda e, c=c: e.tensor_copy(xsb[:, c, :], xsf[:, c, :]),
                                 reads=[B_xs], writes=[B_xs])
                        else:
                            P.op("act", lambda e, c=c, mb=mb: e.activation(
                                bcb[:, c - 8, :], pb[mb][:, 0:T], AF.Silu, bias=scw[:, c, 4:5], scale=1.0),
                                reads=[Bpb[mb], B_sm], writes=[B_bc])
                    if it + 1 < nt:
                        prologue(it + 1)
                    for cch in range(NB if STG >= 2 else 0):
                        csl = slice(cch * 128, (cch + 1) * 128)
                        P.op("pe", [lambda e, k=k, csl=csl, xn=xn: e.matmul(
                            pb[5][:, 256:272], xn[:, k, csl], wz[:, k, 2560:2576], start=(k == 0), stop=(k == 7))
                            for k in range(8)], reads=[B_xn] + B_wz, writes=[Bpb[5]])
                        P.op("dve", lambda e: e.tensor_tensor(dtt[:], pb[5][:, 256:272], h16[:, 0, :], ALU.add),
                             reads=[Bpb[5], B_sm], writes=[B_dt])
                        P.op("act", lambda e: e.activation(dtt[:], dtt[:], AF.Exp), reads=[B_dt], writes=[B_dt])
                        P.op("act", lambda e: e.activation(dtt[:], dtt[:], AF.Ln, bias=onec, scale=1.0),
                             reads=[B_dt, Bc], writes=[B_dt])
                        P.op("dve", lambda e: e.tensor_tensor(adt[:], dtt[:], abc[:], ALU.mult),
                             reads=[B_dt, B_sm], writes=[B_dt])
                        P.op("pe", [lambda e: e.matmul(pb[5][:, 272:288], triu[:], adt[:], start=True, stop=True),
                                    lambda e: e.matmul(pb[5][:, 288:304], ones_f[:], adt[:], start=True, stop=True)],
                             reads=[B_dt, Bc], writes=[Bpb[5]])
                        P.op("dve", lambda e: e.tensor_copy(acs[:], pb[5][:, 272:288]), reads=[Bpb[5]], writes=[B_co])
                        P.op("dve", lambda e: e.tensor_copy(ala[:], pb[5][:, 288:304]), reads=[Bpb[5]], writes=[B_co])
                        P.op("act", lambda e: e.activation(cdec[:], ala[:], AF.Exp), reads=[B_co], writes=[B_co])
                        P.op("dve", lambda e: e.tensor_tensor(coef[:], ala[:], acs[:], ALU.subtract),
                             reads=[B_co], writes=[B_co])
                        P.op("act", lambda e: e.activation(coef[:], coef[:], AF.Exp), reads=[B_co], writes=[B_co])
                        P.op("dve", lambda e: e.tensor_tensor(coef[:], coef[:], dtt[:], ALU.mult),
                             reads=[B_co, B_dt], writes=[B_co])
                        if STG < 3:
                            continue
                        pbt = pb[4][:].bitcast(BF16)
                        P.op("pe", [lambda e, c=c, csl=csl: e.transpose(pbt[:, sl(c)], xsb[:, c, csl], identb[:])
                                    for c in range(8)], reads=[B_xs, Bc], writes=[Bpb[4]])
                        pbt3 = pbt.rearrange("p (h d) -> p h d", h=16)
                        P.op("dve", lambda e: e.tensor_tensor(
                            xdt[:], pbt3, dtt[:].unsqueeze(2).broadcast_to([128, 16, 64]), ALU.mult),
                            reads=[Bpb[4], B_dt], writes=[B_xdt])
                        P.op("dve", lambda e: e.tensor_tensor(
                            xdd[:], pbt3, coef[:].unsqueeze(2).broadcast_to([128, 16, 64]), ALU.mult),
                            reads=[Bpb[4], B_co], writes=[B_xdt])
                        P.op("pe", [lambda e, g=g, csl=csl: e.transpose(pbt[:, sl(g)], bcb[:, g, csl], identb[:])
                                    for g in range(2)], reads=[B_bc, Bc], writes=[Bpb[4]])
                        P.op("act", lambda e: e.copy(btm[:], pbt[:, 0:256].rearrange("p (g n) -> p g n", g=2)),
                             reads=[Bpb[4]], writes=[B_btm])
                        P.op("pe", [lambda e, g=g, csl=csl: e.matmul(
                            pb[5][:, sl(g)], bcb[:, g, csl], bcb[:, 2 + g, csl], start=True, stop=True)
                            for g in range(2)], reads=[B_bc], writes=[Bpb[5]])
                        P.op("dve", lambda e: e.tensor_tensor(
                            cbm[:], pb[5][:, 0:256].rearrange("p (g n) -> p g n", g=2),
                            triu[:].unsqueeze(1).broadcast_to([128, 2, 128]), ALU.mult),
                            reads=[Bpb[5], Bc], writes=[B_cbm])
                        if STG < 4:
                            continue
                        for g in range(2):
                            hsl = slice(g * 8, (g + 1) * 8)
                            P.op("dve", lambda e, hsl=hsl: e.tensor_tensor(
                                R[:], triu[:].unsqueeze(1).broadcast_to([128, 8, 128]),
                                adt[:, hsl].unsqueeze(2).broadcast_to([128, 8, 128]), ALU.mult),
                                reads=[B_dt, Bc], writes=[B_R])
                            P.op("pe", [lambda e, h2=h2: e.matmul(
                                pb[h2 // 2][:, (h2 % 2) * 256:(h2 % 2) * 256 + 256], ones_f[:],
                                R[:, h2 * 2:(h2 + 1) * 2, :], start=True, stop=True)
                                for h2 in range(4)], reads=[B_R, Bc], writes=[Bpb[0], Bpb[1]])
                            for hh in range(2):
                                h4 = slice(hh * 4, (hh + 1) * 4)
                                a4 = slice(g * 8 + hh * 4, g * 8 + hh * 4 + 4)
                                rb = pb[hh][:].rearrange("p (h l) -> p h l", h=4)
                                P.op("dve", lambda e, h4=h4, a4=a4, rb=rb: e.tensor_tensor(
                                    dif[:, h4, :], rb, acs[:, a4].unsqueeze(2).broadcast_to([128, 4, 128]),
                                    ALU.subtract), reads=[Bpb[hh], B_co], writes=[B_dif])
                                P.op("act", lambda e, h4=h4, rb=rb: e.activation(erow[:, h4, :], rb, AF.Exp),
                                     reads=[Bpb[hh]], writes=[B_er])
                            if SUB != 1:
                                P.op("act", lambda e: e.activation(dif[:], dif[:], AF.Exp), reads=[B_dif], writes=[B_dif])
                            P.op("dve", lambda e, g=g: e.scalar_tensor_tensor(
                                MT[:], dif[:], 1.0, cbm[:, g, :].unsqueeze(1).broadcast_to([128, 8, 128]),
                                ALU.min, ALU.mult), reads=[B_dif, B_cbm], writes=[B_MT])
                            P.op("dve", lambda e, g=g, csl=csl: e.tensor_tensor(
                                Cs[:], erow[:], bcb[:, 2 + g, csl].unsqueeze(1).broadcast_to([128, 8, 128]),
                                ALU.mult), reads=[B_er, B_bc], writes=[B_Cs])
                            for hh in range(8 if STG >= 5 else 0):
                                h = g * 8 + hh
                                cch_out = h // 2
                                half = h % 2
                                bank = pb[2 + cch_out // 4]
                                col = (cch_out % 4) * 128
                                P.op("pe", [
                                    lambda e, h=h, hh=hh, half=half, bank=bank, col=col: e.matmul(
                                        bank[half * 64:(half + 1) * 64, col:col + 128], xdt[:, h, :], MT[:, hh, :],
                                        start=True, stop=False),
                                    lambda e, h=h, hh=hh, half=half, bank=bank, col=col: e.matmul(
                                        bank[half * 64:(half + 1) * 64, col:col + 128], prevb[:, h, :], Cs[:, hh, :],
                                        start=False, stop=True)],
                                    reads=[B_xdt, B_MT, B_prevb, B_Cs], writes=[Bpb[2 + cch_out // 4]])
                            if SUB != 2:
                              P.op("pe", lambda e, g=g: e.matmul(
                                pb[6 + g][:], btm[:, g, :], xdd[:, g * 8:(g + 1) * 8, :], start=True, stop=True),
                                reads=[B_btm, B_xdt], writes=[Bpb[6 + g]])
                        for g in range(2 if SUB != 2 else 0):
                            hsl = slice(g * 8, (g + 1) * 8)
                            P.op("dve", lambda e, hsl=hsl: e.tensor_tensor(
                                prev[:, hsl, :], prev[:, hsl, :],
                                cdec[:, hsl].unsqueeze(2).broadcast_to([128, 8, 64]), ALU.mult),
                                reads=[B_co, B_prev], writes=[B_prev])
                            P.op("dve", lambda e, hsl=hsl, g=g: e.tensor_tensor(
                                prev[:, hsl, :], prev[:, hsl, :],
                                pb[6 + g][:].rearrange("p (h d) -> p h d", h=8), ALU.add),
                                reads=[Bpb[6 + g], B_prev], writes=[B_prev])
                        P.op("act", lambda e: e.copy(prevb[:], prev[:]), reads=[B_prev], writes=[B_prevb])
                        for c in range(8):
                            bank = pb[2 + c // 4]
                            col = (c % 4) * 128
                            P.op("dve", lambda e, c=c, bank=bank, col=col, csl=csl: e.scalar_tensor_tensor(
                                yg[:, c, csl], xsf[:, c, csl], vec[:, V_SSD, c:c + 1], bank[:, col:col + 128],
                                ALU.mult, ALU.add), reads=[Bpb[2 + c // 4], B_xs, Bc], writes=[B_yg])
                    P.op("pool", lambda e: e.tensor_tensor(yg[:], yg[:], sz[:], ALU.mult),
                         reads=[B_sz, B_yg], writes=[B_yg])
                    for g in range(2):
                        rms_rstd(yg, B_yg, 4, ones512, sqt, B_sq, pb[6], Bpb[6], grs[g][:], B_grs[g], k0=g * 4)
                    for c in range(8):
                        P.op("dve", lambda e, c=c: e.scalar_tensor_tensor(
                            yn[:, c, :], yg[:, c, :], vec[:, V_SSN, c:c + 1], grs[c // 4][:], ALU.mult, ALU.mult),
                            reads=[B_yg, Bc, B_grs[c // 4]], writes=[B_yn])
                    for m in range(8):
                        mb = m % 2
                        P.op("pe", [lambda e, j=j, m=m, mb=mb: e.matmul(
                            pb[mb][:, 0:T], wo[:, j, sl(m)], (ut[:, j, :] if j < 8 else yn[:, j - 8, :]),
                            start=(j == 0), stop=(j == 15))
                            for j in range(16)], reads=[B_ut, B_yn] + B_wo, writes=[Bpb[mb]])
                        P.op("dve", lambda e, m=m, mb=mb, hs=hs: e.tensor_tensor(
                            hs[:, m, :], hs[:, m, :], pb[mb][:, 0:T], ALU.add),
                            reads=[Bpb[mb], B_ht[s]], writes=[B_ht[s]])
                    P.dma("sp", Hv[:, :, t0:t0 + T], hs[:], reads=[B_ht[s]], writes=[B_H[it]], slot=B_ht[s])
                P.end_phase()

        def att_phase():
            with ExitStack() as pes:
                sb = lambda name, shape, dt: pes.enter_context(nc.sbuf_tensor(uniq(name), shape, dt))
                wq = sb("wq", [128, 8, 1536], BF16)
                wo = sb("wo", [128, 8, D], BF16)
                bqk = sb("bqk", [128, 10], F32)
                bvb = sb("bvb", [128, 256], F32)
                snk = sb("snk", [128, 16], F32)
                msk = sb("msk", [128, 2, 256], F32)
                ht = [sb(f"ht{i}", [128, 8, T], F32) for i in range(2)]
                xn2 = [sb(f"xn{i}", [128, 8, T], BF16) for i in range(2)]
                sqt = sb("sqt", [128, 2, T], BF16)
                rstd2 = [sb(f"rstd{i}", [128, T], F32) for i in range(2)]
                rope = sb("rope", [128, 2, T], F32)
                qf = [sb(f"qf{i}", [128, T], F32) for i in range(2)]
                qb = [sb(f"qb{i}", [128, T], BF16) for i in range(2)]
                t1 = [sb(f"t1{i}", [128, T], F32) for i in range(2)]
                qr = sb("qr", [128, 8, T], BF16)
                kr = sb("kr", [128, 2, 128 + T], BF16)
                vt = sb("vt", [128, 1 + NB, 256], BF16)
                sm = [sb(f"sm{i}", [128, 4, 256], F32) for i in range(2)]
                ee = [sb(f"ee{i}", [128, 4, 256], F32) for i in range(2)]
                pp = [sb(f"pp{i}", [128, 4, 256], BF16) for i in range(2)]
                pT = [sb(f"pT{i}", [128, 8, 128], BF16) for i in range(2)]
                mx = [sb(f"mx{i}", [128, 4], F32) for i in range(2)]
                nmx = [sb(f"nmx{i}", [128, 4], F32) for i in range(2)]
                rs = [sb(f"rs{i}", [128, 4], F32) for i in range(2)]
                es_ = [sb(f"es_{i}", [128, 4], F32) for i in range(2)]
                oT = sb("oT", [128, 8, T], BF16)
                B_wq = P.bufs(8, "wq")
                B_wo = P.buf("wo")
                B_sm_ = P.buf("small")
                B_ht = P.bufs(2, "ht")
                B_xn2 = P.bufs(2, "xn")
                B_sq = P.bufs(2, "sq")
                B_rstd2 = P.bufs(2, "rstd")
                B_rope = P.buf("rope")
                B_qf = P.bufs(2, "qf")
                B_qb = P.bufs(2, "qb")
                B_t1 = P.bufs(2, "t1")
                B_qr = P.buf("qr")
                B_kr = P.buf("kr")
                B_vt = P.buf("vt")
                B_s = P.bufs(2, "sm")
                B_e = P.bufs(2, "ee")
                B_p = P.bufs(2, "pp")
                B_pT = P.bufs(2, "pT")
                B_st = P.bufs(2, "stats")
                B_oT = P.buf("oT")
                for k in range(8):
                    load_w(wq[:, k, :], wqkv_d[sl(k), :], B_wq[k])
                load_w(wo[:], wo_d.rearrange("(k p) m -> p k m", p=128), B_wo)
                P.dma("sp", bqk[:], bqk_d, writes=[B_sm_], slot=B_sm_)
                P.dma("sp", bvb[:], bv_d.partition_broadcast(128), writes=[B_sm_], slot=B_sm_)
                P.dma("sp", snk[:], h16_d[2].partition_broadcast(128), writes=[B_sm_], slot=B_sm_)
                P.dma("sp", msk[:], msk_d.rearrange("a p s -> p a s"), writes=[B_sm_], slot=B_sm_)
                P.op("dve", lambda e: e.memset(kr[:, :, 0:128], 0.0), writes=[B_kr])
                P.op("dve", lambda e: e.memset(vt[:, 0, :], 0.0), writes=[B_vt])
                def prologue(it_):
                    s_ = it_ % 2
                    P.dma("sp", ht[s_][:], Hv[:, :, it_ * T:(it_ + 1) * T], reads=[B_H[it_]], writes=[B_ht[s_]], slot=B_ht[s_])
                    make_xn(ht[s_], B_ht[s_], V_NMIX + 1, xn2[s_], B_xn2[s_], sqt, B_sq, rstd2[s_], B_rstd2[s_], pb[6], Bpb[6])

                prologue(0)
                for it in range(nt):
                    s = it % 2
                    t0 = it * T
                    hs = ht[s]
                    xn = xn2[s]
                    B_xn = B_xn2[s]
                    P.dma("sp", rope[:], rope_d[:, :, t0:t0 + T].rearrange("a p s -> p a s"),
                          writes=[B_rope], slot=B_rope)
                    if it > 0:
                        P.op("dve", lambda e: e.tensor_copy(kr[:, :, 0:128], kr[:, :, T:T + 128]),
                             reads=[B_kr], writes=[B_kr])
                        P.op("dve", lambda e: e.tensor_copy(vt[:, 0, :], vt[:, NB, :]), reads=[B_vt], writes=[B_vt])
                    for c in range(10):
                        jb = c % 2
                        P.op("pe", [lambda e, k=k, c=c, jb=jb, xn=xn: e.matmul(
                            pb[jb][:, 0:T], wq[:, k, sl(c)], xn[:, k, :], start=(k == 0), stop=(k == 7))
                            for k in range(8)], reads=[B_xn] + B_wq, writes=[Bpb[jb]])
                        P.op("act", lambda e, c=c, jb=jb: e.activation(
                            qf[jb][:], pb[jb][:, 0:T], AF.Identity, bias=bqk[:, c:c + 1], scale=1.0),
                            reads=[Bpb[jb], B_sm_], writes=[B_qf[jb]])
                        P.op("pool", lambda e, jb=jb: e.tensor_copy(qb[jb][:], qf[jb][:]),
                             reads=[B_qf[jb]], writes=[B_qb[jb]])
                        P.op("pe", lambda e, jb=jb: e.matmul(pb[2 + jb][:, 0:T], pswap[:], qb[jb][:], start=True, stop=True),
                             reads=[B_qb[jb], Bc], writes=[Bpb[2 + jb]])
                        P.op("dve", lambda e, jb=jb: e.tensor_tensor(t1[jb][:], qf[jb][:], rope[:, 0, :], ALU.mult),
                             reads=[B_qf[jb], B_rope], writes=[B_t1[jb]])
                        P.op("dve", lambda e, jb=jb: e.tensor_tensor(qf[jb][:], pb[2 + jb][:, 0:T], rope[:, 1, :], ALU.mult),
                             reads=[Bpb[2 + jb], B_rope, B_qf[jb]], writes=[B_qf[jb]])
                        dst = qr[:, c, :] if c < 8 else kr[:, c - 8, 128:128 + T]
                        P.op("dve", lambda e, jb=jb, dst=dst: e.tensor_tensor(dst, t1[jb][:], qf[jb][:], ALU.add),
                             reads=[B_t1[jb], B_qf[jb]], writes=[B_qr if c < 8 else B_kr])
                    for b in range(NB):
                        jb = b % 2
                        P.op("pe", [lambda e, k=k, b=b, jb=jb, xn=xn: e.matmul(
                            pb[jb][:, 0:256], xn[:, k, sl(b)], wq[:, k, 1280:1536], start=(k == 0), stop=(k == 7))
                            for k in range(8)], reads=[B_xn] + B_wq, writes=[Bpb[jb]])
                        P.op("dve", lambda e, b=b, jb=jb: e.tensor_tensor(vt[:, 1 + b, :], pb[jb][:, 0:256], bvb[:], ALU.add),
                             reads=[Bpb[jb], B_sm_], writes=[B_vt])
                    if it + 1 < nt:
                        prologue(it + 1)
                    def group_steps(b, kv, a):
                        gblk = it * NB + b
                        mi = 0 if gblk == 0 else 1
                        half = kv % 2
                        hp = slice(half * 64, (half + 1) * 64)
                        kc = kv // 2
                        qc0 = (kv // 2) * 4
                        S0, S1, PTb, Ob = 4 * a, 4 * a + 1, 4 * a + 2, 4 * a + 3
                        sm_, ee_, pp_, pT_ = sm[a], ee[a], pp[a], pT[a]
                        mx_, nmx_, rs_, es2 = mx[a], nmx[a], rs[a], es_[a]
                        Bs, Be, Bp, BpT, Bst = B_s[a], B_e[a], B_p[a], B_pT[a], B_st[a]
                        sg4 = snk[:, kv * 4:(kv + 1) * 4]
                        st = []
                        st.append(lambda: P.op("pe", [lambda e, i=i: e.matmul(
                            pb[S0 + i // 2][:, (i % 2) * 256:(i % 2) * 256 + 256],
                            qr[hp, qc0 + i, sl(b)], kr[hp, kc, b * 128:b * 128 + 256], start=True, stop=True)
                            for i in range(4)], reads=[B_qr, B_kr], writes=[Bpb[S0], Bpb[S1]]))
                        for hh in range(2):
                            st.append(lambda hh=hh: P.op("dve", lambda e: e.scalar_tensor_tensor(
                                sm_[:, hh * 2:hh * 2 + 2, :], pb[S0 + hh][:].rearrange("p (h s) -> p h s", h=2),
                                0.125, msk[:, mi, :].unsqueeze(1).broadcast_to([128, 2, 256]),
                                ALU.mult, ALU.add), reads=[Bpb[S0 + hh], B_sm_], writes=[Bs]))
                        st.append(lambda: P.op("dve", lambda e: e.tensor_reduce(mx_[:], sm_[:], AX.X, ALU.max),
                                               reads=[Bs], writes=[Bst]))
                        st.append(lambda: P.op("dve", lambda e: e.tensor_tensor(mx_[:], mx_[:], sg4, ALU.max),
                                               reads=[Bst, B_sm_], writes=[Bst]))
                        st.append(lambda: P.op("dve", lambda e: e.tensor_scalar_mul(nmx_[:], mx_[:], -1.0),
                                               reads=[Bst], writes=[Bst]))
                        st.append(lambda: P.op("dve", lambda e: e.tensor_tensor(es2[:], sg4, mx_[:], ALU.subtract),
                                               reads=[Bst, B_sm_], writes=[Bst]))
                        st.append(lambda: P.op("dve", lambda e: e.memset(rs_[:], 0.0), writes=[Bst]))
                        for i in range(4):
                            st.append(lambda i=i: P.op("act", lambda e: e.activation(
                                ee_[:, i, :], sm_[:, i, :], AF.Exp, bias=nmx_[:, i:i + 1], scale=1.0,
                                accum_out=rs_[:, i:i + 1]), reads=[Bs, Bst], writes=[Be, Bst]))
                        st.append(lambda: P.op("act", lambda e: e.activation(es2[:], es2[:], AF.Exp),
                                               reads=[Bst], writes=[Bst]))
                        st.append(lambda: P.op("dve", lambda e: e.tensor_tensor(rs_[:], rs_[:], es2[:], ALU.add),
                                               reads=[Bst], writes=[Bst]))
                        st.append(lambda: P.op("dve", lambda e: e.reciprocal(rs_[:], rs_[:]), reads=[Bst], writes=[Bst]))
                        st.append(lambda: P.op("dve", lambda e: e.tensor_tensor(
                            pp_[:], ee_[:], rs_[:].unsqueeze(2).broadcast_to([128, 4, 256]), ALU.mult),
                            reads=[Be, Bst], writes=[Bp]))
                        pbt = pb[PTb][:].bitcast(BF16)
                        st.append(lambda: P.op("pe", [lambda e, i=i, kb=kb: e.transpose(
                            pbt[:, sl(i * 2 + kb)], pp_[:, i, sl(kb)], identb[:])
                            for i in range(4) for kb in range(2)], reads=[Bp, Bc], writes=[Bpb[PTb]]))
                        st.append(lambda: P.op("act", lambda e: e.copy(pT_[:], pbt.rearrange("p (a q) -> p a q", a=8)),
                                               reads=[Bpb[PTb]], writes=[BpT]))
                        st.append(lambda: P.op("pe", [lambda e, i=i, kb=kb: e.matmul(
                            pb[Ob][hp, i * 128:(i + 1) * 128], vt[:, b + kb, kv * 64:(kv + 1) * 64],
                            pT_[:, i * 2 + kb, :], start=(kb == 0), stop=(kb == 1))
                            for i in range(4) for kb in range(2)], reads=[B_vt, BpT], writes=[Bpb[Ob]]))
                        st.append(lambda: P.op("act", lambda e: e.copy(
                            oT[hp, qc0:qc0 + 4, sl(b)], pb[Ob][hp, :].rearrange("p (i q) -> p i q", i=4)),
                            reads=[Bpb[Ob]], writes=[B_oT]))
                        return st

                    groups = [(b, kv) for b in range(NB) for kv in range(4)]
                    allst = [group_steps(b_, kv_, gi % 2) for gi, (b_, kv_) in enumerate(groups)]
                    L_ = len(allst[0])
                    H_ = (L_ + 1) // 2
                    for t_ in range((len(groups) - 1) * H_ + L_):
                        for gi in range(len(groups)):
                            k_ = t_ - gi * H_
                            if 0 <= k_ < L_:
                                allst[gi][k_]()
                    for m in range(8):
                        mb = m % 2
                        P.op("pe", [lambda e, j=j, m=m, mb=mb: e.matmul(
                            pb[mb][:, 0:T], wo[:, j, sl(m)], oT[:, j, :], start=(j == 0), stop=(j == 7))
                            for j in range(8)], reads=[B_oT, B_wo], writes=[Bpb[mb]])
                        P.op("dve", lambda e, m=m, mb=mb, hs=hs: e.scalar_tensor_tensor(
                            hs[:, m, :], pb[mb][:, 0:T], vec[:, V_BO, m:m + 1], hs[:, m, :], ALU.add, ALU.add),
                            reads=[Bpb[mb], B_ht[s], Bc], writes=[B_ht[s]])
                    P.dma("sp", Hv[:, :, t0:t0 + T], hs[:], reads=[B_ht[s]], writes=[B_H[it]], slot=B_ht[s])
                P.end_phase()

        last = phases[-1]
        for ph in phases:
            if ph == "ffn1_0":
                ffn_phase(0, 0, True, False, False)
            elif ph == "conv":
                conv_phase()
            elif ph == "ssd":
                ssd_phase()
            elif ph == "ffn2_0":
                ffn_phase(1, 0, False, True, False)
            elif ph == "ffn1_1":
                ffn_phase(2, 1, False, False, False)
            elif ph == "att":
                att_phase()
            elif ph == "ffn2_1":
                ffn_phase(3, 1, False, True, True)
        P.barrier()
        P.emit()
    return nc


Q_LOWER = [0, 1, 2, 3, 8, 9, 10, 11]
Q_UPPER = [4, 5, 6, 7, 12, 13, 14, 15]
HEAD_ORDER = [h for c in range(8) for h in (Q_LOWER[c], Q_UPPER[c])]


def _pk(v):
    v = np.asarray(v, np.float32)
    return np.array(v.reshape(-1, 128).T, dtype=np.float32, order='C', copy=True)


def prep_shared(inp, S=SEQ):
    f = lambda a: np.array(a, dtype=np.float32, order='C', copy=True)
    sh = {}
    sh["ffn_win"] = f(np.stack([inp["ffn1_w_in"][0], inp["ffn2_w_in"][0], inp["ffn1_w_in"][1], inp["ffn2_w_in"][1]]))
    sh["ffn_wout"] = f(np.stack([inp["ffn1_w_out"][0], inp["ffn2_w_out"][0], inp["ffn1_w_out"][1], inp["ffn2_w_out"][1]]))
    vec = np.zeros((128, 20, 8), np.float32)
    for li in range(2):
        vec[:, 0 + li] = _pk(inp["norm_ffn1"][li])
        vec[:, 2 + li] = _pk(inp["norm_mix"][li])
        vec[:, 4 + li] = _pk(inp["norm_ffn2"][li])
        vec[:, 6 + li] = _pk(inp["ple_norm"][li])
    vec[:, 8] = _pk(inp["final_norm"])
    vec[:, 9] = _pk(inp["conv_dw_b"][0])
    vec[:, 10] = _pk(inp["conv_ln_g"][0])
    vec[:, 11] = _pk(inp["conv_ln_b"][0])
    vec[:, 12] = _pk(np.repeat(np.asarray(inp["ssm_d"][0], np.float32), 64))
    vec[:, 13] = _pk(inp["ssm_norm"][0])
    vec[:, 14] = _pk(inp["att_b_o"][0])
    sh["vecs"] = vec
    sh["ple_wg"] = f(inp["ple_gate_w"])
    sh["ple_wp"] = f(inp["ple_proj_w"])
    sh["hyb_win"] = f(inp["hyb_w_in"][0])
    sh["hyb_wout"] = f(inp["hyb_w_out"][0])
    cw = np.asarray(inp["conv_dw_w"][0], np.float32)
    sh["cw"] = f(cw.T.reshape(8, 128, CW).transpose(1, 0, 2))
    scw = np.asarray(inp["ssm_conv_w"][0], np.float32)
    scb = np.asarray(inp["ssm_conv_b"][0], np.float32)
    sc = np.concatenate([scw, scb[None]], 0)
    sh["scw"] = f(sc.T.reshape(12, 128, 5).transpose(1, 0, 2))
    sinks = np.asarray(inp["att_sinks"][0], np.float32)
    sg_order = [HEAD_ORDER[((kv // 2) * 4 + i) * 2 + kv % 2] for kv in range(4) for i in range(4)]
    sh["h16"] = f(np.stack([inp["ssm_dt_bias"][0], inp["ssm_a_log"][0], sinks[sg_order]]))
    wqkv = np.asarray(inp["att_w_qkv"][0], np.float32)
    bqkv = np.asarray(inp["att_b_qkv"][0], np.float32)
    qcols = np.concatenate([np.arange(h * 64, (h + 1) * 64) for h in HEAD_ORDER])
    cols = np.concatenate([qcols, np.arange(1024, 1536)])
    sh["wqkv"] = f(wqkv[:, cols])
    bp = bqkv[cols]
    sh["bqk"] = _pk(bp[:1280])
    sh["bv"] = f(bp[1280:])
    sh["wo"] = f(np.asarray(inp["att_w_o"][0], np.float32)[qcols, :])
    inv = (np.float32(10000.0) ** (-np.arange(0, 64, 2, dtype=np.float32) / np.float32(64))).astype(np.float32)
    ang = (np.arange(S, dtype=np.float32)[:, None] * inv[None, :]).astype(np.float32)
    cos, sin = np.cos(ang).astype(np.float32), np.sin(ang).astype(np.float32)
    prt = np.arange(128)
    CC = cos.T[prt % 32]
    sgn = np.where((prt % 64) < 32, -1.0, 1.0).astype(np.float32)
    SSn = sin.T[prt % 32] * sgn[:, None]
    sh["rope"] = f(np.stack([CC, SSn]))
    cst = np.zeros((5, 128, 128), np.float32)
    cst[0] = np.eye(128)
    cst[1] = np.triu(np.ones((128, 128)))
    sw = np.zeros((128, 128), np.float32)
    for m in range(128):
        sw[(m + 32) % 64 + (m // 64) * 64, m] = 1.0
    cst[2] = sw
    sh["cst"] = cst
    q = np.arange(128)[:, None]
    sp = np.arange(256)[None, :]
    valid = np.where(sp < 128, sp > q, (sp - 128) <= q)
    m1 = np.where(valid, 0.0, NEG).astype(np.float32)
    m0 = np.where(valid & (sp >= 128), 0.0, NEG).astype(np.float32)
    sh["msk"] = f(np.stack([m0, m1]))
    return sh


_CACHE = {}


def kernel(**inputs):
    x = np.asarray(inputs["x"], np.float32)
    p = np.asarray(inputs["p"], np.float32)
    B, S, _ = x.shape
    sh = prep_shared(inputs, S)
    key = ("full", S)
    if key not in _CACHE:
        _CACHE[key] = build_program(S)
    nc = _CACHE[key]
    in_maps = []
    for b in range(B):
        m = dict(sh)
        m["x"] = np.array(x[b], dtype=np.float32, order='C', copy=True)
        m["p"] = np.array(p[:, b], dtype=np.float32, order='C', copy=True)
        in_maps.append(m)
    res = run_bass_kernel_spmd(nc, in_maps, core_ids=list(range(B)))
    return np.stack([np.asarray(r["y"], np.float32) for r in res.results], 0)
```

```python
from contextlib import ExitStack
import os
import numpy as np
import concourse.bass as bass
import concourse.mybir as mybir
from concourse.bass_utils import run_bass_kernel_spmd

F32 = mybir.dt.float32
BF16 = mybir.dt.bfloat16
AF = mybir.ActivationFunctionType
ALU = mybir.AluOpType
AX = mybir.AxisListType

D = 1024
DFF = 2816
NJ = DFF // 128
PLE = 256
SEQ = 4096
NCORES = 8
T = 256
NB = T // 128
EPS = 1e-6
CW = 31
HYB_IN = 4624
NEG = -30000.0
STG = int(os.environ.get('SSD_STAGE', '99'))
SUB = int(os.environ.get('SUB', '0'))


class Buf:
    __slots__ = ("name", "w", "r", "dsem", "excl")

    def __init__(self, name):
        self.name = name
        self.w = {}
        self.r = {}
        self.dsem = None
        self.excl = False


class Prog:
    ENGS = ("pe", "act", "dve", "pool", "sp")

    def __init__(self, nc, es):
        self.nc = nc
        self.es = es
        self.streams = {e: [] for e in self.ENGS}
        self.sems = {}
        self.cnt = {}
        self.waited = {e: {} for e in self.ENGS}
        self.nbuf = 0
        self.free_dsems = []
        self.phase_dsems = []
        self.ndsem = 0
        for e in ("pe", "act", "dve", "pool"):
            self._mksem("c_" + e)

    def _mksem(self, name):
        self.sems[name] = self.es.enter_context(self.nc.semaphore(name))
        self.cnt[name] = 0
        return name

    def _dsem(self):
        if self.free_dsems:
            s = self.free_dsems.pop()
        else:
            self.ndsem += 1
            s = self._mksem(f"d{self.ndsem}")
        self.phase_dsems.append(s)
        return s

    def buf(self, name=None):
        self.nbuf += 1
        return Buf(name or f"b{self.nbuf}")

    def bufs(self, n, name="b"):
        return [self.buf(f"{name}{i}") for i in range(n)]

    def _need(self, reads, writes):
        need = {}
        for b in reads:
            for s, v in b.w.items():
                if need.get(s, 0) < v:
                    need[s] = v
        for b in writes:
            for d in (b.w, b.r):
                for s, v in d.items():
                    if need.get(s, 0) < v:
                        need[s] = v
        return need

    def _emit_waits(self, eng, need, skip_own=False):
        wd = self.waited[eng]
        own = "c_" + eng
        for s, v in need.items():
            if skip_own and s == own:
                continue
            if wd.get(s, 0) < v:
                wd[s] = v
                h = self.sems[s]
                self.streams[eng].append(lambda e, h=h, v=v: e.wait_ge(h, v))

    def _mark(self, reads, writes, s, v):
        for b in reads:
            if b.r.get(s, 0) < v:
                b.r[s] = v
        for b in writes:
            b.w = {s: v}
            b.r = {}

    def op(self, eng, fns, reads=(), writes=(), skip_own=None):
        if callable(fns):
            fns = [fns]
        if skip_own is None:
            skip_own = (eng == "pe")
        ex = [b for b in reads if b.excl]
        if ex:
            writes = list(writes) + ex
            reads = [b for b in reads if not b.excl]
        self._emit_waits(eng, self._need(reads, writes), skip_own)
        s = "c_" + eng
        self.cnt[s] += 1
        v = self.cnt[s]
        h = self.sems[s]
        st = self.streams[eng]
        for f in fns[:-1]:
            st.append(f)
        last = fns[-1]
        st.append(lambda e, last=last, h=h: last(e).then_inc(h, 1))
        self._mark(reads, writes, s, v)

    def dma(self, q, out, in_, reads=(), writes=(), slot=None):
        self._emit_waits(q, self._need(reads, writes), False)
        if slot.dsem is None:
            slot.dsem = self._dsem()
        s = slot.dsem
        self.cnt[s] += 16
        v = self.cnt[s]
        h = self.sems[s]
        self.streams[q].append(lambda e, out=out, in_=in_, h=h: e.dma_start(out=out, in_=in_).then_inc(h, 16))
        self._mark(reads, writes, s, v)

    def barrier(self):
        need = {s: v for s, v in self.cnt.items() if v > 0}
        for e in self.ENGS:
            self._emit_waits(e, need, False)

    def end_phase(self):
        self.barrier()
        self.free_dsems.extend(self.phase_dsems)
        self.phase_dsems = []

    def emit(self):
        nc = self.nc
        st = self.streams
        with nc.Block() as block:
            @block.tensor
            def _(e):
                for f in st["pe"]:
                    f(e)

            @block.scalar
            def _(e):
                for f in st["act"]:
                    f(e)

            @block.vector
            def _(e):
                for f in st["dve"]:
                    f(e)

            @block.gpsimd
            def _(e):
                for f in st["pool"]:
                    f(e)

            @block.sync
            def _(e):
                for f in st["sp"]:
                    f(e)


class Ctx:
    pass


def sl(i, n=128):
    return slice(i * n, (i + 1) * n)


def build_program(S=SEQ, phases=("ffn1_0", "conv", "ssd", "ffn2_0", "ffn1_1", "att", "ffn2_1"), debug=False):
    nc = bass.Bass("TRN2", target_bir_lowering=False)
    nt = S // T
    dt_in = {}

    def din(name, shape):
        dt_in[name] = nc.dram_tensor(name, list(shape), F32, kind="ExternalInput").ap()
        return dt_in[name]

    x_d = din("x", [S, D])
    p_d = din("p", [2, S, PLE])
    ffn_win = din("ffn_win", [4, D, 2 * DFF])
    ffn_wout = din("ffn_wout", [4, DFF, D])
    vecs = din("vecs", [128, 20, 8])
    ple_wg = din("ple_wg", [2, D, D])
    ple_wp = din("ple_wp", [2, PLE, D])
    hyb_win = din("hyb_win", [D, HYB_IN])
    hyb_wout = din("hyb_wout", [2 * D, D])
    cw_d = din("cw", [128, 8, CW])
    scw_d = din("scw", [128, 12, 5])
    h16_d = din("h16", [3, 16])
    wqkv_d = din("wqkv", [D, 1536])
    bqk_d = din("bqk", [128, 10])
    bv_d = din("bv", [256])
    wo_d = din("wo", [D, D])
    rope_d = din("rope", [2, 128, S])
    cst_d = din("cst", [5, 128, 128])
    msk_d = din("msk", [2, 128, 256])
    if debug:
        out_d = nc.dram_tensor("H", [8, 128, S], F32, kind="ExternalOutput").ap()
        Hd = out_d
        yout_d = nc.dram_tensor("y", [S, D], F32, kind="ExternalOutput").ap()
    else:
        yout_d = nc.dram_tensor("y", [S, D], F32, kind="ExternalOutput").ap()
        Hd = nc.dram_tensor("Hs", [8, 128, S], F32).ap()
    Ud = nc.dram_tensor("Us", [8, 128, S], BF16).ap()
    Hv = Hd.rearrange("k p s -> p k s")
    Uv = Ud.rearrange("k p s -> p k s")

    with ExitStack() as es:
        P = Prog(nc, es)
        C = Ctx()
        gsb = lambda name, shape, dt: es.enter_context(nc.sbuf_tensor(name, shape, dt))
        pb = [es.enter_context(nc.psum_tensor(f"pb{i}", [128, 512], F32)) for i in range(8)]
        Bpb = P.bufs(8, "pb")
        for b_ in Bpb:
            b_.excl = True
        identf = gsb("identf", [128, 128], F32)
        identb = gsb("identb", [128, 128], BF16)
        triu = gsb("triu", [128, 128], F32)
        pswap = gsb("pswap", [128, 128], BF16)
        ones_f = gsb("ones_f", [128, 128], F32)
        ones1k = gsb("ones1k", [128, 128], BF16)
        ones512 = gsb("ones512", [128, 128], BF16)
        cols = gsb("cols", [128, 4], F32)
        vec = gsb("vec", [128, 20, 8], F32)
        Bc = P.buf("consts")
        P.dma("sp", identf[:], cst_d[0], writes=[Bc], slot=Bc)
        P.dma("sp", triu[:], cst_d[1], writes=[Bc], slot=Bc)
        P.dma("sp", vec[:], vecs, writes=[Bc], slot=Bc)
        P.dma("pool", identb[:], cst_d[0], writes=[Bc], slot=Bc)
        P.dma("pool", pswap[:], cst_d[2], writes=[Bc], slot=Bc)
        P.op("dve", lambda e: e.memset(ones_f[:], 1.0), writes=[Bc])
        P.op("dve", lambda e: e.memset(ones1k[:], 1.0 / 1024), writes=[Bc])
        P.op("dve", lambda e: e.memset(ones512[:], 1.0 / 512), writes=[Bc])
        P.op("dve", lambda e: e.memset(cols[:, 0:1], EPS), writes=[Bc])
        P.op("dve", lambda e: e.memset(cols[:, 1:2], 1.0), writes=[Bc])
        P.op("dve", lambda e: e.memset(cols[:, 2:3], 0.0), writes=[Bc])
        epsc = cols[:, 0:1]
        onec = cols[:, 1:2]
        B_H = P.bufs(nt, "H")
        B_U = P.bufs(nt, "U")
        B_Y = P.bufs(nt, "Y")
        P.end_phase()

        V_NF1, V_NMIX, V_NF2, V_PLE = 0, 2, 4, 6
        V_FIN, V_CVB, V_LNG, V_LNB, V_SSD, V_SSN, V_BO = 8, 9, 10, 11, 12, 13, 14

        def rms_rstd(xap, Bx, nk, ones_ap, sqt, Bsq, pbank, Bpbank, rstd, Brstd, k0=0):
            for k in range(nk):
                q = k % 2
                P.op("act", lambda e, k=k, q=q: e.activation(sqt[:, q, :], xap[:, k0 + k, :], AF.Square),
                     reads=[Bx], writes=[Bsq[q]])
                P.op("pe", lambda e, k=k, q=q: e.matmul(pbank[:, 0:T], ones_ap[:], sqt[:, q, :],
                                                       start=(k == 0), stop=(k == nk - 1)),
                     reads=[Bsq[q], Bc], writes=[Bpbank])
            P.op("act", lambda e: e.activation(rstd, pbank[:, 0:T], AF.Sqrt, bias=epsc, scale=1.0),
                 reads=[Bpbank, Bc], writes=[Brstd])
            P.op("dve", lambda e: e.reciprocal(rstd, rstd), reads=[Brstd], writes=[Brstd])

        def make_xn(ht_s, Bht, gidx, xn, Bxn, sqt, Bsq, rstd, Brstd, pbank, Bpbank):
            rms_rstd(ht_s, Bht, 8, ones1k, sqt, Bsq, pbank, Bpbank, rstd[:], Brstd)
            for k in range(8):
                P.op("dve", lambda e, k=k: e.scalar_tensor_tensor(
                    xn[:, k, :], ht_s[:, k, :], vec[:, gidx, k:k + 1], rstd[:], ALU.mult, ALU.mult),
                    reads=[Bht, Bc, Brstd], writes=[Bxn])

        ucnt = [0]

        def uniq(name):
            ucnt[0] += 1
            return f"s{ucnt[0]}_{name}"

        def load_w(dst, src_rows_ap, Bw, q="pool"):
            P.dma(q, dst, src_rows_ap, writes=[Bw], slot=Bw)

        def ffn_phase(fi, li, first, do_ple, final):
            with ExitStack() as pes:
                sb = lambda name, shape, dt: pes.enter_context(nc.sbuf_tensor(uniq(name), shape, dt))
                win = sb("win", [128, 8, 2 * DFF], BF16)
                wout = sb("wout", [128, NJ, D], BF16)
                ht = [sb(f"ht{i}", [128, 8, T], F32) for i in range(2)]
                xn = [sb(f"xn{i}", [128, 8, T], BF16) for i in range(2)]
                sqt = sb("sqt", [128, 2, T], BF16)
                rstd = [sb(f"rstd{i}", [128, T], F32) for i in range(2)]
                sg = [sb(f"sg{i}", [128, T], F32) for i in range(2)]
                hT = sb("hT", [128, NJ, T], BF16)
                B_win = P.bufs(1, "win")
                B_wout = P.bufs(2, "wout")
                B_ht = P.bufs(2, "ht")
                B_xn = P.bufs(2, "xn")
                B_sq = P.bufs(2, "sq")
                B_rstd = P.bufs(2, "rstd")
                B_sg = P.bufs(2, "sg")
                B_hT = P.buf("hT")
                if first:
                    xt = [sb("xt0", [128, D], F32)]
                    B_xt = P.bufs(1, "xt")
                if final:
                    yt = [sb(f"yt{i}", [128, 512], F32) for i in range(2)]
                    B_yt = P.bufs(2, "yt")
                if do_ple:
                    wg = sb("wg", [128, 8, D], BF16)
                    wp = sb("wp", [128, 2, D], BF16)
                    pt = [sb(f"pt{i}", [128, PLE], F32) for i in range(2)]
                    pT = sb("pT", [128, 2, T], BF16)
                    sg2 = [sb(f"sgp{i}", [128, T], F32) for i in range(2)]
                    B_wg = P.buf("wg")
                    B_wp = P.buf("wp")
                    B_pt = P.bufs(2, "pt")
                    B_pT = P.buf("pT")
                    B_sg2 = P.bufs(2, "sgp")
                for k in range(8):
                    load_w(win[:, k, :], ffn_win[fi, sl(k), :], B_win[0])
                wo_v = ffn_wout[fi].rearrange("(j p) m -> p j m", p=128)
                for hh in range(2):
                    load_w(wout[:, hh * 11:(hh + 1) * 11, :], wo_v[:, hh * 11:(hh + 1) * 11, :], B_wout[hh])
                if do_ple:
                    load_w(wg[:], ple_wg[li].rearrange("(k p) m -> p k m", p=128), B_wg)
                    load_w(wp[:], ple_wp[li].rearrange("(k p) m -> p k m", p=128), B_wp)
                gidx = (V_NF1 if not do_ple else V_NF2) + li

                def load_tile(it):
                    s = it % 2
                    t0 = it * T
                    hs = ht[s]
                    if first:
                        for b in range(NB):
                            P.dma("sp", xt[0][:], x_d[t0 + b * 128:t0 + (b + 1) * 128, :],
                                  writes=[B_xt[0]], slot=B_xt[0])
                            for hf in range(2):
                                P.op("pe", [lambda e, kk=kk, hf=hf: e.transpose(
                                    pb[7][:, sl(kk)], xt[0][:, sl(hf * 4 + kk)], identf[:]) for kk in range(4)],
                                    reads=[B_xt[0], Bc], writes=[Bpb[7]])
                                P.op("act", lambda e, hf=hf, b=b, hs=hs: e.copy(
                                    hs[:, hf * 4:(hf + 1) * 4, sl(b)], pb[7][:].rearrange("p (k t) -> p k t", k=4)),
                                    reads=[Bpb[7]], writes=[B_ht[s]])
                    else:
                        P.dma("sp", hs[:], Hv[:, :, t0:t0 + T], reads=[B_H[it]], writes=[B_ht[s]], slot=B_ht[s])

                def prologue(it):
                    s = it % 2
                    make_xn(ht[s], B_ht[s], gidx, xn[s], B_xn[s], sqt, B_sq, rstd[s], B_rstd[s], pb[7], Bpb[7])

                def sq_stat(hs_, s_, m_):
                    q_ = m_ % 2
                    P.op("act", lambda e: e.activation(sqt[:, q_, :], hs_[:, m_, :], AF.Square),
                         reads=[B_ht[s_]], writes=[B_sq[q_]])

                def pe_stat(m_):
                    q_ = m_ % 2
                    P.op("pe", lambda e: e.matmul(pb[6][:, 0:T], ones1k[:], sqt[:, q_, :],
                                                  start=(m_ == 0), stop=(m_ == 7)),
                         reads=[B_sq[q_], Bc], writes=[Bpb[6]])

                def finish_rstd(s_):
                    rs_ = rstd[s_]
                    P.op("act", lambda e: e.activation(rs_[:], pb[6][:, 0:T], AF.Sqrt, bias=epsc, scale=1.0),
                         reads=[Bpb[6], Bc], writes=[B_rstd[s_]])
                    P.op("dve", lambda e: e.reciprocal(rs_[:], rs_[:]), reads=[B_rstd[s_]], writes=[B_rstd[s_]])

                def p_dma(it):
                    t0 = it * T
                    for b in range(NB):
                        P.dma("sp", pt[b][:], p_d[li, t0 + b * 128:t0 + (b + 1) * 128, :],
                              writes=[B_pt[b]], slot=B_pt[b])

                def post_slots(it):
                    s = it % 2
                    t0 = it * T
                    hs = ht[s]
                    xs = xn[s]
                    sl_ = {}

                    def add(j, f):
                        sl_.setdefault(j, []).append(f)

                    def prep():
                        for b in range(NB):
                            P.op("pe", [lambda e, c=c, b=b: e.transpose(
                                pb[7][:, sl(c)], pt[b][:, sl(c)], identf[:]) for c in range(2)],
                                reads=[B_pt[b], Bc], writes=[Bpb[7]])
                            P.op("act", lambda e, b=b: e.copy(
                                pT[:, :, sl(b)], pb[7][:, 0:256].rearrange("p (k t) -> p k t", k=2)),
                                reads=[Bpb[7]], writes=[B_pT])
                        finish_rstd(s)
                        for k in range(8):
                            P.op("dve", lambda e, k=k: e.scalar_tensor_tensor(
                                xs[:, k, :], hs[:, k, :], vec[:, V_PLE + li, k:k + 1], rstd[s][:], ALU.mult, ALU.mult),
                                reads=[B_ht[s], Bc, B_rstd[s]], writes=[B_xn[s]])
                    add(0, prep)

                    def ple_group(m):
                        mb = m % 2
                        P.op("pe", [lambda e, k=k: e.matmul(
                            pb[4][:, 0:T], wg[:, k, sl(m)], xs[:, k, :], start=(k == 0), stop=(k == 7))
                            for k in range(8)], reads=[B_xn[s], B_wg], writes=[Bpb[4]])
                        P.op("pe", [lambda e, c=c: e.matmul(
                            pb[5][:, 0:T], wp[:, c, sl(m)], pT[:, c, :], start=(c == 0), stop=(c == 1))
                            for c in range(2)], reads=[B_pT, B_wp], writes=[Bpb[5]])
                        P.op("act", lambda e: e.activation(sg2[mb][:], pb[4][:, 0:T], AF.Tanh, scale=0.5),
                             reads=[Bpb[4]], writes=[B_sg2[mb]])
                        P.op("dve", lambda e: e.scalar_tensor_tensor(
                            sg2[mb][:], sg2[mb][:], 1.0, pb[5][:, 0:T], ALU.add, ALU.mult),
                            reads=[B_sg2[mb], Bpb[5]], writes=[B_sg2[mb]])
                        P.op("dve", lambda e: e.scalar_tensor_tensor(
                            hs[:, m, :], sg2[mb][:], 0.5, hs[:, m, :], ALU.mult, ALU.add),
                            reads=[B_sg2[mb], B_ht[s]], writes=[B_ht[s]])
                        if final:
                            sq_stat(hs, s, m)
                            if m > 0:
                                pe_stat(m - 1)
                    for m in range(8):
                        add(1 + m, lambda m=m: ple_group(m))

                    def fin_norm():
                        pe_stat(7)
                        finish_rstd(s)
                        for k in range(8):
                            P.op("dve", lambda e, k=k: e.scalar_tensor_tensor(
                                hs[:, k, :], hs[:, k, :], vec[:, V_FIN, k:k + 1], rstd[s][:], ALU.mult, ALU.mult),
                                reads=[B_ht[s], Bc, B_rstd[s]], writes=[B_ht[s]])

                    def out_block(b, hf):
                        P.op("pe", [lambda e, kk=kk: e.transpose(
                            pb[7][:, sl(kk)], hs[:, hf * 4 + kk, sl(b)], identf[:]) for kk in range(4)],
                            reads=[B_ht[s], Bc], writes=[Bpb[7]])
                        P.op("act", lambda e: e.copy(yt[hf][:], pb[7][:]), reads=[Bpb[7]], writes=[B_yt[hf]])
                        P.dma("sp", yout_d[t0 + b * 128:t0 + (b + 1) * 128, hf * 512:(hf + 1) * 512], yt[hf][:],
                              reads=[B_yt[hf]], writes=[B_Y[it]], slot=B_yt[hf])
                    if final:
                        add(9, fin_norm)
                        jj = 10
                        for b in range(NB):
                            for hf in range(2):
                                add(jj, lambda b=b, hf=hf: out_block(b, hf))
                                jj += 1
                    if not final or debug:
                        add(14, lambda: P.dma("sp", Hv[:, :, t0:t0 + T], hs[:], reads=[B_ht[s]],
                                              writes=[B_H[it]], slot=B_ht[s]))
                    return sl_

                load_tile(0)
                prologue(0)
                for it in range(nt):
                    s = it % 2
                    t0 = it * T
                    hs = ht[s]
                    xs = xn[s]
                    slots = post_slots(it - 1) if (do_ple and it > 0) else {}
                    if it + 1 < nt and not first and not do_ple:
                        load_tile(it + 1)
                    for j in range(NJ):
                        jb = j % 2
                        for f_ in slots.get(j, []):
                            f_()
                        if it + 1 < nt:
                            if first and j == 6:
                                load_tile(it + 1)
                            if do_ple and j == 15:
                                load_tile(it + 1)
                            if j == (18 if do_ple else 12):
                                prologue(it + 1)
                        P.op("pe", [lambda e, k=k, j=j, jb=jb, xs=xs: e.matmul(
                            pb[jb][:, 0:T], win[:, k, sl(j)], xs[:, k, :], start=(k == 0), stop=(k == 7))
                            for k in range(8)], reads=[B_xn[s], B_win[0]], writes=[Bpb[jb]])
                        P.op("pe", [lambda e, k=k, j=j, jb=jb, xs=xs: e.matmul(
                            pb[2 + jb][:, 0:T], win[:, k, DFF + j * 128:DFF + (j + 1) * 128], xs[:, k, :],
                            start=(k == 0), stop=(k == 7))
                            for k in range(8)], reads=[B_xn[s], B_win[0]], writes=[Bpb[2 + jb]])
                        P.op("act", lambda e, jb=jb: e.activation(sg[jb][:], pb[jb][:, 0:T], AF.Silu),
                             reads=[Bpb[jb]], writes=[B_sg[jb]])
                        P.op("dve", lambda e, j=j, jb=jb: e.tensor_tensor(
                            hT[:, j, :], sg[jb][:], pb[2 + jb][:, 0:T], ALU.mult),
                            reads=[B_sg[jb], Bpb[2 + jb]], writes=[B_hT])
                    if do_ple:
                        p_dma(it)
                    for m in range(8):
                        mb = 4 + m % 2
                        P.op("pe", [lambda e, j=j, m=m, mb=mb: e.matmul(
                            pb[mb][:, 0:T], wout[:, j, sl(m)], hT[:, j, :], start=(j == 0), stop=(j == NJ - 1))
                            for j in range(NJ)], reads=[B_hT] + B_wout, writes=[Bpb[mb]])
                        P.op("dve", lambda e, m=m, mb=mb, hs=hs: e.scalar_tensor_tensor(
                            hs[:, m, :], pb[mb][:, 0:T], 0.5, hs[:, m, :], ALU.mult, ALU.add),
                            reads=[Bpb[mb], B_ht[s]], writes=[B_ht[s]])
                        if do_ple:
                            sq_stat(hs, s, m)
                            if m > 0:
                                pe_stat(m - 1)
                    if do_ple:
                        pe_stat(7)
                    else:
                        P.dma("sp", Hv[:, :, t0:t0 + T], hs[:], reads=[B_ht[s]], writes=[B_H[it]], slot=B_ht[s])
                if do_ple:
                    last = post_slots(nt - 1)
                    for j in sorted(last):
                        for f_ in last[j]:
                            f_()
                P.end_phase()

        def conv_phase():
            with ExitStack() as pes:
                sb = lambda name, shape, dt: pes.enter_context(nc.sbuf_tensor(uniq(name), shape, dt))
                wcv = sb("wcv", [128, 8, 2048], BF16)
                dg = sb("dg", [128, 8 * CW, 128], BF16)
                cw = sb("cw", [128, 8, CW], F32)
                ht = [sb(f"ht{i}", [128, 8, T], F32) for i in range(2)]
                xn2 = [sb(f"xn{i}", [128, 8, T], BF16) for i in range(2)]
                sqt = sb("sqt", [128, 2, T], BF16)
                rstd2 = [sb(f"rstd{i}", [128, T], F32) for i in range(2)]
                sg = [sb(f"sg{i}", [128, T], F32) for i in range(2)]
                u0 = sb("u0", [128, 8, 30 + T], BF16)
                cv = sb("cv", [128, 8, T], F32)
                cvb = sb("cvb", [128, 2, T], BF16)
                mean = sb("mean", [128, T], F32)
                lrs = sb("lrs", [128, T], F32)
                tmp = [sb(f"tmp{i}", [128, T], F32) for i in range(2)]
                ub = [sb(f"ub{i}", [128, 8, T], BF16) for i in range(2)]
                B_w = P.bufs(8, "wcv")
                B_dg = P.buf("dg")
                B_cw = P.buf("cw")
                B_ht = P.bufs(2, "ht")
                B_xn2 = P.bufs(2, "xn")
                B_sq = P.bufs(2, "sq")
                B_rstd2 = P.bufs(2, "rstd")
                B_sg = P.bufs(2, "sg")
                B_u0 = P.buf("u0")
                B_cv = P.buf("cv")
                B_cvb = P.bufs(2, "cvb")
                B_mean = P.buf("mean")
                B_lrs = P.buf("lrs")
                B_tmp = P.bufs(2, "tmp")
                B_ub = P.bufs(2, "ub")
                for k in range(8):
                    load_w(wcv[:, k, :], hyb_win[sl(k), 0:2048], B_w[k])
                P.dma("sp", cw[:], cw_d, writes=[B_cw], slot=B_cw)
                for c in range(8):
                    P.op("pool", lambda e, c=c: e.tensor_tensor(
                        dg[:, c * CW:(c + 1) * CW, :],
                        identb[:].unsqueeze(1).broadcast_to([128, CW, 128]),
                        cw[:, c, :].unsqueeze(2).broadcast_to([128, CW, 128]), ALU.mult),
                        reads=[B_cw, Bc], writes=[B_dg])
                P.op("dve", lambda e: e.memset(u0[:, :, 0:30], 0.0), writes=[B_u0])
                def prologue(it_):
                    s_ = it_ % 2
                    P.dma("sp", ht[s_][:], Hv[:, :, it_ * T:(it_ + 1) * T], reads=[B_H[it_]], writes=[B_ht[s_]], slot=B_ht[s_])
                    make_xn(ht[s_], B_ht[s_], V_NMIX + 0, xn2[s_], B_xn2[s_], sqt, B_sq, rstd2[s_], B_rstd2[s_], pb[6], Bpb[6])

                prologue(0)
                for it in range(nt):
                    s = it % 2
                    t0 = it * T
                    hs = ht[s]
                    xn = xn2[s]
                    B_xn = B_xn2[s]
                    if it > 0:
                        P.op("dve", lambda e: e.tensor_copy(u0[:, :, 0:30], u0[:, :, T:T + 30]),
                             reads=[B_u0], writes=[B_u0])
                    for c in range(8):
                        jb = c % 2
                        P.op("pe", [lambda e, k=k, c=c, jb=jb, xn=xn: e.matmul(
                            pb[jb][:, 0:T], wcv[:, k, sl(c)], xn[:, k, :], start=(k == 0), stop=(k == 7))
                            for k in range(8)], reads=[B_xn] + B_w, writes=[Bpb[jb]])
                        P.op("pe", [lambda e, k=k, c=c, jb=jb, xn=xn: e.matmul(
                            pb[2 + jb][:, 0:T], wcv[:, k, 1024 + c * 128:1024 + (c + 1) * 128], xn[:, k, :],
                            start=(k == 0), stop=(k == 7))
                            for k in range(8)], reads=[B_xn] + B_w, writes=[Bpb[2 + jb]])
                        P.op("act", lambda e, jb=jb: e.activation(sg[jb][:], pb[2 + jb][:, 0:T], AF.Tanh, scale=0.5),
                             reads=[Bpb[2 + jb]], writes=[B_sg[jb]])
                        P.op("dve", lambda e, c=c, jb=jb: e.scalar_tensor_tensor(
                            u0[:, c, 30:30 + T], sg[jb][:], 1.0, pb[jb][:, 0:T], ALU.add, ALU.mult),
                            reads=[B_sg[jb], Bpb[jb]], writes=[B_u0])
                    if it + 1 < nt:
                        prologue(it + 1)
                    for c in range(8):
                        mb = 4 + c % 2
                        q = c % 2
                        P.op("pe", [lambda e, k=k, c=c, mb=mb: e.matmul(
                            pb[mb][:, 0:T], dg[:, c * CW + k, :], u0[:, c, k:k + T],
                            start=(k == 0), stop=(k == CW - 1))
                            for k in range(CW)], reads=[B_u0, B_dg], writes=[Bpb[mb]])
                        P.op("act", lambda e, c=c, mb=mb: e.activation(
                            cv[:, c, :], pb[mb][:, 0:T], AF.Identity, bias=vec[:, V_CVB, c:c + 1], scale=0.5),
                            reads=[Bpb[mb], Bc], writes=[B_cv])
                        P.op("dve", lambda e, c=c, q=q: e.tensor_copy(cvb[:, q, :], cv[:, c, :]),
                             reads=[B_cv], writes=[B_cvb[q]])
                        P.op("pe", lambda e, c=c, q=q: e.matmul(
                            pb[6][:, 0:T], ones1k[:], cvb[:, q, :], start=(c == 0), stop=(c == 7)),
                            reads=[B_cvb[q], Bc], writes=[Bpb[6]])
                    P.op("act", lambda e: e.copy(mean[:], pb[6][:, 0:T]), reads=[Bpb[6]], writes=[B_mean])
                    for c in range(8):
                        q = c % 2
                        P.op("dve", lambda e, c=c, q=q: e.tensor_tensor(tmp[q][:], cv[:, c, :], mean[:], ALU.subtract),
                             reads=[B_cv, B_mean], writes=[B_tmp[q]])
                        P.op("act", lambda e, q=q: e.activation(sqt[:, q, :], tmp[q][:], AF.Square),
                             reads=[B_tmp[q]], writes=[B_sq[q]])
                        P.op("pe", lambda e, c=c, q=q: e.matmul(
                            pb[7][:, 0:T], ones1k[:], sqt[:, q, :], start=(c == 0), stop=(c == 7)),
                            reads=[B_sq[q], Bc], writes=[Bpb[7]])
                    P.op("act", lambda e: e.activation(lrs[:], pb[7][:, 0:T], AF.Sqrt, bias=epsc, scale=1.0),
                         reads=[Bpb[7], Bc], writes=[B_lrs])
                    P.op("dve", lambda e: e.reciprocal(lrs[:], lrs[:]), reads=[B_lrs], writes=[B_lrs])
                    for c in range(8):
                        q = c % 2
                        P.op("dve", lambda e, c=c, q=q: e.tensor_tensor(tmp[q][:], cv[:, c, :], mean[:], ALU.subtract),
                             reads=[B_cv, B_mean], writes=[B_tmp[q]])
                        P.op("dve", lambda e, q=q: e.tensor_tensor(tmp[q][:], tmp[q][:], lrs[:], ALU.mult),
                             reads=[B_lrs, B_tmp[q]], writes=[B_tmp[q]])
                        P.op("act", lambda e, c=c, q=q, s=s: e.activation(
                            ub[s][:, c, :], tmp[q][:], AF.Silu, bias=vec[:, V_LNB, c:c + 1],
                            scale=vec[:, V_LNG, c:c + 1]),
                            reads=[B_tmp[q], Bc], writes=[B_ub[s]])
                    P.dma("sp", Uv[:, :, t0:t0 + T], ub[s][:], reads=[B_ub[s]], writes=[B_U[it]], slot=B_ub[s])
                P.end_phase()

        def ssd_phase():
            with ExitStack() as pes:
                sb = lambda name, shape, dt: pes.enter_context(nc.sbuf_tensor(uniq(name), shape, dt))
                NZ = HYB_IN - 2048
                wz = sb("wz", [128, 8, NZ], BF16)
                wo = sb("wo", [128, 16, D], BF16)
                scw = sb("scw", [128, 12, 5], F32)
                h16 = sb("h16", [128, 3, 16], F32)
                abc = sb("abc", [128, 16], F32)
                ht = [sb(f"ht{i}", [128, 8, T], F32) for i in range(2)]
                xn2 = [sb(f"xn{i}", [128, 8, T], BF16) for i in range(2)]
                sqt = sb("sqt", [128, 2, T], BF16)
                rstd2 = [sb(f"rstd{i}", [128, T], F32) for i in range(2)]
                sz = sb("sz", [128, 8, T], F32)
                xb = sb("xb", [128, 12, 3 + T], BF16)
                dg4 = sb("dg4", [128, 48, 128], BF16)
                xsf = sb("xsf", [128, 8, T], F32)
                xsb = sb("xsb", [128, 8, T], BF16)
                bcb = sb("bcb", [128, 4, T], BF16)
                dtt = sb("dtt", [128, 16], F32)
                adt = sb("adt", [128, 16], F32)
                acs = sb("acs", [128, 16], F32)
                ala = sb("ala", [128, 16], F32)
                cdec = sb("cdec", [128, 16], F32)
                coef = sb("coef", [128, 16], F32)
                xdt = sb("xdt", [128, 16, 64], BF16)
                xdd = sb("xdd", [128, 16, 64], BF16)
                btm = sb("btm", [128, 2, 128], BF16)
                R = sb("R", [128, 8, 128], F32)
                dif = sb("dif", [128, 8, 128], F32)
                erow = sb("erow", [128, 8, 128], F32)
                MT = sb("MT", [128, 8, 128], BF16)
                Cs = sb("Cs", [128, 8, 128], BF16)
                cbm = sb("cbm", [128, 2, 128], F32)
                prev = sb("prev", [128, 16, 64], F32)
                prevb = sb("prevb", [128, 16, 64], BF16)
                yg = sb("yg", [128, 8, T], F32)
                yn = sb("yn", [128, 8, T], BF16)
                grs = [sb(f"grs{i}", [128, T], F32) for i in range(2)]
                ut = sb("ut", [128, 8, T], BF16)
                B_wz = P.bufs(8, "wz")
                B_wo = P.bufs(2, "wo")
                B_sm = P.buf("small")
                B_ht = P.bufs(2, "ht")
                B_xn2 = P.bufs(2, "xn")
                B_sq = P.bufs(2, "sq")
                B_rstd2 = P.bufs(2, "rstd")
                B_sz = P.buf("sz")
                B_xb = P.buf("xb")
                B_dg4 = P.buf("dg4")
                B_xs = P.buf("xs")
                B_bc = P.buf("bcb")
                B_dt = P.buf("dt")
                B_co = P.buf("coefs")
                B_xdt = P.buf("xdt")
                B_btm = P.buf("btm")
                B_R = P.buf("R")
                B_dif = P.buf("dif")
                B_er = P.buf("erow")
                B_MT = P.buf("MT")
                B_Cs = P.buf("Cs")
                B_cbm = P.buf("cbm")
                B_prev = P.buf("prev")
                B_prevb = P.buf("prevb")
                B_yg = P.buf("yg")
                B_yn = P.buf("yn")
                B_grs = P.bufs(2, "grs")
                B_ut = P.buf("ut")
                for k in range(8):
                    load_w(wz[:, k, :], hyb_win[sl(k), 2048:HYB_IN], B_wz[k])
                wo_v = hyb_wout.rearrange("(j p) m -> p j m", p=128)
                for hh in range(2):
                    load_w(wo[:, hh * 8:(hh + 1) * 8, :], wo_v[:, hh * 8:(hh + 1) * 8, :], B_wo[hh])
                P.dma("sp", scw[:], scw_d, writes=[B_sm], slot=B_sm)
                for i in range(3):
                    P.dma("sp", h16[:, i, :], h16_d[i].partition_broadcast(128), writes=[B_sm], slot=B_sm)
                P.op("act", lambda e: e.activation(abc[:], h16[:, 1, :], AF.Exp), reads=[B_sm], writes=[B_sm])
                P.op("dve", lambda e: e.tensor_scalar_mul(abc[:], abc[:], -1.0), reads=[B_sm], writes=[B_sm])
                for c in range(12):
                    P.op("pool", lambda e, c=c: e.tensor_tensor(
                        dg4[:, c * 4:(c + 1) * 4, :],
                        identb[:].unsqueeze(1).broadcast_to([128, 4, 128]),
                        scw[:, c, 0:4].unsqueeze(2).broadcast_to([128, 4, 128]), ALU.mult),
                        reads=[B_sm, Bc], writes=[B_dg4])
                P.op("dve", lambda e: e.memset(xb[:, :, 0:3], 0.0), writes=[B_xb])
                P.op("dve", lambda e: e.memset(prev[:], 0.0), writes=[B_prev])
                P.op("dve", lambda e: e.memset(prevb[:], 0.0), writes=[B_prevb])
                def prologue(it_):
                    s_ = it_ % 2
                    P.dma("sp", ht[s_][:], Hv[:, :, it_ * T:(it_ + 1) * T], reads=[B_H[it_]], writes=[B_ht[s_]], slot=B_ht[s_])
                    make_xn(ht[s_], B_ht[s_], V_NMIX + 0, xn2[s_], B_xn2[s_], sqt, B_sq, rstd2[s_], B_rstd2[s_], pb[6], Bpb[6])

                prologue(0)
                for it in range(nt):
                    s = it % 2
                    t0 = it * T
                    hs = ht[s]
                    xn = xn2[s]
                    B_xn = B_xn2[s]
                    P.dma("sp", ut[:], Uv[:, :, t0:t0 + T], reads=[B_U[it]], writes=[B_ut], slot=B_ut)
                    if it > 0:
                        P.op("dve", lambda e: e.tensor_copy(xb[:, :, 0:3], xb[:, :, T:T + 3]),
                             reads=[B_xb], writes=[B_xb])
                    for c in range(8):
                        jb = c % 2
                        P.op("pe", [lambda e, k=k, c=c, jb=jb, xn=xn: e.matmul(
                            pb[jb][:, 0:T], wz[:, k, sl(c)], xn[:, k, :], start=(k == 0), stop=(k == 7))
                            for k in range(8)], reads=[B_xn] + B_wz, writes=[Bpb[jb]])
                        P.op("act", lambda e, c=c, jb=jb: e.activation(sz[:, c, :], pb[jb][:, 0:T], AF.Silu),
                             reads=[Bpb[jb]], writes=[B_sz])
                    for c in range(12):
                        jb = c % 2
                        P.op("pe", [lambda e, k=k, c=c, jb=jb, xn=xn: e.matmul(
                            pb[jb][:, 0:T], wz[:, k, 1024 + c * 128:1024 + (c + 1) * 128], xn[:, k, :],
                            start=(k == 0), stop=(k == 7))
                            for k in range(8)], reads=[B_xn] + B_wz, writes=[Bpb[jb]])
                        P.op("act", lambda e, c=c, jb=jb: e.copy(xb[:, c, 3:3 + T], pb[jb][:, 0:T]),
                             reads=[Bpb[jb]], writes=[B_xb])
                    for c in range(12):
                        mb = 2 + c % 2
                        P.op("pe", [lambda e, c=c, k=k, mb=mb: e.matmul(
                            pb[mb][:, 0:T], dg4[:, c * 4 + k, :], xb[:, c, k:k + T], start=(k == 0), stop=(k == 3))
                            for k in range(4)], reads=[B_xb, B_dg4], writes=[Bpb[mb]])
                        if c < 8:
                            P.op("act", lambda e, c=c, mb=mb: e.activation(
                                xsf[:, c, :], pb[mb][:, 0:T], AF.Silu, bias=scw[:, c, 4:5], scale=1.0),
                                reads=[Bpb[mb], B_sm], writes=[B_xs])
                            P.op("pool", lambda e, c=c: e.tensor_copy(xsb[:, c, :], xsf[:, c, :]),
                                 reads=[B_xs], writes=[B_xs])
                        else:
                            P.op("act", lambda e, c=c, mb=mb: e.activation(
                                bcb[:, c - 8, :], pb[mb][:, 0:T], AF.Silu, bias=scw[:, c, 4:5], scale=1.0),
                                reads=[Bpb[mb], B_sm], writes=[B_bc])
                    if it + 1 < nt:
                        prologue(it + 1)
                    for cch in range(NB if STG >= 2 else 0):
                        csl = slice(cch * 128, (cch + 1) * 128)
                        P.op("pe", [lambda e, k=k, csl=csl, xn=xn: e.matmul(
                            pb[5][:, 256:272], xn[:, k, csl], wz[:, k, 2560:2576], start=(k == 0), stop=(k == 7))
                            for k in range(8)], reads=[B_xn] + B_wz, writes=[Bpb[5]])
                        P.op("dve", lambda e: e.tensor_tensor(dtt[:], pb[5][:, 256:272], h16[:, 0, :], ALU.add),
                             reads=[Bpb[5], B_sm], writes=[B_dt])
                        P.op("act", lambda e: e.activation(dtt[:], dtt[:], AF.Exp), reads=[B_dt], writes=[B_dt])
                        P.op("act", lambda e: e.activation(dtt[:], dtt[:], AF.Ln, bias=onec, scale=1.0),
                             reads=[B_dt, Bc], writes=[B_dt])
                        P.op("dve", lambda e: e.tensor_tensor(adt[:], dtt[:], abc[:], ALU.mult),
                             reads=[B_dt, B_sm], writes=[B_dt])
                        P.op("pe", [lambda e: e.matmul(pb[5][:, 272:288], triu[:], adt[:], start=True, stop=True),
                                    lambda e: e.matmul(pb[5][:, 288:304], ones_f[:], adt[:], start=True, stop=True)],
                             reads=[B_dt, Bc], writes=[Bpb[5]])
                        P.op("dve", lambda e: e.tensor_copy(acs[:], pb[5][:, 272:288]), reads=[Bpb[5]], writes=[B_co])
                        P.op("dve", lambda e: e.tensor_copy(ala[:], pb[5][:, 288:304]), reads=[Bpb[5]], writes=[B_co])
                        P.op("act", lambda e: e.activation(cdec[:], ala[:], AF.Exp), reads=[B_co], writes=[B_co])
                        P.op("dve", lambda e: e.tensor_tensor(coef[:], ala[:], acs[:], ALU.subtract),
                             reads=[B_co], writes=[B_co])
                        P.op("act", lambda e: e.activation(coef[:], coef[:], AF.Exp), reads=[B_co], writes=[B_co])
                        P.op("dve", lambda e: e.tensor_tensor(coef[:], coef[:], dtt[:], ALU.mult),
                             reads=[B_co, B_dt], writes=[B_co])
                        if STG < 3:
                            continue
                        pbt = pb[4][:].bitcast(BF16)
                        P.op("pe", [lambda e, c=c, csl=csl: e.transpose(pbt[:, sl(c)], xsb[:, c, csl], identb[:])
                                    for c in range(8)], reads=[B_xs, Bc], writes=[Bpb[4]])
                        pbt3 = pbt.rearrange("p (h d) -> p h d", h=16)
                        P.op("dve", lambda e: e.tensor_tensor(
                            xdt[:], pbt3, dtt[:].unsqueeze(2).broadcast_to([128, 16, 64]), ALU.mult),
                            reads=[Bpb[4], B_dt], writes=[B_xdt])
                        P.op("dve", lambda e: e.tensor_tensor(
                            xdd[:], pbt3, coef[:].unsqueeze(2).broadcast_to([128, 16, 64]), ALU.mult),
                            reads=[Bpb[4], B_co], writes=[B_xdt])
                        P.op("pe", [lambda e, g=g, csl=csl: e.transpose(pbt[:, sl(g)], bcb[:, g, csl], identb[:])
                                    for g in range(2)], reads=[B_bc, Bc], writes=[Bpb[4]])
                        P.op("act", lambda e: e.copy(btm[:], pbt[:, 0:256].rearrange("p (g n) -> p g n", g=2)),
                             reads=[Bpb[4]], writes=[B_btm])
                        P.op("pe", [lambda e, g=g, csl=csl: e.matmul(
                            pb[5][:, sl(g)], bcb[:, g, csl], bcb[:, 2 + g, csl], start=True, stop=True)
                            for g in range(2)], reads=[B_bc], writes=[Bpb[5]])
                        P.op("dve", lambda e: e.tensor_tensor(
                            cbm[:], pb[5][:, 0:256].rearrange("p (g n) -> p g n", g=2),
                            triu[:].unsqueeze(1).broadcast_to([128, 2, 128]), ALU.mult),
                            reads=[Bpb[5], Bc], writes=[B_cbm])
                        if STG < 4:
                            continue
                        for g in range(2):
                            hsl = slice(g * 8, (g + 1) * 8)
                            P.op("dve", lambda e, hsl=hsl: e.tensor_tensor(
                                R[:], triu[:].unsqueeze(1).broadcast_to([128, 8, 128]),
                                adt[:, hsl].unsqueeze(2).broadcast_to([128, 8, 128]), ALU.mult),
                                reads=[B_dt, Bc], writes=[B_R])
                            P.op("pe", [lambda e, h2=h2: e.matmul(
                                pb[h2 // 2][:, (h2 % 2) * 256:(h2 % 2) * 256 + 256], ones_f[:],
                                R[:, h2 * 2:(h2 + 1) * 2, :], start=True, stop=True)
                                for h2 in range(4)], reads=[B_R, Bc], writes=[Bpb[0], Bpb[1]])
                            for hh in range(2):
                                h4 = slice(hh * 4, (hh + 1) * 4)
                                a4 = slice(g * 8 + hh * 4, g * 8 + hh * 4 + 4)
                                rb = pb[hh][:].rearrange("p (h l) -> p h l", h=4)
                                P.op("dve", lambda e, h4=h4, a4=a4, rb=rb: e.tensor_tensor(
                                    dif[:, h4, :], rb, acs[:, a4].unsqueeze(2).broadcast_to([128, 4, 128]),
                                    ALU.subtract), reads=[Bpb[hh], B_co], writes=[B_dif])
                                P.op("act", lambda e, h4=h4, rb=rb: e.activation(erow[:, h4, :], rb, AF.Exp),
                                     reads=[Bpb[hh]], writes=[B_er])
                            if SUB != 1:
                                P.op("act", lambda e: e.activation(dif[:], dif[:], AF.Exp), reads=[B_dif], writes=[B_dif])
                            P.op("dve", lambda e, g=g: e.scalar_tensor_tensor(
                                MT[:], dif[:], 1.0, cbm[:, g, :].unsqueeze(1).broadcast_to([128, 8, 128]),
                                ALU.min, ALU.mult), reads=[B_dif, B_cbm], writes=[B_MT])
                            P.op("dve", lambda e, g=g, csl=csl: e.tensor_tensor(
                                Cs[:], erow[:], bcb[:, 2 + g, csl].unsqueeze(1).broadcast_to([128, 8, 128]),
                                ALU.mult), reads=[B_er, B_bc], writes=[B_Cs])
                            for hh in range(8 if STG >= 5 else 0):
                                h = g * 8 + hh
                                cch_out = h // 2
                                half = h % 2
                                bank = pb[2 + cch_out // 4]
                                col = (cch_out % 4) * 128
                                P.op("pe", [
                                    lambda e, h=h, hh=hh, half=half, bank=bank, col=col: e.matmul(
                                        bank[half * 64:(half + 1) * 64, col:col + 128], xdt[:, h, :], MT[:, hh, :],
                                        start=True, stop=False),
                                    lambda e, h=h, hh=hh, half=half, bank=bank, col=col: e.matmul(
                                        bank[half * 64:(half + 1) * 64, col:col + 128], prevb[:, h, :], Cs[:, hh, :],
                                        start=False, stop=True)],
                                    reads=[B_xdt, B_MT, B_prevb, B_Cs], writes=[Bpb[2 + cch_out // 4]])
                            if SUB != 2:
                              P.op("pe", lambda e, g=g: e.matmul(
                                pb[6 + g][:], btm[:, g, :], xdd[:, g * 8:(g + 1) * 8, :], start=True, stop=True),
                                reads=[B_btm, B_xdt], writes=[Bpb[6 + g]])
                        for g in range(2 if SUB != 2 else 0):
                            hsl = slice(g * 8, (g + 1) * 8)
                            P.op("dve", lambda e, hsl=hsl: e.tensor_tensor(
                                prev[:, hsl, :], prev[:, hsl, :],
                                cdec[:, hsl].unsqueeze(2).broadcast_to([128, 8, 64]), ALU.mult),
                                reads=[B_co, B_prev], writes=[B_prev])
                            P.op("dve", lambda e, hsl=hsl, g=g: e.tensor_tensor(
                                prev[:, hsl, :], prev[:, hsl, :],
                                pb[6 + g][:].rearrange("p (h d) -> p h d", h=8), ALU.add),
                                reads=[Bpb[6 + g], B_prev], writes=[B_prev])
                        P.op("act", lambda e: e.copy(prevb[:], prev[:]), reads=[B_prev], writes=[B_prevb])
                        for c in range(8):
                            bank = pb[2 + c // 4]
                            col = (c % 4) * 128
                            P.op("dve", lambda e, c=c, bank=bank, col=col, csl=csl: e.scalar_tensor_tensor(
                                yg[:, c, csl], xsf[:, c, csl], vec[:, V_SSD, c:c + 1], bank[:, col:col + 128],
                                ALU.mult, ALU.add), reads=[Bpb[2 + c // 4], B_xs, Bc], writes=[B_yg])
                    P.op("pool", lambda e: e.tensor_tensor(yg[:], yg[:], sz[:], ALU.mult),
                         reads=[B_sz, B_yg], writes=[B_yg])
                    for g in range(2):
                        rms_rstd(yg, B_yg, 4, ones512, sqt, B_sq, pb[6], Bpb[6], grs[g][:], B_grs[g], k0=g * 4)
                    for c in range(8):
                        P.op("dve", lambda e, c=c: e.scalar_tensor_tensor(
                            yn[:, c, :], yg[:, c, :], vec[:, V_SSN, c:c + 1], grs[c // 4][:], ALU.mult, ALU.mult),
                            reads=[B_yg, Bc, B_grs[c // 4]], writes=[B_yn])
                    for m in range(8):
                        mb = m % 2
                        P.op("pe", [lambda e, j=j, m=m, mb=mb: e.matmul(
                            pb[mb][:, 0:T], wo[:, j, sl(m)], (ut[:, j, :] if j < 8 else yn[:, j - 8, :]),
                            start=(j == 0), stop=(j == 15))
                            for j in range(16)], reads=[B_ut, B_yn] + B_wo, writes=[Bpb[mb]])
                        P.op("dve", lambda e, m=m, mb=mb, hs=hs: e.tensor_tensor(
                            hs[:, m, :], hs[:, m, :], pb[mb][:, 0:T], ALU.add),
                            reads=[Bpb[mb], B_ht[s]], writes=[B_ht[s]])
                    P.dma("sp", Hv[:, :, t0:t0 + T], hs[:], reads=[B_ht[s]], writes=[B_H[it]], slot=B_ht[s])
                P.end_phase()

        def att_phase():
            with ExitStack() as pes:
                sb = lambda name, shape, dt: pes.enter_context(nc.sbuf_tensor(uniq(name), shape, dt))
                wq = sb("wq", [128, 8, 1536], BF16)
                wo = sb("wo", [128, 8, D], BF16)
                bqk = sb("bqk", [128, 10], F32)
                bvb = sb("bvb", [128, 256], F32)
                snk = sb("snk", [128, 16], F32)
                msk = sb("msk", [128, 2, 256], F32)
                ht = [sb(f"ht{i}", [128, 8, T], F32) for i in range(2)]
                xn2 = [sb(f"xn{i}", [128, 8, T], BF16) for i in range(2)]
                sqt = sb("sqt", [128, 2, T], BF16)
                rstd2 = [sb(f"rstd{i}", [128, T], F32) for i in range(2)]
                rope = sb("rope", [128, 2, T], F32)
                qf = [sb(f"qf{i}", [128, T], F32) for i in range(2)]
                qb = [sb(f"qb{i}", [128, T], BF16) for i in range(2)]
                t1 = [sb(f"t1{i}", [128, T], F32) for i in range(2)]
                qr = sb("qr", [128, 8, T], BF16)
                kr = sb("kr", [128, 2, 128 + T], BF16)
                vt = sb("vt", [128, 1 + NB, 256], BF16)
                sm = [sb(f"sm{i}", [128, 4, 256], F32) for i in range(2)]
                ee = [sb(f"ee{i}", [128, 4, 256], F32) for i in range(2)]
                pp = [sb(f"pp{i}", [128, 4, 256], BF16) for i in range(2)]
                pT = [sb(f"pT{i}", [128, 8, 128], BF16) for i in range(2)]
                mx = [sb(f"mx{i}", [128, 4], F32) for i in range(2)]
                nmx = [sb(f"nmx{i}", [128, 4], F32) for i in range(2)]
                rs = [sb(f"rs{i}", [128, 4], F32) for i in range(2)]
                es_ = [sb(f"es_{i}", [128, 4], F32) for i in range(2)]
                oT = sb("oT", [128, 8, T], BF16)
                B_wq = P.bufs(8, "wq")
                B_wo = P.buf("wo")
                B_sm_ = P.buf("small")
                B_ht = P.bufs(2, "ht")
                B_xn2 = P.bufs(2, "xn")
                B_sq = P.bufs(2, "sq")
                B_rstd2 = P.bufs(2, "rstd")
                B_rope = P.buf("rope")
                B_qf = P.bufs(2, "qf")
                B_qb = P.bufs(2, "qb")
                B_t1 = P.bufs(2, "t1")
                B_qr = P.buf("qr")
                B_kr = P.buf("kr")
                B_vt = P.buf("vt")
                B_s = P.bufs(2, "sm")
                B_e = P.bufs(2, "ee")
                B_p = P.bufs(2, "pp")
                B_pT = P.bufs(2, "pT")
                B_st = P.bufs(2, "stats")
                B_oT = P.buf("oT")
                for k in range(8):
                    load_w(wq[:, k, :], wqkv_d[sl(k), :], B_wq[k])
                load_w(wo[:], wo_d.rearrange("(k p) m -> p k m", p=128), B_wo)
                P.dma("sp", bqk[:], bqk_d, writes=[B_sm_], slot=B_sm_)
                P.dma("sp", bvb[:], bv_d.partition_broadcast(128), writes=[B_sm_], slot=B_sm_)
                P.dma("sp", snk[:], h16_d[2].partition_broadcast(128), writes=[B_sm_], slot=B_sm_)
                P.dma("sp", msk[:], msk_d.rearrange("a p s -> p a s"), writes=[B_sm_], slot=B_sm_)
                P.op("dve", lambda e: e.memset(kr[:, :, 0:128], 0.0), writes=[B_kr])
                P.op("dve", lambda e: e.memset(vt[:, 0, :], 0.0), writes=[B_vt])
                def prologue(it_):
                    s_ = it_ % 2
                    P.dma("sp", ht[s_][:], Hv[:, :, it_ * T:(it_ + 1) * T], reads=[B_H[it_]], writes=[B_ht[s_]], slot=B_ht[s_])
                    make_xn(ht[s_], B_ht[s_], V_NMIX + 1, xn2[s_], B_xn2[s_], sqt, B_sq, rstd2[s_], B_rstd2[s_], pb[6], Bpb[6])

                prologue(0)
                for it in range(nt):
                    s = it % 2
                    t0 = it * T
                    hs = ht[s]
                    xn = xn2[s]
                    B_xn = B_xn2[s]
                    P.dma("sp", rope[:], rope_d[:, :, t0:t0 + T].rearrange("a p s -> p a s"),
                          writes=[B_rope], slot=B_rope)
                    if it > 0:
                        P.op("dve", lambda e: e.tensor_copy(kr[:, :, 0:128], kr[:, :, T:T + 128]),
                             reads=[B_kr], writes=[B_kr])
                        P.op("dve", lambda e: e.tensor_copy(vt[:, 0, :], vt[:, NB, :]), reads=[B_vt], writes=[B_vt])
                    for c in range(10):
                        jb = c % 2
                        P.op("pe", [lambda e, k=k, c=c, jb=jb, xn=xn: e.matmul(
                            pb[jb][:, 0:T], wq[:, k, sl(c)], xn[:, k, :], start=(k == 0), stop=(k == 7))
                            for k in range(8)], reads=[B_xn] + B_wq, writes=[Bpb[jb]])
                        P.op("act", lambda e, c=c, jb=jb: e.activation(
                            qf[jb][:], pb[jb][:, 0:T], AF.Identity, bias=bqk[:, c:c + 1], scale=1.0),
                            reads=[Bpb[jb], B_sm_], writes=[B_qf[jb]])
                        P.op("pool", lambda e, jb=jb: e.tensor_copy(qb[jb][:], qf[jb][:]),
                             reads=[B_qf[jb]], writes=[B_qb[jb]])
                        P.op("pe", lambda e, jb=jb: e.matmul(pb[2 + jb][:, 0:T], pswap[:], qb[jb][:], start=True, stop=True),
                             reads=[B_qb[jb], Bc], writes=[Bpb[2 + jb]])
                        P.op("dve", lambda e, jb=jb: e.tensor_tensor(t1[jb][:], qf[jb][:], rope[:, 0, :], ALU.mult),
                             reads=[B_qf[jb], B_rope], writes=[B_t1[jb]])
                        P.op("dve", lambda e, jb=jb: e.tensor_tensor(qf[jb][:], pb[2 + jb][:, 0:T], rope[:, 1, :], ALU.mult),
                             reads=[Bpb[2 + jb], B_rope, B_qf[jb]], writes=[B_qf[jb]])
                        dst = qr[:, c, :] if c < 8 else kr[:, c - 8, 128:128 + T]
                        P.op("dve", lambda e, jb=jb, dst=dst: e.tensor_tensor(dst, t1[jb][:], qf[jb][:], ALU.add),
                             reads=[B_t1[jb], B_qf[jb]], writes=[B_qr if c < 8 else B_kr])
                    for b in range(NB):
                        jb = b % 2
                        P.op("pe", [lambda e, k=k, b=b, jb=jb, xn=xn: e.matmul(
                            pb[jb][:, 0:256], xn[:, k, sl(b)], wq[:, k, 1280:1536], start=(k == 0), stop=(k == 7))
                            for k in range(8)], reads=[B_xn] + B_wq, writes=[Bpb[jb]])
                        P.op("dve", lambda e, b=b, jb=jb: e.tensor_tensor(vt[:, 1 + b, :], pb[jb][:, 0:256], bvb[:], ALU.add),
                             reads=[Bpb[jb], B_sm_], writes=[B_vt])
                    if it + 1 < nt:
                        prologue(it + 1)
                    def group_steps(b, kv, a):
                        gblk = it * NB + b
                        mi = 0 if gblk == 0 else 1
                        half = kv % 2
                        hp = slice(half * 64, (half + 1) * 64)
                        kc = kv // 2
                        qc0 = (kv // 2) * 4
                        S0, S1, PTb, Ob = 4 * a, 4 * a + 1, 4 * a + 2, 4 * a + 3
                        sm_, ee_, pp_, pT_ = sm[a], ee[a], pp[a], pT[a]
                        mx_, nmx_, rs_, es2 = mx[a], nmx[a], rs[a], es_[a]
                        Bs, Be, Bp, BpT, Bst = B_s[a], B_e[a], B_p[a], B_pT[a], B_st[a]
                        sg4 = snk[:, kv * 4:(kv + 1) * 4]
                        st = []
                        st.append(lambda: P.op("pe", [lambda e, i=i: e.matmul(
                            pb[S0 + i // 2][:, (i % 2) * 256:(i % 2) * 256 + 256],
                            qr[hp, qc0 + i, sl(b)], kr[hp, kc, b * 128:b * 128 + 256], start=True, stop=True)
                            for i in range(4)], reads=[B_qr, B_kr], writes=[Bpb[S0], Bpb[S1]]))
                        for hh in range(2):
                            st.append(lambda hh=hh: P.op("dve", lambda e: e.scalar_tensor_tensor(
                                sm_[:, hh * 2:hh * 2 + 2, :], pb[S0 + hh][:].rearrange("p (h s) -> p h s", h=2),
                                0.125, msk[:, mi, :].unsqueeze(1).broadcast_to([128, 2, 256]),
                                ALU.mult, ALU.add), reads=[Bpb[S0 + hh], B_sm_], writes=[Bs]))
                        st.append(lambda: P.op("dve", lambda e: e.tensor_reduce(mx_[:], sm_[:], AX.X, ALU.max),
                                               reads=[Bs], writes=[Bst]))
                        st.append(lambda: P.op("dve", lambda e: e.tensor_tensor(mx_[:], mx_[:], sg4, ALU.max),
                                               reads=[Bst, B_sm_], writes=[Bst]))
                        st.append(lambda: P.op("dve", lambda e: e.tensor_scalar_mul(nmx_[:], mx_[:], -1.0),
                                               reads=[Bst], writes=[Bst]))
                        st.append(lambda: P.op("dve", lambda e: e.tensor_tensor(es2[:], sg4, mx_[:], ALU.subtract),
                                               reads=[Bst, B_sm_], writes=[Bst]))
                        st.append(lambda: P.op("dve", lambda e: e.memset(rs_[:], 0.0), writes=[Bst]))
                        for i in range(4):
                            st.append(lambda i=i: P.op("act", lambda e: e.activation(
                                ee_[:, i, :], sm_[:, i, :], AF.Exp, bias=nmx_[:, i:i + 1], scale=1.0,
                                accum_out=rs_[:, i:i + 1]), reads=[Bs, Bst], writes=[Be, Bst]))
                        st.append(lambda: P.op("act", lambda e: e.activation(es2[:], es2[:], AF.Exp),
                                               reads=[Bst], writes=[Bst]))
                        st.append(lambda: P.op("dve", lambda e: e.tensor_tensor(rs_[:], rs_[:], es2[:], ALU.add),
                                               reads=[Bst], writes=[Bst]))
                        st.append(lambda: P.op("dve", lambda e: e.reciprocal(rs_[:], rs_[:]), reads=[Bst], writes=[Bst]))
                        st.append(lambda: P.op("dve", lambda e: e.tensor_tensor(
                            pp_[:], ee_[:], rs_[:].unsqueeze(2).broadcast_to([128, 4, 256]), ALU.mult),
                            reads=[Be, Bst], writes=[Bp]))
                        pbt = pb[PTb][:].bitcast(BF16)
                        st.append(lambda: P.op("pe", [lambda e, i=i, kb=kb: e.transpose(
                            pbt[:, sl(i * 2 + kb)], pp_[:, i, sl(kb)], identb[:])
                            for i in range(4) for kb in range(2)], reads=[Bp, Bc], writes=[Bpb[PTb]]))
                        st.append(lambda: P.op("act", lambda e: e.copy(pT_[:], pbt.rearrange("p (a q) -> p a q", a=8)),
                                               reads=[Bpb[PTb]], writes=[BpT]))
                        st.append(lambda: P.op("pe", [lambda e, i=i, kb=kb: e.matmul(
                            pb[Ob][hp, i * 128:(i + 1) * 128], vt[:, b + kb, kv * 64:(kv + 1) * 64],
                            pT_[:, i * 2 + kb, :], start=(kb == 0), stop=(kb == 1))
                            for i in range(4) for kb in range(2)], reads=[B_vt, BpT], writes=[Bpb[Ob]]))
                        st.append(lambda: P.op("act", lambda e: e.copy(
                            oT[hp, qc0:qc0 + 4, sl(b)], pb[Ob][hp, :].rearrange("p (i q) -> p i q", i=4)),
                            reads=[Bpb[Ob]], writes=[B_oT]))
                        return st

                    groups = [(b, kv) for b in range(NB) for kv in range(4)]
                    allst = [group_steps(b_, kv_, gi % 2) for gi, (b_, kv_) in enumerate(groups)]
                    L_ = len(allst[0])
                    H_ = (L_ + 1) // 2
                    for t_ in range((len(groups) - 1) * H_ + L_):
                        for gi in range(len(groups)):
                            k_ = t_ - gi * H_
                            if 0 <= k_ < L_:
                                allst[gi][k_]()
                    for m in range(8):
                        mb = m % 2
                        P.op("pe", [lambda e, j=j, m=m, mb=mb: e.matmul(
                            pb[mb][:, 0:T], wo[:, j, sl(m)], oT[:, j, :], start=(j == 0), stop=(j == 7))
                            for j in range(8)], reads=[B_oT, B_wo], writes=[Bpb[mb]])
                        P.op("dve", lambda e, m=m, mb=mb, hs=hs: e.scalar_tensor_tensor(
                            hs[:, m, :], pb[mb][:, 0:T], vec[:, V_BO, m:m + 1], hs[:, m, :], ALU.add, ALU.add),
                            reads=[Bpb[mb], B_ht[s], Bc], writes=[B_ht[s]])
                    P.dma("sp", Hv[:, :, t0:t0 + T], hs[:], reads=[B_ht[s]], writes=[B_H[it]], slot=B_ht[s])
                P.end_phase()

        last = phases[-1]
        for ph in phases:
            if ph == "ffn1_0":
                ffn_phase(0, 0, True, False, False)
            elif ph == "conv":
                conv_phase()
            elif ph == "ssd":
                ssd_phase()
            elif ph == "ffn2_0":
                ffn_phase(1, 0, False, True, False)
            elif ph == "ffn1_1":
                ffn_phase(2, 1, False, False, False)
            elif ph == "att":
                att_phase()
            elif ph == "ffn2_1":
                ffn_phase(3, 1, False, True, True)
        P.barrier()
        P.emit()
    return nc


Q_LOWER = [0, 1, 2, 3, 8, 9, 10, 11]
Q_UPPER = [4, 5, 6, 7, 12, 13, 14, 15]
HEAD_ORDER = [h for c in range(8) for h in (Q_LOWER[c], Q_UPPER[c])]


def _pk(v):
    v = np.asarray(v, np.float32)
    return np.array(v.reshape(-1, 128).T, dtype=np.float32, order='C', copy=True)


def prep_shared(inp, S=SEQ):
    f = lambda a: np.array(a, dtype=np.float32, order='C', copy=True)
    sh = {}
    sh["ffn_win"] = f(np.stack([inp["ffn1_w_in"][0], inp["ffn2_w_in"][0], inp["ffn1_w_in"][1], inp["ffn2_w_in"][1]]))
    sh["ffn_wout"] = f(np.stack([inp["ffn1_w_out"][0], inp["ffn2_w_out"][0], inp["ffn1_w_out"][1], inp["ffn2_w_out"][1]]))
    vec = np.zeros((128, 20, 8), np.float32)
    for li in range(2):
        vec[:, 0 + li] = _pk(inp["norm_ffn1"][li])
        vec[:, 2 + li] = _pk(inp["norm_mix"][li])
        vec[:, 4 + li] = _pk(inp["norm_ffn2"][li])
        vec[:, 6 + li] = _pk(inp["ple_norm"][li])
    vec[:, 8] = _pk(inp["final_norm"])
    vec[:, 9] = _pk(inp["conv_dw_b"][0])
    vec[:, 10] = _pk(inp["conv_ln_g"][0])
    vec[:, 11] = _pk(inp["conv_ln_b"][0])
    vec[:, 12] = _pk(np.repeat(np.asarray(inp["ssm_d"][0], np.float32), 64))
    vec[:, 13] = _pk(inp["ssm_norm"][0])
    vec[:, 14] = _pk(inp["att_b_o"][0])
    sh["vecs"] = vec
    sh["ple_wg"] = f(inp["ple_gate_w"])
    sh["ple_wp"] = f(inp["ple_proj_w"])
    sh["hyb_win"] = f(inp["hyb_w_in"][0])
    sh["hyb_wout"] = f(inp["hyb_w_out"][0])
    cw = np.asarray(inp["conv_dw_w"][0], np.float32)
    sh["cw"] = f(cw.T.reshape(8, 128, CW).transpose(1, 0, 2))
    scw = np.asarray(inp["ssm_conv_w"][0], np.float32)
    scb = np.asarray(inp["ssm_conv_b"][0], np.float32)
    sc = np.concatenate([scw, scb[None]], 0)
    sh["scw"] = f(sc.T.reshape(12, 128, 5).transpose(1, 0, 2))
    sinks = np.asarray(inp["att_sinks"][0], np.float32)
    sg_order = [HEAD_ORDER[((kv // 2) * 4 + i) * 2 + kv % 2] for kv in range(4) for i in range(4)]
    sh["h16"] = f(np.stack([inp["ssm_dt_bias"][0], inp["ssm_a_log"][0], sinks[sg_order]]))
    wqkv = np.asarray(inp["att_w_qkv"][0], np.float32)
    bqkv = np.asarray(inp["att_b_qkv"][0], np.float32)
    qcols = np.concatenate([np.arange(h * 64, (h + 1) * 64) for h in HEAD_ORDER])
    cols = np.concatenate([qcols, np.arange(1024, 1536)])
    sh["wqkv"] = f(wqkv[:, cols])
    bp = bqkv[cols]
    sh["bqk"] = _pk(bp[:1280])
    sh["bv"] = f(bp[1280:])
    sh["wo"] = f(np.asarray(inp["att_w_o"][0], np.float32)[qcols, :])
    inv = (np.float32(10000.0) ** (-np.arange(0, 64, 2, dtype=np.float32) / np.float32(64))).astype(np.float32)
    ang = (np.arange(S, dtype=np.float32)[:, None] * inv[None, :]).astype(np.float32)
    cos, sin = np.cos(ang).astype(np.float32), np.sin(ang).astype(np.float32)
    prt = np.arange(128)
    CC = cos.T[prt % 32]
    sgn = np.where((prt % 64) < 32, -1.0, 1.0).astype(np.float32)
    SSn = sin.T[prt % 32] * sgn[:, None]
    sh["rope"] = f(np.stack([CC, SSn]))
    cst = np.zeros((5, 128, 128), np.float32)
    cst[0] = np.eye(128)
    cst[1] = np.triu(np.ones((128, 128)))
    sw = np.zeros((128, 128), np.float32)
    for m in range(128):
        sw[(m + 32) % 64 + (m // 64) * 64, m] = 1.0
    cst[2] = sw
    sh["cst"] = cst
    q = np.arange(128)[:, None]
    sp = np.arange(256)[None, :]
    valid = np.where(sp < 128, sp > q, (sp - 128) <= q)
    m1 = np.where(valid, 0.0, NEG).astype(np.float32)
    m0 = np.where(valid & (sp >= 128), 0.0, NEG).astype(np.float32)
    sh["msk"] = f(np.stack([m0, m1]))
    return sh


_CACHE = {}


def kernel(**inputs):
    x = np.asarray(inputs["x"], np.float32)
    p = np.asarray(inputs["p"], np.float32)
    B, S, _ = x.shape
    sh = prep_shared(inputs, S)
    key = ("full", S)
    if key not in _CACHE:
        _CACHE[key] = build_program(S)
    nc = _CACHE[key]
    in_maps = []
    for b in range(B):
        m = dict(sh)
        m["x"] = np.array(x[b], dtype=np.float32, order='C', copy=True)
        m["p"] = np.array(p[:, b], dtype=np.float32, order='C', copy=True)
        in_maps.append(m)
    res = run_bass_kernel_spmd(nc, in_maps, core_ids=list(range(B)))
    return np.stack([np.asarray(r["y"], np.float32) for r in res.results], 0)
```

```python
from contextlib import ExitStack
import os
import numpy as np
import concourse.bass as bass
import concourse.mybir as mybir
from concourse.bass_utils import run_bass_kernel_spmd

F32 = mybir.dt.float32
BF16 = mybir.dt.bfloat16
AF = mybir.ActivationFunctionType
ALU = mybir.AluOpType
AX = mybir.AxisListType

D = 1024
DFF = 2816
NJ = DFF // 128
PLE = 256
SEQ = 4096
NCORES = 8
T = 256
NB = T // 128
EPS = 1e-6
CW = 31
HYB_IN = 4624
NEG = -240000.0
STG = int(os.environ.get('SSD_STAGE', '99'))
SUB = int(os.environ.get('SUB', '0'))


class Buf:
    __slots__ = ("name", "w", "r", "dsem", "excl")

    def __init__(self, name):
        self.name = name
        self.w = {}
        self.r = {}
        self.dsem = None
        self.excl = False


class Prog:
    ENGS = ("pe", "act", "dve", "pool", "sp")

    def __init__(self, nc, es):
        self.nc = nc
        self.es = es
        self.streams = {e: [] for e in self.ENGS}
        self.sems = {}
        self.cnt = {}
        self.waited = {e: {} for e in self.ENGS}
        self.nbuf = 0
        self.free_dsems = []
        self.phase_dsems = []
        self.ndsem = 0
        for e in ("pe", "act", "dve", "pool"):
            self._mksem("c_" + e)

    def _mksem(self, name):
        self.sems[name] = self.es.enter_context(self.nc.semaphore(name))
        self.cnt[name] = 0
        return name

    def _dsem(self):
        if self.free_dsems:
            s = self.free_dsems.pop()
        else:
            self.ndsem += 1
            s = self._mksem(f"d{self.ndsem}")
        self.phase_dsems.append(s)
        return s

    def buf(self, name=None):
        self.nbuf += 1
        return Buf(name or f"b{self.nbuf}")

    def bufs(self, n, name="b"):
        return [self.buf(f"{name}{i}") for i in range(n)]

    def _need(self, reads, writes):
        need = {}
        for b in reads:
            for s, v in b.w.items():
                if need.get(s, 0) < v:
                    need[s] = v
        for b in writes:
            for d in (b.w, b.r):
                for s, v in d.items():
                    if need.get(s, 0) < v:
                        need[s] = v
        return need

    def _emit_waits(self, eng, need, skip_own=False):
        wd = self.waited[eng]
        own = "c_" + eng
        for s, v in need.items():
            if skip_own and s == own:
                continue
            if wd.get(s, 0) < v:
                wd[s] = v
                h = self.sems[s]
                self.streams[eng].append(lambda e, h=h, v=v: e.wait_ge(h, v))

    def _mark(self, reads, writes, s, v):
        for b in reads:
            if b.r.get(s, 0) < v:
                b.r[s] = v
        for b in writes:
            b.w = {s: v}
            b.r = {}

    def op(self, eng, fns, reads=(), writes=(), skip_own=None):
        if callable(fns):
            fns = [fns]
        if skip_own is None:
            skip_own = (eng == "pe")
        ex = [b for b in reads if b.excl]
        if ex:
            writes = list(writes) + ex
            reads = [b for b in reads if not b.excl]
        self._emit_waits(eng, self._need(reads, writes), skip_own)
        s = "c_" + eng
        self.cnt[s] += 1
        v = self.cnt[s]
        h = self.sems[s]
        st = self.streams[eng]
        for f in fns[:-1]:
            st.append(f)
        last = fns[-1]
        st.append(lambda e, last=last, h=h: last(e).then_inc(h, 1))
        self._mark(reads, writes, s, v)

    def dma(self, q, out, in_, reads=(), writes=(), slot=None):
        self._emit_waits(q, self._need(reads, writes), False)
        if slot.dsem is None:
            slot.dsem = self._dsem()
        s = slot.dsem
        self.cnt[s] += 16
        v = self.cnt[s]
        h = self.sems[s]
        self.streams[q].append(lambda e, out=out, in_=in_, h=h: e.dma_start(out=out, in_=in_).then_inc(h, 16))
        self._mark(reads, writes, s, v)

    def barrier(self):
        need = {s: v for s, v in self.cnt.items() if v > 0}
        for e in self.ENGS:
            self._emit_waits(e, need, False)

    def end_phase(self):
        self.barrier()
        self.free_dsems.extend(self.phase_dsems)
        self.phase_dsems = []

    def emit(self):
        nc = self.nc
        st = self.streams
        with nc.Block() as block:
            @block.tensor
            def _(e):
                for f in st["pe"]:
                    f(e)

            @block.scalar
            def _(e):
                for f in st["act"]:
                    f(e)

            @block.vector
            def _(e):
                for f in st["dve"]:
                    f(e)

            @block.gpsimd
            def _(e):
                for f in st["pool"]:
                    f(e)

            @block.sync
            def _(e):
                for f in st["sp"]:
                    f(e)


class Ctx:
    pass


def sl(i, n=128):
    return slice(i * n, (i + 1) * n)


def build_program(S=SEQ, phases=("ffn1_0", "conv", "ssd", "ffn2_0", "ffn1_1", "att", "ffn2_1"), debug=False):
    nc = bass.Bass("TRN2", target_bir_lowering=False)
    nt = S // T
    dt_in = {}

    def din(name, shape):
        dt_in[name] = nc.dram_tensor(name, list(shape), F32, kind="ExternalInput").ap()
        return dt_in[name]

    x_d = din("x", [S, D])
    p_d = din("p", [2, S, PLE])
    ffn_win = din("ffn_win", [4, D, 2 * DFF])
    ffn_wout = din("ffn_wout", [4, DFF, D])
    vecs = din("vecs", [128, 20, 8])
    ple_wg = din("ple_wg", [2, D, D])
    ple_wp = din("ple_wp", [2, PLE, D])
    hyb_win = din("hyb_win", [D, HYB_IN])
    hyb_wout = din("hyb_wout", [2 * D, D])
    cw_d = din("cw", [128, 8, CW])
    scw_d = din("scw", [128, 12, 5])
    h16_d = din("h16", [3, 16])
    wqkv_d = din("wqkv", [D, 1536])
    bqk_d = din("bqk", [128, 10])
    bv_d = din("bv", [256])
    wo_d = din("wo", [D, D])
    rope_d = din("rope", [2, 128, S])
    cst_d = din("cst", [5, 128, 128])
    msk_d = din("msk", [2, 128, 256])
    if debug:
        out_d = nc.dram_tensor("H", [8, 128, S], F32, kind="ExternalOutput").ap()
        Hd = out_d
        yout_d = nc.dram_tensor("y", [S, D], F32, kind="ExternalOutput").ap()
    else:
        yout_d = nc.dram_tensor("y", [S, D], F32, kind="ExternalOutput").ap()
        Hd = nc.dram_tensor("Hs", [8, 128, S], F32).ap()
    Ud = nc.dram_tensor("Us", [8, 128, S], BF16).ap()
    Hv = Hd.rearrange("k p s -> p k s")
    Uv = Ud.rearrange("k p s -> p k s")

    with ExitStack() as es:
        P = Prog(nc, es)
        C = Ctx()
        gsb = lambda name, shape, dt: es.enter_context(nc.sbuf_tensor(name, shape, dt))
        pb = [es.enter_context(nc.psum_tensor(f"pb{i}", [128, 512], F32)) for i in range(8)]
        Bpb = P.bufs(8, "pb")
        for b_ in Bpb:
            b_.excl = True
        identf = gsb("identf", [128, 128], F32)
        identb = gsb("identb", [128, 128], BF16)
        triu = gsb("triu", [128, 128], F32)
        pswap = gsb("pswap", [128, 128], BF16)
        ones_f = gsb("ones_f", [128, 128], F32)
        ones1k = gsb("ones1k", [128, 128], BF16)
        ones512 = gsb("ones512", [128, 128], BF16)
        cols = gsb("cols", [128, 4], F32)
        vec = gsb("vec", [128, 20, 8], F32)
        Bc = P.buf("consts")
        P.dma("sp", identf[:], cst_d[0], writes=[Bc], slot=Bc)
        P.dma("sp", triu[:], cst_d[1], writes=[Bc], slot=Bc)
        P.dma("sp", vec[:], vecs, writes=[Bc], slot=Bc)
        P.dma("pool", identb[:], cst_d[0], writes=[Bc], slot=Bc)
        P.dma("pool", pswap[:], cst_d[2], writes=[Bc], slot=Bc)
        P.op("dve", lambda e: e.memset(ones_f[:], 1.0), writes=[Bc])
        P.op("dve", lambda e: e.memset(ones1k[:], 1.0 / 1024), writes=[Bc])
        P.op("dve", lambda e: e.memset(ones512[:], 1.0 / 512), writes=[Bc])
        P.op("dve", lambda e: e.memset(cols[:, 0:1], EPS), writes=[Bc])
        P.op("dve", lambda e: e.memset(cols[:, 1:2], 1.0), writes=[Bc])
        P.op("dve", lambda e: e.memset(cols[:, 2:3], 0.0), writes=[Bc])
        epsc = cols[:, 0:1]
        onec = cols[:, 1:2]
        B_H = P.bufs(nt, "H")
        B_U = P.bufs(nt, "U")
        B_Y = P.bufs(nt, "Y")
        P.end_phase()

        V_NF1, V_NMIX, V_NF2, V_PLE = 0, 2, 4, 6
        V_FIN, V_CVB, V_LNG, V_LNB, V_SSD, V_SSN, V_BO = 8, 9, 10, 11, 12, 13, 14

        def rms_rstd(xap, Bx, nk, ones_ap, sqt, Bsq, pbank, Bpbank, rstd, Brstd, k0=0):
            for k in range(nk):
                q = k % 2
                P.op("act", lambda e, k=k, q=q: e.activation(sqt[:, q, :], xap[:, k0 + k, :], AF.Square),
                     reads=[Bx], writes=[Bsq[q]])
                P.op("pe", lambda e, k=k, q=q: e.matmul(pbank[:, 0:T], ones_ap[:], sqt[:, q, :],
                                                       start=(k == 0), stop=(k == nk - 1)),
                     reads=[Bsq[q], Bc], writes=[Bpbank])
            P.op("act", lambda e: e.activation(rstd, pbank[:, 0:T], AF.Sqrt, bias=epsc, scale=1.0),
                 reads=[Bpbank, Bc], writes=[Brstd])
            P.op("dve", lambda e: e.reciprocal(rstd, rstd), reads=[Brstd], writes=[Brstd])

        def make_xn(ht_s, Bht, gidx, xn, Bxn, sqt, Bsq, rstd, Brstd, pbank, Bpbank):
            rms_rstd(ht_s, Bht, 8, ones1k, sqt, Bsq, pbank, Bpbank, rstd[:], Brstd)
            for k in range(8):
                P.op("dve", lambda e, k=k: e.scalar_tensor_tensor(
                    xn[:, k, :], ht_s[:, k, :], vec[:, gidx, k:k + 1], rstd[:], ALU.mult, ALU.mult),
                    reads=[Bht, Bc, Brstd], writes=[Bxn])

        ucnt = [0]

        def uniq(name):
            ucnt[0] += 1
            return f"s{ucnt[0]}_{name}"

        def load_w(dst, src_rows_ap, Bw, q="pool"):
            P.dma(q, dst, src_rows_ap, writes=[Bw], slot=Bw)

        def ffn_phase(fi, li, first, do_ple, final):
            with ExitStack() as pes:
                sb = lambda name, shape, dt: pes.enter_context(nc.sbuf_tensor(uniq(name), shape, dt))
                win = sb("win", [128, 8, 2 * DFF], BF16)
                wout = sb("wout", [128, NJ, D], BF16)
                ht = [sb(f"ht{i}", [128, 8, T], F32) for i in range(2)]
                xn = [sb(f"xn{i}", [128, 8, T], BF16) for i in range(2)]
                sqt = sb("sqt", [128, 2, T], BF16)
                rstd = [sb(f"rstd{i}", [128, T], F32) for i in range(2)]
                sg = [sb(f"sg{i}", [128, T], F32) for i in range(2)]
                hT = sb("hT", [128, NJ, T], BF16)
                B_win = P.bufs(1, "win")
                B_wout = P.bufs(2, "wout")
                B_ht = P.bufs(2, "ht")
                B_xn = P.bufs(2, "xn")
                B_sq = P.bufs(2, "sq")
                B_rstd = P.bufs(2, "rstd")
                B_sg = P.bufs(2, "sg")
                B_hT = P.buf("hT")
                if first:
                    xt = [sb("xt0", [128, D], F32)]
                    B_xt = P.bufs(1, "xt")
                if final:
                    yt = [sb(f"yt{i}", [128, 512], F32) for i in range(2)]
                    B_yt = P.bufs(2, "yt")
                if do_ple:
                    wg = sb("wg", [128, 8, D], BF16)
                    wp = sb("wp", [128, 2, D], BF16)
                    pt = [sb(f"pt{i}", [128, PLE], F32) for i in range(2)]
                    pT = sb("pT", [128, 2, T], BF16)
                    sg2 = [sb(f"sgp{i}", [128, T], F32) for i in range(2)]
                    B_wg = P.buf("wg")
                    B_wp = P.buf("wp")
                    B_pt = P.bufs(2, "pt")
                    B_pT = P.buf("pT")
                    B_sg2 = P.bufs(2, "sgp")
                for k in range(8):
                    load_w(win[:, k, :], ffn_win[fi, sl(k), :], B_win[0])
                wo_v = ffn_wout[fi].rearrange("(j p) m -> p j m", p=128)
                for hh in range(2):
                    load_w(wout[:, hh * 11:(hh + 1) * 11, :], wo_v[:, hh * 11:(hh + 1) * 11, :], B_wout[hh])
                if do_ple:
                    load_w(wg[:], ple_wg[li].rearrange("(k p) m -> p k m", p=128), B_wg)
                    load_w(wp[:], ple_wp[li].rearrange("(k p) m -> p k m", p=128), B_wp)
                gidx = (V_NF1 if not do_ple else V_NF2) + li

                def load_tile(it):
                    s = it % 2
                    t0 = it * T
                    hs = ht[s]
                    if first:
                        for b in range(NB):
                            P.dma("sp", xt[0][:], x_d[t0 + b * 128:t0 + (b + 1) * 128, :],
                                  writes=[B_xt[0]], slot=B_xt[0])
                            for hf in range(2):
                                P.op("pe", [lambda e, kk=kk, hf=hf: e.transpose(
                                    pb[7][:, sl(kk)], xt[0][:, sl(hf * 4 + kk)], identf[:]) for kk in range(4)],
                                    reads=[B_xt[0], Bc], writes=[Bpb[7]])
                                P.op("act", lambda e, hf=hf, b=b, hs=hs: e.copy(
                                    hs[:, hf * 4:(hf + 1) * 4, sl(b)], pb[7][:].rearrange("p (k t) -> p k t", k=4)),
                                    reads=[Bpb[7]], writes=[B_ht[s]])
                    else:
                        P.dma("sp", hs[:], Hv[:, :, t0:t0 + T], reads=[B_H[it]], writes=[B_ht[s]], slot=B_ht[s])

                def prologue(it):
                    s = it % 2
                    make_xn(ht[s], B_ht[s], gidx, xn[s], B_xn[s], sqt, B_sq, rstd[s], B_rstd[s], pb[7], Bpb[7])

                def sq_stat(hs_, s_, m_):
                    q_ = m_ % 2
                    P.op("act", lambda e: e.activation(sqt[:, q_, :], hs_[:, m_, :], AF.Square),
                         reads=[B_ht[s_]], writes=[B_sq[q_]])

                def pe_stat(m_):
                    q_ = m_ % 2
                    P.op("pe", lambda e: e.matmul(pb[6][:, 0:T], ones1k[:], sqt[:, q_, :],
                                                  start=(m_ == 0), stop=(m_ == 7)),
                         reads=[B_sq[q_], Bc], writes=[Bpb[6]])

                def finish_rstd(s_):
                    rs_ = rstd[s_]
                    P.op("act", lambda e: e.activation(rs_[:], pb[6][:, 0:T], AF.Sqrt, bias=epsc, scale=1.0),
                         reads=[Bpb[6], Bc], writes=[B_rstd[s_]])
                    P.op("dve", lambda e: e.reciprocal(rs_[:], rs_[:]), reads=[B_rstd[s_]], writes=[B_rstd[s_]])

                def p_dma(it):
                    t0 = it * T
                    for b in range(NB):
                        P.dma("sp", pt[b][:], p_d[li, t0 + b * 128:t0 + (b + 1) * 128, :],
                              writes=[B_pt[b]], slot=B_pt[b])

                def post_slots(it):
                    s = it % 2
                    t0 = it * T
                    hs = ht[s]
                    xs = xn[s]
                    sl_ = {}

                    def add(j, f):
                        sl_.setdefault(j, []).append(f)

                    def prep():
                        for b in range(NB):
                            P.op("pe", [lambda e, c=c, b=b: e.transpose(
                                pb[7][:, sl(c)], pt[b][:, sl(c)], identf[:]) for c in range(2)],
                                reads=[B_pt[b], Bc], writes=[Bpb[7]])
                            P.op("act", lambda e, b=b: e.copy(
                                pT[:, :, sl(b)], pb[7][:, 0:256].rearrange("p (k t) -> p k t", k=2)),
                                reads=[Bpb[7]], writes=[B_pT])
                        finish_rstd(s)
                        for k in range(8):
                            P.op("dve", lambda e, k=k: e.scalar_tensor_tensor(
                                xs[:, k, :], hs[:, k, :], vec[:, V_PLE + li, k:k + 1], rstd[s][:], ALU.mult, ALU.mult),
                                reads=[B_ht[s], Bc, B_rstd[s]], writes=[B_xn[s]])
                    add(0, prep)

                    def ple_group(m):
                        mb = m % 2
                        P.op("pe", [lambda e, k=k: e.matmul(
                            pb[4][:, 0:T], wg[:, k, sl(m)], xs[:, k, :], start=(k == 0), stop=(k == 7))
                            for k in range(8)], reads=[B_xn[s], B_wg], writes=[Bpb[4]])
                        P.op("pe", [lambda e, c=c: e.matmul(
                            pb[5][:, 0:T], wp[:, c, sl(m)], pT[:, c, :], start=(c == 0), stop=(c == 1))
                            for c in range(2)], reads=[B_pT, B_wp], writes=[Bpb[5]])
                        P.op("act", lambda e: e.activation(sg2[mb][:], pb[4][:, 0:T], AF.Tanh, scale=0.5),
                             reads=[Bpb[4]], writes=[B_sg2[mb]])
                        P.op("dve", lambda e: e.scalar_tensor_tensor(
                            sg2[mb][:], sg2[mb][:], 1.0, pb[5][:, 0:T], ALU.add, ALU.mult),
                            reads=[B_sg2[mb], Bpb[5]], writes=[B_sg2[mb]])
                        P.op("dve", lambda e: e.scalar_tensor_tensor(
                            hs[:, m, :], sg2[mb][:], 0.5, hs[:, m, :], ALU.mult, ALU.add),
                            reads=[B_sg2[mb], B_ht[s]], writes=[B_ht[s]])
                        if final:
                            sq_stat(hs, s, m)
                            if m > 0:
                                pe_stat(m - 1)
                    for m in range(8):
                        add(1 + m, lambda m=m: ple_group(m))

                    def fin_norm():
                        pe_stat(7)
                        finish_rstd(s)
                        for k in range(8):
                            P.op("dve", lambda e, k=k: e.scalar_tensor_tensor(
                                hs[:, k, :], hs[:, k, :], vec[:, V_FIN, k:k + 1], rstd[s][:], ALU.mult, ALU.mult),
                                reads=[B_ht[s], Bc, B_rstd[s]], writes=[B_ht[s]])

                    def out_block(b, hf):
                        P.op("pe", [lambda e, kk=kk: e.transpose(
                            pb[7][:, sl(kk)], hs[:, hf * 4 + kk, sl(b)], identf[:]) for kk in range(4)],
                            reads=[B_ht[s], Bc], writes=[Bpb[7]])
                        P.op("act", lambda e: e.copy(yt[hf][:], pb[7][:]), reads=[Bpb[7]], writes=[B_yt[hf]])
                        P.dma("sp", yout_d[t0 + b * 128:t0 + (b + 1) * 128, hf * 512:(hf + 1) * 512], yt[hf][:],
                              reads=[B_yt[hf]], writes=[B_Y[it]], slot=B_yt[hf])
                    if final:
                        add(9, fin_norm)
                        jj = 10
                        for b in range(NB):
                            for hf in range(2):
                                add(jj, lambda b=b, hf=hf: out_block(b, hf))
                                jj += 1
                    if not final or debug:
                        add(14, lambda: P.dma("sp", Hv[:, :, t0:t0 + T], hs[:], reads=[B_ht[s]],
                                              writes=[B_H[it]], slot=B_ht[s]))
                    return sl_

                load_tile(0)
                prologue(0)
                for it in range(nt):
                    s = it % 2
                    t0 = it * T
                    hs = ht[s]
                    xs = xn[s]
                    slots = post_slots(it - 1) if (do_ple and it > 0) else {}
                    if it + 1 < nt and not first and not do_ple:
                        load_tile(it + 1)
                    for j in range(NJ):
                        jb = j % 2
                        for f_ in slots.get(j, []):
                            f_()
                        if it + 1 < nt:
                            if first and j == 6:
                                load_tile(it + 1)
                            if do_ple and j == 15:
                                load_tile(it + 1)
                            if j == (18 if do_ple else 12):
                                prologue(it + 1)
                        P.op("pe", [lambda e, k=k, j=j, jb=jb, xs=xs: e.matmul(
                            pb[jb][:, 0:T], win[:, k, sl(j)], xs[:, k, :], start=(k == 0), stop=(k == 7))
                            for k in range(8)], reads=[B_xn[s], B_win[0]], writes=[Bpb[jb]])
                        P.op("pe", [lambda e, k=k, j=j, jb=jb, xs=xs: e.matmul(
                            pb[2 + jb][:, 0:T], win[:, k, DFF + j * 128:DFF + (j + 1) * 128], xs[:, k, :],
                            start=(k == 0), stop=(k == 7))
                            for k in range(8)], reads=[B_xn[s], B_win[0]], writes=[Bpb[2 + jb]])
                        P.op("act", lambda e, jb=jb: e.activation(sg[jb][:], pb[jb][:, 0:T], AF.Silu),
                             reads=[Bpb[jb]], writes=[B_sg[jb]])
                        P.op("dve", lambda e, j=j, jb=jb: e.tensor_tensor(
                            hT[:, j, :], sg[jb][:], pb[2 + jb][:, 0:T], ALU.mult),
                            reads=[B_sg[jb], Bpb[2 + jb]], writes=[B_hT])
                    if do_ple:
                        p_dma(it)
                    for m in range(8):
                        mb = 4 + m % 2
                        P.op("pe", [lambda e, j=j, m=m, mb=mb: e.matmul(
                            pb[mb][:, 0:T], wout[:, j, sl(m)], hT[:, j, :], start=(j == 0), stop=(j == NJ - 1))
                            for j in range(NJ)], reads=[B_hT] + B_wout, writes=[Bpb[mb]])
                        P.op("dve", lambda e, m=m, mb=mb, hs=hs: e.scalar_tensor_tensor(
                            hs[:, m, :], pb[mb][:, 0:T], 0.5, hs[:, m, :], ALU.mult, ALU.add),
                            reads=[Bpb[mb], B_ht[s]], writes=[B_ht[s]])
                        if do_ple:
                            sq_stat(hs, s, m)
                            if m > 0:
                                pe_stat(m - 1)
                    if do_ple:
                        pe_stat(7)
                    else:
                        P.dma("sp", Hv[:, :, t0:t0 + T], hs[:], reads=[B_ht[s]], writes=[B_H[it]], slot=B_ht[s])
                if do_ple:
                    last = post_slots(nt - 1)
                    for j in sorted(last):
                        for f_ in last[j]:
                            f_()
                P.end_phase()

        def conv_phase():
            with ExitStack() as pes:
                sb = lambda name, shape, dt: pes.enter_context(nc.sbuf_tensor(uniq(name), shape, dt))
                wcv = sb("wcv", [128, 8, 2048], BF16)
                dg = sb("dg", [128, 8 * CW, 128], BF16)
                cw = sb("cw", [128, 8, CW], F32)
                ht = [sb(f"ht{i}", [128, 8, T], F32) for i in range(2)]
                xn2 = [sb(f"xn{i}", [128, 8, T], BF16) for i in range(2)]
                sqt = sb("sqt", [128, 2, T], BF16)
                rstd2 = [sb(f"rstd{i}", [128, T], F32) for i in range(2)]
                sg = [sb(f"sg{i}", [128, T], F32) for i in range(2)]
                u0 = sb("u0", [128, 8, 30 + T], BF16)
                cv = sb("cv", [128, 8, T], F32)
                cvb = sb("cvb", [128, 2, T], BF16)
                mean = sb("mean", [128, T], F32)
                lrs = sb("lrs", [128, T], F32)
                tmp = [sb(f"tmp{i}", [128, T], F32) for i in range(2)]
                ub = [sb(f"ub{i}", [128, 8, T], BF16) for i in range(2)]
                B_w = P.bufs(8, "wcv")
                B_dg = P.buf("dg")
                B_cw = P.buf("cw")
                B_ht = P.bufs(2, "ht")
                B_xn2 = P.bufs(2, "xn")
                B_sq = P.bufs(2, "sq")
                B_rstd2 = P.bufs(2, "rstd")
                B_sg = P.bufs(2, "sg")
                B_u0 = P.buf("u0")
                B_cv = P.buf("cv")
                B_cvb = P.bufs(2, "cvb")
                B_mean = P.buf("mean")
                B_lrs = P.buf("lrs")
                B_tmp = P.bufs(2, "tmp")
                B_ub = P.bufs(2, "ub")
                for k in range(8):
                    load_w(wcv[:, k, :], hyb_win[sl(k), 0:2048], B_w[k])
                P.dma("sp", cw[:], cw_d, writes=[B_cw], slot=B_cw)
                for c in range(8):
                    P.op("pool", lambda e, c=c: e.tensor_tensor(
                        dg[:, c * CW:(c + 1) * CW, :],
                        identb[:].unsqueeze(1).broadcast_to([128, CW, 128]),
                        cw[:, c, :].unsqueeze(2).broadcast_to([128, CW, 128]), ALU.mult),
                        reads=[B_cw, Bc], writes=[B_dg])
                P.op("dve", lambda e: e.memset(u0[:, :, 0:30], 0.0), writes=[B_u0])
                def prologue(it_):
                    s_ = it_ % 2
                    P.dma("sp", ht[s_][:], Hv[:, :, it_ * T:(it_ + 1) * T], reads=[B_H[it_]], writes=[B_ht[s_]], slot=B_ht[s_])
                    make_xn(ht[s_], B_ht[s_], V_NMIX + 0, xn2[s_], B_xn2[s_], sqt, B_sq, rstd2[s_], B_rstd2[s_], pb[6], Bpb[6])

                prologue(0)
                for it in range(nt):
                    s = it % 2
                    t0 = it * T
                    hs = ht[s]
                    xn = xn2[s]
                    B_xn = B_xn2[s]
                    if it > 0:
                        P.op("dve", lambda e: e.tensor_copy(u0[:, :, 0:30], u0[:, :, T:T + 30]),
                             reads=[B_u0], writes=[B_u0])
                    for c in range(8):
                        jb = c % 2
                        P.op("pe", [lambda e, k=k, c=c, jb=jb, xn=xn: e.matmul(
                            pb[jb][:, 0:T], wcv[:, k, sl(c)], xn[:, k, :], start=(k == 0), stop=(k == 7))
                            for k in range(8)], reads=[B_xn] + B_w, writes=[Bpb[jb]])
                        P.op("pe", [lambda e, k=k, c=c, jb=jb, xn=xn: e.matmul(
                            pb[2 + jb][:, 0:T], wcv[:, k, 1024 + c * 128:1024 + (c + 1) * 128], xn[:, k, :],
                            start=(k == 0), stop=(k == 7))
                            for k in range(8)], reads=[B_xn] + B_w, writes=[Bpb[2 + jb]])
                        P.op("act", lambda e, jb=jb: e.activation(sg[jb][:], pb[2 + jb][:, 0:T], AF.Tanh, scale=0.5),
                             reads=[Bpb[2 + jb]], writes=[B_sg[jb]])
                        P.op("dve", lambda e, c=c, jb=jb: e.scalar_tensor_tensor(
                            u0[:, c, 30:30 + T], sg[jb][:], 1.0, pb[jb][:, 0:T], ALU.add, ALU.mult),
                            reads=[B_sg[jb], Bpb[jb]], writes=[B_u0])
                    if it + 1 < nt:
                        prologue(it + 1)
                    for c in range(8):
                        mb = 4 + c % 2
                        q = c % 2
                        P.op("pe", [lambda e, k=k, c=c, mb=mb: e.matmul(
                            pb[mb][:, 0:T], dg[:, c * CW + k, :], u0[:, c, k:k + T],
                            start=(k == 0), stop=(k == CW - 1))
                            for k in range(CW)], reads=[B_u0, B_dg], writes=[Bpb[mb]])
                        P.op("act", lambda e, c=c, mb=mb: e.activation(
                            cv[:, c, :], pb[mb][:, 0:T], AF.Identity, bias=vec[:, V_CVB, c:c + 1], scale=0.5),
                            reads=[Bpb[mb], Bc], writes=[B_cv])
                        P.op("dve", lambda e, c=c, q=q: e.tensor_copy(cvb[:, q, :], cv[:, c, :]),
                             reads=[B_cv], writes=[B_cvb[q]])
                        P.op("pe", lambda e, c=c, q=q: e.matmul(
                            pb[6][:, 0:T], ones1k[:], cvb[:, q, :], start=(c == 0), stop=(c == 7)),
                            reads=[B_cvb[q], Bc], writes=[Bpb[6]])
                    P.op("act", lambda e: e.copy(mean[:], pb[6][:, 0:T]), reads=[Bpb[6]], writes=[B_mean])
                    for c in range(8):
                        q = c % 2
                        P.op("dve", lambda e, c=c, q=q: e.tensor_tensor(tmp[q][:], cv[:, c, :], mean[:], ALU.subtract),
                             reads=[B_cv, B_mean], writes=[B_tmp[q]])
                        P.op("act", lambda e, q=q: e.activation(sqt[:, q, :], tmp[q][:], AF.Square),
                             reads=[B_tmp[q]], writes=[B_sq[q]])
                        P.op("pe", lambda e, c=c, q=q: e.matmul(
                            pb[7][:, 0:T], ones1k[:], sqt[:, q, :], start=(c == 0), stop=(c == 7)),
                            reads=[B_sq[q], Bc], writes=[Bpb[7]])
                    P.op("act", lambda e: e.activation(lrs[:], pb[7][:, 0:T], AF.Sqrt, bias=epsc, scale=1.0),
                         reads=[Bpb[7], Bc], writes=[B_lrs])
                    P.op("dve", lambda e: e.reciprocal(lrs[:], lrs[:]), reads=[B_lrs], writes=[B_lrs])
                    for c in range(8):
                        q = c % 2
                        P.op("dve", lambda e, c=c, q=q: e.tensor_tensor(tmp[q][:], cv[:, c, :], mean[:], ALU.subtract),
                             reads=[B_cv, B_mean], writes=[B_tmp[q]])
                        P.op("dve", lambda e, q=q: e.tensor_tensor(tmp[q][:], tmp[q][:], lrs[:], ALU.mult),
                             reads=[B_lrs, B_tmp[q]], writes=[B_tmp[q]])
                        P.op("act", lambda e, c=c, q=q, s=s: e.activation(
                            ub[s][:, c, :], tmp[q][:], AF.Silu, bias=vec[:, V_LNB, c:c + 1],
                            scale=vec[:, V_LNG, c:c + 1]),
                            reads=[B_tmp[q], Bc], writes=[B_ub[s]])
                    P.dma("sp", Uv[:, :, t0:t0 + T], ub[s][:], reads=[B_ub[s]], writes=[B_U[it]], slot=B_ub[s])
                P.end_phase()

        def ssd_phase():
            with ExitStack() as pes:
                sb = lambda name, shape, dt: pes.enter_context(nc.sbuf_tensor(uniq(name), shape, dt))
                NZ = HYB_IN - 2048
                wz = sb("wz", [128, 8, NZ], BF16)
                wo = sb("wo", [128, 16, D], BF16)
                scw = sb("scw", [128, 12, 5], F32)
                h16 = sb("h16", [128, 3, 16], F32)
                abc = sb("abc", [128, 16], F32)
                ht = [sb(f"ht{i}", [128, 8, T], F32) for i in range(2)]
                xn2 = [sb(f"xn{i}", [128, 8, T], BF16) for i in range(2)]
                sqt = sb("sqt", [128, 2, T], BF16)
                rstd2 = [sb(f"rstd{i}", [128, T], F32) for i in range(2)]
                sz = sb("sz", [128, 8, T], F32)
                xb = sb("xb", [128, 12, 3 + T], BF16)
                dg4 = sb("dg4", [128, 48, 128], BF16)
                xsf = sb("xsf", [128, 8, T], F32)
                xsb = sb("xsb", [128, 8, T], BF16)
                bcb = sb("bcb", [128, 4, T], BF16)
                dtt = sb("dtt", [128, 16], F32)
                adt = sb("adt", [128, 16], F32)
                acs = sb("acs", [128, 16], F32)
                ala = sb("ala", [128, 16], F32)
                cdec = sb("cdec", [128, 16], F32)
                coef = sb("coef", [128, 16], F32)
                xdt = sb("xdt", [128, 16, 64], BF16)
                xdd = sb("xdd", [128, 16, 64], BF16)
                btm = sb("btm", [128, 2, 128], BF16)
                R = sb("R", [128, 8, 128], F32)
                dif = sb("dif", [128, 8, 128], F32)
                erow = sb("erow", [128, 8, 128], F32)
                MT = sb("MT", [128, 8, 128], BF16)
                Cs = sb("Cs", [128, 8, 128], BF16)
                cbm = sb("cbm", [128, 2, 128], F32)
                prev = sb("prev", [128, 16, 64], F32)
                prevb = sb("prevb", [128, 16, 64], BF16)
                yg = sb("yg", [128, 8, T], F32)
                yn = sb("yn", [128, 8, T], BF16)
                grs = [sb(f"grs{i}", [128, T], F32) for i in range(2)]
                ut = sb("ut", [128, 8, T], BF16)
                B_wz = P.bufs(8, "wz")
                B_wo = P.bufs(2, "wo")
                B_sm = P.buf("small")
                B_ht = P.bufs(2, "ht")
                B_xn2 = P.bufs(2, "xn")
                B_sq = P.bufs(2, "sq")
                B_rstd2 = P.bufs(2, "rstd")
                B_sz = P.buf("sz")
                B_xb = P.buf("xb")
                B_dg4 = P.buf("dg4")
                B_xs = P.buf("xs")
                B_bc = P.buf("bcb")
                B_dt = P.buf("dt")
                B_co = P.buf("coefs")
                B_xdt = P.buf("xdt")
                B_btm = P.buf("btm")
                B_R = P.buf("R")
                B_dif = P.buf("dif")
                B_er = P.buf("erow")
                B_MT = P.buf("MT")
                B_Cs = P.buf("Cs")
                B_cbm = P.buf("cbm")
                B_prev = P.buf("prev")
                B_prevb = P.buf("prevb")
                B_yg = P.buf("yg")
                B_yn = P.buf("yn")
                B_grs = P.bufs(2, "grs")
                B_ut = P.buf("ut")
                for k in range(8):
                    load_w(wz[:, k, :], hyb_win[sl(k), 2048:HYB_IN], B_wz[k])
                wo_v = hyb_wout.rearrange("(j p) m -> p j m", p=128)
                for hh in range(2):
                    load_w(wo[:, hh * 8:(hh + 1) * 8, :], wo_v[:, hh * 8:(hh + 1) * 8, :], B_wo[hh])
                P.dma("sp", scw[:], scw_d, writes=[B_sm], slot=B_sm)
                for i in range(3):
                    P.dma("sp", h16[:, i, :], h16_d[i].partition_broadcast(128), writes=[B_sm], slot=B_sm)
                P.op("act", lambda e: e.activation(abc[:], h16[:, 1, :], AF.Exp), reads=[B_sm], writes=[B_sm])
                P.op("dve", lambda e: e.tensor_scalar_mul(abc[:], abc[:], -1.0), reads=[B_sm], writes=[B_sm])
                for c in range(12):
                    P.op("pool", lambda e, c=c: e.tensor_tensor(
                        dg4[:, c * 4:(c + 1) * 4, :],
                        identb[:].unsqueeze(1).broadcast_to([128, 4, 128]),
                        scw[:, c, 0:4].unsqueeze(2).broadcast_to([128, 4, 128]), ALU.mult),
                        reads=[B_sm, Bc], writes=[B_dg4])
                P.op("dve", lambda e: e.memset(xb[:, :, 0:3], 0.0), writes=[B_xb])
                P.op("dve", lambda e: e.memset(prev[:], 0.0), writes=[B_prev])
                P.op("dve", lambda e: e.memset(prevb[:], 0.0), writes=[B_prevb])
                def prologue(it_):
                    s_ = it_ % 2
                    P.dma("sp", ht[s_][:], Hv[:, :, it_ * T:(it_ + 1) * T], reads=[B_H[it_]], writes=[B_ht[s_]], slot=B_ht[s_])
                    make_xn(ht[s_], B_ht[s_], V_NMIX + 0, xn2[s_], B_xn2[s_], sqt, B_sq, rstd2[s_], B_rstd2[s_], pb[6], Bpb[6])

                prologue(0)
                for it in range(nt):
                    s = it % 2
                    t0 = it * T
                    hs = ht[s]
                    xn = xn2[s]
                    B_xn = B_xn2[s]
                    P.dma("sp", ut[:], Uv[:, :, t0:t0 + T], reads=[B_U[it]], writes=[B_ut], slot=B_ut)
                    if it > 0:
                        P.op("dve", lambda e: e.tensor_copy(xb[:, :, 0:3], xb[:, :, T:T + 3]),
                             reads=[B_xb], writes=[B_xb])
                    for c in range(8):
                        jb = c % 2
                        P.op("pe", [lambda e, k=k, c=c, jb=jb, xn=xn: e.matmul(
                            pb[jb][:, 0:T], wz[:, k, sl(c)], xn[:, k, :], start=(k == 0), stop=(k == 7))
                            for k in range(8)], reads=[B_xn] + B_wz, writes=[Bpb[jb]])
                        P.op("act", lambda e, c=c, jb=jb: e.activation(sz[:, c, :], pb[jb][:, 0:T], AF.Silu),
                             reads=[Bpb[jb]], writes=[B_sz])
                    for c in range(12):
                        jb = c % 2
                        P.op("pe", [lambda e, k=k, c=c, jb=jb, xn=xn: e.matmul(
                            pb[jb][:, 0:T], wz[:, k, 1024 + c * 128:1024 + (c + 1) * 128], xn[:, k, :],
                            start=(k == 0), stop=(k == 7))
                            for k in range(8)], reads=[B_xn] + B_wz, writes=[Bpb[jb]])
                        P.op("act", lambda e, c=c, jb=jb: e.copy(xb[:, c, 3:3 + T], pb[jb][:, 0:T]),
                             reads=[Bpb[jb]], writes=[B_xb])
                    for c in range(12):
                        mb = 2 + c % 2
                        P.op("pe", [lambda e, c=c, k=k, mb=mb: e.matmul(
                            pb[mb][:, 0:T], dg4[:, c * 4 + k, :], xb[:, c, k:k + T], start=(k == 0), stop=(k == 3))
                            for k in range(4)], reads=[B_xb, B_dg4], writes=[Bpb[mb]])
                        if c < 8:
                            P.op("act", lambda e, c=c, mb=mb: e.activation(
                                xsf[:, c, :], pb[mb][:, 0:T], AF.Silu, bias=scw[:, c, 4:5], scale=1.0),
                                reads=[Bpb[mb], B_sm], writes=[B_xs])
                            P.op("pool", lambda e, c=c: e.tensor_copy(xsb[:, c, :], xsf[:, c, :]),
                                 reads=[B_xs], writes=[B_xs])
                        else:
                            P.op("act", lambda e, c=c, mb=mb: e.activation(
                                bcb[:, c - 8, :], pb[mb][:, 0:T], AF.Silu, bias=scw[:, c, 4:5], scale=1.0),
                                reads=[Bpb[mb], B_sm], writes=[B_bc])
                    if it + 1 < nt:
                        prologue(it + 1)
                    for cch in range(NB if STG >= 2 else 0):
                        csl = slice(cch * 128, (cch + 1) * 128)
                        P.op("pe", [lambda e, k=k, csl=csl, xn=xn: e.matmul(
                            pb[5][:, 256:272], xn[:, k, csl], wz[:, k, 2560:2576], start=(k == 0), stop=(k == 7))
                            for k in range(8)], reads=[B_xn] + B_wz, writes=[Bpb[5]])
                        P.op("dve", lambda e: e.tensor_tensor(dtt[:], pb[5][:, 256:272], h16[:, 0, :], ALU.add),
                             reads=[Bpb[5], B_sm], writes=[B_dt])
                        P.op("act", lambda e: e.activation(dtt[:], dtt[:], AF.Exp), reads=[B_dt], writes=[B_dt])
                        P.op("act", lambda e: e.activation(dtt[:], dtt[:], AF.Ln, bias=onec, scale=1.0),
                             reads=[B_dt, Bc], writes=[B_dt])
                        P.op("dve", lambda e: e.tensor_tensor(adt[:], dtt[:], abc[:], ALU.mult),
                             reads=[B_dt, B_sm], writes=[B_dt])
                        P.op("pe", [lambda e: e.matmul(pb[5][:, 272:288], triu[:], adt[:], start=True, stop=True),
                                    lambda e: e.matmul(pb[5][:, 288:304], ones_f[:], adt[:], start=True, stop=True)],
                             reads=[B_dt, Bc], writes=[Bpb[5]])
                        P.op("dve", lambda e: e.tensor_copy(acs[:], pb[5][:, 272:288]), reads=[Bpb[5]], writes=[B_co])
                        P.op("dve", lambda e: e.tensor_copy(ala[:], pb[5][:, 288:304]), reads=[Bpb[5]], writes=[B_co])
                        P.op("act", lambda e: e.activation(cdec[:], ala[:], AF.Exp), reads=[B_co], writes=[B_co])
                        P.op("dve", lambda e: e.tensor_tensor(coef[:], ala[:], acs[:], ALU.subtract),
                             reads=[B_co], writes=[B_co])
                        P.op("act", lambda e: e.activation(coef[:], coef[:], AF.Exp), reads=[B_co], writes=[B_co])
                        P.op("dve", lambda e: e.tensor_tensor(coef[:], coef[:], dtt[:], ALU.mult),
                             reads=[B_co, B_dt], writes=[B_co])
                        if STG < 3:
                            continue
                        pbt = pb[4][:].bitcast(BF16)
                        P.op("pe", [lambda e, c=c, csl=csl: e.transpose(pbt[:, sl(c)], xsb[:, c, csl], identb[:])
                                    for c in range(8)], reads=[B_xs, Bc], writes=[Bpb[4]])
                        pbt3 = pbt.rearrange("p (h d) -> p h d", h=16)
                        P.op("dve", lambda e: e.tensor_tensor(
                            xdt[:], pbt3, dtt[:].unsqueeze(2).broadcast_to([128, 16, 64]), ALU.mult),
                            reads=[Bpb[4], B_dt], writes=[B_xdt])
                        P.op("dve", lambda e: e.tensor_tensor(
                            xdd[:], pbt3, coef[:].unsqueeze(2).broadcast_to([128, 16, 64]), ALU.mult),
                            reads=[Bpb[4], B_co], writes=[B_xdt])
                        P.op("pe", [lambda e, g=g, csl=csl: e.transpose(pbt[:, sl(g)], bcb[:, g, csl], identb[:])
                                    for g in range(2)], reads=[B_bc, Bc], writes=[Bpb[4]])
                        P.op("act", lambda e: e.copy(btm[:], pbt[:, 0:256].rearrange("p (g n) -> p g n", g=2)),
                             reads=[Bpb[4]], writes=[B_btm])
                        P.op("pe", [lambda e, g=g, csl=csl: e.matmul(
                            pb[5][:, sl(g)], bcb[:, g, csl], bcb[:, 2 + g, csl], start=True, stop=True)
                            for g in range(2)], reads=[B_bc], writes=[Bpb[5]])
                        P.op("dve", lambda e: e.tensor_tensor(
                            cbm[:], pb[5][:, 0:256].rearrange("p (g n) -> p g n", g=2),
                            triu[:].unsqueeze(1).broadcast_to([128, 2, 128]), ALU.mult),
                            reads=[Bpb[5], Bc], writes=[B_cbm])
                        if STG < 4:
                            continue
                        for g in range(2):
                            hsl = slice(g * 8, (g + 1) * 8)
                            P.op("dve", lambda e, hsl=hsl: e.tensor_tensor(
                                R[:], triu[:].unsqueeze(1).broadcast_to([128, 8, 128]),
                                adt[:, hsl].unsqueeze(2).broadcast_to([128, 8, 128]), ALU.mult),
                                reads=[B_dt, Bc], writes=[B_R])
                            P.op("pe", [lambda e, h2=h2: e.matmul(
                                pb[h2 // 2][:, (h2 % 2) * 256:(h2 % 2) * 256 + 256], ones_f[:],
                                R[:, h2 * 2:(h2 + 1) * 2, :], start=True, stop=True)
                                for h2 in range(4)], reads=[B_R, Bc], writes=[Bpb[0], Bpb[1]])
                            for hh in range(2):
                                h4 = slice(hh * 4, (hh + 1) * 4)
                                a4 = slice(g * 8 + hh * 4, g * 8 + hh * 4 + 4)
                                rb = pb[hh][:].rearrange("p (h l) -> p h l", h=4)
                                P.op("dve", lambda e, h4=h4, a4=a4, rb=rb: e.tensor_tensor(
                                    dif[:, h4, :], rb, acs[:, a4].unsqueeze(2).broadcast_to([128, 4, 128]),
                                    ALU.subtract), reads=[Bpb[hh], B_co], writes=[B_dif])
                                P.op("act", lambda e, h4=h4, rb=rb: e.activation(erow[:, h4, :], rb, AF.Exp),
                                     reads=[Bpb[hh]], writes=[B_er])
                            if SUB != 1:
                                P.op("act", lambda e: e.activation(dif[:], dif[:], AF.Exp), reads=[B_dif], writes=[B_dif])
                            P.op("dve", lambda e, g=g: e.scalar_tensor_tensor(
                                MT[:], dif[:], 1.0, cbm[:, g, :].unsqueeze(1).broadcast_to([128, 8, 128]),
                                ALU.min, ALU.mult), reads=[B_dif, B_cbm], writes=[B_MT])
                            P.op("dve", lambda e, g=g, csl=csl: e.tensor_tensor(
                                Cs[:], erow[:], bcb[:, 2 + g, csl].unsqueeze(1).broadcast_to([128, 8, 128]),
                                ALU.mult), reads=[B_er, B_bc], writes=[B_Cs])
                            for hh in range(8 if STG >= 5 else 0):
                                h = g * 8 + hh
                                cch_out = h // 2
                                half = h % 2
                                bank = pb[2 + cch_out // 4]
                                col = (cch_out % 4) * 128
                                P.op("pe", [
                                    lambda e, h=h, hh=hh, half=half, bank=bank, col=col: e.matmul(
                                        bank[half * 64:(half + 1) * 64, col:col + 128], xdt[:, h, :], MT[:, hh, :],
                                        start=True, stop=False),
                                    lambda e, h=h, hh=hh, half=half, bank=bank, col=col: e.matmul(
                                        bank[half * 64:(half + 1) * 64, col:col + 128], prevb[:, h, :], Cs[:, hh, :],
                                        start=False, stop=True)],
                                    reads=[B_xdt, B_MT, B_prevb, B_Cs], writes=[Bpb[2 + cch_out // 4]])
                            if SUB != 2:
                              P.op("pe", lambda e, g=g: e.matmul(
                                pb[6 + g][:], btm[:, g, :], xdd[:, g * 8:(g + 1) * 8, :], start=True, stop=True),
                                reads=[B_btm, B_xdt], writes=[Bpb[6 + g]])
                        for g in range(2 if SUB != 2 else 0):
                            hsl = slice(g * 8, (g + 1) * 8)
                            P.op("dve", lambda e, hsl=hsl: e.tensor_tensor(
                                prev[:, hsl, :], prev[:, hsl, :],
                                cdec[:, hsl].unsqueeze(2).broadcast_to([128, 8, 64]), ALU.mult),
                                reads=[B_co, B_prev], writes=[B_prev])
                            P.op("dve", lambda e, hsl=hsl, g=g: e.tensor_tensor(
                                prev[:, hsl, :], prev[:, hsl, :],
                                pb[6 + g][:].rearrange("p (h d) -> p h d", h=8), ALU.add),
                                reads=[Bpb[6 + g], B_prev], writes=[B_prev])
                        P.op("act", lambda e: e.copy(prevb[:], prev[:]), reads=[B_prev], writes=[B_prevb])
                        for c in range(8):
                            bank = pb[2 + c // 4]
                            col = (c % 4) * 128
                            P.op("dve", lambda e, c=c, bank=bank, col=col, csl=csl: e.scalar_tensor_tensor(
                                yg[:, c, csl], xsf[:, c, csl], vec[:, V_SSD, c:c + 1], bank[:, col:col + 128],
                                ALU.mult, ALU.add), reads=[Bpb[2 + c // 4], B_xs, Bc], writes=[B_yg])
                    P.op("pool", lambda e: e.tensor_tensor(yg[:], yg[:], sz[:], ALU.mult),
                         reads=[B_sz, B_yg], writes=[B_yg])
                    for g in range(2):
                        rms_rstd(yg, B_yg, 4, ones512, sqt, B_sq, pb[6], Bpb[6], grs[g][:], B_grs[g], k0=g * 4)
                    for c in range(8):
                        P.op("dve", lambda e, c=c: e.scalar_tensor_tensor(
                            yn[:, c, :], yg[:, c, :], vec[:, V_SSN, c:c + 1], grs[c // 4][:], ALU.mult, ALU.mult),
                            reads=[B_yg, Bc, B_grs[c // 4]], writes=[B_yn])
                    for m in range(8):
                        mb = m % 2
                        P.op("pe", [lambda e, j=j, m=m, mb=mb: e.matmul(
                            pb[mb][:, 0:T], wo[:, j, sl(m)], (ut[:, j, :] if j < 8 else yn[:, j - 8, :]),
                            start=(j == 0), stop=(j == 15))
                            for j in range(16)], reads=[B_ut, B_yn] + B_wo, writes=[Bpb[mb]])
                        P.op("dve", lambda e, m=m, mb=mb, hs=hs: e.tensor_tensor(
                            hs[:, m, :], hs[:, m, :], pb[mb][:, 0:T], ALU.add),
                            reads=[Bpb[mb], B_ht[s]], writes=[B_ht[s]])
                    P.dma("sp", Hv[:, :, t0:t0 + T], hs[:], reads=[B_ht[s]], writes=[B_H[it]], slot=B_ht[s])
                P.end_phase()

        def att_phase():
            with ExitStack() as pes:
                sb = lambda name, shape, dt: pes.enter_context(nc.sbuf_tensor(uniq(name), shape, dt))
                wq = sb("wq", [128, 8, 1536], BF16)
                wo = sb("wo", [128, 8, D], BF16)
                bqk = sb("bqk", [128, 10], F32)
                bvb = sb("bvb", [128, 256], F32)
                snk = sb("snk", [128, 16], F32)
                mskb = sb("mskb", [128, 2, 256], BF16)
                ht = [sb(f"ht{i}", [128, 8, T], F32) for i in range(2)]
                xn2 = [sb(f"xn{i}", [128, 8, T], BF16) for i in range(2)]
                sqt = sb("sqt", [128, 2, T], BF16)
                rstd2 = [sb(f"rstd{i}", [128, T], F32) for i in range(2)]
                rope = sb("rope", [128, 2, T], F32)
                qf = [sb(f"qf{i}", [128, T], F32) for i in range(2)]
                qb = [sb(f"qb{i}", [128, T], BF16) for i in range(2)]
                t1 = [sb(f"t1{i}", [128, T], F32) for i in range(2)]
                qr = sb("qr", [128, 8, T], BF16)
                kr = sb("kr", [128, 2, 128 + T], BF16)
                vt = sb("vt", [128, 1 + NB, 256], BF16)
                ee = [sb(f"ee{i}", [128, 4, 256], F32) for i in range(2)]
                pp = [sb(f"pp{i}", [128, 4, 256], BF16) for i in range(2)]
                pT = [sb(f"pT{i}", [128, 8, 128], BF16) for i in range(2)]
                mx = [sb(f"mx{i}", [128, 4], F32) for i in range(2)]
                nmx = [sb(f"nmx{i}", [128, 4], F32) for i in range(2)]
                rs = [sb(f"rs{i}", [128, 4], F32) for i in range(2)]
                es_ = [sb(f"es_{i}", [128, 4], F32) for i in range(2)]
                oT = sb("oT", [128, 8, T], BF16)
                B_wq = P.bufs(8, "wq")
                B_wo = P.buf("wo")
                B_sm_ = P.buf("small")
                B_ht = P.bufs(2, "ht")
                B_xn2 = P.bufs(2, "xn")
                B_sq = P.bufs(2, "sq")
                B_rstd2 = P.bufs(2, "rstd")
                B_rope = P.buf("rope")
                B_qf = P.bufs(2, "qf")
                B_qb = P.bufs(2, "qb")
                B_t1 = P.bufs(2, "t1")
                B_qr = P.buf("qr")
                B_kr = P.buf("kr")
                B_vt = P.buf("vt")
                B_s = P.bufs(2, "sm")
                B_e = P.bufs(2, "ee")
                B_p = P.bufs(2, "pp")
                B_pT = P.bufs(2, "pT")
                B_st = P.bufs(2, "stats")
                B_oT = P.buf("oT")
                for k in range(8):
                    load_w(wq[:, k, :], wqkv_d[sl(k), :], B_wq[k])
                load_w(wo[:], wo_d.rearrange("(k p) m -> p k m", p=128), B_wo)
                P.dma("sp", bqk[:], bqk_d, writes=[B_sm_], slot=B_sm_)
                P.dma("sp", bvb[:], bv_d.partition_broadcast(128), writes=[B_sm_], slot=B_sm_)
                P.dma("sp", snk[:], h16_d[2].partition_broadcast(128), writes=[B_sm_], slot=B_sm_)
                P.dma("pool", mskb[:], msk_d.rearrange("a p s -> p a s"), writes=[B_sm_], slot=B_sm_)
                P.op("dve", lambda e: e.memset(kr[:, :, 0:128], 0.0), writes=[B_kr])
                P.op("dve", lambda e: e.memset(vt[:, 0, :], 0.0), writes=[B_vt])
                def prologue(it_):
                    s_ = it_ % 2
                    P.dma("sp", ht[s_][:], Hv[:, :, it_ * T:(it_ + 1) * T], reads=[B_H[it_]], writes=[B_ht[s_]], slot=B_ht[s_])
                    make_xn(ht[s_], B_ht[s_], V_NMIX + 1, xn2[s_], B_xn2[s_], sqt, B_sq, rstd2[s_], B_rstd2[s_], pb[6], Bpb[6])

                prologue(0)
                for it in range(nt):
                    s = it % 2
                    t0 = it * T
                    hs = ht[s]
                    xn = xn2[s]
                    B_xn = B_xn2[s]
                    P.dma("sp", rope[:], rope_d[:, :, t0:t0 + T].rearrange("a p s -> p a s"),
                          writes=[B_rope], slot=B_rope)
                    if it > 0:
                        P.op("dve", lambda e: e.tensor_copy(kr[:, :, 0:128], kr[:, :, T:T + 128]),
                             reads=[B_kr], writes=[B_kr])
                        P.op("dve", lambda e: e.tensor_copy(vt[:, 0, :], vt[:, NB, :]), reads=[B_vt], writes=[B_vt])
                    for c in range(10):
                        jb = c % 2
                        P.op("pe", [lambda e, k=k, c=c, jb=jb, xn=xn: e.matmul(
                            pb[jb][:, 0:T], wq[:, k, sl(c)], xn[:, k, :], start=(k == 0), stop=(k == 7))
                            for k in range(8)], reads=[B_xn] + B_wq, writes=[Bpb[jb]])
                        P.op("act", lambda e, c=c, jb=jb: e.activation(
                            qf[jb][:], pb[jb][:, 0:T], AF.Identity, bias=bqk[:, c:c + 1], scale=1.0),
                            reads=[Bpb[jb], B_sm_], writes=[B_qf[jb]])
                        P.op("pool", lambda e, jb=jb: e.tensor_copy(qb[jb][:], qf[jb][:]),
                             reads=[B_qf[jb]], writes=[B_qb[jb]])
                        P.op("pe", lambda e, jb=jb: e.matmul(pb[2 + jb][:, 0:T], pswap[:], qb[jb][:], start=True, stop=True),
                             reads=[B_qb[jb], Bc], writes=[Bpb[2 + jb]])
                        P.op("dve", lambda e, jb=jb: e.tensor_tensor(t1[jb][:], qf[jb][:], rope[:, 0, :], ALU.mult),
                             reads=[B_qf[jb], B_rope], writes=[B_t1[jb]])
                        P.op("dve", lambda e, jb=jb: e.tensor_tensor(qf[jb][:], pb[2 + jb][:, 0:T], rope[:, 1, :], ALU.mult),
                             reads=[Bpb[2 + jb], B_rope, B_qf[jb]], writes=[B_qf[jb]])
                        dst = qr[:, c, :] if c < 8 else kr[:, c - 8, 128:128 + T]
                        P.op("dve", lambda e, jb=jb, dst=dst: e.tensor_tensor(dst, t1[jb][:], qf[jb][:], ALU.add),
                             reads=[B_t1[jb], B_qf[jb]], writes=[B_qr if c < 8 else B_kr])
                    for b in range(NB):
                        jb = b % 2
                        P.op("pe", [lambda e, k=k, b=b, jb=jb, xn=xn: e.matmul(
                            pb[jb][:, 0:256], xn[:, k, sl(b)], wq[:, k, 1280:1536], start=(k == 0), stop=(k == 7))
                            for k in range(8)], reads=[B_xn] + B_wq, writes=[Bpb[jb]])
                        P.op("dve", lambda e, b=b, jb=jb: e.tensor_tensor(vt[:, 1 + b, :], pb[jb][:, 0:256], bvb[:], ALU.add),
                             reads=[Bpb[jb], B_sm_], writes=[B_vt])
                    if it + 1 < nt:
                        prologue(it + 1)
                    def mk_group(b, kv, a):
                        gblk = it * NB + b
                        mi = 0 if gblk == 0 else 1
                        half = kv % 2
                        hp = slice(half * 64, (half + 1) * 64)
                        kc = kv // 2
                        qc0 = (kv // 2) * 4
                        S0, S1, PTb, Ob = 4 * a, 4 * a + 1, 4 * a + 2, 4 * a + 3
                        ee_, pp_, pT_ = ee[a], pp[a], pT[a]
                        mx_, nmx_, rs_, es2 = mx[a], nmx[a], rs[a], es_[a]
                        Be, Bp, BpT, Bst = B_e[a], B_p[a], B_pT[a], B_st[a]
                        sg4 = snk[:, kv * 4:(kv + 1) * 4]
                        pbt = pb[PTb][:].bitcast(BF16)
                        G = {}

                        def pe_s():
                            fns = []
                            for i in range(4):
                                dst = pb[S0 + i // 2][:, (i % 2) * 256:(i % 2) * 256 + 256]
                                fns.append(lambda e, i=i, dst=dst: e.matmul(
                                    dst, qr[hp, qc0 + i, sl(b)], kr[hp, kc, b * 128:b * 128 + 256],
                                    start=True, stop=False))
                                fns.append(lambda e, dst=dst: e.matmul(
                                    dst, identb[:], mskb[:, mi, :], start=False, stop=True))
                            P.op("pe", fns, reads=[B_qr, B_kr, B_sm_, Bc], writes=[Bpb[S0], Bpb[S1]])

                        def d1():
                            for hh in range(2):
                                P.op("dve", lambda e, hh=hh: e.tensor_reduce(
                                    mx_[:, hh * 2:hh * 2 + 2], pb[S0 + hh][:].rearrange("p (h s) -> p h s", h=2),
                                    AX.X, ALU.max), reads=[Bpb[S0 + hh]], writes=[Bst])
                            P.op("dve", lambda e: e.scalar_tensor_tensor(mx_[:], mx_[:], 0.125, sg4, ALU.mult, ALU.max),
                                 reads=[Bst, B_sm_], writes=[Bst])
                            P.op("dve", lambda e: e.tensor_scalar_mul(nmx_[:], mx_[:], -1.0), reads=[Bst], writes=[Bst])
                            P.op("dve", lambda e: e.tensor_tensor(es2[:], sg4, mx_[:], ALU.subtract),
                                 reads=[Bst, B_sm_], writes=[Bst])
                            P.op("dve", lambda e: e.memset(rs_[:], 0.0), writes=[Bst])

                        def a1():
                            for i in range(4):
                                src = pb[S0 + i // 2][:, (i % 2) * 256:(i % 2) * 256 + 256]
                                P.op("act", lambda e, i=i, src=src: e.activation(
                                    ee_[:, i, :], src, AF.Exp, bias=nmx_[:, i:i + 1], scale=0.125,
                                    accum_out=rs_[:, i:i + 1]), reads=[Bpb[S0 + i // 2], Bst], writes=[Be, Bst])
                            P.op("act", lambda e: e.activation(es2[:], es2[:], AF.Exp), reads=[Bst], writes=[Bst])

                        def d2():
                            P.op("dve", lambda e: e.tensor_tensor(rs_[:], rs_[:], es2[:], ALU.add), reads=[Bst], writes=[Bst])
                            P.op("dve", lambda e: e.reciprocal(rs_[:], rs_[:]), reads=[Bst], writes=[Bst])
                            P.op("dve", lambda e: e.tensor_tensor(
                                pp_[:], ee_[:], rs_[:].unsqueeze(2).broadcast_to([128, 4, 256]), ALU.mult),
                                reads=[Be, Bst], writes=[Bp])

                        def pe_t():
                            P.op("pe", [lambda e, i=i, kb=kb: e.transpose(
                                pbt[:, sl(i * 2 + kb)], pp_[:, i, sl(kb)], identb[:])
                                for i in range(4) for kb in range(2)], reads=[Bp, Bc], writes=[Bpb[PTb]])

                        def a2():
                            P.op("act", lambda e: e.copy(pT_[:], pbt.rearrange("p (a q) -> p a q", a=8)),
                                 reads=[Bpb[PTb]], writes=[BpT])

                        def pe_pv():
                            P.op("pe", [lambda e, i=i, kb=kb: e.matmul(
                                pb[Ob][hp, i * 128:(i + 1) * 128], vt[:, b + kb, kv * 64:(kv + 1) * 64],
                                pT_[:, i * 2 + kb, :], start=(kb == 0), stop=(kb == 1))
                                for i in range(4) for kb in range(2)], reads=[B_vt, BpT], writes=[Bpb[Ob]])

                        def a3():
                            P.op("act", lambda e: e.copy(
                                oT[hp, qc0:qc0 + 4, sl(b)], pb[Ob][hp, :].rearrange("p (i q) -> p i q", i=4)),
                                reads=[Bpb[Ob]], writes=[B_oT])
                        G.update(pe_s=pe_s, d1=d1, a1=a1, d2=d2, pe_t=pe_t, a2=a2, pe_pv=pe_pv, a3=a3)
                        return G

                    groups = [mk_group(b_, kv_, gi % 2) for gi, (b_, kv_) in
                              enumerate([(b_, kv_) for b_ in range(NB) for kv_ in range(4)])]
                    for c_ in range(len(groups) + 1):
                        cur = groups[c_] if c_ < len(groups) else None
                        prv = groups[c_ - 1] if c_ > 0 else None
                        if cur:
                            cur["pe_s"]()
                        if prv:
                            prv["d2"]()
                            prv["pe_t"]()
                        if cur:
                            cur["d1"]()
                        if prv:
                            prv["a2"]()
                            prv["pe_pv"]()
                        if cur:
                            cur["a1"]()
                        if prv:
                            prv["a3"]()
                    for m in range(8):
                        mb = m % 2
                        P.op("pe", [lambda e, j=j, m=m, mb=mb: e.matmul(
                            pb[mb][:, 0:T], wo[:, j, sl(m)], oT[:, j, :], start=(j == 0), stop=(j == 7))
                            for j in range(8)], reads=[B_oT, B_wo], writes=[Bpb[mb]])
                        P.op("dve", lambda e, m=m, mb=mb, hs=hs: e.scalar_tensor_tensor(
                            hs[:, m, :], pb[mb][:, 0:T], vec[:, V_BO, m:m + 1], hs[:, m, :], ALU.add, ALU.add),
                            reads=[Bpb[mb], B_ht[s], Bc], writes=[B_ht[s]])
                    P.dma("sp", Hv[:, :, t0:t0 + T], hs[:], reads=[B_ht[s]], writes=[B_H[it]], slot=B_ht[s])
                P.end_phase()

        last = phases[-1]
        for ph in phases:
            if ph == "ffn1_0":
                ffn_phase(0, 0, True, False, False)
            elif ph == "conv":
                conv_phase()
            elif ph == "ssd":
                ssd_phase()
            elif ph == "ffn2_0":
                ffn_phase(1, 0, False, True, False)
            elif ph == "ffn1_1":
                ffn_phase(2, 1, False, False, False)
            elif ph == "att":
                att_phase()
            elif ph == "ffn2_1":
                ffn_phase(3, 1, False, True, True)
        P.barrier()
        P.emit()
    return nc


Q_LOWER = [0, 1, 2, 3, 8, 9, 10, 11]
Q_UPPER = [4, 5, 6, 7, 12, 13, 14, 15]
HEAD_ORDER = [h for c in range(8) for h in (Q_LOWER[c], Q_UPPER[c])]


def _pk(v):
    v = np.asarray(v, np.float32)
    return np.array(v.reshape(-1, 128).T, dtype=np.float32, order='C', copy=True)


def prep_shared(inp, S=SEQ):
    f = lambda a: np.array(a, dtype=np.float32, order='C', copy=True)
    sh = {}
    sh["ffn_win"] = f(np.stack([inp["ffn1_w_in"][0], inp["ffn2_w_in"][0], inp["ffn1_w_in"][1], inp["ffn2_w_in"][1]]))
    sh["ffn_wout"] = f(np.stack([inp["ffn1_w_out"][0], inp["ffn2_w_out"][0], inp["ffn1_w_out"][1], inp["ffn2_w_out"][1]]))
    vec = np.zeros((128, 20, 8), np.float32)
    for li in range(2):
        vec[:, 0 + li] = _pk(inp["norm_ffn1"][li])
        vec[:, 2 + li] = _pk(inp["norm_mix"][li])
        vec[:, 4 + li] = _pk(inp["norm_ffn2"][li])
        vec[:, 6 + li] = _pk(inp["ple_norm"][li])
    vec[:, 8] = _pk(inp["final_norm"])
    vec[:, 9] = _pk(inp["conv_dw_b"][0])
    vec[:, 10] = _pk(inp["conv_ln_g"][0])
    vec[:, 11] = _pk(inp["conv_ln_b"][0])
    vec[:, 12] = _pk(np.repeat(np.asarray(inp["ssm_d"][0], np.float32), 64))
    vec[:, 13] = _pk(inp["ssm_norm"][0])
    vec[:, 14] = _pk(inp["att_b_o"][0])
    sh["vecs"] = vec
    sh["ple_wg"] = f(inp["ple_gate_w"])
    sh["ple_wp"] = f(inp["ple_proj_w"])
    sh["hyb_win"] = f(inp["hyb_w_in"][0])
    sh["hyb_wout"] = f(inp["hyb_w_out"][0])
    cw = np.asarray(inp["conv_dw_w"][0], np.float32)
    sh["cw"] = f(cw.T.reshape(8, 128, CW).transpose(1, 0, 2))
    scw = np.asarray(inp["ssm_conv_w"][0], np.float32)
    scb = np.asarray(inp["ssm_conv_b"][0], np.float32)
    sc = np.concatenate([scw, scb[None]], 0)
    sh["scw"] = f(sc.T.reshape(12, 128, 5).transpose(1, 0, 2))
    sinks = np.asarray(inp["att_sinks"][0], np.float32)
    sg_order = [HEAD_ORDER[((kv // 2) * 4 + i) * 2 + kv % 2] for kv in range(4) for i in range(4)]
    sh["h16"] = f(np.stack([inp["ssm_dt_bias"][0], inp["ssm_a_log"][0], sinks[sg_order]]))
    wqkv = np.asarray(inp["att_w_qkv"][0], np.float32)
    bqkv = np.asarray(inp["att_b_qkv"][0], np.float32)
    qcols = np.concatenate([np.arange(h * 64, (h + 1) * 64) for h in HEAD_ORDER])
    cols = np.concatenate([qcols, np.arange(1024, 1536)])
    sh["wqkv"] = f(wqkv[:, cols])
    bp = bqkv[cols]
    sh["bqk"] = _pk(bp[:1280])
    sh["bv"] = f(bp[1280:])
    sh["wo"] = f(np.asarray(inp["att_w_o"][0], np.float32)[qcols, :])
    inv = (np.float32(10000.0) ** (-np.arange(0, 64, 2, dtype=np.float32) / np.float32(64))).astype(np.float32)
    ang = (np.arange(S, dtype=np.float32)[:, None] * inv[None, :]).astype(np.float32)
    cos, sin = np.cos(ang).astype(np.float32), np.sin(ang).astype(np.float32)
    prt = np.arange(128)
    CC = cos.T[prt % 32]
    sgn = np.where((prt % 64) < 32, -1.0, 1.0).astype(np.float32)
    SSn = sin.T[prt % 32] * sgn[:, None]
    sh["rope"] = f(np.stack([CC, SSn]))
    cst = np.zeros((5, 128, 128), np.float32)
    cst[0] = np.eye(128)
    cst[1] = np.triu(np.ones((128, 128)))
    sw = np.zeros((128, 128), np.float32)
    for m in range(128):
        sw[(m + 32) % 64 + (m // 64) * 64, m] = 1.0
    cst[2] = sw
    sh["cst"] = cst
    q = np.arange(128)[:, None]
    sp = np.arange(256)[None, :]
    valid = np.where(sp < 128, sp > q, (sp - 128) <= q)
    m1 = np.where(valid, 0.0, NEG).astype(np.float32)
    m0 = np.where(valid & (sp >= 128), 0.0, NEG).astype(np.float32)
    sh["msk"] = f(np.stack([m0, m1]))
    return sh


_CACHE = {}


def kernel(**inputs):
    x = np.asarray(inputs["x"], np.float32)
    p = np.asarray(inputs["p"], np.float32)
    B, S, _ = x.shape
    sh = prep_shared(inputs, S)
    key = ("full", S)
    if key not in _CACHE:
        _CACHE[key] = build_program(S)
    nc = _CACHE[key]
    in_maps = []
    for b in range(B):
        m = dict(sh)
        m["x"] = np.array(x[b], dtype=np.float32, order='C', copy=True)
        m["p"] = np.array(p[:, b], dtype=np.float32, order='C', copy=True)
        in_maps.append(m)
    res = run_bass_kernel_spmd(nc, in_maps, core_ids=list(range(B)))
    return np.stack([np.asarray(r["y"], np.float32) for r in res.results], 0)
```

```python
from contextlib import ExitStack
import os
import numpy as np
import concourse.bass as bass
import concourse.mybir as mybir
from concourse.bass_utils import run_bass_kernel_spmd

F32 = mybir.dt.float32
BF16 = mybir.dt.bfloat16
AF = mybir.ActivationFunctionType
ALU = mybir.AluOpType
AX = mybir.AxisListType

D = 1024
DFF = 2816
NJ = DFF // 128
PLE = 256
SEQ = 4096
NCORES = 8
T = 256
NB = T // 128
EPS = 1e-6
CW = 31
HYB_IN = 4624
NEG = -240000.0
STG = int(os.environ.get('SSD_STAGE', '99'))
SUB = int(os.environ.get('SUB', '0'))


class Buf:
    __slots__ = ("name", "w", "r", "dsem", "excl")

    def __init__(self, name):
        self.name = name
        self.w = {}
        self.r = {}
        self.dsem = None
        self.excl = False


class Prog:
    ENGS = ("pe", "act", "dve", "pool", "sp")

    def __init__(self, nc, es):
        self.nc = nc
        self.es = es
        self.streams = {e: [] for e in self.ENGS}
        self.sems = {}
        self.cnt = {}
        self.waited = {e: {} for e in self.ENGS}
        self.nbuf = 0
        self.free_dsems = []
        self.phase_dsems = []
        self.ndsem = 0
        for e in ("pe", "act", "dve", "pool"):
            self._mksem("c_" + e)

    def _mksem(self, name):
        self.sems[name] = self.es.enter_context(self.nc.semaphore(name))
        self.cnt[name] = 0
        return name

    def _dsem(self):
        if self.free_dsems:
            s = self.free_dsems.pop()
        else:
            self.ndsem += 1
            s = self._mksem(f"d{self.ndsem}")
        self.phase_dsems.append(s)
        return s

    def buf(self, name=None):
        self.nbuf += 1
        return Buf(name or f"b{self.nbuf}")

    def bufs(self, n, name="b"):
        return [self.buf(f"{name}{i}") for i in range(n)]

    def _need(self, reads, writes):
        need = {}
        for b in reads:
            for s, v in b.w.items():
                if need.get(s, 0) < v:
                    need[s] = v
        for b in writes:
            for d in (b.w, b.r):
                for s, v in d.items():
                    if need.get(s, 0) < v:
                        need[s] = v
        return need

    def _emit_waits(self, eng, need, skip_own=False):
        wd = self.waited[eng]
        own = "c_" + eng
        for s, v in need.items():
            if skip_own and s == own:
                continue
            if wd.get(s, 0) < v:
                wd[s] = v
                h = self.sems[s]
                self.streams[eng].append(lambda e, h=h, v=v: e.wait_ge(h, v))

    def _mark(self, reads, writes, s, v):
        for b in reads:
            if b.r.get(s, 0) < v:
                b.r[s] = v
        for b in writes:
            b.w = {s: v}
            b.r = {}

    def op(self, eng, fns, reads=(), writes=(), skip_own=None):
        if callable(fns):
            fns = [fns]
        if skip_own is None:
            skip_own = (eng == "pe")
        ex = [b for b in reads if b.excl]
        if ex:
            writes = list(writes) + ex
            reads = [b for b in reads if not b.excl]
        self._emit_waits(eng, self._need(reads, writes), skip_own)
        s = "c_" + eng
        self.cnt[s] += 1
        v = self.cnt[s]
        h = self.sems[s]
        st = self.streams[eng]
        for f in fns[:-1]:
            st.append(f)
        last = fns[-1]
        st.append(lambda e, last=last, h=h: last(e).then_inc(h, 1))
        self._mark(reads, writes, s, v)

    def dma(self, q, out, in_, reads=(), writes=(), slot=None):
        self._emit_waits(q, self._need(reads, writes), False)
        if slot.dsem is None:
            slot.dsem = self._dsem()
        s = slot.dsem
        self.cnt[s] += 16
        v = self.cnt[s]
        h = self.sems[s]
        self.streams[q].append(lambda e, out=out, in_=in_, h=h: e.dma_start(out=out, in_=in_).then_inc(h, 16))
        self._mark(reads, writes, s, v)

    def barrier(self):
        need = {s: v for s, v in self.cnt.items() if v > 0}
        for e in self.ENGS:
            self._emit_waits(e, need, False)

    def end_phase(self):
        self.barrier()
        self.free_dsems.extend(self.phase_dsems)
        self.phase_dsems = []

    def emit(self):
        nc = self.nc
        st = self.streams
        with nc.Block() as block:
            @block.tensor
            def _(e):
                for f in st["pe"]:
                    f(e)

            @block.scalar
            def _(e):
                for f in st["act"]:
                    f(e)

            @block.vector
            def _(e):
                for f in st["dve"]:
                    f(e)

            @block.gpsimd
            def _(e):
                for f in st["pool"]:
                    f(e)

            @block.sync
            def _(e):
                for f in st["sp"]:
                    f(e)


class Ctx:
    pass


def sl(i, n=128):
    return slice(i * n, (i + 1) * n)


def build_program(S=SEQ, phases=("ffn1_0", "conv", "ssd", "ffn2_0", "ffn1_1", "att", "ffn2_1"), debug=False):
    nc = bass.Bass("TRN2", target_bir_lowering=False)
    nt = S // T
    dt_in = {}

    def din(name, shape):
        dt_in[name] = nc.dram_tensor(name, list(shape), F32, kind="ExternalInput").ap()
        return dt_in[name]

    x_d = din("x", [S, D])
    p_d = din("p", [2, S, PLE])
    ffn_win = din("ffn_win", [4, D, 2 * DFF])
    ffn_wout = din("ffn_wout", [4, DFF, D])
    vecs = din("vecs", [128, 20, 8])
    ple_wg = din("ple_wg", [2, D, D])
    ple_wp = din("ple_wp", [2, PLE, D])
    hyb_win = din("hyb_win", [D, HYB_IN])
    hyb_wout = din("hyb_wout", [2 * D, D])
    cw_d = din("cw", [128, 8, CW])
    scw_d = din("scw", [128, 12, 5])
    h16_d = din("h16", [3, 16])
    wqkv_d = din("wqkv", [D, 1536])
    bqk_d = din("bqk", [128, 10])
    bv_d = din("bv", [256])
    wo_d = din("wo", [D, D])
    rope_d = din("rope", [2, 128, S])
    cst_d = din("cst", [5, 128, 128])
    msk_d = din("msk", [2, 128, 256])
    if debug:
        out_d = nc.dram_tensor("H", [8, 128, S], F32, kind="ExternalOutput").ap()
        Hd = out_d
        yout_d = nc.dram_tensor("y", [S, D], F32, kind="ExternalOutput").ap()
    else:
        yout_d = nc.dram_tensor("y", [S, D], F32, kind="ExternalOutput").ap()
        Hd = nc.dram_tensor("Hs", [8, 128, S], F32).ap()
    Ud = nc.dram_tensor("Us", [8, 128, S], BF16).ap()
    Hv = Hd.rearrange("k p s -> p k s")
    Uv = Ud.rearrange("k p s -> p k s")

    with ExitStack() as es:
        P = Prog(nc, es)
        C = Ctx()
        gsb = lambda name, shape, dt: es.enter_context(nc.sbuf_tensor(name, shape, dt))
        pb = [es.enter_context(nc.psum_tensor(f"pb{i}", [128, 512], F32)) for i in range(8)]
        Bpb = P.bufs(8, "pb")
        for b_ in Bpb:
            b_.excl = True
        identf = gsb("identf", [128, 128], F32)
        identb = gsb("identb", [128, 128], BF16)
        triu = gsb("triu", [128, 128], F32)
        pswap = gsb("pswap", [128, 128], BF16)
        pswapf = gsb("pswapf", [128, 128], F32)
        ones_f = gsb("ones_f", [128, 128], F32)
        ones1k = gsb("ones1k", [128, 128], BF16)
        ones512 = gsb("ones512", [128, 128], BF16)
        cols = gsb("cols", [128, 4], F32)
        vec = gsb("vec", [128, 20, 8], F32)
        Bc = P.buf("consts")
        P.dma("sp", identf[:], cst_d[0], writes=[Bc], slot=Bc)
        P.dma("sp", triu[:], cst_d[1], writes=[Bc], slot=Bc)
        P.dma("sp", pswapf[:], cst_d[2], writes=[Bc], slot=Bc)
        P.dma("sp", vec[:], vecs, writes=[Bc], slot=Bc)
        P.dma("pool", identb[:], cst_d[0], writes=[Bc], slot=Bc)
        P.dma("pool", pswap[:], cst_d[2], writes=[Bc], slot=Bc)
        P.op("dve", lambda e: e.memset(ones_f[:], 1.0), writes=[Bc])
        P.op("dve", lambda e: e.memset(ones1k[:], 1.0 / 1024), writes=[Bc])
        P.op("dve", lambda e: e.memset(ones512[:], 1.0 / 512), writes=[Bc])
        P.op("dve", lambda e: e.memset(cols[:, 0:1], EPS), writes=[Bc])
        P.op("dve", lambda e: e.memset(cols[:, 1:2], 1.0), writes=[Bc])
        P.op("dve", lambda e: e.memset(cols[:, 2:3], 0.0), writes=[Bc])
        epsc = cols[:, 0:1]
        onec = cols[:, 1:2]
        B_H = P.bufs(nt, "H")
        B_U = P.bufs(nt, "U")
        B_Y = P.bufs(nt, "Y")
        P.end_phase()

        V_NF1, V_NMIX, V_NF2, V_PLE = 0, 2, 4, 6
        V_FIN, V_CVB, V_LNG, V_LNB, V_SSD, V_SSN, V_BO = 8, 9, 10, 11, 12, 13, 14

        def rms_rstd(xap, Bx, nk, ones_ap, sqt, Bsq, pbank, Bpbank, rstd, Brstd, k0=0):
            for k in range(nk):
                q = k % 2
                P.op("act", lambda e, k=k, q=q: e.activation(sqt[:, q, :], xap[:, k0 + k, :], AF.Square),
                     reads=[Bx], writes=[Bsq[q]])
                P.op("pe", lambda e, k=k, q=q: e.matmul(pbank[:, 0:T], ones_ap[:], sqt[:, q, :],
                                                       start=(k == 0), stop=(k == nk - 1)),
                     reads=[Bsq[q], Bc], writes=[Bpbank])
            P.op("act", lambda e: e.activation(rstd, pbank[:, 0:T], AF.Sqrt, bias=epsc, scale=1.0),
                 reads=[Bpbank, Bc], writes=[Brstd])
            P.op("dve", lambda e: e.reciprocal(rstd, rstd), reads=[Brstd], writes=[Brstd])

        def make_xn(ht_s, Bht, gidx, xn, Bxn, sqt, Bsq, rstd, Brstd, pbank, Bpbank):
            rms_rstd(ht_s, Bht, 8, ones1k, sqt, Bsq, pbank, Bpbank, rstd[:], Brstd)
            for k in range(8):
                P.op("dve", lambda e, k=k: e.scalar_tensor_tensor(
                    xn[:, k, :], ht_s[:, k, :], vec[:, gidx, k:k + 1], rstd[:], ALU.mult, ALU.mult),
                    reads=[Bht, Bc, Brstd], writes=[Bxn])

        ucnt = [0]

        def uniq(name):
            ucnt[0] += 1
            return f"s{ucnt[0]}_{name}"

        def load_w(dst, src_rows_ap, Bw, q="pool"):
            P.dma(q, dst, src_rows_ap, writes=[Bw], slot=Bw)

        def ffn_phase(fi, li, first, do_ple, final):
            with ExitStack() as pes:
                sb = lambda name, shape, dt: pes.enter_context(nc.sbuf_tensor(uniq(name), shape, dt))
                win = sb("win", [128, 8, 2 * DFF], BF16)
                wout = sb("wout", [128, NJ, D], BF16)
                ht = [sb(f"ht{i}", [128, 8, T], F32) for i in range(2)]
                xn = [sb(f"xn{i}", [128, 8, T], BF16) for i in range(2)]
                sqt = sb("sqt", [128, 2, T], BF16)
                rstd = [sb(f"rstd{i}", [128, T], F32) for i in range(2)]
                sg = [sb(f"sg{i}", [128, T], F32) for i in range(2)]
                hT = sb("hT", [128, NJ, T], BF16)
                B_win = P.bufs(1, "win")
                B_wout = P.bufs(2, "wout")
                B_ht = P.bufs(2, "ht")
                B_xn = P.bufs(2, "xn")
                B_sq = P.bufs(2, "sq")
                B_rstd = P.bufs(2, "rstd")
                B_sg = P.bufs(2, "sg")
                B_hT = P.buf("hT")
                if first:
                    xt = [sb("xt0", [128, D], F32)]
                    B_xt = P.bufs(1, "xt")
                if final:
                    yt = [sb(f"yt{i}", [128, 512], F32) for i in range(2)]
                    B_yt = P.bufs(2, "yt")
                if do_ple:
                    wg = sb("wg", [128, 8, D], BF16)
                    wp = sb("wp", [128, 2, D], BF16)
                    pt = [sb(f"pt{i}", [128, PLE], F32) for i in range(2)]
                    pT = sb("pT", [128, 2, T], BF16)
                    sg2 = [sb(f"sgp{i}", [128, T], F32) for i in range(2)]
                    B_wg = P.buf("wg")
                    B_wp = P.buf("wp")
                    B_pt = P.bufs(2, "pt")
                    B_pT = P.buf("pT")
                    B_sg2 = P.bufs(2, "sgp")
                for k in range(8):
                    load_w(win[:, k, :], ffn_win[fi, sl(k), :], B_win[0])
                wo_v = ffn_wout[fi].rearrange("(j p) m -> p j m", p=128)
                for hh in range(2):
                    load_w(wout[:, hh * 11:(hh + 1) * 11, :], wo_v[:, hh * 11:(hh + 1) * 11, :], B_wout[hh])
                if do_ple:
                    load_w(wg[:], ple_wg[li].rearrange("(k p) m -> p k m", p=128), B_wg)
                    load_w(wp[:], ple_wp[li].rearrange("(k p) m -> p k m", p=128), B_wp)
                gidx = (V_NF1 if not do_ple else V_NF2) + li

                def load_tile(it):
                    s = it % 2
                    t0 = it * T
                    hs = ht[s]
                    if first:
                        for b in range(NB):
                            P.dma("sp", xt[0][:], x_d[t0 + b * 128:t0 + (b + 1) * 128, :],
                                  writes=[B_xt[0]], slot=B_xt[0])
                            for hf in range(2):
                                P.op("pe", [lambda e, kk=kk, hf=hf: e.transpose(
                                    pb[7][:, sl(kk)], xt[0][:, sl(hf * 4 + kk)], identf[:]) for kk in range(4)],
                                    reads=[B_xt[0], Bc], writes=[Bpb[7]])
                                P.op("act", lambda e, hf=hf, b=b, hs=hs: e.copy(
                                    hs[:, hf * 4:(hf + 1) * 4, sl(b)], pb[7][:].rearrange("p (k t) -> p k t", k=4)),
                                    reads=[Bpb[7]], writes=[B_ht[s]])
                    else:
                        P.dma("sp", hs[:], Hv[:, :, t0:t0 + T], reads=[B_H[it]], writes=[B_ht[s]], slot=B_ht[s])

                def prologue(it):
                    s = it % 2
                    make_xn(ht[s], B_ht[s], gidx, xn[s], B_xn[s], sqt, B_sq, rstd[s], B_rstd[s], pb[7], Bpb[7])

                def sq_stat(hs_, s_, m_):
                    q_ = m_ % 2
                    P.op("act", lambda e: e.activation(sqt[:, q_, :], hs_[:, m_, :], AF.Square),
                         reads=[B_ht[s_]], writes=[B_sq[q_]])

                def pe_stat(m_):
                    q_ = m_ % 2
                    P.op("pe", lambda e: e.matmul(pb[6][:, 0:T], ones1k[:], sqt[:, q_, :],
                                                  start=(m_ == 0), stop=(m_ == 7)),
                         reads=[B_sq[q_], Bc], writes=[Bpb[6]])

                def finish_rstd(s_):
                    rs_ = rstd[s_]
                    P.op("act", lambda e: e.activation(rs_[:], pb[6][:, 0:T], AF.Sqrt, bias=epsc, scale=1.0),
                         reads=[Bpb[6], Bc], writes=[B_rstd[s_]])
                    P.op("dve", lambda e: e.reciprocal(rs_[:], rs_[:]), reads=[B_rstd[s_]], writes=[B_rstd[s_]])

                def p_dma(it):
                    t0 = it * T
                    for b in range(NB):
                        P.dma("sp", pt[b][:], p_d[li, t0 + b * 128:t0 + (b + 1) * 128, :],
                              writes=[B_pt[b]], slot=B_pt[b])

                def post_slots(it):
                    s = it % 2
                    t0 = it * T
                    hs = ht[s]
                    xs = xn[s]
                    sl_ = {}

                    def add(j, f):
                        sl_.setdefault(j, []).append(f)

                    def prep():
                        for b in range(NB):
                            P.op("pe", [lambda e, c=c, b=b: e.transpose(
                                pb[7][:, sl(c)], pt[b][:, sl(c)], identf[:]) for c in range(2)],
                                reads=[B_pt[b], Bc], writes=[Bpb[7]])
                            P.op("act", lambda e, b=b: e.copy(
                                pT[:, :, sl(b)], pb[7][:, 0:256].rearrange("p (k t) -> p k t", k=2)),
                                reads=[Bpb[7]], writes=[B_pT])
                        finish_rstd(s)
                        for k in range(8):
                            P.op("dve", lambda e, k=k: e.scalar_tensor_tensor(
                                xs[:, k, :], hs[:, k, :], vec[:, V_PLE + li, k:k + 1], rstd[s][:], ALU.mult, ALU.mult),
                                reads=[B_ht[s], Bc, B_rstd[s]], writes=[B_xn[s]])
                    add(0, prep)

                    def ple_group(m):
                        mb = m % 2
                        P.op("pe", [lambda e, k=k: e.matmul(
                            pb[4][:, 0:T], wg[:, k, sl(m)], xs[:, k, :], start=(k == 0), stop=(k == 7))
                            for k in range(8)], reads=[B_xn[s], B_wg], writes=[Bpb[4]])
                        P.op("pe", [lambda e, c=c: e.matmul(
                            pb[5][:, 0:T], wp[:, c, sl(m)], pT[:, c, :], start=(c == 0), stop=(c == 1))
                            for c in range(2)], reads=[B_pT, B_wp], writes=[Bpb[5]])
                        P.op("act", lambda e: e.activation(sg2[mb][:], pb[4][:, 0:T], AF.Tanh, scale=0.5),
                             reads=[Bpb[4]], writes=[B_sg2[mb]])
                        P.op("dve", lambda e: e.scalar_tensor_tensor(
                            sg2[mb][:], sg2[mb][:], 1.0, pb[5][:, 0:T], ALU.add, ALU.mult),
                            reads=[B_sg2[mb], Bpb[5]], writes=[B_sg2[mb]])
                        P.op("dve", lambda e: e.scalar_tensor_tensor(
                            hs[:, m, :], sg2[mb][:], 0.5, hs[:, m, :], ALU.mult, ALU.add),
                            reads=[B_sg2[mb], B_ht[s]], writes=[B_ht[s]])
                        if final:
                            sq_stat(hs, s, m)
                            if m > 0:
                                pe_stat(m - 1)
                    for m in range(8):
                        add(1 + m, lambda m=m: ple_group(m))

                    def fin_norm():
                        pe_stat(7)
                        finish_rstd(s)
                        for k in range(8):
                            P.op("dve", lambda e, k=k: e.scalar_tensor_tensor(
                                hs[:, k, :], hs[:, k, :], vec[:, V_FIN, k:k + 1], rstd[s][:], ALU.mult, ALU.mult),
                                reads=[B_ht[s], Bc, B_rstd[s]], writes=[B_ht[s]])

                    def out_block(b, hf):
                        P.op("pe", [lambda e, kk=kk: e.transpose(
                            pb[7][:, sl(kk)], hs[:, hf * 4 + kk, sl(b)], identf[:]) for kk in range(4)],
                            reads=[B_ht[s], Bc], writes=[Bpb[7]])
                        P.op("act", lambda e: e.copy(yt[hf][:], pb[7][:]), reads=[Bpb[7]], writes=[B_yt[hf]])
                        P.dma("sp", yout_d[t0 + b * 128:t0 + (b + 1) * 128, hf * 512:(hf + 1) * 512], yt[hf][:],
                              reads=[B_yt[hf]], writes=[B_Y[it]], slot=B_yt[hf])
                    if final:
                        add(9, fin_norm)
                        jj = 10
                        for b in range(NB):
                            for hf in range(2):
                                add(jj, lambda b=b, hf=hf: out_block(b, hf))
                                jj += 1
                    if not final or debug:
                        add(14, lambda: P.dma("sp", Hv[:, :, t0:t0 + T], hs[:], reads=[B_ht[s]],
                                              writes=[B_H[it]], slot=B_ht[s]))
                    return sl_

                load_tile(0)
                prologue(0)
                for it in range(nt):
                    s = it % 2
                    t0 = it * T
                    hs = ht[s]
                    xs = xn[s]
                    slots = post_slots(it - 1) if (do_ple and it > 0) else {}
                    if it + 1 < nt and not first and not do_ple:
                        load_tile(it + 1)
                    for j in range(NJ):
                        jb = j % 2
                        for f_ in slots.get(j, []):
                            f_()
                        if it + 1 < nt:
                            if first and j == 6:
                                load_tile(it + 1)
                            if do_ple and j == 15:
                                load_tile(it + 1)
                            if j == (18 if do_ple else 12):
                                prologue(it + 1)
                        P.op("pe", [lambda e, k=k, j=j, jb=jb, xs=xs: e.matmul(
                            pb[jb][:, 0:T], win[:, k, sl(j)], xs[:, k, :], start=(k == 0), stop=(k == 7))
                            for k in range(8)], reads=[B_xn[s], B_win[0]], writes=[Bpb[jb]])
                        P.op("pe", [lambda e, k=k, j=j, jb=jb, xs=xs: e.matmul(
                            pb[2 + jb][:, 0:T], win[:, k, DFF + j * 128:DFF + (j + 1) * 128], xs[:, k, :],
                            start=(k == 0), stop=(k == 7))
                            for k in range(8)], reads=[B_xn[s], B_win[0]], writes=[Bpb[2 + jb]])
                        P.op("act", lambda e, jb=jb: e.activation(sg[jb][:], pb[jb][:, 0:T], AF.Silu),
                             reads=[Bpb[jb]], writes=[B_sg[jb]])
                        P.op("dve", lambda e, j=j, jb=jb: e.tensor_tensor(
                            hT[:, j, :], sg[jb][:], pb[2 + jb][:, 0:T], ALU.mult),
                            reads=[B_sg[jb], Bpb[2 + jb]], writes=[B_hT])
                    if do_ple:
                        p_dma(it)
                    for m in range(8):
                        mb = 4 + m % 2
                        P.op("pe", [lambda e, j=j, m=m, mb=mb: e.matmul(
                            pb[mb][:, 0:T], wout[:, j, sl(m)], hT[:, j, :], start=(j == 0), stop=(j == NJ - 1))
                            for j in range(NJ)], reads=[B_hT] + B_wout, writes=[Bpb[mb]])
                        P.op("dve", lambda e, m=m, mb=mb, hs=hs: e.scalar_tensor_tensor(
                            hs[:, m, :], pb[mb][:, 0:T], 0.5, hs[:, m, :], ALU.mult, ALU.add),
                            reads=[Bpb[mb], B_ht[s]], writes=[B_ht[s]])
                        if do_ple:
                            sq_stat(hs, s, m)
                            if m > 0:
                                pe_stat(m - 1)
                    if do_ple:
                        pe_stat(7)
                    else:
                        P.dma("sp", Hv[:, :, t0:t0 + T], hs[:], reads=[B_ht[s]], writes=[B_H[it]], slot=B_ht[s])
                if do_ple:
                    last = post_slots(nt - 1)
                    for j in sorted(last):
                        for f_ in last[j]:
                            f_()
                P.end_phase()

        def conv_phase():
            with ExitStack() as pes:
                sb = lambda name, shape, dt: pes.enter_context(nc.sbuf_tensor(uniq(name), shape, dt))
                wcv = sb("wcv", [128, 8, 2048], BF16)
                dg = sb("dg", [128, 8 * CW, 128], BF16)
                cw = sb("cw", [128, 8, CW], F32)
                ht = [sb(f"ht{i}", [128, 8, T], F32) for i in range(2)]
                xn2 = [sb(f"xn{i}", [128, 8, T], BF16) for i in range(2)]
                sqt = sb("sqt", [128, 2, T], BF16)
                rstd2 = [sb(f"rstd{i}", [128, T], F32) for i in range(2)]
                sg = [sb(f"sg{i}", [128, T], F32) for i in range(2)]
                u0 = sb("u0", [128, 8, 30 + T], BF16)
                cv = sb("cv", [128, 8, T], F32)
                cvb = sb("cvb", [128, 2, T], BF16)
                mean = sb("mean", [128, T], F32)
                lrs = sb("lrs", [128, T], F32)
                tmp = [sb(f"tmp{i}", [128, T], F32) for i in range(2)]
                ub = [sb(f"ub{i}", [128, 8, T], BF16) for i in range(2)]
                B_w = P.bufs(8, "wcv")
                B_dg = P.buf("dg")
                B_cw = P.buf("cw")
                B_ht = P.bufs(2, "ht")
                B_xn2 = P.bufs(2, "xn")
                B_sq = P.bufs(2, "sq")
                B_rstd2 = P.bufs(2, "rstd")
                B_sg = P.bufs(2, "sg")
                B_u0 = P.buf("u0")
                B_cv = P.buf("cv")
                B_cvb = P.bufs(2, "cvb")
                B_mean = P.buf("mean")
                B_lrs = P.buf("lrs")
                B_tmp = P.bufs(2, "tmp")
                B_ub = P.bufs(2, "ub")
                for k in range(8):
                    load_w(wcv[:, k, :], hyb_win[sl(k), 0:2048], B_w[k])
                P.dma("sp", cw[:], cw_d, writes=[B_cw], slot=B_cw)
                for c in range(8):
                    P.op("pool", lambda e, c=c: e.tensor_tensor(
                        dg[:, c * CW:(c + 1) * CW, :],
                        identb[:].unsqueeze(1).broadcast_to([128, CW, 128]),
                        cw[:, c, :].unsqueeze(2).broadcast_to([128, CW, 128]), ALU.mult),
                        reads=[B_cw, Bc], writes=[B_dg])
                P.op("dve", lambda e: e.memset(u0[:, :, 0:30], 0.0), writes=[B_u0])
                def prologue(it_):
                    s_ = it_ % 2
                    P.dma("sp", ht[s_][:], Hv[:, :, it_ * T:(it_ + 1) * T], reads=[B_H[it_]], writes=[B_ht[s_]], slot=B_ht[s_])
                    make_xn(ht[s_], B_ht[s_], V_NMIX + 0, xn2[s_], B_xn2[s_], sqt, B_sq, rstd2[s_], B_rstd2[s_], pb[6], Bpb[6])

                prologue(0)
                for it in range(nt):
                    s = it % 2
                    t0 = it * T
                    hs = ht[s]
                    xn = xn2[s]
                    B_xn = B_xn2[s]
                    if it > 0:
                        P.op("dve", lambda e: e.tensor_copy(u0[:, :, 0:30], u0[:, :, T:T + 30]),
                             reads=[B_u0], writes=[B_u0])
                    for c in range(8):
                        jb = c % 2
                        P.op("pe", [lambda e, k=k, c=c, jb=jb, xn=xn: e.matmul(
                            pb[jb][:, 0:T], wcv[:, k, sl(c)], xn[:, k, :], start=(k == 0), stop=(k == 7))
                            for k in range(8)], reads=[B_xn] + B_w, writes=[Bpb[jb]])
                        P.op("pe", [lambda e, k=k, c=c, jb=jb, xn=xn: e.matmul(
                            pb[2 + jb][:, 0:T], wcv[:, k, 1024 + c * 128:1024 + (c + 1) * 128], xn[:, k, :],
                            start=(k == 0), stop=(k == 7))
                            for k in range(8)], reads=[B_xn] + B_w, writes=[Bpb[2 + jb]])
                        P.op("act", lambda e, jb=jb: e.activation(sg[jb][:], pb[2 + jb][:, 0:T], AF.Tanh, scale=0.5),
                             reads=[Bpb[2 + jb]], writes=[B_sg[jb]])
                        P.op("dve", lambda e, c=c, jb=jb: e.scalar_tensor_tensor(
                            u0[:, c, 30:30 + T], sg[jb][:], 1.0, pb[jb][:, 0:T], ALU.add, ALU.mult),
                            reads=[B_sg[jb], Bpb[jb]], writes=[B_u0])
                    if it + 1 < nt:
                        prologue(it + 1)
                    for c in range(8):
                        mb = 4 + c % 2
                        q = c % 2
                        P.op("pe", [lambda e, k=k, c=c, mb=mb: e.matmul(
                            pb[mb][:, 0:T], dg[:, c * CW + k, :], u0[:, c, k:k + T],
                            start=(k == 0), stop=(k == CW - 1))
                            for k in range(CW)], reads=[B_u0, B_dg], writes=[Bpb[mb]])
                        P.op("act", lambda e, c=c, mb=mb: e.activation(
                            cv[:, c, :], pb[mb][:, 0:T], AF.Identity, bias=vec[:, V_CVB, c:c + 1], scale=0.5),
                            reads=[Bpb[mb], Bc], writes=[B_cv])
                        P.op("dve", lambda e, c=c, q=q: e.tensor_copy(cvb[:, q, :], cv[:, c, :]),
                             reads=[B_cv], writes=[B_cvb[q]])
                        P.op("pe", lambda e, c=c, q=q: e.matmul(
                            pb[6][:, 0:T], ones1k[:], cvb[:, q, :], start=(c == 0), stop=(c == 7)),
                            reads=[B_cvb[q], Bc], writes=[Bpb[6]])
                    P.op("act", lambda e: e.copy(mean[:], pb[6][:, 0:T]), reads=[Bpb[6]], writes=[B_mean])
                    for c in range(8):
                        q = c % 2
                        P.op("dve", lambda e, c=c, q=q: e.tensor_tensor(tmp[q][:], cv[:, c, :], mean[:], ALU.subtract),
                             reads=[B_cv, B_mean], writes=[B_tmp[q]])
                        P.op("act", lambda e, q=q: e.activation(sqt[:, q, :], tmp[q][:], AF.Square),
                             reads=[B_tmp[q]], writes=[B_sq[q]])
                        P.op("pe", lambda e, c=c, q=q: e.matmul(
                            pb[7][:, 0:T], ones1k[:], sqt[:, q, :], start=(c == 0), stop=(c == 7)),
                            reads=[B_sq[q], Bc], writes=[Bpb[7]])
                    P.op("act", lambda e: e.activation(lrs[:], pb[7][:, 0:T], AF.Sqrt, bias=epsc, scale=1.0),
                         reads=[Bpb[7], Bc], writes=[B_lrs])
                    P.op("dve", lambda e: e.reciprocal(lrs[:], lrs[:]), reads=[B_lrs], writes=[B_lrs])
                    for c in range(8):
                        q = c % 2
                        P.op("dve", lambda e, c=c, q=q: e.tensor_tensor(tmp[q][:], cv[:, c, :], mean[:], ALU.subtract),
                             reads=[B_cv, B_mean], writes=[B_tmp[q]])
                        P.op("dve", lambda e, q=q: e.tensor_tensor(tmp[q][:], tmp[q][:], lrs[:], ALU.mult),
                             reads=[B_lrs, B_tmp[q]], writes=[B_tmp[q]])
                        P.op("act", lambda e, c=c, q=q, s=s: e.activation(
                            ub[s][:, c, :], tmp[q][:], AF.Silu, bias=vec[:, V_LNB, c:c + 1],
                            scale=vec[:, V_LNG, c:c + 1]),
                            reads=[B_tmp[q], Bc], writes=[B_ub[s]])
                    P.dma("sp", Uv[:, :, t0:t0 + T], ub[s][:], reads=[B_ub[s]], writes=[B_U[it]], slot=B_ub[s])
                P.end_phase()

        def ssd_phase():
            with ExitStack() as pes:
                sb = lambda name, shape, dt: pes.enter_context(nc.sbuf_tensor(uniq(name), shape, dt))
                NZ = HYB_IN - 2048
                wz = sb("wz", [128, 8, NZ], BF16)
                wo = sb("wo", [128, 16, D], BF16)
                scw = sb("scw", [128, 12, 5], F32)
                h16 = sb("h16", [128, 3, 16], F32)
                abc = sb("abc", [128, 16], F32)
                ht = [sb(f"ht{i}", [128, 8, T], F32) for i in range(2)]
                xn2 = [sb(f"xn{i}", [128, 8, T], BF16) for i in range(2)]
                sqt = sb("sqt", [128, 2, T], BF16)
                rstd2 = [sb(f"rstd{i}", [128, T], F32) for i in range(2)]
                sz = sb("sz", [128, 8, T], F32)
                xb = sb("xb", [128, 12, 3 + T], BF16)
                dg4 = sb("dg4", [128, 48, 128], BF16)
                xsf = sb("xsf", [128, 8, T], F32)
                xsb = sb("xsb", [128, 8, T], BF16)
                bcb = sb("bcb", [128, 4, T], BF16)
                dtt = sb("dtt", [128, 16], F32)
                adt = sb("adt", [128, 16], F32)
                acs = sb("acs", [128, 16], F32)
                ala = sb("ala", [128, 16], F32)
                cdec = sb("cdec", [128, 16], F32)
                coef = sb("coef", [128, 16], F32)
                xdt = sb("xdt", [128, 16, 64], BF16)
                xdd = sb("xdd", [128, 16, 64], BF16)
                btm = sb("btm", [128, 2, 128], BF16)
                R = sb("R", [128, 8, 128], F32)
                dif = sb("dif", [128, 8, 128], F32)
                erow = sb("erow", [128, 8, 128], F32)
                MT = sb("MT", [128, 8, 128], BF16)
                Cs = sb("Cs", [128, 8, 128], BF16)
                cbm = sb("cbm", [128, 2, 128], F32)
                prev = sb("prev", [128, 16, 64], F32)
                prevb = sb("prevb", [128, 16, 64], BF16)
                yg = sb("yg", [128, 8, T], F32)
                yn = sb("yn", [128, 8, T], BF16)
                grs = [sb(f"grs{i}", [128, T], F32) for i in range(2)]
                ut = sb("ut", [128, 8, T], BF16)
                B_wz = P.bufs(8, "wz")
                B_wo = P.bufs(2, "wo")
                B_sm = P.buf("small")
                B_ht = P.bufs(2, "ht")
                B_xn2 = P.bufs(2, "xn")
                B_sq = P.bufs(2, "sq")
                B_rstd2 = P.bufs(2, "rstd")
                B_sz = P.buf("sz")
                B_xb = P.buf("xb")
                B_dg4 = P.buf("dg4")
                B_xs = P.buf("xs")
                B_bc = P.buf("bcb")
                B_dt = P.buf("dt")
                B_co = P.buf("coefs")
                B_xdt = P.buf("xdt")
                B_btm = P.buf("btm")
                B_R = P.buf("R")
                B_dif = P.buf("dif")
                B_er = P.buf("erow")
                B_MT = P.buf("MT")
                B_Cs = P.buf("Cs")
                B_cbm = P.buf("cbm")
                B_prev = P.buf("prev")
                B_prevb = P.buf("prevb")
                B_yg = P.buf("yg")
                B_yn = P.buf("yn")
                B_grs = P.bufs(2, "grs")
                B_ut = P.buf("ut")
                for k in range(8):
                    load_w(wz[:, k, :], hyb_win[sl(k), 2048:HYB_IN], B_wz[k])
                wo_v = hyb_wout.rearrange("(j p) m -> p j m", p=128)
                for hh in range(2):
                    load_w(wo[:, hh * 8:(hh + 1) * 8, :], wo_v[:, hh * 8:(hh + 1) * 8, :], B_wo[hh])
                P.dma("sp", scw[:], scw_d, writes=[B_sm], slot=B_sm)
                for i in range(3):
                    P.dma("sp", h16[:, i, :], h16_d[i].partition_broadcast(128), writes=[B_sm], slot=B_sm)
                P.op("act", lambda e: e.activation(abc[:], h16[:, 1, :], AF.Exp), reads=[B_sm], writes=[B_sm])
                P.op("dve", lambda e: e.tensor_scalar_mul(abc[:], abc[:], -1.0), reads=[B_sm], writes=[B_sm])
                for c in range(12):
                    P.op("pool", lambda e, c=c: e.tensor_tensor(
                        dg4[:, c * 4:(c + 1) * 4, :],
                        identb[:].unsqueeze(1).broadcast_to([128, 4, 128]),
                        scw[:, c, 0:4].unsqueeze(2).broadcast_to([128, 4, 128]), ALU.mult),
                        reads=[B_sm, Bc], writes=[B_dg4])
                P.op("dve", lambda e: e.memset(xb[:, :, 0:3], 0.0), writes=[B_xb])
                P.op("dve", lambda e: e.memset(prev[:], 0.0), writes=[B_prev])
                P.op("dve", lambda e: e.memset(prevb[:], 0.0), writes=[B_prevb])
                def prologue(it_):
                    s_ = it_ % 2
                    P.dma("sp", ht[s_][:], Hv[:, :, it_ * T:(it_ + 1) * T], reads=[B_H[it_]], writes=[B_ht[s_]], slot=B_ht[s_])
                    make_xn(ht[s_], B_ht[s_], V_NMIX + 0, xn2[s_], B_xn2[s_], sqt, B_sq, rstd2[s_], B_rstd2[s_], pb[6], Bpb[6])

                prologue(0)
                for it in range(nt):
                    s = it % 2
                    t0 = it * T
                    hs = ht[s]
                    xn = xn2[s]
                    B_xn = B_xn2[s]
                    P.dma("sp", ut[:], Uv[:, :, t0:t0 + T], reads=[B_U[it]], writes=[B_ut], slot=B_ut)
                    if it > 0:
                        P.op("dve", lambda e: e.tensor_copy(xb[:, :, 0:3], xb[:, :, T:T + 3]),
                             reads=[B_xb], writes=[B_xb])
                    for c in range(8):
                        jb = c % 2
                        P.op("pe", [lambda e, k=k, c=c, jb=jb, xn=xn: e.matmul(
                            pb[jb][:, 0:T], wz[:, k, sl(c)], xn[:, k, :], start=(k == 0), stop=(k == 7))
                            for k in range(8)], reads=[B_xn] + B_wz, writes=[Bpb[jb]])
                        P.op("act", lambda e, c=c, jb=jb: e.activation(sz[:, c, :], pb[jb][:, 0:T], AF.Silu),
                             reads=[Bpb[jb]], writes=[B_sz])
                    for c in range(12):
                        jb = c % 2
                        P.op("pe", [lambda e, k=k, c=c, jb=jb, xn=xn: e.matmul(
                            pb[jb][:, 0:T], wz[:, k, 1024 + c * 128:1024 + (c + 1) * 128], xn[:, k, :],
                            start=(k == 0), stop=(k == 7))
                            for k in range(8)], reads=[B_xn] + B_wz, writes=[Bpb[jb]])
                        P.op("act", lambda e, c=c, jb=jb: e.copy(xb[:, c, 3:3 + T], pb[jb][:, 0:T]),
                             reads=[Bpb[jb]], writes=[B_xb])
                    for c in range(12):
                        mb = 2 + c % 2
                        P.op("pe", [lambda e, c=c, k=k, mb=mb: e.matmul(
                            pb[mb][:, 0:T], dg4[:, c * 4 + k, :], xb[:, c, k:k + T], start=(k == 0), stop=(k == 3))
                            for k in range(4)], reads=[B_xb, B_dg4], writes=[Bpb[mb]])
                        if c < 8:
                            P.op("act", lambda e, c=c, mb=mb: e.activation(
                                xsf[:, c, :], pb[mb][:, 0:T], AF.Silu, bias=scw[:, c, 4:5], scale=1.0),
                                reads=[Bpb[mb], B_sm], writes=[B_xs])
                            P.op("pool", lambda e, c=c: e.tensor_copy(xsb[:, c, :], xsf[:, c, :]),
                                 reads=[B_xs], writes=[B_xs])
                        else:
                            P.op("act", lambda e, c=c, mb=mb: e.activation(
                                bcb[:, c - 8, :], pb[mb][:, 0:T], AF.Silu, bias=scw[:, c, 4:5], scale=1.0),
                                reads=[Bpb[mb], B_sm], writes=[B_bc])
                    if it + 1 < nt:
                        prologue(it + 1)
                    for cch in range(NB if STG >= 2 else 0):
                        csl = slice(cch * 128, (cch + 1) * 128)
                        P.op("pe", [lambda e, k=k, csl=csl, xn=xn: e.matmul(
                            pb[5][:, 256:272], xn[:, k, csl], wz[:, k, 2560:2576], start=(k == 0), stop=(k == 7))
                            for k in range(8)], reads=[B_xn] + B_wz, writes=[Bpb[5]])
                        P.op("dve", lambda e: e.tensor_tensor(dtt[:], pb[5][:, 256:272], h16[:, 0, :], ALU.add),
                             reads=[Bpb[5], B_sm], writes=[B_dt])
                        P.op("act", lambda e: e.activation(dtt[:], dtt[:], AF.Exp), reads=[B_dt], writes=[B_dt])
                        P.op("act", lambda e: e.activation(dtt[:], dtt[:], AF.Ln, bias=onec, scale=1.0),
                             reads=[B_dt, Bc], writes=[B_dt])
                        P.op("dve", lambda e: e.tensor_tensor(adt[:], dtt[:], abc[:], ALU.mult),
                             reads=[B_dt, B_sm], writes=[B_dt])
                        P.op("pe", [lambda e: e.matmul(pb[5][:, 272:288], triu[:], adt[:], start=True, stop=True),
                                    lambda e: e.matmul(pb[5][:, 288:304], ones_f[:], adt[:], start=True, stop=True)],
                             reads=[B_dt, Bc], writes=[Bpb[5]])
                        P.op("dve", lambda e: e.tensor_copy(acs[:], pb[5][:, 272:288]), reads=[Bpb[5]], writes=[B_co])
                        P.op("dve", lambda e: e.tensor_copy(ala[:], pb[5][:, 288:304]), reads=[Bpb[5]], writes=[B_co])
                        P.op("act", lambda e: e.activation(cdec[:], ala[:], AF.Exp), reads=[B_co], writes=[B_co])
                        P.op("dve", lambda e: e.tensor_tensor(coef[:], ala[:], acs[:], ALU.subtract),
                             reads=[B_co], writes=[B_co])
                        P.op("act", lambda e: e.activation(coef[:], coef[:], AF.Exp), reads=[B_co], writes=[B_co])
                        P.op("dve", lambda e: e.tensor_tensor(coef[:], coef[:], dtt[:], ALU.mult),
                             reads=[B_co, B_dt], writes=[B_co])
                        if STG < 3:
                            continue
                        pbt = pb[4][:].bitcast(BF16)
                        P.op("pe", [lambda e, c=c, csl=csl: e.transpose(pbt[:, sl(c)], xsb[:, c, csl], identb[:])
                                    for c in range(8)], reads=[B_xs, Bc], writes=[Bpb[4]])
                        pbt3 = pbt.rearrange("p (h d) -> p h d", h=16)
                        P.op("dve", lambda e: e.tensor_tensor(
                            xdt[:], pbt3, dtt[:].unsqueeze(2).broadcast_to([128, 16, 64]), ALU.mult),
                            reads=[Bpb[4], B_dt], writes=[B_xdt])
                        P.op("dve", lambda e: e.tensor_tensor(
                            xdd[:], pbt3, coef[:].unsqueeze(2).broadcast_to([128, 16, 64]), ALU.mult),
                            reads=[Bpb[4], B_co], writes=[B_xdt])
                        P.op("pe", [lambda e, g=g, csl=csl: e.transpose(pbt[:, sl(g)], bcb[:, g, csl], identb[:])
                                    for g in range(2)], reads=[B_bc, Bc], writes=[Bpb[4]])
                        P.op("act", lambda e: e.copy(btm[:], pbt[:, 0:256].rearrange("p (g n) -> p g n", g=2)),
                             reads=[Bpb[4]], writes=[B_btm])
                        P.op("pe", [lambda e, g=g, csl=csl: e.matmul(
                            pb[5][:, sl(g)], bcb[:, g, csl], bcb[:, 2 + g, csl], start=True, stop=True)
                            for g in range(2)], reads=[B_bc], writes=[Bpb[5]])
                        P.op("dve", lambda e: e.tensor_tensor(
                            cbm[:], pb[5][:, 0:256].rearrange("p (g n) -> p g n", g=2),
                            triu[:].unsqueeze(1).broadcast_to([128, 2, 128]), ALU.mult),
                            reads=[Bpb[5], Bc], writes=[B_cbm])
                        if STG < 4:
                            continue
                        for g in range(2):
                            hsl = slice(g * 8, (g + 1) * 8)
                            P.op("dve", lambda e, hsl=hsl: e.tensor_tensor(
                                R[:], triu[:].unsqueeze(1).broadcast_to([128, 8, 128]),
                                adt[:, hsl].unsqueeze(2).broadcast_to([128, 8, 128]), ALU.mult),
                                reads=[B_dt, Bc], writes=[B_R])
                            P.op("pe", [lambda e, h2=h2: e.matmul(
                                pb[h2 // 2][:, (h2 % 2) * 256:(h2 % 2) * 256 + 256], ones_f[:],
                                R[:, h2 * 2:(h2 + 1) * 2, :], start=True, stop=True)
                                for h2 in range(4)], reads=[B_R, Bc], writes=[Bpb[0], Bpb[1]])
                            for hh in range(2):
                                h4 = slice(hh * 4, (hh + 1) * 4)
                                a4 = slice(g * 8 + hh * 4, g * 8 + hh * 4 + 4)
                                rb = pb[hh][:].rearrange("p (h l) -> p h l", h=4)
                                P.op("dve", lambda e, h4=h4, a4=a4, rb=rb: e.tensor_tensor(
                                    dif[:, h4, :], rb, acs[:, a4].unsqueeze(2).broadcast_to([128, 4, 128]),
                                    ALU.subtract), reads=[Bpb[hh], B_co], writes=[B_dif])
                                P.op("act", lambda e, h4=h4, rb=rb: e.activation(erow[:, h4, :], rb, AF.Exp),
                                     reads=[Bpb[hh]], writes=[B_er])
                            if SUB != 1:
                                P.op("act", lambda e: e.activation(dif[:], dif[:], AF.Exp), reads=[B_dif], writes=[B_dif])
                            P.op("dve", lambda e, g=g: e.scalar_tensor_tensor(
                                MT[:], dif[:], 1.0, cbm[:, g, :].unsqueeze(1).broadcast_to([128, 8, 128]),
                                ALU.min, ALU.mult), reads=[B_dif, B_cbm], writes=[B_MT])
                            P.op("dve", lambda e, g=g, csl=csl: e.tensor_tensor(
                                Cs[:], erow[:], bcb[:, 2 + g, csl].unsqueeze(1).broadcast_to([128, 8, 128]),
                                ALU.mult), reads=[B_er, B_bc], writes=[B_Cs])
                            for hh in range(8 if STG >= 5 else 0):
                                h = g * 8 + hh
                                cch_out = h // 2
                                half = h % 2
                                bank = pb[2 + cch_out // 4]
                                col = (cch_out % 4) * 128
                                P.op("pe", [
                                    lambda e, h=h, hh=hh, half=half, bank=bank, col=col: e.matmul(
                                        bank[half * 64:(half + 1) * 64, col:col + 128], xdt[:, h, :], MT[:, hh, :],
                                        start=True, stop=False),
                                    lambda e, h=h, hh=hh, half=half, bank=bank, col=col: e.matmul(
                                        bank[half * 64:(half + 1) * 64, col:col + 128], prevb[:, h, :], Cs[:, hh, :],
                                        start=False, stop=True)],
                                    reads=[B_xdt, B_MT, B_prevb, B_Cs], writes=[Bpb[2 + cch_out // 4]])
                            if SUB != 2:
                              P.op("pe", lambda e, g=g: e.matmul(
                                pb[6 + g][:], btm[:, g, :], xdd[:, g * 8:(g + 1) * 8, :], start=True, stop=True),
                                reads=[B_btm, B_xdt], writes=[Bpb[6 + g]])
                        for g in range(2 if SUB != 2 else 0):
                            hsl = slice(g * 8, (g + 1) * 8)
                            P.op("dve", lambda e, hsl=hsl: e.tensor_tensor(
                                prev[:, hsl, :], prev[:, hsl, :],
                                cdec[:, hsl].unsqueeze(2).broadcast_to([128, 8, 64]), ALU.mult),
                                reads=[B_co, B_prev], writes=[B_prev])
                            P.op("dve", lambda e, hsl=hsl, g=g: e.tensor_tensor(
                                prev[:, hsl, :], prev[:, hsl, :],
                                pb[6 + g][:].rearrange("p (h d) -> p h d", h=8), ALU.add),
                                reads=[Bpb[6 + g], B_prev], writes=[B_prev])
                        P.op("act", lambda e: e.copy(prevb[:], prev[:]), reads=[B_prev], writes=[B_prevb])
                        for c in range(8):
                            bank = pb[2 + c // 4]
                            col = (c % 4) * 128
                            P.op("dve", lambda e, c=c, bank=bank, col=col, csl=csl: e.scalar_tensor_tensor(
                                yg[:, c, csl], xsf[:, c, csl], vec[:, V_SSD, c:c + 1], bank[:, col:col + 128],
                                ALU.mult, ALU.add), reads=[Bpb[2 + c // 4], B_xs, Bc], writes=[B_yg])
                    P.op("pool", lambda e: e.tensor_tensor(yg[:], yg[:], sz[:], ALU.mult),
                         reads=[B_sz, B_yg], writes=[B_yg])
                    for g in range(2):
                        rms_rstd(yg, B_yg, 4, ones512, sqt, B_sq, pb[6], Bpb[6], grs[g][:], B_grs[g], k0=g * 4)
                    for c in range(8):
                        P.op("dve", lambda e, c=c: e.scalar_tensor_tensor(
                            yn[:, c, :], yg[:, c, :], vec[:, V_SSN, c:c + 1], grs[c // 4][:], ALU.mult, ALU.mult),
                            reads=[B_yg, Bc, B_grs[c // 4]], writes=[B_yn])
                    for m in range(8):
                        mb = m % 2
                        P.op("pe", [lambda e, j=j, m=m, mb=mb: e.matmul(
                            pb[mb][:, 0:T], wo[:, j, sl(m)], (ut[:, j, :] if j < 8 else yn[:, j - 8, :]),
                            start=(j == 0), stop=(j == 15))
                            for j in range(16)], reads=[B_ut, B_yn] + B_wo, writes=[Bpb[mb]])
                        P.op("dve", lambda e, m=m, mb=mb, hs=hs: e.tensor_tensor(
                            hs[:, m, :], hs[:, m, :], pb[mb][:, 0:T], ALU.add),
                            reads=[Bpb[mb], B_ht[s]], writes=[B_ht[s]])
                    P.dma("sp", Hv[:, :, t0:t0 + T], hs[:], reads=[B_ht[s]], writes=[B_H[it]], slot=B_ht[s])
                P.end_phase()

        def att_phase():
            with ExitStack() as pes:
                sb = lambda name, shape, dt: pes.enter_context(nc.sbuf_tensor(uniq(name), shape, dt))
                wq = sb("wq", [128, 8, 1536], BF16)
                wo = sb("wo", [128, 8, D], BF16)
                bqk = sb("bqk", [128, 10], F32)
                bvb = sb("bvb", [128, 256], F32)
                snk = sb("snk", [128, 16], F32)
                mskb = sb("mskb", [128, 2, 256], BF16)
                ht = [sb(f"ht{i}", [128, 8, T], F32) for i in range(2)]
                xn2 = [sb(f"xn{i}", [128, 8, T], BF16) for i in range(2)]
                sqt = sb("sqt", [128, 2, T], BF16)
                rstd2 = [sb(f"rstd{i}", [128, T], F32) for i in range(2)]
                rope = sb("rope", [128, 2, T], F32)
                qf = [sb(f"qf{i}", [128, T], F32) for i in range(2)]
                qb = [sb(f"qb{i}", [128, T], BF16) for i in range(2)]
                t1 = [sb(f"t1{i}", [128, T], F32) for i in range(2)]
                qr = sb("qr", [128, 8, T], BF16)
                kr = sb("kr", [128, 2, 128 + T], BF16)
                vt = sb("vt", [128, 1 + NB, 256], BF16)
                ee = [sb(f"ee{i}", [128, 4, 256], F32) for i in range(2)]
                pp = [sb(f"pp{i}", [128, 4, 256], BF16) for i in range(2)]
                pT = [sb(f"pT{i}", [128, 8, 128], BF16) for i in range(2)]
                mx = [sb(f"mx{i}", [128, 4], F32) for i in range(2)]
                nmx = [sb(f"nmx{i}", [128, 4], F32) for i in range(2)]
                rs = [sb(f"rs{i}", [128, 4], F32) for i in range(2)]
                es_ = [sb(f"es_{i}", [128, 4], F32) for i in range(2)]
                oT = sb("oT", [128, 8, T], BF16)
                B_wq = P.bufs(8, "wq")
                B_wo = P.buf("wo")
                B_sm_ = P.buf("small")
                B_ht = P.bufs(2, "ht")
                B_xn2 = P.bufs(2, "xn")
                B_sq = P.bufs(2, "sq")
                B_rstd2 = P.bufs(2, "rstd")
                B_rope = P.buf("rope")
                B_qf = P.bufs(2, "qf")
                B_qb = P.bufs(2, "qb")
                B_t1 = P.bufs(2, "t1")
                B_qr = P.buf("qr")
                B_kr = P.buf("kr")
                B_vt = P.buf("vt")
                B_s = P.bufs(2, "sm")
                B_e = P.bufs(2, "ee")
                B_p = P.bufs(2, "pp")
                B_pT = P.bufs(2, "pT")
                B_st = P.bufs(2, "stats")
                B_oT = P.buf("oT")
                for k in range(8):
                    load_w(wq[:, k, :], wqkv_d[sl(k), :], B_wq[k])
                load_w(wo[:], wo_d.rearrange("(k p) m -> p k m", p=128), B_wo)
                P.dma("sp", bqk[:], bqk_d, writes=[B_sm_], slot=B_sm_)
                P.dma("sp", bvb[:], bv_d.partition_broadcast(128), writes=[B_sm_], slot=B_sm_)
                P.dma("sp", snk[:], h16_d[2].partition_broadcast(128), writes=[B_sm_], slot=B_sm_)
                P.dma("pool", mskb[:], msk_d.rearrange("a p s -> p a s"), writes=[B_sm_], slot=B_sm_)
                P.op("dve", lambda e: e.memset(kr[:, :, 0:128], 0.0), writes=[B_kr])
                P.op("dve", lambda e: e.memset(vt[:, 0, :], 0.0), writes=[B_vt])
                def prologue(it_):
                    s_ = it_ % 2
                    P.dma("sp", ht[s_][:], Hv[:, :, it_ * T:(it_ + 1) * T], reads=[B_H[it_]], writes=[B_ht[s_]], slot=B_ht[s_])
                    make_xn(ht[s_], B_ht[s_], V_NMIX + 1, xn2[s_], B_xn2[s_], sqt, B_sq, rstd2[s_], B_rstd2[s_], pb[6], Bpb[6])

                prologue(0)
                for it in range(nt):
                    s = it % 2
                    t0 = it * T
                    hs = ht[s]
                    xn = xn2[s]
                    B_xn = B_xn2[s]
                    P.dma("sp", rope[:], rope_d[:, :, t0:t0 + T].rearrange("a p s -> p a s"),
                          writes=[B_rope], slot=B_rope)
                    if it > 0:
                        P.op("dve", lambda e: e.tensor_copy(kr[:, :, 0:128], kr[:, :, T:T + 128]),
                             reads=[B_kr], writes=[B_kr])
                        P.op("dve", lambda e: e.tensor_copy(vt[:, 0, :], vt[:, NB, :]), reads=[B_vt], writes=[B_vt])
                    for c in range(10):
                        jb = c % 2
                        P.op("pe", [lambda e, k=k, c=c, jb=jb, xn=xn: e.matmul(
                            pb[jb][:, 0:T], wq[:, k, sl(c)], xn[:, k, :], start=(k == 0), stop=(k == 7))
                            for k in range(8)], reads=[B_xn] + B_wq, writes=[Bpb[jb]])
                        P.op("act", lambda e, c=c, jb=jb: e.activation(
                            qf[jb][:], pb[jb][:, 0:T], AF.Identity, bias=bqk[:, c:c + 1], scale=1.0),
                            reads=[Bpb[jb], B_sm_], writes=[B_qf[jb]])
                        P.op("pe", lambda e, jb=jb: e.matmul(pb[2 + jb][:, 0:T], pswapf[:], qf[jb][:], start=True, stop=True),
                             reads=[B_qf[jb], Bc], writes=[Bpb[2 + jb]])
                        P.op("dve", lambda e, jb=jb: e.tensor_tensor(t1[jb][:], qf[jb][:], rope[:, 0, :], ALU.mult),
                             reads=[B_qf[jb], B_rope], writes=[B_t1[jb]])
                        P.op("dve", lambda e, jb=jb: e.tensor_tensor(qf[jb][:], pb[2 + jb][:, 0:T], rope[:, 1, :], ALU.mult),
                             reads=[Bpb[2 + jb], B_rope, B_qf[jb]], writes=[B_qf[jb]])
                        dst = qr[:, c, :] if c < 8 else kr[:, c - 8, 128:128 + T]
                        P.op("dve", lambda e, jb=jb, dst=dst: e.tensor_tensor(dst, t1[jb][:], qf[jb][:], ALU.add),
                             reads=[B_t1[jb], B_qf[jb]], writes=[B_qr if c < 8 else B_kr])
                    for b in range(NB):
                        jb = b % 2
                        P.op("pe", [lambda e, k=k, b=b, jb=jb, xn=xn: e.matmul(
                            pb[jb][:, 0:256], xn[:, k, sl(b)], wq[:, k, 1280:1536], start=(k == 0), stop=(k == 7))
                            for k in range(8)], reads=[B_xn] + B_wq, writes=[Bpb[jb]])
                        P.op("dve", lambda e, b=b, jb=jb: e.tensor_tensor(vt[:, 1 + b, :], pb[jb][:, 0:256], bvb[:], ALU.add),
                             reads=[Bpb[jb], B_sm_], writes=[B_vt])
                    if it + 1 < nt:
                        prologue(it + 1)
                    def mk_group(b, kv, a):
                        gblk = it * NB + b
                        mi = 0 if gblk == 0 else 1
                        half = kv % 2
                        hp = slice(half * 64, (half + 1) * 64)
                        kc = kv // 2
                        qc0 = (kv // 2) * 4
                        S0, S1, PTb, Ob = 4 * a, 4 * a + 1, 4 * a + 2, 4 * a + 3
                        ee_, pp_, pT_ = ee[a], pp[a], pT[a]
                        mx_, nmx_, rs_, es2 = mx[a], nmx[a], rs[a], es_[a]
                        Be, Bp, BpT, Bst = B_e[a], B_p[a], B_pT[a], B_st[a]
                        sg4 = snk[:, kv * 4:(kv + 1) * 4]
                        pbt = pb[PTb][:].bitcast(BF16)
                        G = {}

                        def pe_s():
                            fns = []
                            for i in range(4):
                                dst = pb[S0 + i // 2][:, (i % 2) * 256:(i % 2) * 256 + 256]
                                fns.append(lambda e, i=i, dst=dst: e.matmul(
                                    dst, qr[hp, qc0 + i, sl(b)], kr[hp, kc, b * 128:b * 128 + 256],
                                    start=True, stop=False))
                                fns.append(lambda e, dst=dst: e.matmul(
                                    dst, identb[:], mskb[:, mi, :], start=False, stop=True))
                            P.op("pe", fns, reads=[B_qr, B_kr, B_sm_, Bc], writes=[Bpb[S0], Bpb[S1]])

                        def d1():
                            for hh in range(2):
                                P.op("dve", lambda e, hh=hh: e.tensor_reduce(
                                    mx_[:, hh * 2:hh * 2 + 2], pb[S0 + hh][:].rearrange("p (h s) -> p h s", h=2),
                                    AX.X, ALU.max), reads=[Bpb[S0 + hh]], writes=[Bst])
                            P.op("dve", lambda e: e.scalar_tensor_tensor(mx_[:], mx_[:], 0.125, sg4, ALU.mult, ALU.max),
                                 reads=[Bst, B_sm_], writes=[Bst])
                            P.op("dve", lambda e: e.tensor_scalar_mul(nmx_[:], mx_[:], -1.0), reads=[Bst], writes=[Bst])
                            P.op("dve", lambda e: e.tensor_tensor(es2[:], sg4, mx_[:], ALU.subtract),
                                 reads=[Bst, B_sm_], writes=[Bst])
                            P.op("dve", lambda e: e.memset(rs_[:], 0.0), writes=[Bst])

                        def a1():
                            for i in range(4):
                                src = pb[S0 + i // 2][:, (i % 2) * 256:(i % 2) * 256 + 256]
                                P.op("act", lambda e, i=i, src=src: e.activation(
                                    ee_[:, i, :], src, AF.Exp, bias=nmx_[:, i:i + 1], scale=0.125,
                                    accum_out=rs_[:, i:i + 1]), reads=[Bpb[S0 + i // 2], Bst], writes=[Be, Bst])
                            P.op("act", lambda e: e.activation(es2[:], es2[:], AF.Exp), reads=[Bst], writes=[Bst])

                        def d2():
                            P.op("dve", lambda e: e.tensor_tensor(rs_[:], rs_[:], es2[:], ALU.add), reads=[Bst], writes=[Bst])
                            P.op("dve", lambda e: e.reciprocal(rs_[:], rs_[:]), reads=[Bst], writes=[Bst])
                            P.op("dve", lambda e: e.tensor_tensor(
                                pp_[:], ee_[:], rs_[:].unsqueeze(2).broadcast_to([128, 4, 256]), ALU.mult),
                                reads=[Be, Bst], writes=[Bp])

                        def pe_t():
                            P.op("pe", [lambda e, i=i, kb=kb: e.transpose(
                                pbt[:, sl(i * 2 + kb)], pp_[:, i, sl(kb)], identb[:])
                                for i in range(4) for kb in range(2)], reads=[Bp, Bc], writes=[Bpb[PTb]])

                        def a2():
                            P.op("act", lambda e: e.copy(pT_[:], pbt.rearrange("p (a q) -> p a q", a=8)),
                                 reads=[Bpb[PTb]], writes=[BpT])

                        def pe_pv():
                            P.op("pe", [lambda e, i=i, kb=kb: e.matmul(
                                pb[Ob][hp, i * 128:(i + 1) * 128], vt[:, b + kb, kv * 64:(kv + 1) * 64],
                                pT_[:, i * 2 + kb, :], start=(kb == 0), stop=(kb == 1))
                                for i in range(4) for kb in range(2)], reads=[B_vt, BpT], writes=[Bpb[Ob]])

                        def a3():
                            P.op("act", lambda e: e.copy(
                                oT[hp, qc0:qc0 + 4, sl(b)], pb[Ob][hp, :].rearrange("p (i q) -> p i q", i=4)),
                                reads=[Bpb[Ob]], writes=[B_oT])
                        G.update(pe_s=pe_s, d1=d1, a1=a1, d2=d2, pe_t=pe_t, a2=a2, pe_pv=pe_pv, a3=a3)
                        return G

                    groups = [mk_group(b_, kv_, gi % 2) for gi, (b_, kv_) in
                              enumerate([(b_, kv_) for b_ in range(NB) for kv_ in range(4)])]
                    ng_ = len(groups)
                    groups[0]["pe_s"]()
                    for c_ in range(ng_ + 1):
                        cur = groups[c_] if c_ < ng_ else None
                        prv = groups[c_ - 1] if c_ > 0 else None
                        if cur:
                            cur["d1"]()
                        if c_ + 1 < ng_:
                            groups[c_ + 1]["pe_s"]()
                        if cur:
                            cur["a1"]()
                        if prv:
                            prv["d2"]()
                            prv["pe_t"]()
                            prv["a2"]()
                            prv["pe_pv"]()
                            prv["a3"]()
                    for m in range(8):
                        mb = m % 2
                        P.op("pe", [lambda e, j=j, m=m, mb=mb: e.matmul(
                            pb[mb][:, 0:T], wo[:, j, sl(m)], oT[:, j, :], start=(j == 0), stop=(j == 7))
                            for j in range(8)], reads=[B_oT, B_wo], writes=[Bpb[mb]])
                        P.op("dve", lambda e, m=m, mb=mb, hs=hs: e.scalar_tensor_tensor(
                            hs[:, m, :], pb[mb][:, 0:T], vec[:, V_BO, m:m + 1], hs[:, m, :], ALU.add, ALU.add),
                            reads=[Bpb[mb], B_ht[s], Bc], writes=[B_ht[s]])
                    P.dma("sp", Hv[:, :, t0:t0 + T], hs[:], reads=[B_ht[s]], writes=[B_H[it]], slot=B_ht[s])
                P.end_phase()

        last = phases[-1]
        for ph in phases:
            if ph == "ffn1_0":
                ffn_phase(0, 0, True, False, False)
            elif ph == "conv":
                conv_phase()
            elif ph == "ssd":
                ssd_phase()
            elif ph == "ffn2_0":
                ffn_phase(1, 0, False, True, False)
            elif ph == "ffn1_1":
                ffn_phase(2, 1, False, False, False)
            elif ph == "att":
                att_phase()
            elif ph == "ffn2_1":
                ffn_phase(3, 1, False, True, True)
        P.barrier()
        P.emit()
    return nc


Q_LOWER = [0, 1, 2, 3, 8, 9, 10, 11]
Q_UPPER = [4, 5, 6, 7, 12, 13, 14, 15]
HEAD_ORDER = [h for c in range(8) for h in (Q_LOWER[c], Q_UPPER[c])]


def _pk(v):
    v = np.asarray(v, np.float32)
    return np.array(v.reshape(-1, 128).T, dtype=np.float32, order='C', copy=True)


def prep_shared(inp, S=SEQ):
    f = lambda a: np.array(a, dtype=np.float32, order='C', copy=True)
    sh = {}
    sh["ffn_win"] = f(np.stack([inp["ffn1_w_in"][0], inp["ffn2_w_in"][0], inp["ffn1_w_in"][1], inp["ffn2_w_in"][1]]))
    sh["ffn_wout"] = f(np.stack([inp["ffn1_w_out"][0], inp["ffn2_w_out"][0], inp["ffn1_w_out"][1], inp["ffn2_w_out"][1]]))
    vec = np.zeros((128, 20, 8), np.float32)
    for li in range(2):
        vec[:, 0 + li] = _pk(inp["norm_ffn1"][li])
        vec[:, 2 + li] = _pk(inp["norm_mix"][li])
        vec[:, 4 + li] = _pk(inp["norm_ffn2"][li])
        vec[:, 6 + li] = _pk(inp["ple_norm"][li])
    vec[:, 8] = _pk(inp["final_norm"])
    vec[:, 9] = _pk(inp["conv_dw_b"][0])
    vec[:, 10] = _pk(inp["conv_ln_g"][0])
    vec[:, 11] = _pk(inp["conv_ln_b"][0])
    vec[:, 12] = _pk(np.repeat(np.asarray(inp["ssm_d"][0], np.float32), 64))
    vec[:, 13] = _pk(inp["ssm_norm"][0])
    vec[:, 14] = _pk(inp["att_b_o"][0])
    sh["vecs"] = vec
    sh["ple_wg"] = f(inp["ple_gate_w"])
    sh["ple_wp"] = f(inp["ple_proj_w"])
    sh["hyb_win"] = f(inp["hyb_w_in"][0])
    sh["hyb_wout"] = f(inp["hyb_w_out"][0])
    cw = np.asarray(inp["conv_dw_w"][0], np.float32)
    sh["cw"] = f(cw.T.reshape(8, 128, CW).transpose(1, 0, 2))
    scw = np.asarray(inp["ssm_conv_w"][0], np.float32)
    scb = np.asarray(inp["ssm_conv_b"][0], np.float32)
    sc = np.concatenate([scw, scb[None]], 0)
    sh["scw"] = f(sc.T.reshape(12, 128, 5).transpose(1, 0, 2))
    sinks = np.asarray(inp["att_sinks"][0], np.float32)
    sg_order = [HEAD_ORDER[((kv // 2) * 4 + i) * 2 + kv % 2] for kv in range(4) for i in range(4)]
    sh["h16"] = f(np.stack([inp["ssm_dt_bias"][0], inp["ssm_a_log"][0], sinks[sg_order]]))
    wqkv = np.asarray(inp["att_w_qkv"][0], np.float32)
    bqkv = np.asarray(inp["att_b_qkv"][0], np.float32)
    qcols = np.concatenate([np.arange(h * 64, (h + 1) * 64) for h in HEAD_ORDER])
    cols = np.concatenate([qcols, np.arange(1024, 1536)])
    sh["wqkv"] = f(wqkv[:, cols])
    bp = bqkv[cols]
    sh["bqk"] = _pk(bp[:1280])
    sh["bv"] = f(bp[1280:])
    sh["wo"] = f(np.asarray(inp["att_w_o"][0], np.float32)[qcols, :])
    inv = (np.float32(10000.0) ** (-np.arange(0, 64, 2, dtype=np.float32) / np.float32(64))).astype(np.float32)
    ang = (np.arange(S, dtype=np.float32)[:, None] * inv[None, :]).astype(np.float32)
    cos, sin = np.cos(ang).astype(np.float32), np.sin(ang).astype(np.float32)
    prt = np.arange(128)
    CC = cos.T[prt % 32]
    sgn = np.where((prt % 64) < 32, -1.0, 1.0).astype(np.float32)
    SSn = sin.T[prt % 32] * sgn[:, None]
    sh["rope"] = f(np.stack([CC, SSn]))
    cst = np.zeros((5, 128, 128), np.float32)
    cst[0] = np.eye(128)
    cst[1] = np.triu(np.ones((128, 128)))
    sw = np.zeros((128, 128), np.float32)
    for m in range(128):
        sw[(m + 32) % 64 + (m // 64) * 64, m] = 1.0
    cst[2] = sw
    sh["cst"] = cst
    q = np.arange(128)[:, None]
    sp = np.arange(256)[None, :]
    valid = np.where(sp < 128, sp > q, (sp - 128) <= q)
    m1 = np.where(valid, 0.0, NEG).astype(np.float32)
    m0 = np.where(valid & (sp >= 128), 0.0, NEG).astype(np.float32)
    sh["msk"] = f(np.stack([m0, m1]))
    return sh


_CACHE = {}


def kernel(**inputs):
    x = np.asarray(inputs["x"], np.float32)
    p = np.asarray(inputs["p"], np.float32)
    B, S, _ = x.shape
    sh = prep_shared(inputs, S)
    key = ("full", S)
    if key not in _CACHE:
        _CACHE[key] = build_program(S)
    nc = _CACHE[key]
    in_maps = []
    for b in range(B):
        m = dict(sh)
        m["x"] = np.array(x[b], dtype=np.float32, order='C', copy=True)
        m["p"] = np.array(p[:, b], dtype=np.float32, order='C', copy=True)
        in_maps.append(m)
    res = run_bass_kernel_spmd(nc, in_maps, core_ids=list(range(B)))
    return np.stack([np.asarray(r["y"], np.float32) for r in res.results], 0)
```

```python
from contextlib import ExitStack
import os
import numpy as np
import concourse.bass as bass
import concourse.mybir as mybir
from concourse.bass_utils import run_bass_kernel_spmd

F32 = mybir.dt.float32
BF16 = mybir.dt.bfloat16
AF = mybir.ActivationFunctionType
ALU = mybir.AluOpType
AX = mybir.AxisListType

D = 1024
DFF = 2816
NJ = DFF // 128
PLE = 256
SEQ = 4096
NCORES = 8
T = 256
NB = T // 128
EPS = 1e-6
CW = 31
HYB_IN = 4624
NEG = -240000.0
STG = int(os.environ.get('SSD_STAGE', '99'))
SUB = int(os.environ.get('SUB', '0'))


class Buf:
    __slots__ = ("name", "w", "r", "dsem", "excl")

    def __init__(self, name):
        self.name = name
        self.w = {}
        self.r = {}
        self.dsem = None
        self.excl = False


class Prog:
    ENGS = ("pe", "act", "dve", "pool", "sp")

    def __init__(self, nc, es):
        self.nc = nc
        self.es = es
        self.streams = {e: [] for e in self.ENGS}
        self.sems = {}
        self.cnt = {}
        self.waited = {e: {} for e in self.ENGS}
        self.nbuf = 0
        self.free_dsems = []
        self.phase_dsems = []
        self.ndsem = 0
        for e in ("pe", "act", "dve", "pool"):
            self._mksem("c_" + e)

    def _mksem(self, name):
        self.sems[name] = self.es.enter_context(self.nc.semaphore(name))
        self.cnt[name] = 0
        return name

    def _dsem(self):
        if self.free_dsems:
            s = self.free_dsems.pop()
        else:
            self.ndsem += 1
            s = self._mksem(f"d{self.ndsem}")
        self.phase_dsems.append(s)
        return s

    def buf(self, name=None):
        self.nbuf += 1
        return Buf(name or f"b{self.nbuf}")

    def bufs(self, n, name="b"):
        return [self.buf(f"{name}{i}") for i in range(n)]

    def _need(self, reads, writes):
        need = {}
        for b in reads:
            for s, v in b.w.items():
                if need.get(s, 0) < v:
                    need[s] = v
        for b in writes:
            for d in (b.w, b.r):
                for s, v in d.items():
                    if need.get(s, 0) < v:
                        need[s] = v
        return need

    def _emit_waits(self, eng, need, skip_own=False):
        wd = self.waited[eng]
        own = "c_" + eng
        for s, v in need.items():
            if skip_own and s == own:
                continue
            if wd.get(s, 0) < v:
                wd[s] = v
                h = self.sems[s]
                self.streams[eng].append(lambda e, h=h, v=v: e.wait_ge(h, v))

    def _mark(self, reads, writes, s, v):
        for b in reads:
            if b.r.get(s, 0) < v:
                b.r[s] = v
        for b in writes:
            b.w = {s: v}
            b.r = {}

    def op(self, eng, fns, reads=(), writes=(), skip_own=None):
        if callable(fns):
            fns = [fns]
        if skip_own is None:
            skip_own = (eng == "pe")
        ex = [b for b in reads if b.excl]
        if ex:
            writes = list(writes) + ex
            reads = [b for b in reads if not b.excl]
        self._emit_waits(eng, self._need(reads, writes), skip_own)
        s = "c_" + eng
        self.cnt[s] += 1
        v = self.cnt[s]
        h = self.sems[s]
        st = self.streams[eng]
        for f in fns[:-1]:
            st.append(f)
        last = fns[-1]
        st.append(lambda e, last=last, h=h: last(e).then_inc(h, 1))
        self._mark(reads, writes, s, v)

    def dma(self, q, out, in_, reads=(), writes=(), slot=None):
        self._emit_waits(q, self._need(reads, writes), False)
        if slot.dsem is None:
            slot.dsem = self._dsem()
        s = slot.dsem
        self.cnt[s] += 16
        v = self.cnt[s]
        h = self.sems[s]
        self.streams[q].append(lambda e, out=out, in_=in_, h=h: e.dma_start(out=out, in_=in_).then_inc(h, 16))
        self._mark(reads, writes, s, v)

    def barrier(self):
        need = {s: v for s, v in self.cnt.items() if v > 0}
        for e in self.ENGS:
            self._emit_waits(e, need, False)

    def end_phase(self):
        self.barrier()
        self.free_dsems.extend(self.phase_dsems)
        self.phase_dsems = []

    def emit(self):
        nc = self.nc
        st = self.streams
        with nc.Block() as block:
            @block.tensor
            def _(e):
                for f in st["pe"]:
                    f(e)

            @block.scalar
            def _(e):
                for f in st["act"]:
                    f(e)

            @block.vector
            def _(e):
                for f in st["dve"]:
                    f(e)

            @block.gpsimd
            def _(e):
                for f in st["pool"]:
                    f(e)

            @block.sync
            def _(e):
                for f in st["sp"]:
                    f(e)


class Ctx:
    pass


def sl(i, n=128):
    return slice(i * n, (i + 1) * n)


def build_program(S=SEQ, phases=("ffn1_0", "conv", "ssd", "ffn2_0", "ffn1_1", "att", "ffn2_1"), debug=False):
    nc = bass.Bass("TRN2", target_bir_lowering=False)
    nt = S // T
    dt_in = {}

    def din(name, shape):
        dt_in[name] = nc.dram_tensor(name, list(shape), F32, kind="ExternalInput").ap()
        return dt_in[name]

    x_d = din("x", [S, D])
    p_d = din("p", [2, S, PLE])
    ffn_win = din("ffn_win", [4, D, 2 * DFF])
    ffn_wout = din("ffn_wout", [4, DFF, D])
    vecs = din("vecs", [128, 20, 8])
    ple_wg = din("ple_wg", [2, D, D])
    ple_wp = din("ple_wp", [2, PLE, D])
    hyb_win = din("hyb_win", [D, HYB_IN])
    hyb_wout = din("hyb_wout", [2 * D, D])
    cw_d = din("cw", [128, 8, CW])
    scw_d = din("scw", [128, 12, 5])
    h16_d = din("h16", [3, 16])
    wqkv_d = din("wqkv", [D, 1536])
    bqk_d = din("bqk", [128, 10])
    bv_d = din("bv", [256])
    wo_d = din("wo", [D, D])
    rope_d = din("rope", [2, 128, S])
    cst_d = din("cst", [5, 128, 128])
    msk_d = din("msk", [2, 128, 256])
    if debug:
        out_d = nc.dram_tensor("H", [8, 128, S], F32, kind="ExternalOutput").ap()
        Hd = out_d
        yout_d = nc.dram_tensor("y", [S, D], F32, kind="ExternalOutput").ap()
    else:
        yout_d = nc.dram_tensor("y", [S, D], F32, kind="ExternalOutput").ap()
        Hd = nc.dram_tensor("Hs", [8, 128, S], F32).ap()
    Ud = nc.dram_tensor("Us", [8, 128, S], BF16).ap()
    Hv = Hd.rearrange("k p s -> p k s")
    Uv = Ud.rearrange("k p s -> p k s")

    with ExitStack() as es:
        P = Prog(nc, es)
        C = Ctx()
        gsb = lambda name, shape, dt: es.enter_context(nc.sbuf_tensor(name, shape, dt))
        pb = [es.enter_context(nc.psum_tensor(f"pb{i}", [128, 512], F32)) for i in range(8)]
        Bpb = P.bufs(8, "pb")
        for b_ in Bpb:
            b_.excl = True
        identf = gsb("identf", [128, 128], F32)
        identb = gsb("identb", [128, 128], BF16)
        triu = gsb("triu", [128, 128], F32)
        pswap = gsb("pswap", [128, 128], BF16)
        pswapf = gsb("pswapf", [128, 128], F32)
        ones_f = gsb("ones_f", [128, 128], F32)
        ones1k = gsb("ones1k", [128, 128], BF16)
        ones512 = gsb("ones512", [128, 128], BF16)
        cols = gsb("cols", [128, 4], F32)
        vec = gsb("vec", [128, 20, 8], F32)
        Bc = P.buf("consts")
        P.dma("sp", identf[:], cst_d[0], writes=[Bc], slot=Bc)
        P.dma("sp", triu[:], cst_d[1], writes=[Bc], slot=Bc)
        P.dma("sp", pswapf[:], cst_d[2], writes=[Bc], slot=Bc)
        P.dma("sp", vec[:], vecs, writes=[Bc], slot=Bc)
        P.dma("pool", identb[:], cst_d[0], writes=[Bc], slot=Bc)
        P.dma("pool", pswap[:], cst_d[2], writes=[Bc], slot=Bc)
        P.op("dve", lambda e: e.memset(ones_f[:], 1.0), writes=[Bc])
        P.op("dve", lambda e: e.memset(ones1k[:], 1.0 / 1024), writes=[Bc])
        P.op("dve", lambda e: e.memset(ones512[:], 1.0 / 512), writes=[Bc])
        P.op("dve", lambda e: e.memset(cols[:, 0:1], EPS), writes=[Bc])
        P.op("dve", lambda e: e.memset(cols[:, 1:2], 1.0), writes=[Bc])
        P.op("dve", lambda e: e.memset(cols[:, 2:3], 0.0), writes=[Bc])
        epsc = cols[:, 0:1]
        onec = cols[:, 1:2]
        B_H = P.bufs(nt, "H")
        B_U = P.bufs(nt, "U")
        B_Y = P.bufs(nt, "Y")
        P.end_phase()

        V_NF1, V_NMIX, V_NF2, V_PLE = 0, 2, 4, 6
        V_FIN, V_CVB, V_LNG, V_LNB, V_SSD, V_SSN, V_BO = 8, 9, 10, 11, 12, 13, 14

        def rms_rstd(xap, Bx, nk, ones_ap, sqt, Bsq, pbank, Bpbank, rstd, Brstd, k0=0):
            for k in range(nk):
                q = k % 2
                P.op("act", lambda e, k=k, q=q: e.activation(sqt[:, q, :], xap[:, k0 + k, :], AF.Square),
                     reads=[Bx], writes=[Bsq[q]])
                P.op("pe", lambda e, k=k, q=q: e.matmul(pbank[:, 0:T], ones_ap[:], sqt[:, q, :],
                                                       start=(k == 0), stop=(k == nk - 1)),
                     reads=[Bsq[q], Bc], writes=[Bpbank])
            P.op("act", lambda e: e.activation(rstd, pbank[:, 0:T], AF.Sqrt, bias=epsc, scale=1.0),
                 reads=[Bpbank, Bc], writes=[Brstd])
            P.op("dve", lambda e: e.reciprocal(rstd, rstd), reads=[Brstd], writes=[Brstd])

        def make_xn(ht_s, Bht, gidx, xn, Bxn, sqt, Bsq, rstd, Brstd, pbank, Bpbank):
            rms_rstd(ht_s, Bht, 8, ones1k, sqt, Bsq, pbank, Bpbank, rstd[:], Brstd)
            for k in range(8):
                P.op("dve", lambda e, k=k: e.scalar_tensor_tensor(
                    xn[:, k, :], ht_s[:, k, :], vec[:, gidx, k:k + 1], rstd[:], ALU.mult, ALU.mult),
                    reads=[Bht, Bc, Brstd], writes=[Bxn])

        ucnt = [0]

        def uniq(name):
            ucnt[0] += 1
            return f"s{ucnt[0]}_{name}"

        def load_w(dst, src_rows_ap, Bw, q="pool"):
            P.dma(q, dst, src_rows_ap, writes=[Bw], slot=Bw)

        def ffn_phase(fi, li, first, do_ple, final):
            with ExitStack() as pes:
                sb = lambda name, shape, dt: pes.enter_context(nc.sbuf_tensor(uniq(name), shape, dt))
                win = sb("win", [128, 8, 2 * DFF], BF16)
                wout = sb("wout", [128, NJ, D], BF16)
                ht = [sb(f"ht{i}", [128, 8, T], F32) for i in range(2)]
                xn = [sb(f"xn{i}", [128, 8, T], BF16) for i in range(2)]
                sqt = sb("sqt", [128, 2, T], BF16)
                rstd = [sb(f"rstd{i}", [128, T], F32) for i in range(2)]
                sg = [sb(f"sg{i}", [128, T], F32) for i in range(2)]
                hT = sb("hT", [128, NJ, T], BF16)
                B_win = P.bufs(1, "win")
                B_wout = P.bufs(2, "wout")
                B_ht = P.bufs(2, "ht")
                B_xn = P.bufs(2, "xn")
                B_sq = P.bufs(2, "sq")
                B_rstd = P.bufs(2, "rstd")
                B_sg = P.bufs(2, "sg")
                B_hT = P.buf("hT")
                if first:
                    xt = [sb("xt0", [128, D], F32)]
                    B_xt = P.bufs(1, "xt")
                if final:
                    yt = [sb(f"yt{i}", [128, 512], F32) for i in range(2)]
                    B_yt = P.bufs(2, "yt")
                if do_ple:
                    wg = sb("wg", [128, 8, D], BF16)
                    wp = sb("wp", [128, 2, D], BF16)
                    pt = [sb(f"pt{i}", [128, PLE], F32) for i in range(2)]
                    pT = sb("pT", [128, 2, T], BF16)
                    sg2 = [sb(f"sgp{i}", [128, T], F32) for i in range(2)]
                    B_wg = P.buf("wg")
                    B_wp = P.buf("wp")
                    B_pt = P.bufs(2, "pt")
                    B_pT = P.buf("pT")
                    B_sg2 = P.bufs(2, "sgp")
                for k in range(8):
                    load_w(win[:, k, :], ffn_win[fi, sl(k), :], B_win[0])
                wo_v = ffn_wout[fi].rearrange("(j p) m -> p j m", p=128)
                for hh in range(2):
                    load_w(wout[:, hh * 11:(hh + 1) * 11, :], wo_v[:, hh * 11:(hh + 1) * 11, :], B_wout[hh])
                if do_ple:
                    load_w(wg[:], ple_wg[li].rearrange("(k p) m -> p k m", p=128), B_wg)
                    load_w(wp[:], ple_wp[li].rearrange("(k p) m -> p k m", p=128), B_wp)
                gidx = (V_NF1 if not do_ple else V_NF2) + li

                def load_tile(it):
                    s = it % 2
                    t0 = it * T
                    hs = ht[s]
                    if first:
                        for b in range(NB):
                            P.dma("sp", xt[0][:], x_d[t0 + b * 128:t0 + (b + 1) * 128, :],
                                  writes=[B_xt[0]], slot=B_xt[0])
                            for hf in range(2):
                                P.op("pe", [lambda e, kk=kk, hf=hf: e.transpose(
                                    pb[7][:, sl(kk)], xt[0][:, sl(hf * 4 + kk)], identf[:]) for kk in range(4)],
                                    reads=[B_xt[0], Bc], writes=[Bpb[7]])
                                P.op("act", lambda e, hf=hf, b=b, hs=hs: e.copy(
                                    hs[:, hf * 4:(hf + 1) * 4, sl(b)], pb[7][:].rearrange("p (k t) -> p k t", k=4)),
                                    reads=[Bpb[7]], writes=[B_ht[s]])
                    else:
                        P.dma("sp", hs[:], Hv[:, :, t0:t0 + T], reads=[B_H[it]], writes=[B_ht[s]], slot=B_ht[s])

                def prologue(it):
                    s = it % 2
                    make_xn(ht[s], B_ht[s], gidx, xn[s], B_xn[s], sqt, B_sq, rstd[s], B_rstd[s], pb[7], Bpb[7])

                def sq_stat(hs_, s_, m_):
                    q_ = m_ % 2
                    P.op("act", lambda e: e.activation(sqt[:, q_, :], hs_[:, m_, :], AF.Square),
                         reads=[B_ht[s_]], writes=[B_sq[q_]])

                def pe_stat(m_):
                    q_ = m_ % 2
                    P.op("pe", lambda e: e.matmul(pb[6][:, 0:T], ones1k[:], sqt[:, q_, :],
                                                  start=(m_ == 0), stop=(m_ == 7)),
                         reads=[B_sq[q_], Bc], writes=[Bpb[6]])

                def finish_rstd(s_):
                    rs_ = rstd[s_]
                    P.op("act", lambda e: e.activation(rs_[:], pb[6][:, 0:T], AF.Sqrt, bias=epsc, scale=1.0),
                         reads=[Bpb[6], Bc], writes=[B_rstd[s_]])
                    P.op("dve", lambda e: e.reciprocal(rs_[:], rs_[:]), reads=[B_rstd[s_]], writes=[B_rstd[s_]])

                def p_dma(it):
                    t0 = it * T
                    for b in range(NB):
                        P.dma("sp", pt[b][:], p_d[li, t0 + b * 128:t0 + (b + 1) * 128, :],
                              writes=[B_pt[b]], slot=B_pt[b])

                def post_slots(it):
                    s = it % 2
                    t0 = it * T
                    hs = ht[s]
                    xs = xn[s]
                    sl_ = {}

                    def add(j, f):
                        sl_.setdefault(j, []).append(f)

                    def prep():
                        for b in range(NB):
                            P.op("pe", [lambda e, c=c, b=b: e.transpose(
                                pb[7][:, sl(c)], pt[b][:, sl(c)], identf[:]) for c in range(2)],
                                reads=[B_pt[b], Bc], writes=[Bpb[7]])
                            P.op("act", lambda e, b=b: e.copy(
                                pT[:, :, sl(b)], pb[7][:, 0:256].rearrange("p (k t) -> p k t", k=2)),
                                reads=[Bpb[7]], writes=[B_pT])
                        finish_rstd(s)
                        for k in range(8):
                            P.op("dve", lambda e, k=k: e.scalar_tensor_tensor(
                                xs[:, k, :], hs[:, k, :], vec[:, V_PLE + li, k:k + 1], rstd[s][:], ALU.mult, ALU.mult),
                                reads=[B_ht[s], Bc, B_rstd[s]], writes=[B_xn[s]])
                    add(0, prep)

                    def ple_group(m):
                        mb = m % 2
                        P.op("pe", [lambda e, k=k: e.matmul(
                            pb[4][:, 0:T], wg[:, k, sl(m)], xs[:, k, :], start=(k == 0), stop=(k == 7))
                            for k in range(8)], reads=[B_xn[s], B_wg], writes=[Bpb[4]])
                        P.op("pe", [lambda e, c=c: e.matmul(
                            pb[5][:, 0:T], wp[:, c, sl(m)], pT[:, c, :], start=(c == 0), stop=(c == 1))
                            for c in range(2)], reads=[B_pT, B_wp], writes=[Bpb[5]])
                        P.op("act", lambda e: e.activation(sg2[mb][:], pb[4][:, 0:T], AF.Tanh, scale=0.5),
                             reads=[Bpb[4]], writes=[B_sg2[mb]])
                        P.op("dve", lambda e: e.scalar_tensor_tensor(
                            sg2[mb][:], sg2[mb][:], 1.0, pb[5][:, 0:T], ALU.add, ALU.mult),
                            reads=[B_sg2[mb], Bpb[5]], writes=[B_sg2[mb]])
                        P.op("dve", lambda e: e.scalar_tensor_tensor(
                            hs[:, m, :], sg2[mb][:], 0.5, hs[:, m, :], ALU.mult, ALU.add),
                            reads=[B_sg2[mb], B_ht[s]], writes=[B_ht[s]])
                        if final:
                            sq_stat(hs, s, m)
                            if m > 0:
                                pe_stat(m - 1)
                    for m in range(8):
                        add(1 + m, lambda m=m: ple_group(m))

                    def fin_norm():
                        pe_stat(7)
                        finish_rstd(s)
                        for k in range(8):
                            P.op("dve", lambda e, k=k: e.scalar_tensor_tensor(
                                hs[:, k, :], hs[:, k, :], vec[:, V_FIN, k:k + 1], rstd[s][:], ALU.mult, ALU.mult),
                                reads=[B_ht[s], Bc, B_rstd[s]], writes=[B_ht[s]])

                    def out_block(b, hf):
                        P.op("pe", [lambda e, kk=kk: e.transpose(
                            pb[7][:, sl(kk)], hs[:, hf * 4 + kk, sl(b)], identf[:]) for kk in range(4)],
                            reads=[B_ht[s], Bc], writes=[Bpb[7]])
                        P.op("act", lambda e: e.copy(yt[hf][:], pb[7][:]), reads=[Bpb[7]], writes=[B_yt[hf]])
                        P.dma("sp", yout_d[t0 + b * 128:t0 + (b + 1) * 128, hf * 512:(hf + 1) * 512], yt[hf][:],
                              reads=[B_yt[hf]], writes=[B_Y[it]], slot=B_yt[hf])
                    if final:
                        add(9, fin_norm)
                        jj = 10
                        for b in range(NB):
                            for hf in range(2):
                                add(jj, lambda b=b, hf=hf: out_block(b, hf))
                                jj += 1
                    if not final or debug:
                        add(14, lambda: P.dma("sp", Hv[:, :, t0:t0 + T], hs[:], reads=[B_ht[s]],
                                              writes=[B_H[it]], slot=B_ht[s]))
                    return sl_

                load_tile(0)
                prologue(0)
                for it in range(nt):
                    s = it % 2
                    t0 = it * T
                    hs = ht[s]
                    xs = xn[s]
                    slots = post_slots(it - 1) if (do_ple and it > 0) else {}
                    if it + 1 < nt and not first and not do_ple:
                        load_tile(it + 1)
                    for j in range(NJ):
                        jb = j % 2
                        for f_ in slots.get(j, []):
                            f_()
                        if it + 1 < nt:
                            if first and j == 6:
                                load_tile(it + 1)
                            if do_ple and j == 15:
                                load_tile(it + 1)
                            if j == (18 if do_ple else 12):
                                prologue(it + 1)
                        P.op("pe", [lambda e, k=k, j=j, jb=jb, xs=xs: e.matmul(
                            pb[jb][:, 0:T], win[:, k, sl(j)], xs[:, k, :], start=(k == 0), stop=(k == 7))
                            for k in range(8)], reads=[B_xn[s], B_win[0]], writes=[Bpb[jb]])
                        P.op("pe", [lambda e, k=k, j=j, jb=jb, xs=xs: e.matmul(
                            pb[2 + jb][:, 0:T], win[:, k, DFF + j * 128:DFF + (j + 1) * 128], xs[:, k, :],
                            start=(k == 0), stop=(k == 7))
                            for k in range(8)], reads=[B_xn[s], B_win[0]], writes=[Bpb[2 + jb]])
                        P.op("act", lambda e, jb=jb: e.activation(sg[jb][:], pb[jb][:, 0:T], AF.Silu),
                             reads=[Bpb[jb]], writes=[B_sg[jb]])
                        P.op("dve", lambda e, j=j, jb=jb: e.tensor_tensor(
                            hT[:, j, :], sg[jb][:], pb[2 + jb][:, 0:T], ALU.mult),
                            reads=[B_sg[jb], Bpb[2 + jb]], writes=[B_hT])
                    if do_ple:
                        p_dma(it)
                    for m in range(8):
                        mb = 4 + m % 2
                        P.op("pe", [lambda e, j=j, m=m, mb=mb: e.matmul(
                            pb[mb][:, 0:T], wout[:, j, sl(m)], hT[:, j, :], start=(j == 0), stop=(j == NJ - 1))
                            for j in range(NJ)], reads=[B_hT] + B_wout, writes=[Bpb[mb]])
                        P.op("dve", lambda e, m=m, mb=mb, hs=hs: e.scalar_tensor_tensor(
                            hs[:, m, :], pb[mb][:, 0:T], 0.5, hs[:, m, :], ALU.mult, ALU.add),
                            reads=[Bpb[mb], B_ht[s]], writes=[B_ht[s]])
                        if do_ple:
                            sq_stat(hs, s, m)
                            if m > 0:
                                pe_stat(m - 1)
                    if do_ple:
                        pe_stat(7)
                    else:
                        P.dma("sp", Hv[:, :, t0:t0 + T], hs[:], reads=[B_ht[s]], writes=[B_H[it]], slot=B_ht[s])
                if do_ple:
                    last = post_slots(nt - 1)
                    for j in sorted(last):
                        for f_ in last[j]:
                            f_()
                P.end_phase()

        def conv_phase():
            with ExitStack() as pes:
                sb = lambda name, shape, dt: pes.enter_context(nc.sbuf_tensor(uniq(name), shape, dt))
                wcv = sb("wcv", [128, 8, 2048], BF16)
                dg = sb("dg", [128, 8 * CW, 128], BF16)
                cw = sb("cw", [128, 8, CW], F32)
                ht = [sb(f"ht{i}", [128, 8, T], F32) for i in range(2)]
                xn2 = [sb(f"xn{i}", [128, 8, T], BF16) for i in range(2)]
                sqt = sb("sqt", [128, 2, T], BF16)
                rstd2 = [sb(f"rstd{i}", [128, T], F32) for i in range(2)]
                sg = [sb(f"sg{i}", [128, T], F32) for i in range(2)]
                u0 = sb("u0", [128, 8, 30 + T], BF16)
                cv = sb("cv", [128, 8, T], F32)
                cvb = sb("cvb", [128, 2, T], BF16)
                mean = sb("mean", [128, T], F32)
                lrs = sb("lrs", [128, T], F32)
                tmp = [sb(f"tmp{i}", [128, T], F32) for i in range(2)]
                ub = [sb(f"ub{i}", [128, 8, T], BF16) for i in range(2)]
                B_w = P.bufs(8, "wcv")
                B_dg = P.buf("dg")
                B_cw = P.buf("cw")
                B_ht = P.bufs(2, "ht")
                B_xn2 = P.bufs(2, "xn")
                B_sq = P.bufs(2, "sq")
                B_rstd2 = P.bufs(2, "rstd")
                B_sg = P.bufs(2, "sg")
                B_u0 = P.buf("u0")
                B_cv = P.buf("cv")
                B_cvb = P.bufs(2, "cvb")
                B_mean = P.buf("mean")
                B_lrs = P.buf("lrs")
                B_tmp = P.bufs(2, "tmp")
                B_ub = P.bufs(2, "ub")
                for k in range(8):
                    load_w(wcv[:, k, :], hyb_win[sl(k), 0:2048], B_w[k])
                P.dma("sp", cw[:], cw_d, writes=[B_cw], slot=B_cw)
                for c in range(8):
                    P.op("pool", lambda e, c=c: e.tensor_tensor(
                        dg[:, c * CW:(c + 1) * CW, :],
                        identb[:].unsqueeze(1).broadcast_to([128, CW, 128]),
                        cw[:, c, :].unsqueeze(2).broadcast_to([128, CW, 128]), ALU.mult),
                        reads=[B_cw, Bc], writes=[B_dg])
                P.op("dve", lambda e: e.memset(u0[:, :, 0:30], 0.0), writes=[B_u0])
                def prologue(it_):
                    s_ = it_ % 2
                    P.dma("sp", ht[s_][:], Hv[:, :, it_ * T:(it_ + 1) * T], reads=[B_H[it_]], writes=[B_ht[s_]], slot=B_ht[s_])
                    make_xn(ht[s_], B_ht[s_], V_NMIX + 0, xn2[s_], B_xn2[s_], sqt, B_sq, rstd2[s_], B_rstd2[s_], pb[6], Bpb[6])

                prologue(0)
                for it in range(nt):
                    s = it % 2
                    t0 = it * T
                    hs = ht[s]
                    xn = xn2[s]
                    B_xn = B_xn2[s]
                    if it > 0:
                        P.op("dve", lambda e: e.tensor_copy(u0[:, :, 0:30], u0[:, :, T:T + 30]),
                             reads=[B_u0], writes=[B_u0])
                    for c in range(8):
                        jb = c % 2
                        P.op("pe", [lambda e, k=k, c=c, jb=jb, xn=xn: e.matmul(
                            pb[jb][:, 0:T], wcv[:, k, sl(c)], xn[:, k, :], start=(k == 0), stop=(k == 7))
                            for k in range(8)], reads=[B_xn] + B_w, writes=[Bpb[jb]])
                        P.op("pe", [lambda e, k=k, c=c, jb=jb, xn=xn: e.matmul(
                            pb[2 + jb][:, 0:T], wcv[:, k, 1024 + c * 128:1024 + (c + 1) * 128], xn[:, k, :],
                            start=(k == 0), stop=(k == 7))
                            for k in range(8)], reads=[B_xn] + B_w, writes=[Bpb[2 + jb]])
                        P.op("act", lambda e, jb=jb: e.activation(sg[jb][:], pb[2 + jb][:, 0:T], AF.Tanh, scale=0.5),
                             reads=[Bpb[2 + jb]], writes=[B_sg[jb]])
                        P.op("dve", lambda e, c=c, jb=jb: e.scalar_tensor_tensor(
                            u0[:, c, 30:30 + T], sg[jb][:], 1.0, pb[jb][:, 0:T], ALU.add, ALU.mult),
                            reads=[B_sg[jb], Bpb[jb]], writes=[B_u0])
                    if it + 1 < nt:
                        prologue(it + 1)
                    for c in range(8):
                        mb = 4 + c % 2
                        q = c % 2
                        P.op("pe", [lambda e, k=k, c=c, mb=mb: e.matmul(
                            pb[mb][:, 0:T], dg[:, c * CW + k, :], u0[:, c, k:k + T],
                            start=(k == 0), stop=(k == CW - 1))
                            for k in range(CW)], reads=[B_u0, B_dg], writes=[Bpb[mb]])
                        P.op("act", lambda e, c=c, mb=mb: e.activation(
                            cv[:, c, :], pb[mb][:, 0:T], AF.Identity, bias=vec[:, V_CVB, c:c + 1], scale=0.5),
                            reads=[Bpb[mb], Bc], writes=[B_cv])
                        P.op("dve", lambda e, c=c, q=q: e.tensor_copy(cvb[:, q, :], cv[:, c, :]),
                             reads=[B_cv], writes=[B_cvb[q]])
                        P.op("pe", lambda e, c=c, q=q: e.matmul(
                            pb[6][:, 0:T], ones1k[:], cvb[:, q, :], start=(c == 0), stop=(c == 7)),
                            reads=[B_cvb[q], Bc], writes=[Bpb[6]])
                    P.op("act", lambda e: e.copy(mean[:], pb[6][:, 0:T]), reads=[Bpb[6]], writes=[B_mean])
                    for c in range(8):
                        q = c % 2
                        P.op("dve", lambda e, c=c, q=q: e.tensor_tensor(tmp[q][:], cv[:, c, :], mean[:], ALU.subtract),
                             reads=[B_cv, B_mean], writes=[B_tmp[q]])
                        P.op("act", lambda e, q=q: e.activation(sqt[:, q, :], tmp[q][:], AF.Square),
                             reads=[B_tmp[q]], writes=[B_sq[q]])
                        P.op("pe", lambda e, c=c, q=q: e.matmul(
                            pb[7][:, 0:T], ones1k[:], sqt[:, q, :], start=(c == 0), stop=(c == 7)),
                            reads=[B_sq[q], Bc], writes=[Bpb[7]])
                    P.op("act", lambda e: e.activation(lrs[:], pb[7][:, 0:T], AF.Sqrt, bias=epsc, scale=1.0),
                         reads=[Bpb[7], Bc], writes=[B_lrs])
                    P.op("dve", lambda e: e.reciprocal(lrs[:], lrs[:]), reads=[B_lrs], writes=[B_lrs])
                    for c in range(8):
                        q = c % 2
                        P.op("dve", lambda e, c=c, q=q: e.tensor_tensor(tmp[q][:], cv[:, c, :], mean[:], ALU.subtract),
                             reads=[B_cv, B_mean], writes=[B_tmp[q]])
                        P.op("dve", lambda e, q=q: e.tensor_tensor(tmp[q][:], tmp[q][:], lrs[:], ALU.mult),
                             reads=[B_lrs, B_tmp[q]], writes=[B_tmp[q]])
                        P.op("act", lambda e, c=c, q=q, s=s: e.activation(
                            ub[s][:, c, :], tmp[q][:], AF.Silu, bias=vec[:, V_LNB, c:c + 1],
                            scale=vec[:, V_LNG, c:c + 1]),
                            reads=[B_tmp[q], Bc], writes=[B_ub[s]])
                    P.dma("sp", Uv[:, :, t0:t0 + T], ub[s][:], reads=[B_ub[s]], writes=[B_U[it]], slot=B_ub[s])
                P.end_phase()

        def ssd_phase():
            with ExitStack() as pes:
                sb = lambda name, shape, dt: pes.enter_context(nc.sbuf_tensor(uniq(name), shape, dt))
                NZ = HYB_IN - 2048
                wz = sb("wz", [128, 8, NZ], BF16)
                wo = sb("wo", [128, 16, D], BF16)
                scw = sb("scw", [128, 12, 5], F32)
                h16 = sb("h16", [128, 3, 16], F32)
                abc = sb("abc", [128, 16], F32)
                ht = [sb(f"ht{i}", [128, 8, T], F32) for i in range(2)]
                xn2 = [sb(f"xn{i}", [128, 8, T], BF16) for i in range(2)]
                sqt = sb("sqt", [128, 2, T], BF16)
                rstd2 = [sb(f"rstd{i}", [128, T], F32) for i in range(2)]
                sz = sb("sz", [128, 8, T], F32)
                xb = sb("xb", [128, 12, 3 + T], BF16)
                dg4 = sb("dg4", [128, 48, 128], BF16)
                xsf = sb("xsf", [128, 8, T], F32)
                xsb = sb("xsb", [128, 8, T], BF16)
                bcb = sb("bcb", [128, 4, T], BF16)
                dtt = sb("dtt", [128, 16], F32)
                adt = sb("adt", [128, 16], F32)
                acs = sb("acs", [128, 16], F32)
                ala = sb("ala", [128, 16], F32)
                cdec = sb("cdec", [128, 16], F32)
                coef = sb("coef", [128, 16], F32)
                xdt = sb("xdt", [128, 16, 64], BF16)
                xdd = sb("xdd", [128, 16, 64], BF16)
                btm = sb("btm", [128, 2, 128], BF16)
                R2 = [sb(f"R{i}", [128, 8, 128], F32) for i in range(2)]
                dif = sb("dif", [128, 8, 128], F32)
                erow = sb("erow", [128, 8, 128], F32)
                MT = sb("MT", [128, 8, 128], BF16)
                Cs = sb("Cs", [128, 8, 128], BF16)
                cbm = sb("cbm", [128, 2, 128], F32)
                prev = sb("prev", [128, 16, 64], F32)
                prevb = sb("prevb", [128, 16, 64], BF16)
                yg = sb("yg", [128, 8, T], F32)
                yn = sb("yn", [128, 8, T], BF16)
                grs = [sb(f"grs{i}", [128, T], F32) for i in range(2)]
                ut = sb("ut", [128, 8, T], BF16)
                B_wz = P.bufs(8, "wz")
                B_wo = P.bufs(2, "wo")
                B_sm = P.buf("small")
                B_ht = P.bufs(2, "ht")
                B_xn2 = P.bufs(2, "xn")
                B_sq = P.bufs(2, "sq")
                B_rstd2 = P.bufs(2, "rstd")
                B_sz = P.buf("sz")
                B_xb = P.buf("xb")
                B_dg4 = P.buf("dg4")
                B_xs = P.buf("xs")
                B_bc = P.buf("bcb")
                B_dt = P.buf("dt")
                B_co = P.buf("coefs")
                B_xdt = P.buf("xdt")
                B_btm = P.buf("btm")
                B_R2 = P.bufs(2, "R")
                B_dif = P.buf("dif")
                B_er = P.buf("erow")
                B_MT = P.buf("MT")
                B_Cs = P.buf("Cs")
                B_cbm = P.buf("cbm")
                B_prev = P.buf("prev")
                B_prevb = P.buf("prevb")
                B_yg = P.buf("yg")
                B_yn = P.buf("yn")
                B_grs = P.bufs(2, "grs")
                B_ut = P.buf("ut")
                for k in range(8):
                    load_w(wz[:, k, :], hyb_win[sl(k), 2048:HYB_IN], B_wz[k])
                wo_v = hyb_wout.rearrange("(j p) m -> p j m", p=128)
                for hh in range(2):
                    load_w(wo[:, hh * 8:(hh + 1) * 8, :], wo_v[:, hh * 8:(hh + 1) * 8, :], B_wo[hh])
                P.dma("sp", scw[:], scw_d, writes=[B_sm], slot=B_sm)
                for i in range(3):
                    P.dma("sp", h16[:, i, :], h16_d[i].partition_broadcast(128), writes=[B_sm], slot=B_sm)
                P.op("act", lambda e: e.activation(abc[:], h16[:, 1, :], AF.Exp), reads=[B_sm], writes=[B_sm])
                P.op("dve", lambda e: e.tensor_scalar_mul(abc[:], abc[:], -1.0), reads=[B_sm], writes=[B_sm])
                for c in range(12):
                    P.op("pool", lambda e, c=c: e.tensor_tensor(
                        dg4[:, c * 4:(c + 1) * 4, :],
                        identb[:].unsqueeze(1).broadcast_to([128, 4, 128]),
                        scw[:, c, 0:4].unsqueeze(2).broadcast_to([128, 4, 128]), ALU.mult),
                        reads=[B_sm, Bc], writes=[B_dg4])
                P.op("dve", lambda e: e.memset(xb[:, :, 0:3], 0.0), writes=[B_xb])
                P.op("dve", lambda e: e.memset(prev[:], 0.0), writes=[B_prev])
                P.op("dve", lambda e: e.memset(prevb[:], 0.0), writes=[B_prevb])
                def prologue(it_):
                    s_ = it_ % 2
                    P.dma("sp", ht[s_][:], Hv[:, :, it_ * T:(it_ + 1) * T], reads=[B_H[it_]], writes=[B_ht[s_]], slot=B_ht[s_])
                    make_xn(ht[s_], B_ht[s_], V_NMIX + 0, xn2[s_], B_xn2[s_], sqt, B_sq, rstd2[s_], B_rstd2[s_], pb[6], Bpb[6])

                prologue(0)
                for it in range(nt):
                    s = it % 2
                    t0 = it * T
                    hs = ht[s]
                    xn = xn2[s]
                    B_xn = B_xn2[s]
                    P.dma("sp", ut[:], Uv[:, :, t0:t0 + T], reads=[B_U[it]], writes=[B_ut], slot=B_ut)
                    if it > 0:
                        P.op("dve", lambda e: e.tensor_copy(xb[:, :, 0:3], xb[:, :, T:T + 3]),
                             reads=[B_xb], writes=[B_xb])
                    for c in range(8):
                        jb = c % 2
                        P.op("pe", [lambda e, k=k, c=c, jb=jb, xn=xn: e.matmul(
                            pb[jb][:, 0:T], wz[:, k, sl(c)], xn[:, k, :], start=(k == 0), stop=(k == 7))
                            for k in range(8)], reads=[B_xn] + B_wz, writes=[Bpb[jb]])
                        P.op("act", lambda e, c=c, jb=jb: e.activation(sz[:, c, :], pb[jb][:, 0:T], AF.Silu),
                             reads=[Bpb[jb]], writes=[B_sz])
                    for c in range(12):
                        jb = c % 2
                        P.op("pe", [lambda e, k=k, c=c, jb=jb, xn=xn: e.matmul(
                            pb[jb][:, 0:T], wz[:, k, 1024 + c * 128:1024 + (c + 1) * 128], xn[:, k, :],
                            start=(k == 0), stop=(k == 7))
                            for k in range(8)], reads=[B_xn] + B_wz, writes=[Bpb[jb]])
                        P.op("act", lambda e, c=c, jb=jb: e.copy(xb[:, c, 3:3 + T], pb[jb][:, 0:T]),
                             reads=[Bpb[jb]], writes=[B_xb])
                    for c in range(12):
                        mb = 2 + c % 2
                        P.op("pe", [lambda e, c=c, k=k, mb=mb: e.matmul(
                            pb[mb][:, 0:T], dg4[:, c * 4 + k, :], xb[:, c, k:k + T], start=(k == 0), stop=(k == 3))
                            for k in range(4)], reads=[B_xb, B_dg4], writes=[Bpb[mb]])
                        if c < 8:
                            P.op("act", lambda e, c=c, mb=mb: e.activation(
                                xsf[:, c, :], pb[mb][:, 0:T], AF.Silu, bias=scw[:, c, 4:5], scale=1.0),
                                reads=[Bpb[mb], B_sm], writes=[B_xs])
                            P.op("pool", lambda e, c=c: e.tensor_copy(xsb[:, c, :], xsf[:, c, :]),
                                 reads=[B_xs], writes=[B_xs])
                        else:
                            P.op("act", lambda e, c=c, mb=mb: e.activation(
                                bcb[:, c - 8, :], pb[mb][:, 0:T], AF.Silu, bias=scw[:, c, 4:5], scale=1.0),
                                reads=[Bpb[mb], B_sm], writes=[B_bc])
                    if it + 1 < nt:
                        prologue(it + 1)
                    for cch in range(NB if STG >= 2 else 0):
                        csl = slice(cch * 128, (cch + 1) * 128)
                        P.op("pe", [lambda e, k=k, csl=csl, xn=xn: e.matmul(
                            pb[5][:, 256:272], xn[:, k, csl], wz[:, k, 2560:2576], start=(k == 0), stop=(k == 7))
                            for k in range(8)], reads=[B_xn] + B_wz, writes=[Bpb[5]])
                        P.op("dve", lambda e: e.tensor_tensor(dtt[:], pb[5][:, 256:272], h16[:, 0, :], ALU.add),
                             reads=[Bpb[5], B_sm], writes=[B_dt])
                        P.op("act", lambda e: e.activation(dtt[:], dtt[:], AF.Exp), reads=[B_dt], writes=[B_dt])
                        P.op("act", lambda e: e.activation(dtt[:], dtt[:], AF.Ln, bias=onec, scale=1.0),
                             reads=[B_dt, Bc], writes=[B_dt])
                        P.op("dve", lambda e: e.tensor_tensor(adt[:], dtt[:], abc[:], ALU.mult),
                             reads=[B_dt, B_sm], writes=[B_dt])
                        P.op("pe", [lambda e: e.matmul(pb[5][:, 272:288], triu[:], adt[:], start=True, stop=True),
                                    lambda e: e.matmul(pb[5][:, 288:304], ones_f[:], adt[:], start=True, stop=True)],
                             reads=[B_dt, Bc], writes=[Bpb[5]])
                        P.op("dve", lambda e: e.tensor_copy(acs[:], pb[5][:, 272:288]), reads=[Bpb[5]], writes=[B_co])
                        P.op("dve", lambda e: e.tensor_copy(ala[:], pb[5][:, 288:304]), reads=[Bpb[5]], writes=[B_co])
                        P.op("act", lambda e: e.activation(cdec[:], ala[:], AF.Exp), reads=[B_co], writes=[B_co])
                        P.op("dve", lambda e: e.tensor_tensor(coef[:], ala[:], acs[:], ALU.subtract),
                             reads=[B_co], writes=[B_co])
                        P.op("act", lambda e: e.activation(coef[:], coef[:], AF.Exp), reads=[B_co], writes=[B_co])
                        P.op("dve", lambda e: e.tensor_tensor(coef[:], coef[:], dtt[:], ALU.mult),
                             reads=[B_co, B_dt], writes=[B_co])
                        if STG < 3:
                            continue
                        pbt = pb[4][:].bitcast(BF16)
                        P.op("pe", [lambda e, c=c, csl=csl: e.transpose(pbt[:, sl(c)], xsb[:, c, csl], identb[:])
                                    for c in range(8)], reads=[B_xs, Bc], writes=[Bpb[4]])
                        pbt3 = pbt.rearrange("p (h d) -> p h d", h=16)
                        P.op("dve", lambda e: e.tensor_tensor(
                            xdt[:], pbt3, dtt[:].unsqueeze(2).broadcast_to([128, 16, 64]), ALU.mult),
                            reads=[Bpb[4], B_dt], writes=[B_xdt])
                        P.op("dve", lambda e: e.tensor_tensor(
                            xdd[:], pbt3, coef[:].unsqueeze(2).broadcast_to([128, 16, 64]), ALU.mult),
                            reads=[Bpb[4], B_co], writes=[B_xdt])
                        P.op("pe", [lambda e, g=g, csl=csl: e.transpose(pbt[:, sl(g)], bcb[:, g, csl], identb[:])
                                    for g in range(2)], reads=[B_bc, Bc], writes=[Bpb[4]])
                        P.op("act", lambda e: e.copy(btm[:], pbt[:, 0:256].rearrange("p (g n) -> p g n", g=2)),
                             reads=[Bpb[4]], writes=[B_btm])
                        P.op("pe", [lambda e, g=g, csl=csl: e.matmul(
                            pb[5][:, sl(g)], bcb[:, g, csl], bcb[:, 2 + g, csl], start=True, stop=True)
                            for g in range(2)], reads=[B_bc], writes=[Bpb[5]])
                        P.op("dve", lambda e: e.tensor_tensor(
                            cbm[:], pb[5][:, 0:256].rearrange("p (g n) -> p g n", g=2),
                            triu[:].unsqueeze(1).broadcast_to([128, 2, 128]), ALU.mult),
                            reads=[Bpb[5], Bc], writes=[B_cbm])
                        if STG < 4:
                            continue
                        rbk = [(0, 1), (6, 7)]
                        for g in range(2):
                            hsl = slice(g * 8, (g + 1) * 8)
                            P.op("dve", lambda e, hsl=hsl, g=g: e.tensor_tensor(
                                R2[g][:], triu[:].unsqueeze(1).broadcast_to([128, 8, 128]),
                                adt[:, hsl].unsqueeze(2).broadcast_to([128, 8, 128]), ALU.mult),
                                reads=[B_dt, Bc], writes=[B_R2[g]])
                            P.op("pe", [lambda e, h2=h2, g=g: e.matmul(
                                pb[rbk[g][h2 // 2]][:, (h2 % 2) * 256:(h2 % 2) * 256 + 256], ones_f[:],
                                R2[g][:, h2 * 2:(h2 + 1) * 2, :], start=True, stop=True)
                                for h2 in range(4)], reads=[B_R2[g], Bc], writes=[Bpb[rbk[g][0]], Bpb[rbk[g][1]]])
                        for g in range(2):
                            hsl = slice(g * 8, (g + 1) * 8)
                            for hh in range(2):
                                h4 = slice(hh * 4, (hh + 1) * 4)
                                a4 = slice(g * 8 + hh * 4, g * 8 + hh * 4 + 4)
                                bk = rbk[g][hh]
                                rb = pb[bk][:].rearrange("p (h l) -> p h l", h=4)
                                P.op("dve", lambda e, h4=h4, a4=a4, rb=rb: e.tensor_tensor(
                                    dif[:, h4, :], rb, acs[:, a4].unsqueeze(2).broadcast_to([128, 4, 128]),
                                    ALU.subtract), reads=[Bpb[bk], B_co], writes=[B_dif])
                                P.op("act", lambda e, h4=h4, rb=rb: e.activation(erow[:, h4, :], rb, AF.Exp),
                                     reads=[Bpb[bk]], writes=[B_er])
                            if SUB != 1:
                                P.op("act", lambda e: e.activation(dif[:], dif[:], AF.Exp), reads=[B_dif], writes=[B_dif])
                            P.op("dve", lambda e, g=g: e.scalar_tensor_tensor(
                                MT[:], dif[:], 1.0, cbm[:, g, :].unsqueeze(1).broadcast_to([128, 8, 128]),
                                ALU.min, ALU.mult), reads=[B_dif, B_cbm], writes=[B_MT])
                            P.op("dve", lambda e, g=g, csl=csl: e.tensor_tensor(
                                Cs[:], erow[:], bcb[:, 2 + g, csl].unsqueeze(1).broadcast_to([128, 8, 128]),
                                ALU.mult), reads=[B_er, B_bc], writes=[B_Cs])
                            for hh in range(8 if STG >= 5 else 0):
                                h = g * 8 + hh
                                cch_out = h // 2
                                half = h % 2
                                bank = pb[2 + cch_out // 4]
                                col = (cch_out % 4) * 128
                                P.op("pe", [
                                    lambda e, h=h, hh=hh, half=half, bank=bank, col=col: e.matmul(
                                        bank[half * 64:(half + 1) * 64, col:col + 128], xdt[:, h, :], MT[:, hh, :],
                                        start=True, stop=False),
                                    lambda e, h=h, hh=hh, half=half, bank=bank, col=col: e.matmul(
                                        bank[half * 64:(half + 1) * 64, col:col + 128], prevb[:, h, :], Cs[:, hh, :],
                                        start=False, stop=True)],
                                    reads=[B_xdt, B_MT, B_prevb, B_Cs], writes=[Bpb[2 + cch_out // 4]])
                            if SUB != 2:
                              P.op("pe", lambda e, g=g: e.matmul(
                                pb[4 + g][:], btm[:, g, :], xdd[:, g * 8:(g + 1) * 8, :], start=True, stop=True),
                                reads=[B_btm, B_xdt], writes=[Bpb[4 + g]])
                        for g in range(2 if SUB != 2 else 0):
                            hsl = slice(g * 8, (g + 1) * 8)
                            P.op("dve", lambda e, hsl=hsl: e.tensor_tensor(
                                prev[:, hsl, :], prev[:, hsl, :],
                                cdec[:, hsl].unsqueeze(2).broadcast_to([128, 8, 64]), ALU.mult),
                                reads=[B_co, B_prev], writes=[B_prev])
                            P.op("dve", lambda e, hsl=hsl, g=g: e.tensor_tensor(
                                prev[:, hsl, :], prev[:, hsl, :],
                                pb[4 + g][:].rearrange("p (h d) -> p h d", h=8), ALU.add),
                                reads=[Bpb[4 + g], B_prev], writes=[B_prev])
                        P.op("act", lambda e: e.copy(prevb[:], prev[:]), reads=[B_prev], writes=[B_prevb])
                        for c in range(8):
                            bank = pb[2 + c // 4]
                            col = (c % 4) * 128
                            P.op("dve", lambda e, c=c, bank=bank, col=col, csl=csl: e.scalar_tensor_tensor(
                                yg[:, c, csl], xsf[:, c, csl], vec[:, V_SSD, c:c + 1], bank[:, col:col + 128],
                                ALU.mult, ALU.add), reads=[Bpb[2 + c // 4], B_xs, Bc], writes=[B_yg])
                    P.op("pool", lambda e: e.tensor_tensor(yg[:], yg[:], sz[:], ALU.mult),
                         reads=[B_sz, B_yg], writes=[B_yg])
                    for g in range(2):
                        rms_rstd(yg, B_yg, 4, ones512, sqt, B_sq, pb[6], Bpb[6], grs[g][:], B_grs[g], k0=g * 4)
                    for c in range(8):
                        P.op("dve", lambda e, c=c: e.scalar_tensor_tensor(
                            yn[:, c, :], yg[:, c, :], vec[:, V_SSN, c:c + 1], grs[c // 4][:], ALU.mult, ALU.mult),
                            reads=[B_yg, Bc, B_grs[c // 4]], writes=[B_yn])
                    for m in range(8):
                        mb = m % 2
                        P.op("pe", [lambda e, j=j, m=m, mb=mb: e.matmul(
                            pb[mb][:, 0:T], wo[:, j, sl(m)], (ut[:, j, :] if j < 8 else yn[:, j - 8, :]),
                            start=(j == 0), stop=(j == 15))
                            for j in range(16)], reads=[B_ut, B_yn] + B_wo, writes=[Bpb[mb]])
                        P.op("dve", lambda e, m=m, mb=mb, hs=hs: e.tensor_tensor(
                            hs[:, m, :], hs[:, m, :], pb[mb][:, 0:T], ALU.add),
                            reads=[Bpb[mb], B_ht[s]], writes=[B_ht[s]])
                    P.dma("sp", Hv[:, :, t0:t0 + T], hs[:], reads=[B_ht[s]], writes=[B_H[it]], slot=B_ht[s])
                P.end_phase()

        def att_phase():
            with ExitStack() as pes:
                sb = lambda name, shape, dt: pes.enter_context(nc.sbuf_tensor(uniq(name), shape, dt))
                wq = sb("wq", [128, 8, 1536], BF16)
                wo = sb("wo", [128, 8, D], BF16)
                bqk = sb("bqk", [128, 10], F32)
                bvb = sb("bvb", [128, 256], F32)
                snk = sb("snk", [128, 16], F32)
                mskb = sb("mskb", [128, 2, 256], BF16)
                ht = [sb(f"ht{i}", [128, 8, T], F32) for i in range(2)]
                xn2 = [sb(f"xn{i}", [128, 8, T], BF16) for i in range(2)]
                sqt = sb("sqt", [128, 2, T], BF16)
                rstd2 = [sb(f"rstd{i}", [128, T], F32) for i in range(2)]
                rope = sb("rope", [128, 2, T], F32)
                qf = [sb(f"qf{i}", [128, T], F32) for i in range(2)]
                qb = [sb(f"qb{i}", [128, T], BF16) for i in range(2)]
                t1 = [sb(f"t1{i}", [128, T], F32) for i in range(2)]
                qr = sb("qr", [128, 8, T], BF16)
                kr = sb("kr", [128, 2, 128 + T], BF16)
                vt = sb("vt", [128, 1 + NB, 256], BF16)
                ee = [sb(f"ee{i}", [128, 4, 256], F32) for i in range(2)]
                pp = [sb(f"pp{i}", [128, 4, 256], BF16) for i in range(2)]
                pT = [sb(f"pT{i}", [128, 8, 128], BF16) for i in range(2)]
                mx = [sb(f"mx{i}", [128, 4], F32) for i in range(2)]
                nmx = [sb(f"nmx{i}", [128, 4], F32) for i in range(2)]
                rs = [sb(f"rs{i}", [128, 4], F32) for i in range(2)]
                es_ = [sb(f"es_{i}", [128, 4], F32) for i in range(2)]
                oT = sb("oT", [128, 8, T], BF16)
                B_wq = P.bufs(8, "wq")
                B_wo = P.buf("wo")
                B_sm_ = P.buf("small")
                B_ht = P.bufs(2, "ht")
                B_xn2 = P.bufs(2, "xn")
                B_sq = P.bufs(2, "sq")
                B_rstd2 = P.bufs(2, "rstd")
                B_rope = P.buf("rope")
                B_qf = P.bufs(2, "qf")
                B_qb = P.bufs(2, "qb")
                B_t1 = P.bufs(2, "t1")
                B_qr = P.buf("qr")
                B_kr = P.buf("kr")
                B_vt = P.buf("vt")
                B_s = P.bufs(2, "sm")
                B_e = P.bufs(2, "ee")
                B_p = P.bufs(2, "pp")
                B_pT = P.bufs(2, "pT")
                B_st = P.bufs(2, "stats")
                B_oT = P.buf("oT")
                for k in range(8):
                    load_w(wq[:, k, :], wqkv_d[sl(k), :], B_wq[k])
                load_w(wo[:], wo_d.rearrange("(k p) m -> p k m", p=128), B_wo)
                P.dma("sp", bqk[:], bqk_d, writes=[B_sm_], slot=B_sm_)
                P.dma("sp", bvb[:], bv_d.partition_broadcast(128), writes=[B_sm_], slot=B_sm_)
                P.dma("sp", snk[:], h16_d[2].partition_broadcast(128), writes=[B_sm_], slot=B_sm_)
                P.dma("pool", mskb[:], msk_d.rearrange("a p s -> p a s"), writes=[B_sm_], slot=B_sm_)
                P.op("dve", lambda e: e.memset(kr[:, :, 0:128], 0.0), writes=[B_kr])
                P.op("dve", lambda e: e.memset(vt[:, 0, :], 0.0), writes=[B_vt])
                def prologue(it_):
                    s_ = it_ % 2
                    P.dma("sp", ht[s_][:], Hv[:, :, it_ * T:(it_ + 1) * T], reads=[B_H[it_]], writes=[B_ht[s_]], slot=B_ht[s_])
                    make_xn(ht[s_], B_ht[s_], V_NMIX + 1, xn2[s_], B_xn2[s_], sqt, B_sq, rstd2[s_], B_rstd2[s_], pb[6], Bpb[6])

                prologue(0)
                for it in range(nt):
                    s = it % 2
                    t0 = it * T
                    hs = ht[s]
                    xn = xn2[s]
                    B_xn = B_xn2[s]
                    P.dma("sp", rope[:], rope_d[:, :, t0:t0 + T].rearrange("a p s -> p a s"),
                          writes=[B_rope], slot=B_rope)
                    if it > 0:
                        P.op("dve", lambda e: e.tensor_copy(kr[:, :, 0:128], kr[:, :, T:T + 128]),
                             reads=[B_kr], writes=[B_kr])
                        P.op("dve", lambda e: e.tensor_copy(vt[:, 0, :], vt[:, NB, :]), reads=[B_vt], writes=[B_vt])
                    for c in range(10):
                        jb = c % 2
                        P.op("pe", [lambda e, k=k, c=c, jb=jb, xn=xn: e.matmul(
                            pb[jb][:, 0:T], wq[:, k, sl(c)], xn[:, k, :], start=(k == 0), stop=(k == 7))
                            for k in range(8)], reads=[B_xn] + B_wq, writes=[Bpb[jb]])
                        P.op("act", lambda e, c=c, jb=jb: e.activation(
                            qf[jb][:], pb[jb][:, 0:T], AF.Identity, bias=bqk[:, c:c + 1], scale=1.0),
                            reads=[Bpb[jb], B_sm_], writes=[B_qf[jb]])
                        P.op("pe", lambda e, jb=jb: e.matmul(pb[2 + jb][:, 0:T], pswapf[:], qf[jb][:], start=True, stop=True),
                             reads=[B_qf[jb], Bc], writes=[Bpb[2 + jb]])
                        P.op("dve", lambda e, jb=jb: e.tensor_tensor(t1[jb][:], qf[jb][:], rope[:, 0, :], ALU.mult),
                             reads=[B_qf[jb], B_rope], writes=[B_t1[jb]])
                        P.op("dve", lambda e, jb=jb: e.tensor_tensor(qf[jb][:], pb[2 + jb][:, 0:T], rope[:, 1, :], ALU.mult),
                             reads=[Bpb[2 + jb], B_rope, B_qf[jb]], writes=[B_qf[jb]])
                        dst = qr[:, c, :] if c < 8 else kr[:, c - 8, 128:128 + T]
                        P.op("dve", lambda e, jb=jb, dst=dst: e.tensor_tensor(dst, t1[jb][:], qf[jb][:], ALU.add),
                             reads=[B_t1[jb], B_qf[jb]], writes=[B_qr if c < 8 else B_kr])
                    for b in range(NB):
                        jb = b % 2
                        P.op("pe", [lambda e, k=k, b=b, jb=jb, xn=xn: e.matmul(
                            pb[jb][:, 0:256], xn[:, k, sl(b)], wq[:, k, 1280:1536], start=(k == 0), stop=(k == 7))
                            for k in range(8)], reads=[B_xn] + B_wq, writes=[Bpb[jb]])
                        P.op("dve", lambda e, b=b, jb=jb: e.tensor_tensor(vt[:, 1 + b, :], pb[jb][:, 0:256], bvb[:], ALU.add),
                             reads=[Bpb[jb], B_sm_], writes=[B_vt])
                    if it + 1 < nt:
                        prologue(it + 1)
                    def mk_group(b, kv, a):
                        gblk = it * NB + b
                        mi = 0 if gblk == 0 else 1
                        half = kv % 2
                        hp = slice(half * 64, (half + 1) * 64)
                        kc = kv // 2
                        qc0 = (kv // 2) * 4
                        S0, S1, PTb, Ob = 4 * a, 4 * a + 1, 4 * a + 2, 4 * a + 3
                        ee_, pp_, pT_ = ee[a], pp[a], pT[a]
                        mx_, nmx_, rs_, es2 = mx[a], nmx[a], rs[a], es_[a]
                        Be, Bp, BpT, Bst = B_e[a], B_p[a], B_pT[a], B_st[a]
                        sg4 = snk[:, kv * 4:(kv + 1) * 4]
                        pbt = pb[PTb][:].bitcast(BF16)
                        G = {}

                        def pe_s():
                            fns = []
                            for i in range(4):
                                dst = pb[S0 + i // 2][:, (i % 2) * 256:(i % 2) * 256 + 256]
                                fns.append(lambda e, i=i, dst=dst: e.matmul(
                                    dst, qr[hp, qc0 + i, sl(b)], kr[hp, kc, b * 128:b * 128 + 256],
                                    start=True, stop=False))
                                fns.append(lambda e, dst=dst: e.matmul(
                                    dst, identb[:], mskb[:, mi, :], start=False, stop=True))
                            P.op("pe", fns, reads=[B_qr, B_kr, B_sm_, Bc], writes=[Bpb[S0], Bpb[S1]])

                        def d1():
                            for hh in range(2):
                                P.op("dve", lambda e, hh=hh: e.tensor_reduce(
                                    mx_[:, hh * 2:hh * 2 + 2], pb[S0 + hh][:].rearrange("p (h s) -> p h s", h=2),
                                    AX.X, ALU.max), reads=[Bpb[S0 + hh]], writes=[Bst])
                            P.op("dve", lambda e: e.scalar_tensor_tensor(mx_[:], mx_[:], 0.125, sg4, ALU.mult, ALU.max),
                                 reads=[Bst, B_sm_], writes=[Bst])
                            P.op("dve", lambda e: e.tensor_scalar_mul(nmx_[:], mx_[:], -1.0), reads=[Bst], writes=[Bst])
                            P.op("dve", lambda e: e.tensor_tensor(es2[:], sg4, mx_[:], ALU.subtract),
                                 reads=[Bst, B_sm_], writes=[Bst])
                            P.op("dve", lambda e: e.memset(rs_[:], 0.0), writes=[Bst])

                        def a1():
                            for i in range(4):
                                src = pb[S0 + i // 2][:, (i % 2) * 256:(i % 2) * 256 + 256]
                                P.op("act", lambda e, i=i, src=src: e.activation(
                                    ee_[:, i, :], src, AF.Exp, bias=nmx_[:, i:i + 1], scale=0.125,
                                    accum_out=rs_[:, i:i + 1]), reads=[Bpb[S0 + i // 2], Bst], writes=[Be, Bst])
                            P.op("act", lambda e: e.activation(es2[:], es2[:], AF.Exp), reads=[Bst], writes=[Bst])

                        def d2():
                            P.op("dve", lambda e: e.tensor_tensor(rs_[:], rs_[:], es2[:], ALU.add), reads=[Bst], writes=[Bst])
                            P.op("dve", lambda e: e.reciprocal(rs_[:], rs_[:]), reads=[Bst], writes=[Bst])
                            P.op("dve", lambda e: e.tensor_tensor(
                                pp_[:], ee_[:], rs_[:].unsqueeze(2).broadcast_to([128, 4, 256]), ALU.mult),
                                reads=[Be, Bst], writes=[Bp])

                        def pe_t():
                            P.op("pe", [lambda e, i=i, kb=kb: e.transpose(
                                pbt[:, sl(i * 2 + kb)], pp_[:, i, sl(kb)], identb[:])
                                for i in range(4) for kb in range(2)], reads=[Bp, Bc], writes=[Bpb[PTb]])

                        def a2():
                            P.op("act", lambda e: e.copy(pT_[:], pbt.rearrange("p (a q) -> p a q", a=8)),
                                 reads=[Bpb[PTb]], writes=[BpT])

                        def pe_pv():
                            P.op("pe", [lambda e, i=i, kb=kb: e.matmul(
                                pb[Ob][hp, i * 128:(i + 1) * 128], vt[:, b + kb, kv * 64:(kv + 1) * 64],
                                pT_[:, i * 2 + kb, :], start=(kb == 0), stop=(kb == 1))
                                for i in range(4) for kb in range(2)], reads=[B_vt, BpT], writes=[Bpb[Ob]])

                        def a3():
                            P.op("act", lambda e: e.copy(
                                oT[hp, qc0:qc0 + 4, sl(b)], pb[Ob][hp, :].rearrange("p (i q) -> p i q", i=4)),
                                reads=[Bpb[Ob]], writes=[B_oT])
                        G.update(pe_s=pe_s, d1=d1, a1=a1, d2=d2, pe_t=pe_t, a2=a2, pe_pv=pe_pv, a3=a3)
                        return G

                    groups = [mk_group(b_, kv_, gi % 2) for gi, (b_, kv_) in
                              enumerate([(b_, kv_) for b_ in range(NB) for kv_ in range(4)])]
                    ng_ = len(groups)
                    groups[0]["pe_s"]()
                    for c_ in range(ng_ + 1):
                        cur = groups[c_] if c_ < ng_ else None
                        prv = groups[c_ - 1] if c_ > 0 else None
                        if cur:
                            cur["d1"]()
                        if c_ + 1 < ng_:
                            groups[c_ + 1]["pe_s"]()
                        if cur:
                            cur["a1"]()
                        if prv:
                            prv["d2"]()
                            prv["pe_t"]()
                            prv["a2"]()
                            prv["pe_pv"]()
                            prv["a3"]()
                    for m in range(8):
                        mb = m % 2
                        P.op("pe", [lambda e, j=j, m=m, mb=mb: e.matmul(
                            pb[mb][:, 0:T], wo[:, j, sl(m)], oT[:, j, :], start=(j == 0), stop=(j == 7))
                            for j in range(8)], reads=[B_oT, B_wo], writes=[Bpb[mb]])
                        P.op("dve", lambda e, m=m, mb=mb, hs=hs: e.scalar_tensor_tensor(
                            hs[:, m, :], pb[mb][:, 0:T], vec[:, V_BO, m:m + 1], hs[:, m, :], ALU.add, ALU.add),
                            reads=[Bpb[mb], B_ht[s], Bc], writes=[B_ht[s]])
                    P.dma("sp", Hv[:, :, t0:t0 + T], hs[:], reads=[B_ht[s]], writes=[B_H[it]], slot=B_ht[s])
                P.end_phase()

        last = phases[-1]
        for ph in phases:
            if ph == "ffn1_0":
                ffn_phase(0, 0, True, False, False)
            elif ph == "conv":
                conv_phase()
            elif ph == "ssd":
                ssd_phase()
            elif ph == "ffn2_0":
                ffn_phase(1, 0, False, True, False)
            elif ph == "ffn1_1":
                ffn_phase(2, 1, False, False, False)
            elif ph == "att":
                att_phase()
            elif ph == "ffn2_1":
                ffn_phase(3, 1, False, True, True)
        P.barrier()
        P.emit()
    return nc


Q_LOWER = [0, 1, 2, 3, 8, 9, 10, 11]
Q_UPPER = [4, 5, 6, 7, 12, 13, 14, 15]
HEAD_ORDER = [h for c in range(8) for h in (Q_LOWER[c], Q_UPPER[c])]


def _pk(v):
    v = np.asarray(v, np.float32)
    return np.array(v.reshape(-1, 128).T, dtype=np.float32, order='C', copy=True)


def prep_shared(inp, S=SEQ):
    f = lambda a: np.array(a, dtype=np.float32, order='C', copy=True)
    sh = {}
    sh["ffn_win"] = f(np.stack([inp["ffn1_w_in"][0], inp["ffn2_w_in"][0], inp["ffn1_w_in"][1], inp["ffn2_w_in"][1]]))
    sh["ffn_wout"] = f(np.stack([inp["ffn1_w_out"][0], inp["ffn2_w_out"][0], inp["ffn1_w_out"][1], inp["ffn2_w_out"][1]]))
    vec = np.zeros((128, 20, 8), np.float32)
    for li in range(2):
        vec[:, 0 + li] = _pk(inp["norm_ffn1"][li])
        vec[:, 2 + li] = _pk(inp["norm_mix"][li])
        vec[:, 4 + li] = _pk(inp["norm_ffn2"][li])
        vec[:, 6 + li] = _pk(inp["ple_norm"][li])
    vec[:, 8] = _pk(inp["final_norm"])
    vec[:, 9] = _pk(inp["conv_dw_b"][0])
    vec[:, 10] = _pk(inp["conv_ln_g"][0])
    vec[:, 11] = _pk(inp["conv_ln_b"][0])
    vec[:, 12] = _pk(np.repeat(np.asarray(inp["ssm_d"][0], np.float32), 64))
    vec[:, 13] = _pk(inp["ssm_norm"][0])
    vec[:, 14] = _pk(inp["att_b_o"][0])
    sh["vecs"] = vec
    sh["ple_wg"] = f(inp["ple_gate_w"])
    sh["ple_wp"] = f(inp["ple_proj_w"])
    sh["hyb_win"] = f(inp["hyb_w_in"][0])
    sh["hyb_wout"] = f(inp["hyb_w_out"][0])
    cw = np.asarray(inp["conv_dw_w"][0], np.float32)
    sh["cw"] = f(cw.T.reshape(8, 128, CW).transpose(1, 0, 2))
    scw = np.asarray(inp["ssm_conv_w"][0], np.float32)
    scb = np.asarray(inp["ssm_conv_b"][0], np.float32)
    sc = np.concatenate([scw, scb[None]], 0)
    sh["scw"] = f(sc.T.reshape(12, 128, 5).transpose(1, 0, 2))
    sinks = np.asarray(inp["att_sinks"][0], np.float32)
    sg_order = [HEAD_ORDER[((kv // 2) * 4 + i) * 2 + kv % 2] for kv in range(4) for i in range(4)]
    sh["h16"] = f(np.stack([inp["ssm_dt_bias"][0], inp["ssm_a_log"][0], sinks[sg_order]]))
    wqkv = np.asarray(inp["att_w_qkv"][0], np.float32)
    bqkv = np.asarray(inp["att_b_qkv"][0], np.float32)
    qcols = np.concatenate([np.arange(h * 64, (h + 1) * 64) for h in HEAD_ORDER])
    cols = np.concatenate([qcols, np.arange(1024, 1536)])
    sh["wqkv"] = f(wqkv[:, cols])
    bp = bqkv[cols]
    sh["bqk"] = _pk(bp[:1280])
    sh["bv"] = f(bp[1280:])
    sh["wo"] = f(np.asarray(inp["att_w_o"][0], np.float32)[qcols, :])
    inv = (np.float32(10000.0) ** (-np.arange(0, 64, 2, dtype=np.float32) / np.float32(64))).astype(np.float32)
    ang = (np.arange(S, dtype=np.float32)[:, None] * inv[None, :]).astype(np.float32)
    cos, sin = np.cos(ang).astype(np.float32), np.sin(ang).astype(np.float32)
    prt = np.arange(128)
    CC = cos.T[prt % 32]
    sgn = np.where((prt % 64) < 32, -1.0, 1.0).astype(np.float32)
    SSn = sin.T[prt % 32] * sgn[:, None]
    sh["rope"] = f(np.stack([CC, SSn]))
    cst = np.zeros((5, 128, 128), np.float32)
    cst[0] = np.eye(128)
    cst[1] = np.triu(np.ones((128, 128)))
    sw = np.zeros((128, 128), np.float32)
    for m in range(128):
        sw[(m + 32) % 64 + (m // 64) * 64, m] = 1.0
    cst[2] = sw
    sh["cst"] = cst
    q = np.arange(128)[:, None]
    sp = np.arange(256)[None, :]
    valid = np.where(sp < 128, sp > q, (sp - 128) <= q)
    m1 = np.where(valid, 0.0, NEG).astype(np.float32)
    m0 = np.where(valid & (sp >= 128), 0.0, NEG).astype(np.float32)
    sh["msk"] = f(np.stack([m0, m1]))
    return sh


_CACHE = {}


def kernel(**inputs):
    x = np.asarray(inputs["x"], np.float32)
    p = np.asarray(inputs["p"], np.float32)
    B, S, _ = x.shape
    sh = prep_shared(inputs, S)
    key = ("full", S)
    if key not in _CACHE:
        _CACHE[key] = build_program(S)
    nc = _CACHE[key]
    in_maps = []
    for b in range(B):
        m = dict(sh)
        m["x"] = np.array(x[b], dtype=np.float32, order='C', copy=True)
        m["p"] = np.array(p[:, b], dtype=np.float32, order='C', copy=True)
        in_maps.append(m)
    res = run_bass_kernel_spmd(nc, in_maps, core_ids=list(range(B)))
    return np.stack([np.asarray(r["y"], np.float32) for r in res.results], 0)
```

```python
from contextlib import ExitStack
import os
import numpy as np
import concourse.bass as bass
import concourse.mybir as mybir
from concourse.bass_utils import run_bass_kernel_spmd

F32 = mybir.dt.float32
BF16 = mybir.dt.bfloat16
AF = mybir.ActivationFunctionType
ALU = mybir.AluOpType
AX = mybir.AxisListType

D = 1024
DFF = 2816
NJ = DFF // 128
PLE = 256
SEQ = 4096
NCORES = 8
T = 256
NB = T // 128
EPS = 1e-6
CW = 31
HYB_IN = 4624
NEG = -240000.0
STG = int(os.environ.get('SSD_STAGE', '99'))
SUB = int(os.environ.get('SUB', '0'))


class Buf:
    __slots__ = ("name", "w", "r", "dsem", "excl")

    def __init__(self, name):
        self.name = name
        self.w = {}
        self.r = {}
        self.dsem = None
        self.excl = False


class Prog:
    ENGS = ("pe", "act", "dve", "pool", "sp")

    def __init__(self, nc, es):
        self.nc = nc
        self.es = es
        self.streams = {e: [] for e in self.ENGS}
        self.sems = {}
        self.cnt = {}
        self.waited = {e: {} for e in self.ENGS}
        self.nbuf = 0
        self.free_dsems = []
        self.phase_dsems = []
        self.ndsem = 0
        for e in ("pe", "act", "dve", "pool"):
            self._mksem("c_" + e)

    def _mksem(self, name):
        self.sems[name] = self.es.enter_context(self.nc.semaphore(name))
        self.cnt[name] = 0
        return name

    def _dsem(self):
        if self.free_dsems:
            s = self.free_dsems.pop()
        else:
            self.ndsem += 1
            s = self._mksem(f"d{self.ndsem}")
        self.phase_dsems.append(s)
        return s

    def buf(self, name=None):
        self.nbuf += 1
        return Buf(name or f"b{self.nbuf}")

    def bufs(self, n, name="b"):
        return [self.buf(f"{name}{i}") for i in range(n)]

    def _need(self, reads, writes):
        need = {}
        for b in reads:
            for s, v in b.w.items():
                if need.get(s, 0) < v:
                    need[s] = v
        for b in writes:
            for d in (b.w, b.r):
                for s, v in d.items():
                    if need.get(s, 0) < v:
                        need[s] = v
        return need

    def _emit_waits(self, eng, need, skip_own=False):
        wd = self.waited[eng]
        own = "c_" + eng
        for s, v in need.items():
            if skip_own and s == own:
                continue
            if wd.get(s, 0) < v:
                wd[s] = v
                h = self.sems[s]
                self.streams[eng].append(lambda e, h=h, v=v: e.wait_ge(h, v))

    def _mark(self, reads, writes, s, v):
        for b in reads:
            if b.r.get(s, 0) < v:
                b.r[s] = v
        for b in writes:
            b.w = {s: v}
            b.r = {}

    def op(self, eng, fns, reads=(), writes=(), skip_own=None):
        if callable(fns):
            fns = [fns]
        if skip_own is None:
            skip_own = (eng == "pe")
        ex = [b for b in reads if b.excl]
        if ex:
            writes = list(writes) + ex
            reads = [b for b in reads if not b.excl]
        self._emit_waits(eng, self._need(reads, writes), skip_own)
        s = "c_" + eng
        self.cnt[s] += 1
        v = self.cnt[s]
        h = self.sems[s]
        st = self.streams[eng]
        for f in fns[:-1]:
            st.append(f)
        last = fns[-1]
        st.append(lambda e, last=last, h=h: last(e).then_inc(h, 1))
        self._mark(reads, writes, s, v)

    def dma(self, q, out, in_, reads=(), writes=(), slot=None):
        self._emit_waits(q, self._need(reads, writes), False)
        if slot.dsem is None:
            slot.dsem = self._dsem()
        s = slot.dsem
        self.cnt[s] += 16
        v = self.cnt[s]
        h = self.sems[s]
        self.streams[q].append(lambda e, out=out, in_=in_, h=h: e.dma_start(out=out, in_=in_).then_inc(h, 16))
        self._mark(reads, writes, s, v)

    def barrier(self):
        need = {s: v for s, v in self.cnt.items() if v > 0}
        for e in self.ENGS:
            self._emit_waits(e, need, False)

    def end_phase(self):
        self.barrier()
        self.free_dsems.extend(self.phase_dsems)
        self.phase_dsems = []

    def emit(self):
        nc = self.nc
        st = self.streams
        with nc.Block() as block:
            @block.tensor
            def _(e):
                for f in st["pe"]:
                    f(e)

            @block.scalar
            def _(e):
                for f in st["act"]:
                    f(e)

            @block.vector
            def _(e):
                for f in st["dve"]:
                    f(e)

            @block.gpsimd
            def _(e):
                for f in st["pool"]:
                    f(e)

            @block.sync
            def _(e):
                for f in st["sp"]:
                    f(e)


class Ctx:
    pass


def sl(i, n=128):
    return slice(i * n, (i + 1) * n)


def build_program(S=SEQ, phases=("ffn1_0", "conv", "ssd", "ffn2_0", "ffn1_1", "att", "ffn2_1"), debug=False):
    nc = bass.Bass("TRN2", target_bir_lowering=False)
    nt = S // T
    dt_in = {}

    def din(name, shape):
        dt_in[name] = nc.dram_tensor(name, list(shape), F32, kind="ExternalInput").ap()
        return dt_in[name]

    x_d = din("x", [S, D])
    p_d = din("p", [2, S, PLE])
    ffn_win = din("ffn_win", [4, D, 2 * DFF])
    ffn_wout = din("ffn_wout", [4, DFF, D])
    vecs = din("vecs", [128, 20, 8])
    ple_wg = din("ple_wg", [2, D, D])
    ple_wp = din("ple_wp", [2, PLE, D])
    hyb_win = din("hyb_win", [D, HYB_IN])
    hyb_wout = din("hyb_wout", [2 * D, D])
    cw_d = din("cw", [128, 8, CW])
    scw_d = din("scw", [128, 12, 5])
    h16_d = din("h16", [3, 16])
    wqkv_d = din("wqkv", [D, 1536])
    bqk_d = din("bqk", [128, 10])
    bv_d = din("bv", [256])
    wo_d = din("wo", [D, D])
    rope_d = din("rope", [2, 128, S])
    cst_d = din("cst", [5, 128, 128])
    msk_d = din("msk", [2, 128, 256])
    if debug:
        out_d = nc.dram_tensor("H", [8, 128, S], F32, kind="ExternalOutput").ap()
        Hd = out_d
        yout_d = nc.dram_tensor("y", [S, D], F32, kind="ExternalOutput").ap()
    else:
        yout_d = nc.dram_tensor("y", [S, D], F32, kind="ExternalOutput").ap()
        Hd = nc.dram_tensor("Hs", [8, 128, S], F32).ap()
    Ud = nc.dram_tensor("Us", [8, 128, S], BF16).ap()
    Hv = Hd.rearrange("k p s -> p k s")
    Uv = Ud.rearrange("k p s -> p k s")

    with ExitStack() as es:
        P = Prog(nc, es)
        C = Ctx()
        gsb = lambda name, shape, dt: es.enter_context(nc.sbuf_tensor(name, shape, dt))
        pb = [es.enter_context(nc.psum_tensor(f"pb{i}", [128, 512], F32)) for i in range(8)]
        Bpb = P.bufs(8, "pb")
        for b_ in Bpb:
            b_.excl = True
        identf = gsb("identf", [128, 128], F32)
        identb = gsb("identb", [128, 128], BF16)
        triu = gsb("triu", [128, 128], F32)
        pswap = gsb("pswap", [128, 128], BF16)
        pswapf = gsb("pswapf", [128, 128], F32)
        ones_f = gsb("ones_f", [128, 128], F32)
        ones1k = gsb("ones1k", [128, 128], BF16)
        ones512 = gsb("ones512", [128, 128], BF16)
        cols = gsb("cols", [128, 4], F32)
        vec = gsb("vec", [128, 20, 8], F32)
        Bc = P.buf("consts")
        P.dma("sp", identf[:], cst_d[0], writes=[Bc], slot=Bc)
        P.dma("sp", triu[:], cst_d[1], writes=[Bc], slot=Bc)
        P.dma("sp", pswapf[:], cst_d[2], writes=[Bc], slot=Bc)
        P.dma("sp", vec[:], vecs, writes=[Bc], slot=Bc)
        P.dma("pool", identb[:], cst_d[0], writes=[Bc], slot=Bc)
        P.dma("pool", pswap[:], cst_d[2], writes=[Bc], slot=Bc)
        P.op("dve", lambda e: e.memset(ones_f[:], 1.0), writes=[Bc])
        P.op("dve", lambda e: e.memset(ones1k[:], 1.0 / 1024), writes=[Bc])
        P.op("dve", lambda e: e.memset(ones512[:], 1.0 / 512), writes=[Bc])
        P.op("dve", lambda e: e.memset(cols[:, 0:1], EPS), writes=[Bc])
        P.op("dve", lambda e: e.memset(cols[:, 1:2], 1.0), writes=[Bc])
        P.op("dve", lambda e: e.memset(cols[:, 2:3], 0.0), writes=[Bc])
        epsc = cols[:, 0:1]
        onec = cols[:, 1:2]
        B_H = P.bufs(nt, "H")
        B_U = P.bufs(nt, "U")
        B_Y = P.bufs(nt, "Y")
        P.end_phase()

        V_NF1, V_NMIX, V_NF2, V_PLE = 0, 2, 4, 6
        V_FIN, V_CVB, V_LNG, V_LNB, V_SSD, V_SSN, V_BO = 8, 9, 10, 11, 12, 13, 14

        def rms_rstd(xap, Bx, nk, ones_ap, sqt, Bsq, pbank, Bpbank, rstd, Brstd, k0=0):
            for k in range(nk):
                q = k % 2
                P.op("act", lambda e, k=k, q=q: e.activation(sqt[:, q, :], xap[:, k0 + k, :], AF.Square),
                     reads=[Bx], writes=[Bsq[q]])
                P.op("pe", lambda e, k=k, q=q: e.matmul(pbank[:, 0:T], ones_ap[:], sqt[:, q, :],
                                                       start=(k == 0), stop=(k == nk - 1)),
                     reads=[Bsq[q], Bc], writes=[Bpbank])
            P.op("act", lambda e: e.activation(rstd, pbank[:, 0:T], AF.Sqrt, bias=epsc, scale=1.0),
                 reads=[Bpbank, Bc], writes=[Brstd])
            P.op("dve", lambda e: e.reciprocal(rstd, rstd), reads=[Brstd], writes=[Brstd])

        def make_xn(ht_s, Bht, gidx, xn, Bxn, sqt, Bsq, rstd, Brstd, pbank, Bpbank):
            rms_rstd(ht_s, Bht, 8, ones1k, sqt, Bsq, pbank, Bpbank, rstd[:], Brstd)
            for k in range(8):
                P.op("dve", lambda e, k=k: e.scalar_tensor_tensor(
                    xn[:, k, :], ht_s[:, k, :], vec[:, gidx, k:k + 1], rstd[:], ALU.mult, ALU.mult),
                    reads=[Bht, Bc, Brstd], writes=[Bxn])

        ucnt = [0]

        def uniq(name):
            ucnt[0] += 1
            return f"s{ucnt[0]}_{name}"

        def load_w(dst, src_rows_ap, Bw, q="pool"):
            P.dma(q, dst, src_rows_ap, writes=[Bw], slot=Bw)

        def ffn_phase(fi, li, first, do_ple, final):
            with ExitStack() as pes:
                sb = lambda name, shape, dt: pes.enter_context(nc.sbuf_tensor(uniq(name), shape, dt))
                win = sb("win", [128, 8, 2 * DFF], BF16)
                wout = sb("wout", [128, NJ, D], BF16)
                ht = [sb(f"ht{i}", [128, 8, T], F32) for i in range(2)]
                xn = [sb(f"xn{i}", [128, 8, T], BF16) for i in range(2)]
                sqt = sb("sqt", [128, 2, T], BF16)
                rstd = [sb(f"rstd{i}", [128, T], F32) for i in range(2)]
                sg = [sb(f"sg{i}", [128, T], F32) for i in range(2)]
                hT = sb("hT", [128, NJ, T], BF16)
                B_win = P.bufs(1, "win")
                B_wout = P.bufs(2, "wout")
                B_ht = P.bufs(2, "ht")
                B_xn = P.bufs(2, "xn")
                B_sq = P.bufs(2, "sq")
                B_rstd = P.bufs(2, "rstd")
                B_sg = P.bufs(2, "sg")
                B_hT = P.buf("hT")
                if first:
                    xt = [sb("xt0", [128, D], F32)]
                    B_xt = P.bufs(1, "xt")
                if final:
                    yt = [sb(f"yt{i}", [128, 512], F32) for i in range(2)]
                    B_yt = P.bufs(2, "yt")
                if do_ple:
                    wg = sb("wg", [128, 8, D], BF16)
                    wp = sb("wp", [128, 2, D], BF16)
                    pt = [sb(f"pt{i}", [128, PLE], F32) for i in range(2)]
                    pT = sb("pT", [128, 2, T], BF16)
                    sg2 = [sb(f"sgp{i}", [128, T], F32) for i in range(2)]
                    B_wg = P.buf("wg")
                    B_wp = P.buf("wp")
                    B_pt = P.bufs(2, "pt")
                    B_pT = P.buf("pT")
                    B_sg2 = P.bufs(2, "sgp")
                for k in range(8):
                    load_w(win[:, k, :], ffn_win[fi, sl(k), :], B_win[0])
                wo_v = ffn_wout[fi].rearrange("(j p) m -> p j m", p=128)
                for hh in range(2):
                    load_w(wout[:, hh * 11:(hh + 1) * 11, :], wo_v[:, hh * 11:(hh + 1) * 11, :], B_wout[hh])
                if do_ple:
                    load_w(wg[:], ple_wg[li].rearrange("(k p) m -> p k m", p=128), B_wg)
                    load_w(wp[:], ple_wp[li].rearrange("(k p) m -> p k m", p=128), B_wp)
                gidx = (V_NF1 if not do_ple else V_NF2) + li

                def load_tile(it):
                    s = it % 2
                    t0 = it * T
                    hs = ht[s]
                    if first:
                        for b in range(NB):
                            P.dma("sp", xt[0][:], x_d[t0 + b * 128:t0 + (b + 1) * 128, :],
                                  writes=[B_xt[0]], slot=B_xt[0])
                            for hf in range(2):
                                P.op("pe", [lambda e, kk=kk, hf=hf: e.transpose(
                                    pb[7][:, sl(kk)], xt[0][:, sl(hf * 4 + kk)], identf[:]) for kk in range(4)],
                                    reads=[B_xt[0], Bc], writes=[Bpb[7]])
                                P.op("act", lambda e, hf=hf, b=b, hs=hs: e.copy(
                                    hs[:, hf * 4:(hf + 1) * 4, sl(b)], pb[7][:].rearrange("p (k t) -> p k t", k=4)),
                                    reads=[Bpb[7]], writes=[B_ht[s]])
                    else:
                        P.dma("sp", hs[:], Hv[:, :, t0:t0 + T], reads=[B_H[it]], writes=[B_ht[s]], slot=B_ht[s])

                def prologue(it):
                    s = it % 2
                    make_xn(ht[s], B_ht[s], gidx, xn[s], B_xn[s], sqt, B_sq, rstd[s], B_rstd[s], pb[7], Bpb[7])

                def sq_stat(hs_, s_, m_):
                    q_ = m_ % 2
                    P.op("act", lambda e: e.activation(sqt[:, q_, :], hs_[:, m_, :], AF.Square),
                         reads=[B_ht[s_]], writes=[B_sq[q_]])

                def pe_stat(m_):
                    q_ = m_ % 2
                    P.op("pe", lambda e: e.matmul(pb[6][:, 0:T], ones1k[:], sqt[:, q_, :],
                                                  start=(m_ == 0), stop=(m_ == 7)),
                         reads=[B_sq[q_], Bc], writes=[Bpb[6]])

                def finish_rstd(s_):
                    rs_ = rstd[s_]
                    P.op("act", lambda e: e.activation(rs_[:], pb[6][:, 0:T], AF.Sqrt, bias=epsc, scale=1.0),
                         reads=[Bpb[6], Bc], writes=[B_rstd[s_]])
                    P.op("dve", lambda e: e.reciprocal(rs_[:], rs_[:]), reads=[B_rstd[s_]], writes=[B_rstd[s_]])

                def p_dma(it):
                    t0 = it * T
                    for b in range(NB):
                        P.dma("sp", pt[b][:], p_d[li, t0 + b * 128:t0 + (b + 1) * 128, :],
                              writes=[B_pt[b]], slot=B_pt[b])

                def post_slots(it):
                    s = it % 2
                    t0 = it * T
                    hs = ht[s]
                    xs = xn[s]
                    sl_ = {}

                    def add(j, f):
                        sl_.setdefault(j, []).append(f)

                    def prep():
                        for b in range(NB):
                            P.op("pe", [lambda e, c=c, b=b: e.transpose(
                                pb[7][:, sl(c)], pt[b][:, sl(c)], identf[:]) for c in range(2)],
                                reads=[B_pt[b], Bc], writes=[Bpb[7]])
                            P.op("act", lambda e, b=b: e.copy(
                                pT[:, :, sl(b)], pb[7][:, 0:256].rearrange("p (k t) -> p k t", k=2)),
                                reads=[Bpb[7]], writes=[B_pT])
                        finish_rstd(s)
                        for k in range(8):
                            P.op("dve", lambda e, k=k: e.scalar_tensor_tensor(
                                xs[:, k, :], hs[:, k, :], vec[:, V_PLE + li, k:k + 1], rstd[s][:], ALU.mult, ALU.mult),
                                reads=[B_ht[s], Bc, B_rstd[s]], writes=[B_xn[s]])
                    add(0, prep)

                    def ple_group(m):
                        mb = m % 2
                        P.op("pe", [lambda e, k=k: e.matmul(
                            pb[4][:, 0:T], wg[:, k, sl(m)], xs[:, k, :], start=(k == 0), stop=(k == 7))
                            for k in range(8)], reads=[B_xn[s], B_wg], writes=[Bpb[4]])
                        P.op("pe", [lambda e, c=c: e.matmul(
                            pb[5][:, 0:T], wp[:, c, sl(m)], pT[:, c, :], start=(c == 0), stop=(c == 1))
                            for c in range(2)], reads=[B_pT, B_wp], writes=[Bpb[5]])
                        P.op("act", lambda e: e.activation(sg2[mb][:], pb[4][:, 0:T], AF.Tanh, scale=0.5),
                             reads=[Bpb[4]], writes=[B_sg2[mb]])
                        P.op("dve", lambda e: e.scalar_tensor_tensor(
                            sg2[mb][:], sg2[mb][:], 1.0, pb[5][:, 0:T], ALU.add, ALU.mult),
                            reads=[B_sg2[mb], Bpb[5]], writes=[B_sg2[mb]])
                        P.op("dve", lambda e: e.scalar_tensor_tensor(
                            hs[:, m, :], sg2[mb][:], 0.5, hs[:, m, :], ALU.mult, ALU.add),
                            reads=[B_sg2[mb], B_ht[s]], writes=[B_ht[s]])
                        if final:
                            sq_stat(hs, s, m)
                            if m > 0:
                                pe_stat(m - 1)
                    for m in range(8):
                        add(1 + m, lambda m=m: ple_group(m))

                    def fin_norm():
                        pe_stat(7)
                        finish_rstd(s)
                        for k in range(8):
                            P.op("dve", lambda e, k=k: e.scalar_tensor_tensor(
                                hs[:, k, :], hs[:, k, :], vec[:, V_FIN, k:k + 1], rstd[s][:], ALU.mult, ALU.mult),
                                reads=[B_ht[s], Bc, B_rstd[s]], writes=[B_ht[s]])

                    def out_block(b, hf):
                        P.op("pe", [lambda e, kk=kk: e.transpose(
                            pb[7][:, sl(kk)], hs[:, hf * 4 + kk, sl(b)], identf[:]) for kk in range(4)],
                            reads=[B_ht[s], Bc], writes=[Bpb[7]])
                        P.op("act", lambda e: e.copy(yt[hf][:], pb[7][:]), reads=[Bpb[7]], writes=[B_yt[hf]])
                        P.dma("sp", yout_d[t0 + b * 128:t0 + (b + 1) * 128, hf * 512:(hf + 1) * 512], yt[hf][:],
                              reads=[B_yt[hf]], writes=[B_Y[it]], slot=B_yt[hf])
                    if final:
                        add(9, fin_norm)
                        jj = 10
                        for b in range(NB):
                            for hf in range(2):
                                add(jj, lambda b=b, hf=hf: out_block(b, hf))
                                jj += 1
                    if not final or debug:
                        add(14, lambda: P.dma("sp", Hv[:, :, t0:t0 + T], hs[:], reads=[B_ht[s]],
                                              writes=[B_H[it]], slot=B_ht[s]))
                    return sl_

                load_tile(0)
                prologue(0)
                for it in range(nt):
                    s = it % 2
                    t0 = it * T
                    hs = ht[s]
                    xs = xn[s]
                    slots = post_slots(it - 1) if (do_ple and it > 0) else {}
                    if it + 1 < nt and not first and not do_ple:
                        load_tile(it + 1)
                    for j in range(NJ):
                        jb = j % 2
                        for f_ in slots.get(j, []):
                            f_()
                        if it + 1 < nt:
                            if first and j == 6:
                                load_tile(it + 1)
                            if do_ple and j == 15:
                                load_tile(it + 1)
                            if j == (18 if do_ple else 12):
                                prologue(it + 1)
                        P.op("pe", [lambda e, k=k, j=j, jb=jb, xs=xs: e.matmul(
                            pb[jb][:, 0:T], win[:, k, sl(j)], xs[:, k, :], start=(k == 0), stop=(k == 7))
                            for k in range(8)], reads=[B_xn[s], B_win[0]], writes=[Bpb[jb]])
                        P.op("pe", [lambda e, k=k, j=j, jb=jb, xs=xs: e.matmul(
                            pb[2 + jb][:, 0:T], win[:, k, DFF + j * 128:DFF + (j + 1) * 128], xs[:, k, :],
                            start=(k == 0), stop=(k == 7))
                            for k in range(8)], reads=[B_xn[s], B_win[0]], writes=[Bpb[2 + jb]])
                        P.op("act", lambda e, jb=jb: e.activation(sg[jb][:], pb[jb][:, 0:T], AF.Silu),
                             reads=[Bpb[jb]], writes=[B_sg[jb]])
                        P.op("dve", lambda e, j=j, jb=jb: e.tensor_tensor(
                            hT[:, j, :], sg[jb][:], pb[2 + jb][:, 0:T], ALU.mult),
                            reads=[B_sg[jb], Bpb[2 + jb]], writes=[B_hT])
                    if do_ple:
                        p_dma(it)
                    for m in range(8):
                        mb = 4 + m % 2
                        P.op("pe", [lambda e, j=j, m=m, mb=mb: e.matmul(
                            pb[mb][:, 0:T], wout[:, j, sl(m)], hT[:, j, :], start=(j == 0), stop=(j == NJ - 1))
                            for j in range(NJ)], reads=[B_hT] + B_wout, writes=[Bpb[mb]])
                        P.op("dve", lambda e, m=m, mb=mb, hs=hs: e.scalar_tensor_tensor(
                            hs[:, m, :], pb[mb][:, 0:T], 0.5, hs[:, m, :], ALU.mult, ALU.add),
                            reads=[Bpb[mb], B_ht[s]], writes=[B_ht[s]])
                        if do_ple:
                            sq_stat(hs, s, m)
                            if m > 0:
                                pe_stat(m - 1)
                    if do_ple:
                        pe_stat(7)
                    else:
                        P.dma("sp", Hv[:, :, t0:t0 + T], hs[:], reads=[B_ht[s]], writes=[B_H[it]], slot=B_ht[s])
                if do_ple:
                    last = post_slots(nt - 1)
                    for j in sorted(last):
                        for f_ in last[j]:
                            f_()
                P.end_phase()

        def conv_phase():
            with ExitStack() as pes:
                sb = lambda name, shape, dt: pes.enter_context(nc.sbuf_tensor(uniq(name), shape, dt))
                wcv = sb("wcv", [128, 8, 2048], BF16)
                dg = sb("dg", [128, 8 * CW, 128], BF16)
                cw = sb("cw", [128, 8, CW], F32)
                ht = [sb(f"ht{i}", [128, 8, T], F32) for i in range(2)]
                xn2 = [sb(f"xn{i}", [128, 8, T], BF16) for i in range(2)]
                sqt = sb("sqt", [128, 2, T], BF16)
                rstd2 = [sb(f"rstd{i}", [128, T], F32) for i in range(2)]
                sg = [sb(f"sg{i}", [128, T], F32) for i in range(2)]
                u0 = sb("u0", [128, 8, 30 + T], BF16)
                cv = sb("cv", [128, 8, T], F32)
                cvb = sb("cvb", [128, 2, T], BF16)
                mean = sb("mean", [128, T], F32)
                lrs = sb("lrs", [128, T], F32)
                tmp = [sb(f"tmp{i}", [128, T], F32) for i in range(2)]
                ub = [sb(f"ub{i}", [128, 8, T], BF16) for i in range(2)]
                B_w = P.bufs(8, "wcv")
                B_dg = P.buf("dg")
                B_cw = P.buf("cw")
                B_ht = P.bufs(2, "ht")
                B_xn2 = P.bufs(2, "xn")
                B_sq = P.bufs(2, "sq")
                B_rstd2 = P.bufs(2, "rstd")
                B_sg = P.bufs(2, "sg")
                B_u0 = P.buf("u0")
                B_cv = P.buf("cv")
                B_cvb = P.bufs(2, "cvb")
                B_mean = P.buf("mean")
                B_lrs = P.buf("lrs")
                B_tmp = P.bufs(2, "tmp")
                B_ub = P.bufs(2, "ub")
                for k in range(8):
                    load_w(wcv[:, k, :], hyb_win[sl(k), 0:2048], B_w[k])
                P.dma("sp", cw[:], cw_d, writes=[B_cw], slot=B_cw)
                for c in range(8):
                    P.op("pool", lambda e, c=c: e.tensor_tensor(
                        dg[:, c * CW:(c + 1) * CW, :],
                        identb[:].unsqueeze(1).broadcast_to([128, CW, 128]),
                        cw[:, c, :].unsqueeze(2).broadcast_to([128, CW, 128]), ALU.mult),
                        reads=[B_cw, Bc], writes=[B_dg])
                P.op("dve", lambda e: e.memset(u0[:, :, 0:30], 0.0), writes=[B_u0])
                def prologue(it_):
                    s_ = it_ % 2
                    P.dma("sp", ht[s_][:], Hv[:, :, it_ * T:(it_ + 1) * T], reads=[B_H[it_]], writes=[B_ht[s_]], slot=B_ht[s_])
                    make_xn(ht[s_], B_ht[s_], V_NMIX + 0, xn2[s_], B_xn2[s_], sqt, B_sq, rstd2[s_], B_rstd2[s_], pb[6], Bpb[6])

                prologue(0)
                for it in range(nt):
                    s = it % 2
                    t0 = it * T
                    hs = ht[s]
                    xn = xn2[s]
                    B_xn = B_xn2[s]
                    if it > 0:
                        P.op("dve", lambda e: e.tensor_copy(u0[:, :, 0:30], u0[:, :, T:T + 30]),
                             reads=[B_u0], writes=[B_u0])
                    for c in range(8):
                        jb = c % 2
                        P.op("pe", [lambda e, k=k, c=c, jb=jb, xn=xn: e.matmul(
                            pb[jb][:, 0:T], wcv[:, k, sl(c)], xn[:, k, :], start=(k == 0), stop=(k == 7))
                            for k in range(8)], reads=[B_xn] + B_w, writes=[Bpb[jb]])
                        P.op("pe", [lambda e, k=k, c=c, jb=jb, xn=xn: e.matmul(
                            pb[2 + jb][:, 0:T], wcv[:, k, 1024 + c * 128:1024 + (c + 1) * 128], xn[:, k, :],
                            start=(k == 0), stop=(k == 7))
                            for k in range(8)], reads=[B_xn] + B_w, writes=[Bpb[2 + jb]])
                        P.op("act", lambda e, jb=jb: e.activation(sg[jb][:], pb[2 + jb][:, 0:T], AF.Tanh, scale=0.5),
                             reads=[Bpb[2 + jb]], writes=[B_sg[jb]])
                        P.op("dve", lambda e, c=c, jb=jb: e.scalar_tensor_tensor(
                            u0[:, c, 30:30 + T], sg[jb][:], 1.0, pb[jb][:, 0:T], ALU.add, ALU.mult),
                            reads=[B_sg[jb], Bpb[jb]], writes=[B_u0])
                    if it + 1 < nt:
                        prologue(it + 1)
                    for c in range(8):
                        mb = 4 + c % 2
                        q = c % 2
                        P.op("pe", [lambda e, k=k, c=c, mb=mb: e.matmul(
                            pb[mb][:, 0:T], dg[:, c * CW + k, :], u0[:, c, k:k + T],
                            start=(k == 0), stop=(k == CW - 1))
                            for k in range(CW)], reads=[B_u0, B_dg], writes=[Bpb[mb]])
                        P.op("act", lambda e, c=c, mb=mb: e.activation(
                            cv[:, c, :], pb[mb][:, 0:T], AF.Identity, bias=vec[:, V_CVB, c:c + 1], scale=0.5),
                            reads=[Bpb[mb], Bc], writes=[B_cv])
                        P.op("dve", lambda e, c=c, q=q: e.tensor_copy(cvb[:, q, :], cv[:, c, :]),
                             reads=[B_cv], writes=[B_cvb[q]])
                        P.op("pe", lambda e, c=c, q=q: e.matmul(
                            pb[6][:, 0:T], ones1k[:], cvb[:, q, :], start=(c == 0), stop=(c == 7)),
                            reads=[B_cvb[q], Bc], writes=[Bpb[6]])
                    P.op("act", lambda e: e.copy(mean[:], pb[6][:, 0:T]), reads=[Bpb[6]], writes=[B_mean])
                    for c in range(8):
                        q = c % 2
                        P.op("dve", lambda e, c=c, q=q: e.tensor_tensor(tmp[q][:], cv[:, c, :], mean[:], ALU.subtract),
                             reads=[B_cv, B_mean], writes=[B_tmp[q]])
                        P.op("act", lambda e, q=q: e.activation(sqt[:, q, :], tmp[q][:], AF.Square),
                             reads=[B_tmp[q]], writes=[B_sq[q]])
                        P.op("pe", lambda e, c=c, q=q: e.matmul(
                            pb[7][:, 0:T], ones1k[:], sqt[:, q, :], start=(c == 0), stop=(c == 7)),
                            reads=[B_sq[q], Bc], writes=[Bpb[7]])
                    P.op("act", lambda e: e.activation(lrs[:], pb[7][:, 0:T], AF.Sqrt, bias=epsc, scale=1.0),
                         reads=[Bpb[7], Bc], writes=[B_lrs])
                    P.op("dve", lambda e: e.reciprocal(lrs[:], lrs[:]), reads=[B_lrs], writes=[B_lrs])
                    for c in range(8):
                        q = c % 2
                        P.op("dve", lambda e, c=c, q=q: e.tensor_tensor(tmp[q][:], cv[:, c, :], mean[:], ALU.subtract),
                             reads=[B_cv, B_mean], writes=[B_tmp[q]])
                        P.op("dve", lambda e, q=q: e.tensor_tensor(tmp[q][:], tmp[q][:], lrs[:], ALU.mult),
                             reads=[B_lrs, B_tmp[q]], writes=[B_tmp[q]])
                        P.op("act", lambda e, c=c, q=q, s=s: e.activation(
                            ub[s][:, c, :], tmp[q][:], AF.Silu, bias=vec[:, V_LNB, c:c + 1],
                            scale=vec[:, V_LNG, c:c + 1]),
                            reads=[B_tmp[q], Bc], writes=[B_ub[s]])
                    P.dma("sp", Uv[:, :, t0:t0 + T], ub[s][:], reads=[B_ub[s]], writes=[B_U[it]], slot=B_ub[s])
                P.end_phase()

        def ssd_phase():
            with ExitStack() as pes:
                sb = lambda name, shape, dt: pes.enter_context(nc.sbuf_tensor(uniq(name), shape, dt))
                NZ = HYB_IN - 2048
                wz = sb("wz", [128, 8, NZ], BF16)
                wo = sb("wo", [128, 16, D], BF16)
                scw = sb("scw", [128, 12, 5], F32)
                h16 = sb("h16", [128, 3, 16], F32)
                abc = sb("abc", [128, 16], F32)
                ht = [sb(f"ht{i}", [128, 8, T], F32) for i in range(2)]
                xn2 = [sb(f"xn{i}", [128, 8, T], BF16) for i in range(2)]
                sqt = sb("sqt", [128, 2, T], BF16)
                rstd2 = [sb(f"rstd{i}", [128, T], F32) for i in range(2)]
                sz = sb("sz", [128, 8, T], F32)
                xb = sb("xb", [128, 12, 3 + T], BF16)
                dg4 = sb("dg4", [128, 48, 128], BF16)
                xsf = sb("xsf", [128, 8, T], F32)
                xsb = sb("xsb", [128, 8, T], BF16)
                bcb = sb("bcb", [128, 4, T], BF16)
                dtt = sb("dtt", [128, 16], F32)
                adt = sb("adt", [128, 16], F32)
                acs = sb("acs", [128, 16], F32)
                ala = sb("ala", [128, 16], F32)
                cdec = sb("cdec", [128, 16], F32)
                coef = sb("coef", [128, 16], F32)
                xdt = sb("xdt", [128, 16, 64], BF16)
                xdd = sb("xdd", [128, 16, 64], BF16)
                btm = sb("btm", [128, 2, 128], BF16)
                R2 = [sb(f"R{i}", [128, 8, 128], F32) for i in range(2)]
                dif = sb("dif", [128, 8, 128], F32)
                erow = sb("erow", [128, 8, 128], F32)
                MT = sb("MT", [128, 8, 128], BF16)
                Cs = sb("Cs", [128, 8, 128], BF16)
                cbm = sb("cbm", [128, 2, 128], F32)
                prev = sb("prev", [128, 16, 64], F32)
                prevb = sb("prevb", [128, 16, 64], BF16)
                yg = sb("yg", [128, 8, T], F32)
                yn = sb("yn", [128, 8, T], BF16)
                grs = [sb(f"grs{i}", [128, T], F32) for i in range(2)]
                ut = sb("ut", [128, 8, T], BF16)
                B_wz = P.bufs(8, "wz")
                B_wo = P.bufs(2, "wo")
                B_sm = P.buf("small")
                B_ht = P.bufs(2, "ht")
                B_xn2 = P.bufs(2, "xn")
                B_sq = P.bufs(2, "sq")
                B_rstd2 = P.bufs(2, "rstd")
                B_sz = P.buf("sz")
                B_xb = P.buf("xb")
                B_dg4 = P.buf("dg4")
                B_xsf = P.bufs(8, "xsf")
                B_xsb = P.bufs(8, "xsb")
                B_bc = P.buf("bcb")
                B_dt = P.buf("dt")
                B_co = P.buf("coefs")
                B_xdt = P.buf("xdt")
                B_btm = P.buf("btm")
                B_R2 = P.bufs(2, "R")
                B_dif = P.buf("dif")
                B_er = P.buf("erow")
                B_MT = P.buf("MT")
                B_Cs = P.buf("Cs")
                B_cbm = P.buf("cbm")
                B_prev = P.buf("prev")
                B_prevb = P.buf("prevb")
                B_yg = P.buf("yg")
                B_yn = P.buf("yn")
                B_grs = P.bufs(2, "grs")
                B_ut = P.buf("ut")
                for k in range(8):
                    load_w(wz[:, k, :], hyb_win[sl(k), 2048:HYB_IN], B_wz[k])
                wo_v = hyb_wout.rearrange("(j p) m -> p j m", p=128)
                for hh in range(2):
                    load_w(wo[:, hh * 8:(hh + 1) * 8, :], wo_v[:, hh * 8:(hh + 1) * 8, :], B_wo[hh])
                P.dma("sp", scw[:], scw_d, writes=[B_sm], slot=B_sm)
                for i in range(3):
                    P.dma("sp", h16[:, i, :], h16_d[i].partition_broadcast(128), writes=[B_sm], slot=B_sm)
                P.op("act", lambda e: e.activation(abc[:], h16[:, 1, :], AF.Exp), reads=[B_sm], writes=[B_sm])
                P.op("dve", lambda e: e.tensor_scalar_mul(abc[:], abc[:], -1.0), reads=[B_sm], writes=[B_sm])
                for c in range(12):
                    P.op("pool", lambda e, c=c: e.tensor_tensor(
                        dg4[:, c * 4:(c + 1) * 4, :],
                        identb[:].unsqueeze(1).broadcast_to([128, 4, 128]),
                        scw[:, c, 0:4].unsqueeze(2).broadcast_to([128, 4, 128]), ALU.mult),
                        reads=[B_sm, Bc], writes=[B_dg4])
                P.op("dve", lambda e: e.memset(xb[:, :, 0:3], 0.0), writes=[B_xb])
                P.op("dve", lambda e: e.memset(prev[:], 0.0), writes=[B_prev])
                P.op("dve", lambda e: e.memset(prevb[:], 0.0), writes=[B_prevb])
                def prologue(it_):
                    s_ = it_ % 2
                    P.dma("sp", ht[s_][:], Hv[:, :, it_ * T:(it_ + 1) * T], reads=[B_H[it_]], writes=[B_ht[s_]], slot=B_ht[s_])
                    make_xn(ht[s_], B_ht[s_], V_NMIX + 0, xn2[s_], B_xn2[s_], sqt, B_sq, rstd2[s_], B_rstd2[s_], pb[6], Bpb[6])

                prologue(0)
                for it in range(nt):
                    s = it % 2
                    t0 = it * T
                    hs = ht[s]
                    xn = xn2[s]
                    B_xn = B_xn2[s]
                    P.dma("sp", ut[:], Uv[:, :, t0:t0 + T], reads=[B_U[it]], writes=[B_ut], slot=B_ut)
                    if it > 0:
                        P.op("dve", lambda e: e.tensor_copy(xb[:, :, 0:3], xb[:, :, T:T + 3]),
                             reads=[B_xb], writes=[B_xb])
                    for c in range(8):
                        jb = c % 2
                        P.op("pe", [lambda e, k=k, c=c, jb=jb, xn=xn: e.matmul(
                            pb[jb][:, 0:T], wz[:, k, sl(c)], xn[:, k, :], start=(k == 0), stop=(k == 7))
                            for k in range(8)], reads=[B_xn] + B_wz, writes=[Bpb[jb]])
                        P.op("act", lambda e, c=c, jb=jb: e.activation(sz[:, c, :], pb[jb][:, 0:T], AF.Silu),
                             reads=[Bpb[jb]], writes=[B_sz])
                    for c in range(12):
                        jb = c % 2
                        P.op("pe", [lambda e, k=k, c=c, jb=jb, xn=xn: e.matmul(
                            pb[jb][:, 0:T], wz[:, k, 1024 + c * 128:1024 + (c + 1) * 128], xn[:, k, :],
                            start=(k == 0), stop=(k == 7))
                            for k in range(8)], reads=[B_xn] + B_wz, writes=[Bpb[jb]])
                        P.op("act", lambda e, c=c, jb=jb: e.copy(xb[:, c, 3:3 + T], pb[jb][:, 0:T]),
                             reads=[Bpb[jb]], writes=[B_xb])
                    for c in range(12):
                        mb = 2 + c % 2
                        P.op("pe", [lambda e, c=c, k=k, mb=mb: e.matmul(
                            pb[mb][:, 0:T], dg4[:, c * 4 + k, :], xb[:, c, k:k + T], start=(k == 0), stop=(k == 3))
                            for k in range(4)], reads=[B_xb, B_dg4], writes=[Bpb[mb]])
                        if c < 8:
                            P.op("act", lambda e, c=c, mb=mb: e.activation(
                                xsf[:, c, :], pb[mb][:, 0:T], AF.Silu, bias=scw[:, c, 4:5], scale=1.0),
                                reads=[Bpb[mb], B_sm], writes=[B_xsf[c]])
                            P.op("pool", lambda e, c=c: e.tensor_copy(xsb[:, c, :], xsf[:, c, :]),
                                 reads=[B_xsf[c]], writes=[B_xsb[c]])
                        else:
                            P.op("act", lambda e, c=c, mb=mb: e.activation(
                                bcb[:, c - 8, :], pb[mb][:, 0:T], AF.Silu, bias=scw[:, c, 4:5], scale=1.0),
                                reads=[Bpb[mb], B_sm], writes=[B_bc])
                    if it + 1 < nt:
                        prologue(it + 1)
                    for cch in range(NB if STG >= 2 else 0):
                        csl = slice(cch * 128, (cch + 1) * 128)
                        P.op("pe", [lambda e, k=k, csl=csl, xn=xn: e.matmul(
                            pb[5][:, 256:272], xn[:, k, csl], wz[:, k, 2560:2576], start=(k == 0), stop=(k == 7))
                            for k in range(8)], reads=[B_xn] + B_wz, writes=[Bpb[5]])
                        P.op("dve", lambda e: e.tensor_tensor(dtt[:], pb[5][:, 256:272], h16[:, 0, :], ALU.add),
                             reads=[Bpb[5], B_sm], writes=[B_dt])
                        P.op("act", lambda e: e.activation(dtt[:], dtt[:], AF.Exp), reads=[B_dt], writes=[B_dt])
                        P.op("act", lambda e: e.activation(dtt[:], dtt[:], AF.Ln, bias=onec, scale=1.0),
                             reads=[B_dt, Bc], writes=[B_dt])
                        P.op("dve", lambda e: e.tensor_tensor(adt[:], dtt[:], abc[:], ALU.mult),
                             reads=[B_dt, B_sm], writes=[B_dt])
                        P.op("pe", [lambda e: e.matmul(pb[5][:, 272:288], triu[:], adt[:], start=True, stop=True),
                                    lambda e: e.matmul(pb[5][:, 288:304], ones_f[:], adt[:], start=True, stop=True)],
                             reads=[B_dt, Bc], writes=[Bpb[5]])
                        P.op("dve", lambda e: e.tensor_copy(acs[:], pb[5][:, 272:288]), reads=[Bpb[5]], writes=[B_co])
                        P.op("dve", lambda e: e.tensor_copy(ala[:], pb[5][:, 288:304]), reads=[Bpb[5]], writes=[B_co])
                        P.op("act", lambda e: e.activation(cdec[:], ala[:], AF.Exp), reads=[B_co], writes=[B_co])
                        P.op("dve", lambda e: e.tensor_tensor(coef[:], ala[:], acs[:], ALU.subtract),
                             reads=[B_co], writes=[B_co])
                        P.op("act", lambda e: e.activation(coef[:], coef[:], AF.Exp), reads=[B_co], writes=[B_co])
                        P.op("dve", lambda e: e.tensor_tensor(coef[:], coef[:], dtt[:], ALU.mult),
                             reads=[B_co, B_dt], writes=[B_co])
                        if STG < 3:
                            continue
                        pbt = pb[4][:].bitcast(BF16)
                        P.op("pe", [lambda e, c=c, csl=csl: e.transpose(pbt[:, sl(c)], xsb[:, c, csl], identb[:])
                                    for c in range(8)], reads=B_xsb + [Bc], writes=[Bpb[4]])
                        pbt3 = pbt.rearrange("p (h d) -> p h d", h=16)
                        P.op("dve", lambda e: e.tensor_tensor(
                            xdt[:], pbt3, dtt[:].unsqueeze(2).broadcast_to([128, 16, 64]), ALU.mult),
                            reads=[Bpb[4], B_dt], writes=[B_xdt])
                        P.op("dve", lambda e: e.tensor_tensor(
                            xdd[:], pbt3, coef[:].unsqueeze(2).broadcast_to([128, 16, 64]), ALU.mult),
                            reads=[Bpb[4], B_co], writes=[B_xdt])
                        P.op("pe", [lambda e, g=g, csl=csl: e.transpose(pbt[:, sl(g)], bcb[:, g, csl], identb[:])
                                    for g in range(2)], reads=[B_bc, Bc], writes=[Bpb[4]])
                        P.op("act", lambda e: e.copy(btm[:], pbt[:, 0:256].rearrange("p (g n) -> p g n", g=2)),
                             reads=[Bpb[4]], writes=[B_btm])
                        P.op("pe", [lambda e, g=g, csl=csl: e.matmul(
                            pb[5][:, sl(g)], bcb[:, g, csl], bcb[:, 2 + g, csl], start=True, stop=True)
                            for g in range(2)], reads=[B_bc], writes=[Bpb[5]])
                        P.op("dve", lambda e: e.tensor_tensor(
                            cbm[:], pb[5][:, 0:256].rearrange("p (g n) -> p g n", g=2),
                            triu[:].unsqueeze(1).broadcast_to([128, 2, 128]), ALU.mult),
                            reads=[Bpb[5], Bc], writes=[B_cbm])
                        if STG < 4:
                            continue
                        rbk = [(0, 1), (6, 7)]
                        for g in range(2):
                            hsl = slice(g * 8, (g + 1) * 8)
                            P.op("pool", lambda e, hsl=hsl, g=g: e.tensor_tensor(
                                R2[g][:], triu[:].unsqueeze(1).broadcast_to([128, 8, 128]),
                                adt[:, hsl].unsqueeze(2).broadcast_to([128, 8, 128]), ALU.mult),
                                reads=[B_dt, Bc], writes=[B_R2[g]])
                            P.op("pe", [lambda e, h2=h2, g=g: e.matmul(
                                pb[rbk[g][h2 // 2]][:, (h2 % 2) * 256:(h2 % 2) * 256 + 256], ones_f[:],
                                R2[g][:, h2 * 2:(h2 + 1) * 2, :], start=True, stop=True)
                                for h2 in range(4)], reads=[B_R2[g], Bc], writes=[Bpb[rbk[g][0]], Bpb[rbk[g][1]]])
                        for g in range(2):
                            hsl = slice(g * 8, (g + 1) * 8)
                            for hh in range(2):
                                h4 = slice(hh * 4, (hh + 1) * 4)
                                a4 = slice(g * 8 + hh * 4, g * 8 + hh * 4 + 4)
                                bk = rbk[g][hh]
                                rb = pb[bk][:].rearrange("p (h l) -> p h l", h=4)
                                P.op("dve", lambda e, h4=h4, a4=a4, rb=rb: e.tensor_tensor(
                                    dif[:, h4, :], rb, acs[:, a4].unsqueeze(2).broadcast_to([128, 4, 128]),
                                    ALU.subtract), reads=[Bpb[bk], B_co], writes=[B_dif])
                                P.op("act", lambda e, h4=h4, rb=rb: e.activation(erow[:, h4, :], rb, AF.Exp),
                                     reads=[Bpb[bk]], writes=[B_er])
                            if SUB != 1:
                                P.op("act", lambda e: e.activation(dif[:], dif[:], AF.Exp), reads=[B_dif], writes=[B_dif])
                            P.op("dve", lambda e, g=g: e.scalar_tensor_tensor(
                                MT[:], dif[:], 1.0, cbm[:, g, :].unsqueeze(1).broadcast_to([128, 8, 128]),
                                ALU.min, ALU.mult), reads=[B_dif, B_cbm], writes=[B_MT])
                            P.op("pool", lambda e, g=g, csl=csl: e.tensor_tensor(
                                Cs[:], erow[:], bcb[:, 2 + g, csl].unsqueeze(1).broadcast_to([128, 8, 128]),
                                ALU.mult), reads=[B_er, B_bc], writes=[B_Cs])
                            for hh in range(8 if STG >= 5 else 0):
                                h = g * 8 + hh
                                cch_out = h // 2
                                half = h % 2
                                bank = pb[2 + cch_out // 4]
                                col = (cch_out % 4) * 128
                                P.op("pe", [
                                    lambda e, h=h, hh=hh, half=half, bank=bank, col=col: e.matmul(
                                        bank[half * 64:(half + 1) * 64, col:col + 128], xdt[:, h, :], MT[:, hh, :],
                                        start=True, stop=False),
                                    lambda e, h=h, hh=hh, half=half, bank=bank, col=col: e.matmul(
                                        bank[half * 64:(half + 1) * 64, col:col + 128], prevb[:, h, :], Cs[:, hh, :],
                                        start=False, stop=True)],
                                    reads=[B_xdt, B_MT, B_prevb, B_Cs], writes=[Bpb[2 + cch_out // 4]])
                            if SUB != 2:
                              P.op("pe", lambda e, g=g: e.matmul(
                                pb[4 + g][:], btm[:, g, :], xdd[:, g * 8:(g + 1) * 8, :], start=True, stop=True),
                                reads=[B_btm, B_xdt], writes=[Bpb[4 + g]])
                        for g in range(2 if SUB != 2 else 0):
                            hsl = slice(g * 8, (g + 1) * 8)
                            P.op("dve", lambda e, hsl=hsl: e.tensor_tensor(
                                prev[:, hsl, :], prev[:, hsl, :],
                                cdec[:, hsl].unsqueeze(2).broadcast_to([128, 8, 64]), ALU.mult),
                                reads=[B_co, B_prev], writes=[B_prev])
                            P.op("dve", lambda e, hsl=hsl, g=g: e.tensor_tensor(
                                prev[:, hsl, :], prev[:, hsl, :],
                                pb[4 + g][:].rearrange("p (h d) -> p h d", h=8), ALU.add),
                                reads=[Bpb[4 + g], B_prev], writes=[B_prev])
                        P.op("act", lambda e: e.copy(prevb[:], prev[:]), reads=[B_prev], writes=[B_prevb])
                        for c in range(8):
                            bank = pb[2 + c // 4]
                            col = (c % 4) * 128
                            P.op("dve", lambda e, c=c, bank=bank, col=col, csl=csl: e.scalar_tensor_tensor(
                                yg[:, c, csl], xsf[:, c, csl], vec[:, V_SSD, c:c + 1], bank[:, col:col + 128],
                                ALU.mult, ALU.add), reads=[Bpb[2 + c // 4], B_xsf[c], Bc], writes=[B_yg])
                    P.op("pool", lambda e: e.tensor_tensor(yg[:], yg[:], sz[:], ALU.mult),
                         reads=[B_sz, B_yg], writes=[B_yg])
                    for g in range(2):
                        rms_rstd(yg, B_yg, 4, ones512, sqt, B_sq, pb[6], Bpb[6], grs[g][:], B_grs[g], k0=g * 4)
                    for c in range(8):
                        P.op("dve", lambda e, c=c: e.scalar_tensor_tensor(
                            yn[:, c, :], yg[:, c, :], vec[:, V_SSN, c:c + 1], grs[c // 4][:], ALU.mult, ALU.mult),
                            reads=[B_yg, Bc, B_grs[c // 4]], writes=[B_yn])
                    for m in range(8):
                        mb = m % 2
                        P.op("pe", [lambda e, j=j, m=m, mb=mb: e.matmul(
                            pb[mb][:, 0:T], wo[:, j, sl(m)], (ut[:, j, :] if j < 8 else yn[:, j - 8, :]),
                            start=(j == 0), stop=(j == 15))
                            for j in range(16)], reads=[B_ut, B_yn] + B_wo, writes=[Bpb[mb]])
                        P.op("dve", lambda e, m=m, mb=mb, hs=hs: e.tensor_tensor(
                            hs[:, m, :], hs[:, m, :], pb[mb][:, 0:T], ALU.add),
                            reads=[Bpb[mb], B_ht[s]], writes=[B_ht[s]])
                    P.dma("sp", Hv[:, :, t0:t0 + T], hs[:], reads=[B_ht[s]], writes=[B_H[it]], slot=B_ht[s])
                P.end_phase()

        def att_phase():
            with ExitStack() as pes:
                sb = lambda name, shape, dt: pes.enter_context(nc.sbuf_tensor(uniq(name), shape, dt))
                wq = sb("wq", [128, 8, 1536], BF16)
                wo = sb("wo", [128, 8, D], BF16)
                bqk = sb("bqk", [128, 10], F32)
                bvb = sb("bvb", [128, 256], F32)
                snk = sb("snk", [128, 16], F32)
                mskb = sb("mskb", [128, 2, 256], BF16)
                ht = [sb(f"ht{i}", [128, 8, T], F32) for i in range(2)]
                xn2 = [sb(f"xn{i}", [128, 8, T], BF16) for i in range(2)]
                sqt = sb("sqt", [128, 2, T], BF16)
                rstd2 = [sb(f"rstd{i}", [128, T], F32) for i in range(2)]
                rope = sb("rope", [128, 2, T], F32)
                qf = [sb(f"qf{i}", [128, T], F32) for i in range(2)]
                qb = [sb(f"qb{i}", [128, T], BF16) for i in range(2)]
                t1 = [sb(f"t1{i}", [128, T], F32) for i in range(2)]
                qr = sb("qr", [128, 8, T], BF16)
                kr = sb("kr", [128, 2, 128 + T], BF16)
                vt = sb("vt", [128, 1 + NB, 256], BF16)
                ee = [sb(f"ee{i}", [128, 4, 256], F32) for i in range(2)]
                pp = [sb(f"pp{i}", [128, 4, 256], BF16) for i in range(2)]
                pT = [sb(f"pT{i}", [128, 8, 128], BF16) for i in range(2)]
                mx = [sb(f"mx{i}", [128, 4], F32) for i in range(2)]
                nmx = [sb(f"nmx{i}", [128, 4], F32) for i in range(2)]
                rs = [sb(f"rs{i}", [128, 4], F32) for i in range(2)]
                es_ = [sb(f"es_{i}", [128, 4], F32) for i in range(2)]
                oT = sb("oT", [128, 8, T], BF16)
                B_wq = P.bufs(8, "wq")
                B_wo = P.buf("wo")
                B_sm_ = P.buf("small")
                B_ht = P.bufs(2, "ht")
                B_xn2 = P.bufs(2, "xn")
                B_sq = P.bufs(2, "sq")
                B_rstd2 = P.bufs(2, "rstd")
                B_rope = P.buf("rope")
                B_qf = P.bufs(2, "qf")
                B_qb = P.bufs(2, "qb")
                B_t1 = P.bufs(2, "t1")
                B_qr = P.buf("qr")
                B_kr = P.buf("kr")
                B_vt = P.buf("vt")
                B_s = P.bufs(2, "sm")
                B_e = P.bufs(2, "ee")
                B_p = P.bufs(2, "pp")
                B_pT = P.bufs(2, "pT")
                B_st = P.bufs(2, "stats")
                B_oT = P.buf("oT")
                for k in range(8):
                    load_w(wq[:, k, :], wqkv_d[sl(k), :], B_wq[k])
                load_w(wo[:], wo_d.rearrange("(k p) m -> p k m", p=128), B_wo)
                P.dma("sp", bqk[:], bqk_d, writes=[B_sm_], slot=B_sm_)
                P.dma("sp", bvb[:], bv_d.partition_broadcast(128), writes=[B_sm_], slot=B_sm_)
                P.dma("sp", snk[:], h16_d[2].partition_broadcast(128), writes=[B_sm_], slot=B_sm_)
                P.dma("pool", mskb[:], msk_d.rearrange("a p s -> p a s"), writes=[B_sm_], slot=B_sm_)
                P.op("dve", lambda e: e.memset(kr[:, :, 0:128], 0.0), writes=[B_kr])
                P.op("dve", lambda e: e.memset(vt[:, 0, :], 0.0), writes=[B_vt])
                def prologue(it_):
                    s_ = it_ % 2
                    P.dma("sp", ht[s_][:], Hv[:, :, it_ * T:(it_ + 1) * T], reads=[B_H[it_]], writes=[B_ht[s_]], slot=B_ht[s_])
                    make_xn(ht[s_], B_ht[s_], V_NMIX + 1, xn2[s_], B_xn2[s_], sqt, B_sq, rstd2[s_], B_rstd2[s_], pb[6], Bpb[6])

                prologue(0)
                for it in range(nt):
                    s = it % 2
                    t0 = it * T
                    hs = ht[s]
                    xn = xn2[s]
                    B_xn = B_xn2[s]
                    P.dma("sp", rope[:], rope_d[:, :, t0:t0 + T].rearrange("a p s -> p a s"),
                          writes=[B_rope], slot=B_rope)
                    if it > 0:
                        P.op("dve", lambda e: e.tensor_copy(kr[:, :, 0:128], kr[:, :, T:T + 128]),
                             reads=[B_kr], writes=[B_kr])
                        P.op("dve", lambda e: e.tensor_copy(vt[:, 0, :], vt[:, NB, :]), reads=[B_vt], writes=[B_vt])
                    for c in range(10):
                        jb = c % 2
                        P.op("pe", [lambda e, k=k, c=c, jb=jb, xn=xn: e.matmul(
                            pb[jb][:, 0:T], wq[:, k, sl(c)], xn[:, k, :], start=(k == 0), stop=(k == 7))
                            for k in range(8)], reads=[B_xn] + B_wq, writes=[Bpb[jb]])
                        P.op("act", lambda e, c=c, jb=jb: e.activation(
                            qf[jb][:], pb[jb][:, 0:T], AF.Identity, bias=bqk[:, c:c + 1], scale=1.0),
                            reads=[Bpb[jb], B_sm_], writes=[B_qf[jb]])
                        P.op("pe", lambda e, jb=jb: e.matmul(pb[2 + jb][:, 0:T], pswapf[:], qf[jb][:], start=True, stop=True),
                             reads=[B_qf[jb], Bc], writes=[Bpb[2 + jb]])
                        P.op("dve", lambda e, jb=jb: e.tensor_tensor(t1[jb][:], qf[jb][:], rope[:, 0, :], ALU.mult),
                             reads=[B_qf[jb], B_rope], writes=[B_t1[jb]])
                        P.op("dve", lambda e, jb=jb: e.tensor_tensor(qf[jb][:], pb[2 + jb][:, 0:T], rope[:, 1, :], ALU.mult),
                             reads=[Bpb[2 + jb], B_rope, B_qf[jb]], writes=[B_qf[jb]])
                        dst = qr[:, c, :] if c < 8 else kr[:, c - 8, 128:128 + T]
                        P.op("dve", lambda e, jb=jb, dst=dst: e.tensor_tensor(dst, t1[jb][:], qf[jb][:], ALU.add),
                             reads=[B_t1[jb], B_qf[jb]], writes=[B_qr if c < 8 else B_kr])
                    for b in range(NB):
                        jb = b % 2
                        P.op("pe", [lambda e, k=k, b=b, jb=jb, xn=xn: e.matmul(
                            pb[jb][:, 0:256], xn[:, k, sl(b)], wq[:, k, 1280:1536], start=(k == 0), stop=(k == 7))
                            for k in range(8)], reads=[B_xn] + B_wq, writes=[Bpb[jb]])
                        P.op("dve", lambda e, b=b, jb=jb: e.tensor_tensor(vt[:, 1 + b, :], pb[jb][:, 0:256], bvb[:], ALU.add),
                             reads=[Bpb[jb], B_sm_], writes=[B_vt])
                    if it + 1 < nt:
                        prologue(it + 1)
                    def mk_group(b, kv, a):
                        gblk = it * NB + b
                        mi = 0 if gblk == 0 else 1
                        half = kv % 2
                        hp = slice(half * 64, (half + 1) * 64)
                        kc = kv // 2
                        qc0 = (kv // 2) * 4
                        S0, S1, PTb, Ob = 4 * a, 4 * a + 1, 4 * a + 2, 4 * a + 3
                        ee_, pp_, pT_ = ee[a], pp[a], pT[a]
                        mx_, nmx_, rs_, es2 = mx[a], nmx[a], rs[a], es_[a]
                        Be, Bp, BpT, Bst = B_e[a], B_p[a], B_pT[a], B_st[a]
                        sg4 = snk[:, kv * 4:(kv + 1) * 4]
                        pbt = pb[PTb][:].bitcast(BF16)
                        G = {}

                        def pe_s():
                            fns = []
                            for i in range(4):
                                dst = pb[S0 + i // 2][:, (i % 2) * 256:(i % 2) * 256 + 256]
                                fns.append(lambda e, i=i, dst=dst: e.matmul(
                                    dst, qr[hp, qc0 + i, sl(b)], kr[hp, kc, b * 128:b * 128 + 256],
                                    start=True, stop=False))
                                fns.append(lambda e, dst=dst: e.matmul(
                                    dst, identb[:], mskb[:, mi, :], start=False, stop=True))
                            P.op("pe", fns, reads=[B_qr, B_kr, B_sm_, Bc], writes=[Bpb[S0], Bpb[S1]])

                        def d1():
                            for hh in range(2):
                                P.op("dve", lambda e, hh=hh: e.tensor_reduce(
                                    mx_[:, hh * 2:hh * 2 + 2], pb[S0 + hh][:].rearrange("p (h s) -> p h s", h=2),
                                    AX.X, ALU.max), reads=[Bpb[S0 + hh]], writes=[Bst])
                            P.op("dve", lambda e: e.scalar_tensor_tensor(mx_[:], mx_[:], 0.125, sg4, ALU.mult, ALU.max),
                                 reads=[Bst, B_sm_], writes=[Bst])
                            P.op("dve", lambda e: e.tensor_scalar_mul(nmx_[:], mx_[:], -1.0), reads=[Bst], writes=[Bst])
                            P.op("dve", lambda e: e.tensor_tensor(es2[:], sg4, mx_[:], ALU.subtract),
                                 reads=[Bst, B_sm_], writes=[Bst])
                            P.op("dve", lambda e: e.memset(rs_[:], 0.0), writes=[Bst])

                        def a1():
                            for i in range(4):
                                src = pb[S0 + i // 2][:, (i % 2) * 256:(i % 2) * 256 + 256]
                                P.op("act", lambda e, i=i, src=src: e.activation(
                                    ee_[:, i, :], src, AF.Exp, bias=nmx_[:, i:i + 1], scale=0.125,
                                    accum_out=rs_[:, i:i + 1]), reads=[Bpb[S0 + i // 2], Bst], writes=[Be, Bst])
                            P.op("act", lambda e: e.activation(es2[:], es2[:], AF.Exp), reads=[Bst], writes=[Bst])

                        def d2():
                            P.op("dve", lambda e: e.tensor_tensor(rs_[:], rs_[:], es2[:], ALU.add), reads=[Bst], writes=[Bst])
                            P.op("dve", lambda e: e.reciprocal(rs_[:], rs_[:]), reads=[Bst], writes=[Bst])
                            P.op("dve", lambda e: e.tensor_tensor(
                                pp_[:], ee_[:], rs_[:].unsqueeze(2).broadcast_to([128, 4, 256]), ALU.mult),
                                reads=[Be, Bst], writes=[Bp])

                        def pe_t():
                            P.op("pe", [lambda e, i=i, kb=kb: e.transpose(
                                pbt[:, sl(i * 2 + kb)], pp_[:, i, sl(kb)], identb[:])
                                for i in range(4) for kb in range(2)], reads=[Bp, Bc], writes=[Bpb[PTb]])

                        def a2():
                            P.op("act", lambda e: e.copy(pT_[:], pbt.rearrange("p (a q) -> p a q", a=8)),
                                 reads=[Bpb[PTb]], writes=[BpT])

                        def pe_pv():
                            P.op("pe", [lambda e, i=i, kb=kb: e.matmul(
                                pb[Ob][hp, i * 128:(i + 1) * 128], vt[:, b + kb, kv * 64:(kv + 1) * 64],
                                pT_[:, i * 2 + kb, :], start=(kb == 0), stop=(kb == 1))
                                for i in range(4) for kb in range(2)], reads=[B_vt, BpT], writes=[Bpb[Ob]])

                        def a3():
                            P.op("act", lambda e: e.copy(
                                oT[hp, qc0:qc0 + 4, sl(b)], pb[Ob][hp, :].rearrange("p (i q) -> p i q", i=4)),
                                reads=[Bpb[Ob]], writes=[B_oT])
                        G.update(pe_s=pe_s, d1=d1, a1=a1, d2=d2, pe_t=pe_t, a2=a2, pe_pv=pe_pv, a3=a3)
                        return G

                    groups = [mk_group(b_, kv_, gi % 2) for gi, (b_, kv_) in
                              enumerate([(b_, kv_) for b_ in range(NB) for kv_ in range(4)])]
                    ng_ = len(groups)
                    groups[0]["pe_s"]()
                    for c_ in range(ng_ + 1):
                        cur = groups[c_] if c_ < ng_ else None
                        prv = groups[c_ - 1] if c_ > 0 else None
                        if cur:
                            cur["d1"]()
                        if c_ + 1 < ng_:
                            groups[c_ + 1]["pe_s"]()
                        if cur:
                            cur["a1"]()
                        if prv:
                            prv["d2"]()
                            prv["pe_t"]()
                            prv["a2"]()
                            prv["pe_pv"]()
                            prv["a3"]()
                    for m in range(8):
                        mb = m % 2
                        P.op("pe", [lambda e, j=j, m=m, mb=mb: e.matmul(
                            pb[mb][:, 0:T], wo[:, j, sl(m)], oT[:, j, :], start=(j == 0), stop=(j == 7))
                            for j in range(8)], reads=[B_oT, B_wo], writes=[Bpb[mb]])
                        P.op("dve", lambda e, m=m, mb=mb, hs=hs: e.scalar_tensor_tensor(
                            hs[:, m, :], pb[mb][:, 0:T], vec[:, V_BO, m:m + 1], hs[:, m, :], ALU.add, ALU.add),
                            reads=[Bpb[mb], B_ht[s], Bc], writes=[B_ht[s]])
                    P.dma("sp", Hv[:, :, t0:t0 + T], hs[:], reads=[B_ht[s]], writes=[B_H[it]], slot=B_ht[s])
                P.end_phase()

        last = phases[-1]
        for ph in phases:
            if ph == "ffn1_0":
                ffn_phase(0, 0, True, False, False)
            elif ph == "conv":
                conv_phase()
            elif ph == "ssd":
                ssd_phase()
            elif ph == "ffn2_0":
                ffn_phase(1, 0, False, True, False)
            elif ph == "ffn1_1":
                ffn_phase(2, 1, False, False, False)
            elif ph == "att":
                att_phase()
            elif ph == "ffn2_1":
                ffn_phase(3, 1, False, True, True)
        P.barrier()
        P.emit()
    return nc


Q_LOWER = [0, 1, 2, 3, 8, 9, 10, 11]
Q_UPPER = [4, 5, 6, 7, 12, 13, 14, 15]
HEAD_ORDER = [h for c in range(8) for h in (Q_LOWER[c], Q_UPPER[c])]


def _pk(v):
    v = np.asarray(v, np.float32)
    return np.array(v.reshape(-1, 128).T, dtype=np.float32, order='C', copy=True)


def prep_shared(inp, S=SEQ):
    f = lambda a: np.array(a, dtype=np.float32, order='C', copy=True)
    sh = {}
    sh["ffn_win"] = f(np.stack([inp["ffn1_w_in"][0], inp["ffn2_w_in"][0], inp["ffn1_w_in"][1], inp["ffn2_w_in"][1]]))
    sh["ffn_wout"] = f(np.stack([inp["ffn1_w_out"][0], inp["ffn2_w_out"][0], inp["ffn1_w_out"][1], inp["ffn2_w_out"][1]]))
    vec = np.zeros((128, 20, 8), np.float32)
    for li in range(2):
        vec[:, 0 + li] = _pk(inp["norm_ffn1"][li])
        vec[:, 2 + li] = _pk(inp["norm_mix"][li])
        vec[:, 4 + li] = _pk(inp["norm_ffn2"][li])
        vec[:, 6 + li] = _pk(inp["ple_norm"][li])
    vec[:, 8] = _pk(inp["final_norm"])
    vec[:, 9] = _pk(inp["conv_dw_b"][0])
    vec[:, 10] = _pk(inp["conv_ln_g"][0])
    vec[:, 11] = _pk(inp["conv_ln_b"][0])
    vec[:, 12] = _pk(np.repeat(np.asarray(inp["ssm_d"][0], np.float32), 64))
    vec[:, 13] = _pk(inp["ssm_norm"][0])
    vec[:, 14] = _pk(inp["att_b_o"][0])
    sh["vecs"] = vec
    sh["ple_wg"] = f(inp["ple_gate_w"])
    sh["ple_wp"] = f(inp["ple_proj_w"])
    sh["hyb_win"] = f(inp["hyb_w_in"][0])
    sh["hyb_wout"] = f(inp["hyb_w_out"][0])
    cw = np.asarray(inp["conv_dw_w"][0], np.float32)
    sh["cw"] = f(cw.T.reshape(8, 128, CW).transpose(1, 0, 2))
    scw = np.asarray(inp["ssm_conv_w"][0], np.float32)
    scb = np.asarray(inp["ssm_conv_b"][0], np.float32)
    sc = np.concatenate([scw, scb[None]], 0)
    sh["scw"] = f(sc.T.reshape(12, 128, 5).transpose(1, 0, 2))
    sinks = np.asarray(inp["att_sinks"][0], np.float32)
    sg_order = [HEAD_ORDER[((kv // 2) * 4 + i) * 2 + kv % 2] for kv in range(4) for i in range(4)]
    sh["h16"] = f(np.stack([inp["ssm_dt_bias"][0], inp["ssm_a_log"][0], sinks[sg_order]]))
    wqkv = np.asarray(inp["att_w_qkv"][0], np.float32)
    bqkv = np.asarray(inp["att_b_qkv"][0], np.float32)
    qcols = np.concatenate([np.arange(h * 64, (h + 1) * 64) for h in HEAD_ORDER])
    cols = np.concatenate([qcols, np.arange(1024, 1536)])
    sh["wqkv"] = f(wqkv[:, cols])
    bp = bqkv[cols]
    sh["bqk"] = _pk(bp[:1280])
    sh["bv"] = f(bp[1280:])
    sh["wo"] = f(np.asarray(inp["att_w_o"][0], np.float32)[qcols, :])
    inv = (np.float32(10000.0) ** (-np.arange(0, 64, 2, dtype=np.float32) / np.float32(64))).astype(np.float32)
    ang = (np.arange(S, dtype=np.float32)[:, None] * inv[None, :]).astype(np.float32)
    cos, sin = np.cos(ang).astype(np.float32), np.sin(ang).astype(np.float32)
    prt = np.arange(128)
    CC = cos.T[prt % 32]
    sgn = np.where((prt % 64) < 32, -1.0, 1.0).astype(np.float32)
    SSn = sin.T[prt % 32] * sgn[:, None]
    sh["rope"] = f(np.stack([CC, SSn]))
    cst = np.zeros((5, 128, 128), np.float32)
    cst[0] = np.eye(128)
    cst[1] = np.triu(np.ones((128, 128)))
    sw = np.zeros((128, 128), np.float32)
    for m in range(128):
        sw[(m + 32) % 64 + (m // 64) * 64, m] = 1.0
    cst[2] = sw
    sh["cst"] = cst
    q = np.arange(128)[:, None]
    sp = np.arange(256)[None, :]
    valid = np.where(sp < 128, sp > q, (sp - 128) <= q)
    m1 = np.where(valid, 0.0, NEG).astype(np.float32)
    m0 = np.where(valid & (sp >= 128), 0.0, NEG).astype(np.float32)
    sh["msk"] = f(np.stack([m0, m1]))
    return sh


_CACHE = {}


def kernel(**inputs):
    x = np.asarray(inputs["x"], np.float32)
    p = np.asarray(inputs["p"], np.float32)
    B, S, _ = x.shape
    sh = prep_shared(inputs, S)
    key = ("full", S)
    if key not in _CACHE:
        _CACHE[key] = build_program(S)
    nc = _CACHE[key]
    in_maps = []
    for b in range(B):
        m = dict(sh)
        m["x"] = np.array(x[b], dtype=np.float32, order='C', copy=True)
        m["p"] = np.array(p[:, b], dtype=np.float32, order='C', copy=True)
        in_maps.append(m)
    res = run_bass_kernel_spmd(nc, in_maps, core_ids=list(range(B)))
    return np.stack([np.asarray(r["y"], np.float32) for r in res.results], 0)
```

```python
from contextlib import ExitStack
import os
import numpy as np
import concourse.bass as bass
import concourse.mybir as mybir
from concourse.bass_utils import run_bass_kernel_spmd

F32 = mybir.dt.float32
BF16 = mybir.dt.bfloat16
AF = mybir.ActivationFunctionType
ALU = mybir.AluOpType
AX = mybir.AxisListType

D = 1024
DFF = 2816
NJ = DFF // 128
PLE = 256
SEQ = 4096
NCORES = 8
T = 256
NB = T // 128
EPS = 1e-6
CW = 31
HYB_IN = 4624
NEG = -240000.0
STG = int(os.environ.get('SSD_STAGE', '99'))
SUB = int(os.environ.get('SUB', '0'))


class Buf:
    __slots__ = ("name", "w", "r", "dsem", "excl")

    def __init__(self, name):
        self.name = name
        self.w = {}
        self.r = {}
        self.dsem = None
        self.excl = False


class Prog:
    ENGS = ("pe", "act", "dve", "pool", "sp")

    def __init__(self, nc, es):
        self.nc = nc
        self.es = es
        self.streams = {e: [] for e in self.ENGS}
        self.sems = {}
        self.cnt = {}
        self.waited = {e: {} for e in self.ENGS}
        self.nbuf = 0
        self.free_dsems = []
        self.phase_dsems = []
        self.ndsem = 0
        for e in ("pe", "act", "dve", "pool"):
            self._mksem("c_" + e)

    def _mksem(self, name):
        self.sems[name] = self.es.enter_context(self.nc.semaphore(name))
        self.cnt[name] = 0
        return name

    def _dsem(self):
        if self.free_dsems:
            s = self.free_dsems.pop()
        else:
            self.ndsem += 1
            s = self._mksem(f"d{self.ndsem}")
        self.phase_dsems.append(s)
        return s

    def buf(self, name=None):
        self.nbuf += 1
        return Buf(name or f"b{self.nbuf}")

    def bufs(self, n, name="b"):
        return [self.buf(f"{name}{i}") for i in range(n)]

    def _need(self, reads, writes):
        need = {}
        for b in reads:
            for s, v in b.w.items():
                if need.get(s, 0) < v:
                    need[s] = v
        for b in writes:
            for d in (b.w, b.r):
                for s, v in d.items():
                    if need.get(s, 0) < v:
                        need[s] = v
        return need

    def _emit_waits(self, eng, need, skip_own=False):
        wd = self.waited[eng]
        own = "c_" + eng
        for s, v in need.items():
            if skip_own and s == own:
                continue
            if wd.get(s, 0) < v:
                wd[s] = v
                h = self.sems[s]
                self.streams[eng].append(lambda e, h=h, v=v: e.wait_ge(h, v))

    def _mark(self, reads, writes, s, v):
        for b in reads:
            if b.r.get(s, 0) < v:
                b.r[s] = v
        for b in writes:
            b.w = {s: v}
            b.r = {}

    def op(self, eng, fns, reads=(), writes=(), skip_own=None):
        if callable(fns):
            fns = [fns]
        if skip_own is None:
            skip_own = (eng == "pe")
        ex = [b for b in reads if b.excl]
        if ex:
            writes = list(writes) + ex
            reads = [b for b in reads if not b.excl]
        self._emit_waits(eng, self._need(reads, writes), skip_own)
        s = "c_" + eng
        self.cnt[s] += 1
        v = self.cnt[s]
        h = self.sems[s]
        st = self.streams[eng]
        for f in fns[:-1]:
            st.append(f)
        last = fns[-1]
        st.append(lambda e, last=last, h=h: last(e).then_inc(h, 1))
        self._mark(reads, writes, s, v)

    def dma(self, q, out, in_, reads=(), writes=(), slot=None):
        self._emit_waits(q, self._need(reads, writes), False)
        if slot.dsem is None:
            slot.dsem = self._dsem()
        s = slot.dsem
        self.cnt[s] += 16
        v = self.cnt[s]
        h = self.sems[s]
        self.streams[q].append(lambda e, out=out, in_=in_, h=h: e.dma_start(out=out, in_=in_).then_inc(h, 16))
        self._mark(reads, writes, s, v)

    def barrier(self):
        need = {s: v for s, v in self.cnt.items() if v > 0}
        for e in self.ENGS:
            self._emit_waits(e, need, False)

    def end_phase(self):
        self.barrier()
        self.free_dsems.extend(self.phase_dsems)
        self.phase_dsems = []

    def emit(self):
        nc = self.nc
        st = self.streams
        with nc.Block() as block:
            @block.tensor
            def _(e):
                for f in st["pe"]:
                    f(e)

            @block.scalar
            def _(e):
                for f in st["act"]:
                    f(e)

            @block.vector
            def _(e):
                for f in st["dve"]:
                    f(e)

            @block.gpsimd
            def _(e):
                for f in st["pool"]:
                    f(e)

            @block.sync
            def _(e):
                for f in st["sp"]:
                    f(e)


class Ctx:
    pass


def sl(i, n=128):
    return slice(i * n, (i + 1) * n)


def build_program(S=SEQ, phases=("ffn1_0", "conv", "ssd", "ffn2_0", "ffn1_1", "att", "ffn2_1"), debug=False):
    nc = bass.Bass("TRN2", target_bir_lowering=False)
    nt = S // T
    dt_in = {}

    def din(name, shape):
        dt_in[name] = nc.dram_tensor(name, list(shape), F32, kind="ExternalInput").ap()
        return dt_in[name]

    x_d = din("x", [S, D])
    p_d = din("p", [2, S, PLE])
    ffn_win = din("ffn_win", [4, D, 2 * DFF])
    ffn_wout = din("ffn_wout", [4, DFF, D])
    vecs = din("vecs", [128, 20, 8])
    ple_wg = din("ple_wg", [2, D, D])
    ple_wp = din("ple_wp", [2, PLE, D])
    hyb_win = din("hyb_win", [D, HYB_IN])
    hyb_wout = din("hyb_wout", [2 * D, D])
    cw_d = din("cw", [128, 8, CW])
    scw_d = din("scw", [128, 12, 5])
    h16_d = din("h16", [3, 16])
    wqkv_d = din("wqkv", [D, 1536])
    bqk_d = din("bqk", [128, 10])
    bv_d = din("bv", [256])
    wo_d = din("wo", [D, D])
    rope_d = din("rope", [2, 128, S])
    cst_d = din("cst", [5, 128, 128])
    msk_d = din("msk", [2, 128, 256])
    if debug:
        out_d = nc.dram_tensor("H", [8, 128, S], F32, kind="ExternalOutput").ap()
        Hd = out_d
        yout_d = nc.dram_tensor("y", [S, D], F32, kind="ExternalOutput").ap()
    else:
        yout_d = nc.dram_tensor("y", [S, D], F32, kind="ExternalOutput").ap()
        Hd = nc.dram_tensor("Hs", [8, 128, S], F32).ap()
    Ud = nc.dram_tensor("Us", [8, 128, S], BF16).ap()
    Hv = Hd.rearrange("k p s -> p k s")
    Uv = Ud.rearrange("k p s -> p k s")

    with ExitStack() as es:
        P = Prog(nc, es)
        C = Ctx()
        gsb = lambda name, shape, dt: es.enter_context(nc.sbuf_tensor(name, shape, dt))
        pb = [es.enter_context(nc.psum_tensor(f"pb{i}", [128, 512], F32)) for i in range(8)]
        Bpb = P.bufs(8, "pb")
        for b_ in Bpb:
            b_.excl = True
        identf = gsb("identf", [128, 128], F32)
        identb = gsb("identb", [128, 128], BF16)
        triu = gsb("triu", [128, 128], F32)
        pswap = gsb("pswap", [128, 128], BF16)
        pswapf = gsb("pswapf", [128, 128], F32)
        ones_f = gsb("ones_f", [128, 128], F32)
        ones1k = gsb("ones1k", [128, 128], BF16)
        ones512 = gsb("ones512", [128, 128], BF16)
        cols = gsb("cols", [128, 4], F32)
        vec = gsb("vec", [128, 20, 8], F32)
        Bc = P.buf("consts")
        P.dma("sp", identf[:], cst_d[0], writes=[Bc], slot=Bc)
        P.dma("sp", triu[:], cst_d[1], writes=[Bc], slot=Bc)
        P.dma("sp", pswapf[:], cst_d[2], writes=[Bc], slot=Bc)
        P.dma("sp", vec[:], vecs, writes=[Bc], slot=Bc)
        P.dma("pool", identb[:], cst_d[0], writes=[Bc], slot=Bc)
        P.dma("pool", pswap[:], cst_d[2], writes=[Bc], slot=Bc)
        P.op("dve", lambda e: e.memset(ones_f[:], 1.0), writes=[Bc])
        P.op("dve", lambda e: e.memset(ones1k[:], 1.0 / 1024), writes=[Bc])
        P.op("dve", lambda e: e.memset(ones512[:], 1.0 / 512), writes=[Bc])
        P.op("dve", lambda e: e.memset(cols[:, 0:1], EPS), writes=[Bc])
        P.op("dve", lambda e: e.memset(cols[:, 1:2], 1.0), writes=[Bc])
        P.op("dve", lambda e: e.memset(cols[:, 2:3], 0.0), writes=[Bc])
        epsc = cols[:, 0:1]
        onec = cols[:, 1:2]
        B_H = P.bufs(nt, "H")
        B_U = P.bufs(nt, "U")
        B_Y = P.bufs(nt, "Y")
        P.end_phase()

        V_NF1, V_NMIX, V_NF2, V_PLE = 0, 2, 4, 6
        V_FIN, V_CVB, V_LNG, V_LNB, V_SSD, V_SSN, V_BO = 8, 9, 10, 11, 12, 13, 14

        def rms_rstd(xap, Bx, nk, ones_ap, sqt, Bsq, pbank, Bpbank, rstd, Brstd, k0=0):
            for k in range(nk):
                q = k % 2
                P.op("act", lambda e, k=k, q=q: e.activation(sqt[:, q, :], xap[:, k0 + k, :], AF.Square),
                     reads=[Bx], writes=[Bsq[q]])
                P.op("pe", lambda e, k=k, q=q: e.matmul(pbank[:, 0:T], ones_ap[:], sqt[:, q, :],
                                                       start=(k == 0), stop=(k == nk - 1)),
                     reads=[Bsq[q], Bc], writes=[Bpbank])
            P.op("act", lambda e: e.activation(rstd, pbank[:, 0:T], AF.Sqrt, bias=epsc, scale=1.0),
                 reads=[Bpbank, Bc], writes=[Brstd])
            P.op("dve", lambda e: e.reciprocal(rstd, rstd), reads=[Brstd], writes=[Brstd])

        def make_xn(ht_s, Bht, gidx, xn, Bxn, sqt, Bsq, rstd, Brstd, pbank, Bpbank):
            rms_rstd(ht_s, Bht, 8, ones1k, sqt, Bsq, pbank, Bpbank, rstd[:], Brstd)
            for k in range(8):
                P.op("dve", lambda e, k=k: e.scalar_tensor_tensor(
                    xn[:, k, :], ht_s[:, k, :], vec[:, gidx, k:k + 1], rstd[:], ALU.mult, ALU.mult),
                    reads=[Bht, Bc, Brstd], writes=[Bxn])

        ucnt = [0]

        def uniq(name):
            ucnt[0] += 1
            return f"s{ucnt[0]}_{name}"

        def load_w(dst, src_rows_ap, Bw, q="pool"):
            P.dma(q, dst, src_rows_ap, writes=[Bw], slot=Bw)

        def ffn_phase(fi, li, first, do_ple, final):
            with ExitStack() as pes:
                sb = lambda name, shape, dt: pes.enter_context(nc.sbuf_tensor(uniq(name), shape, dt))
                win = sb("win", [128, 8, 2 * DFF], BF16)
                wout = sb("wout", [128, NJ, D], BF16)
                ht = [sb(f"ht{i}", [128, 8, T], F32) for i in range(2)]
                xn = [sb(f"xn{i}", [128, 8, T], BF16) for i in range(2)]
                sqt = sb("sqt", [128, 2, T], BF16)
                rstd = [sb(f"rstd{i}", [128, T], F32) for i in range(2)]
                sg = [sb(f"sg{i}", [128, T], F32) for i in range(2)]
                hT = sb("hT", [128, NJ, T], BF16)
                B_win = P.bufs(1, "win")
                B_wout = P.bufs(2, "wout")
                B_ht = P.bufs(2, "ht")
                B_xn = P.bufs(2, "xn")
                B_sq = P.bufs(2, "sq")
                B_rstd = P.bufs(2, "rstd")
                B_sg = P.bufs(2, "sg")
                B_hT = P.buf("hT")
                if first:
                    xt = [sb("xt0", [128, D], F32)]
                    B_xt = P.bufs(1, "xt")
                if final:
                    yt = [sb(f"yt{i}", [128, 512], F32) for i in range(2)]
                    B_yt = P.bufs(2, "yt")
                if do_ple:
                    wg = sb("wg", [128, 8, D], BF16)
                    wp = sb("wp", [128, 2, D], BF16)
                    pt = [sb(f"pt{i}", [128, PLE], F32) for i in range(2)]
                    pT = sb("pT", [128, 2, T], BF16)
                    sg2 = [sb(f"sgp{i}", [128, T], F32) for i in range(2)]
                    B_wg = P.buf("wg")
                    B_wp = P.buf("wp")
                    B_pt = P.bufs(2, "pt")
                    B_pT = P.buf("pT")
                    B_sg2 = P.bufs(2, "sgp")
                for k in range(8):
                    load_w(win[:, k, :], ffn_win[fi, sl(k), :], B_win[0])
                wo_v = ffn_wout[fi].rearrange("(j p) m -> p j m", p=128)
                for hh in range(2):
                    load_w(wout[:, hh * 11:(hh + 1) * 11, :], wo_v[:, hh * 11:(hh + 1) * 11, :], B_wout[hh])
                if do_ple:
                    load_w(wg[:], ple_wg[li].rearrange("(k p) m -> p k m", p=128), B_wg)
                    load_w(wp[:], ple_wp[li].rearrange("(k p) m -> p k m", p=128), B_wp)
                gidx = (V_NF1 if not do_ple else V_NF2) + li

                def load_tile(it):
                    s = it % 2
                    t0 = it * T
                    hs = ht[s]
                    if first:
                        for b in range(NB):
                            P.dma("sp", xt[0][:], x_d[t0 + b * 128:t0 + (b + 1) * 128, :],
                                  writes=[B_xt[0]], slot=B_xt[0])
                            for hf in range(2):
                                P.op("pe", [lambda e, kk=kk, hf=hf: e.transpose(
                                    pb[7][:, sl(kk)], xt[0][:, sl(hf * 4 + kk)], identf[:]) for kk in range(4)],
                                    reads=[B_xt[0], Bc], writes=[Bpb[7]])
                                P.op("act", lambda e, hf=hf, b=b, hs=hs: e.copy(
                                    hs[:, hf * 4:(hf + 1) * 4, sl(b)], pb[7][:].rearrange("p (k t) -> p k t", k=4)),
                                    reads=[Bpb[7]], writes=[B_ht[s]])
                    else:
                        P.dma("sp", hs[:], Hv[:, :, t0:t0 + T], reads=[B_H[it]], writes=[B_ht[s]], slot=B_ht[s])

                def prologue(it):
                    s = it % 2
                    make_xn(ht[s], B_ht[s], gidx, xn[s], B_xn[s], sqt, B_sq, rstd[s], B_rstd[s], pb[7], Bpb[7])

                def sq_stat(hs_, s_, m_):
                    q_ = m_ % 2
                    P.op("act", lambda e: e.activation(sqt[:, q_, :], hs_[:, m_, :], AF.Square),
                         reads=[B_ht[s_]], writes=[B_sq[q_]])

                def pe_stat(m_):
                    q_ = m_ % 2
                    P.op("pe", lambda e: e.matmul(pb[6][:, 0:T], ones1k[:], sqt[:, q_, :],
                                                  start=(m_ == 0), stop=(m_ == 7)),
                         reads=[B_sq[q_], Bc], writes=[Bpb[6]])

                def finish_rstd(s_):
                    rs_ = rstd[s_]
                    P.op("act", lambda e: e.activation(rs_[:], pb[6][:, 0:T], AF.Sqrt, bias=epsc, scale=1.0),
                         reads=[Bpb[6], Bc], writes=[B_rstd[s_]])
                    P.op("dve", lambda e: e.reciprocal(rs_[:], rs_[:]), reads=[B_rstd[s_]], writes=[B_rstd[s_]])

                def p_dma(it):
                    t0 = it * T
                    for b in range(NB):
                        P.dma("sp", pt[b][:], p_d[li, t0 + b * 128:t0 + (b + 1) * 128, :],
                              writes=[B_pt[b]], slot=B_pt[b])

                def post_slots(it):
                    s = it % 2
                    t0 = it * T
                    hs = ht[s]
                    xs = xn[s]
                    sl_ = {}

                    def add(j, f):
                        sl_.setdefault(j, []).append(f)

                    def prep():
                        for b in range(NB):
                            P.op("pe", [lambda e, c=c, b=b: e.transpose(
                                pb[7][:, sl(c)], pt[b][:, sl(c)], identf[:]) for c in range(2)],
                                reads=[B_pt[b], Bc], writes=[Bpb[7]])
                            P.op("act", lambda e, b=b: e.copy(
                                pT[:, :, sl(b)], pb[7][:, 0:256].rearrange("p (k t) -> p k t", k=2)),
                                reads=[Bpb[7]], writes=[B_pT])
                        finish_rstd(s)
                        for k in range(8):
                            P.op("dve", lambda e, k=k: e.scalar_tensor_tensor(
                                xs[:, k, :], hs[:, k, :], vec[:, V_PLE + li, k:k + 1], rstd[s][:], ALU.mult, ALU.mult),
                                reads=[B_ht[s], Bc, B_rstd[s]], writes=[B_xn[s]])
                    add(0, prep)

                    def ple_group(m):
                        mb = m % 2
                        P.op("pe", [lambda e, k=k: e.matmul(
                            pb[4][:, 0:T], wg[:, k, sl(m)], xs[:, k, :], start=(k == 0), stop=(k == 7))
                            for k in range(8)], reads=[B_xn[s], B_wg], writes=[Bpb[4]])
                        P.op("pe", [lambda e, c=c: e.matmul(
                            pb[5][:, 0:T], wp[:, c, sl(m)], pT[:, c, :], start=(c == 0), stop=(c == 1))
                            for c in range(2)], reads=[B_pT, B_wp], writes=[Bpb[5]])
                        P.op("act", lambda e: e.activation(sg2[mb][:], pb[4][:, 0:T], AF.Tanh, scale=0.5),
                             reads=[Bpb[4]], writes=[B_sg2[mb]])
                        P.op("dve", lambda e: e.scalar_tensor_tensor(
                            sg2[mb][:], sg2[mb][:], 1.0, pb[5][:, 0:T], ALU.add, ALU.mult),
                            reads=[B_sg2[mb], Bpb[5]], writes=[B_sg2[mb]])
                        P.op("dve", lambda e: e.scalar_tensor_tensor(
                            hs[:, m, :], sg2[mb][:], 0.5, hs[:, m, :], ALU.mult, ALU.add),
                            reads=[B_sg2[mb], B_ht[s]], writes=[B_ht[s]])
                        if final:
                            sq_stat(hs, s, m)
                            if m > 0:
                                pe_stat(m - 1)
                    for m in range(8):
                        add(1 + m, lambda m=m: ple_group(m))

                    def fin_norm():
                        pe_stat(7)
                        finish_rstd(s)
                        for k in range(8):
                            P.op("dve", lambda e, k=k: e.scalar_tensor_tensor(
                                hs[:, k, :], hs[:, k, :], vec[:, V_FIN, k:k + 1], rstd[s][:], ALU.mult, ALU.mult),
                                reads=[B_ht[s], Bc, B_rstd[s]], writes=[B_ht[s]])

                    def out_block(b, hf):
                        P.op("pe", [lambda e, kk=kk: e.transpose(
                            pb[7][:, sl(kk)], hs[:, hf * 4 + kk, sl(b)], identf[:]) for kk in range(4)],
                            reads=[B_ht[s], Bc], writes=[Bpb[7]])
                        P.op("act", lambda e: e.copy(yt[hf][:], pb[7][:]), reads=[Bpb[7]], writes=[B_yt[hf]])
                        P.dma("sp", yout_d[t0 + b * 128:t0 + (b + 1) * 128, hf * 512:(hf + 1) * 512], yt[hf][:],
                              reads=[B_yt[hf]], writes=[B_Y[it]], slot=B_yt[hf])
                    if final:
                        add(9, fin_norm)
                        jj = 10
                        for b in range(NB):
                            for hf in range(2):
                                add(jj, lambda b=b, hf=hf: out_block(b, hf))
                                jj += 1
                    if not final or debug:
                        add(14, lambda: P.dma("sp", Hv[:, :, t0:t0 + T], hs[:], reads=[B_ht[s]],
                                              writes=[B_H[it]], slot=B_ht[s]))
                    return sl_

                load_tile(0)
                prologue(0)
                for it in range(nt):
                    s = it % 2
                    t0 = it * T
                    hs = ht[s]
                    xs = xn[s]
                    slots = post_slots(it - 1) if (do_ple and it > 0) else {}
                    if it + 1 < nt and not first and not do_ple:
                        load_tile(it + 1)
                    for j in range(NJ):
                        jb = j % 2
                        for f_ in slots.get(j, []):
                            f_()
                        if it + 1 < nt:
                            if first and j == 6:
                                load_tile(it + 1)
                            if do_ple and j == 15:
                                load_tile(it + 1)
                            if j == (18 if do_ple else 12):
                                prologue(it + 1)
                        P.op("pe", [lambda e, k=k, j=j, jb=jb, xs=xs: e.matmul(
                            pb[jb][:, 0:T], win[:, k, sl(j)], xs[:, k, :], start=(k == 0), stop=(k == 7))
                            for k in range(8)], reads=[B_xn[s], B_win[0]], writes=[Bpb[jb]])
                        P.op("pe", [lambda e, k=k, j=j, jb=jb, xs=xs: e.matmul(
                            pb[2 + jb][:, 0:T], win[:, k, DFF + j * 128:DFF + (j + 1) * 128], xs[:, k, :],
                            start=(k == 0), stop=(k == 7))
                            for k in range(8)], reads=[B_xn[s], B_win[0]], writes=[Bpb[2 + jb]])
                        P.op("act", lambda e, jb=jb: e.activation(sg[jb][:], pb[jb][:, 0:T], AF.Silu),
                             reads=[Bpb[jb]], writes=[B_sg[jb]])
                        P.op("dve", lambda e, j=j, jb=jb: e.tensor_tensor(
                            hT[:, j, :], sg[jb][:], pb[2 + jb][:, 0:T], ALU.mult),
                            reads=[B_sg[jb], Bpb[2 + jb]], writes=[B_hT])
                    if do_ple:
                        p_dma(it)
                    for m in range(8):
                        mb = 4 + m % 2
                        P.op("pe", [lambda e, j=j, m=m, mb=mb: e.matmul(
                            pb[mb][:, 0:T], wout[:, j, sl(m)], hT[:, j, :], start=(j == 0), stop=(j == NJ - 1))
                            for j in range(NJ)], reads=[B_hT] + B_wout, writes=[Bpb[mb]])
                        P.op("dve", lambda e, m=m, mb=mb, hs=hs: e.scalar_tensor_tensor(
                            hs[:, m, :], pb[mb][:, 0:T], 0.5, hs[:, m, :], ALU.mult, ALU.add),
                            reads=[Bpb[mb], B_ht[s]], writes=[B_ht[s]])
                        if do_ple:
                            sq_stat(hs, s, m)
                            if m > 0:
                                pe_stat(m - 1)
                    if do_ple:
                        pe_stat(7)
                    else:
                        P.dma("sp", Hv[:, :, t0:t0 + T], hs[:], reads=[B_ht[s]], writes=[B_H[it]], slot=B_ht[s])
                if do_ple:
                    last = post_slots(nt - 1)
                    for j in sorted(last):
                        for f_ in last[j]:
                            f_()
                P.end_phase()

        def conv_phase():
            with ExitStack() as pes:
                sb = lambda name, shape, dt: pes.enter_context(nc.sbuf_tensor(uniq(name), shape, dt))
                wcv = sb("wcv", [128, 8, 2048], BF16)
                dg = sb("dg", [128, 8 * CW, 128], BF16)
                cw = sb("cw", [128, 8, CW], F32)
                ht = [sb(f"ht{i}", [128, 8, T], F32) for i in range(2)]
                xn2 = [sb(f"xn{i}", [128, 8, T], BF16) for i in range(2)]
                sqt = sb("sqt", [128, 2, T], BF16)
                rstd2 = [sb(f"rstd{i}", [128, T], F32) for i in range(2)]
                sg = [sb(f"sg{i}", [128, T], F32) for i in range(2)]
                u0 = sb("u0", [128, 8, 30 + T], BF16)
                cv2 = [sb(f"cv{i}", [128, 8, T], F32) for i in range(2)]
                sqB = sb("sqB", [128, 2, T], BF16)
                cvb = sb("cvb", [128, 2, T], BF16)
                mean = sb("mean", [128, T], F32)
                lrs = sb("lrs", [128, T], F32)
                tmp = [sb(f"tmp{i}", [128, T], F32) for i in range(2)]
                ub = [sb(f"ub{i}", [128, 8, T], BF16) for i in range(2)]
                B_w = P.bufs(8, "wcv")
                B_dg = P.buf("dg")
                B_cw = P.buf("cw")
                B_ht = P.bufs(2, "ht")
                B_xn2 = P.bufs(2, "xn")
                B_sq = P.bufs(2, "sq")
                B_rstd2 = P.bufs(2, "rstd")
                B_sg = P.bufs(2, "sg")
                B_u0 = P.buf("u0")
                B_cv2 = P.bufs(2, "cv")
                B_sqB = P.bufs(2, "sqB")
                B_cvb = P.bufs(2, "cvb")
                B_mean = P.buf("mean")
                B_lrs = P.buf("lrs")
                B_tmp = P.bufs(2, "tmp")
                B_ub = P.bufs(2, "ub")
                for k in range(8):
                    load_w(wcv[:, k, :], hyb_win[sl(k), 0:2048], B_w[k])
                P.dma("sp", cw[:], cw_d, writes=[B_cw], slot=B_cw)
                for c in range(8):
                    P.op("pool", lambda e, c=c: e.tensor_tensor(
                        dg[:, c * CW:(c + 1) * CW, :],
                        identb[:].unsqueeze(1).broadcast_to([128, CW, 128]),
                        cw[:, c, :].unsqueeze(2).broadcast_to([128, CW, 128]), ALU.mult),
                        reads=[B_cw, Bc], writes=[B_dg])
                P.op("dve", lambda e: e.memset(u0[:, :, 0:30], 0.0), writes=[B_u0])
                def prologue(it_):
                    s_ = it_ % 2
                    P.dma("sp", ht[s_][:], Hv[:, :, it_ * T:(it_ + 1) * T], reads=[B_H[it_]], writes=[B_ht[s_]], slot=B_ht[s_])
                    make_xn(ht[s_], B_ht[s_], V_NMIX + 0, xn2[s_], B_xn2[s_], sqt, B_sq, rstd2[s_], B_rstd2[s_], pb[5], Bpb[5])

                prologue(0)
                def ln_b(it_):
                    s_ = it_ % 2
                    cv = cv2[s_]
                    Bcv = B_cv2[s_]
                    mbank = 6 + s_
                    Bd = {}

                    def pre():
                        P.op("act", lambda e: e.copy(mean[:], pb[mbank][:, 0:T]), reads=[Bpb[mbank]], writes=[B_mean])

                    def var(c):
                        q = c % 2
                        P.op("dve", lambda e: e.tensor_tensor(tmp[q][:], cv[:, c, :], mean[:], ALU.subtract),
                             reads=[Bcv, B_mean], writes=[B_tmp[q]])
                        P.op("act", lambda e: e.activation(sqB[:, q, :], tmp[q][:], AF.Square),
                             reads=[B_tmp[q]], writes=[B_sqB[q]])
                        if c > 0:
                            vstat(c - 1)

                    def vstat(c):
                        q = c % 2
                        P.op("pe", lambda e: e.matmul(pb[4][:, 0:T], ones1k[:], sqB[:, q, :],
                                                      start=(c == 0), stop=(c == 7)),
                             reads=[B_sqB[q], Bc], writes=[Bpb[4]])

                    def mid():
                        vstat(7)
                        P.op("act", lambda e: e.activation(lrs[:], pb[4][:, 0:T], AF.Sqrt, bias=epsc, scale=1.0),
                             reads=[Bpb[4], Bc], writes=[B_lrs])
                        P.op("dve", lambda e: e.reciprocal(lrs[:], lrs[:]), reads=[B_lrs], writes=[B_lrs])

                    def fin(c):
                        q = c % 2
                        P.op("dve", lambda e: e.tensor_tensor(tmp[q][:], cv[:, c, :], mean[:], ALU.subtract),
                             reads=[Bcv, B_mean], writes=[B_tmp[q]])
                        P.op("dve", lambda e: e.tensor_tensor(tmp[q][:], tmp[q][:], lrs[:], ALU.mult),
                             reads=[B_lrs, B_tmp[q]], writes=[B_tmp[q]])
                        P.op("act", lambda e: e.activation(
                            ub[s_][:, c, :], tmp[q][:], AF.Silu, bias=vec[:, V_LNB, c:c + 1],
                            scale=vec[:, V_LNG, c:c + 1]),
                            reads=[B_tmp[q], Bc], writes=[B_ub[s_]])

                    def store():
                        P.dma("sp", Uv[:, :, it_ * T:(it_ + 1) * T], ub[s_][:], reads=[B_ub[s_]],
                              writes=[B_U[it_]], slot=B_ub[s_])
                    Bd.update(pre=pre, var=var, mid=mid, fin=fin, store=store)
                    return Bd

                for it in range(nt):
                    s = it % 2
                    t0 = it * T
                    hs = ht[s]
                    xn = xn2[s]
                    B_xn = B_xn2[s]
                    cv = cv2[s]
                    Bcv = B_cv2[s]
                    mbank = 6 + s
                    Bp_ = ln_b(it - 1) if it > 0 else None
                    if it > 0:
                        P.op("dve", lambda e: e.tensor_copy(u0[:, :, 0:30], u0[:, :, T:T + 30]),
                             reads=[B_u0], writes=[B_u0])
                    if Bp_:
                        Bp_["pre"]()
                    for c in range(8):
                        jb = c % 2
                        P.op("pe", [lambda e, k=k, c=c, jb=jb, xn=xn: e.matmul(
                            pb[jb][:, 0:T], wcv[:, k, sl(c)], xn[:, k, :], start=(k == 0), stop=(k == 7))
                            for k in range(8)], reads=[B_xn] + B_w, writes=[Bpb[jb]])
                        P.op("pe", [lambda e, k=k, c=c, jb=jb, xn=xn: e.matmul(
                            pb[2 + jb][:, 0:T], wcv[:, k, 1024 + c * 128:1024 + (c + 1) * 128], xn[:, k, :],
                            start=(k == 0), stop=(k == 7))
                            for k in range(8)], reads=[B_xn] + B_w, writes=[Bpb[2 + jb]])
                        P.op("act", lambda e, jb=jb: e.activation(sg[jb][:], pb[2 + jb][:, 0:T], AF.Tanh, scale=0.5),
                             reads=[Bpb[2 + jb]], writes=[B_sg[jb]])
                        P.op("dve", lambda e, c=c, jb=jb: e.scalar_tensor_tensor(
                            u0[:, c, 30:30 + T], sg[jb][:], 1.0, pb[jb][:, 0:T], ALU.add, ALU.mult),
                            reads=[B_sg[jb], Bpb[jb]], writes=[B_u0])
                        if Bp_:
                            Bp_["var"](c)
                    if Bp_:
                        Bp_["mid"]()
                    if it + 1 < nt:
                        prologue(it + 1)
                    for c in range(8):
                        mb = 4 + c % 2
                        q = c % 2
                        P.op("pe", [lambda e, k=k, c=c, mb=mb: e.matmul(
                            pb[mb][:, 0:T], dg[:, c * CW + k, :], u0[:, c, k:k + T],
                            start=(k == 0), stop=(k == CW - 1))
                            for k in range(CW)], reads=[B_u0, B_dg], writes=[Bpb[mb]])
                        P.op("act", lambda e, c=c, mb=mb, cv=cv: e.activation(
                            cv[:, c, :], pb[mb][:, 0:T], AF.Identity, bias=vec[:, V_CVB, c:c + 1], scale=0.5),
                            reads=[Bpb[mb], Bc], writes=[Bcv])
                        P.op("dve", lambda e, c=c, q=q, cv=cv: e.tensor_copy(cvb[:, q, :], cv[:, c, :]),
                             reads=[Bcv], writes=[B_cvb[q]])
                        P.op("pe", lambda e, c=c, q=q, mbank=mbank: e.matmul(
                            pb[mbank][:, 0:T], ones1k[:], cvb[:, q, :], start=(c == 0), stop=(c == 7)),
                            reads=[B_cvb[q], Bc], writes=[Bpb[mbank]])
                        if Bp_:
                            Bp_["fin"](c)
                    if Bp_:
                        Bp_["store"]()
                Bl_ = ln_b(nt - 1)
                Bl_["pre"]()
                for c in range(8):
                    Bl_["var"](c)
                Bl_["mid"]()
                for c in range(8):
                    Bl_["fin"](c)
                Bl_["store"]()
                P.end_phase()

        def ssd_phase():
            with ExitStack() as pes:
                sb = lambda name, shape, dt: pes.enter_context(nc.sbuf_tensor(uniq(name), shape, dt))
                NZ = HYB_IN - 2048
                wz = sb("wz", [128, 8, NZ], BF16)
                wo = sb("wo", [128, 16, D], BF16)
                scw = sb("scw", [128, 12, 5], F32)
                h16 = sb("h16", [128, 3, 16], F32)
                abc = sb("abc", [128, 16], F32)
                ht = [sb(f"ht{i}", [128, 8, T], F32) for i in range(2)]
                xn2 = [sb(f"xn{i}", [128, 8, T], BF16) for i in range(2)]
                sqt = sb("sqt", [128, 2, T], BF16)
                rstd2 = [sb(f"rstd{i}", [128, T], F32) for i in range(2)]
                sz = sb("sz", [128, 8, T], F32)
                xb = sb("xb", [128, 12, 3 + T], BF16)
                dg4 = sb("dg4", [128, 48, 128], BF16)
                xsf = sb("xsf", [128, 8, T], F32)
                xsb = sb("xsb", [128, 8, T], BF16)
                bcb = sb("bcb", [128, 4, T], BF16)
                dtt = sb("dtt", [128, 16], F32)
                adt = sb("adt", [128, 16], F32)
                acs = sb("acs", [128, 16], F32)
                ala = sb("ala", [128, 16], F32)
                cdec = sb("cdec", [128, 16], F32)
                coef = sb("coef", [128, 16], F32)
                xdt = sb("xdt", [128, 16, 64], BF16)
                xdd = sb("xdd", [128, 16, 64], BF16)
                btm = sb("btm", [128, 2, 128], BF16)
                R2 = [sb(f"R{i}", [128, 8, 128], F32) for i in range(2)]
                dif = sb("dif", [128, 8, 128], F32)
                erow = sb("erow", [128, 8, 128], F32)
                MT = sb("MT", [128, 8, 128], BF16)
                Cs = sb("Cs", [128, 8, 128], BF16)
                cbm = sb("cbm", [128, 2, 128], F32)
                prev = sb("prev", [128, 16, 64], F32)
                prevb = sb("prevb", [128, 16, 64], BF16)
                yg = sb("yg", [128, 8, T], F32)
                yn = sb("yn", [128, 8, T], BF16)
                grs = [sb(f"grs{i}", [128, T], F32) for i in range(2)]
                ut = sb("ut", [128, 8, T], BF16)
                B_wz = P.bufs(8, "wz")
                B_wo = P.bufs(2, "wo")
                B_sm = P.buf("small")
                B_ht = P.bufs(2, "ht")
                B_xn2 = P.bufs(2, "xn")
                B_sq = P.bufs(2, "sq")
                B_rstd2 = P.bufs(2, "rstd")
                B_sz = P.buf("sz")
                B_xb = P.buf("xb")
                B_dg4 = P.buf("dg4")
                B_xsf = P.bufs(8, "xsf")
                B_xsb = P.bufs(8, "xsb")
                B_bc = P.buf("bcb")
                B_dt = P.buf("dt")
                B_co = P.buf("coefs")
                B_xdt = P.buf("xdt")
                B_btm = P.buf("btm")
                B_R2 = P.bufs(2, "R")
                B_dif = P.buf("dif")
                B_er = P.buf("erow")
                B_MT = P.buf("MT")
                B_Cs = P.buf("Cs")
                B_cbm = P.buf("cbm")
                B_prev = P.buf("prev")
                B_prevb = P.buf("prevb")
                B_yg = P.buf("yg")
                B_yn = P.buf("yn")
                B_grs = P.bufs(2, "grs")
                B_ut = P.buf("ut")
                for k in range(8):
                    load_w(wz[:, k, :], hyb_win[sl(k), 2048:HYB_IN], B_wz[k])
                wo_v = hyb_wout.rearrange("(j p) m -> p j m", p=128)
                for hh in range(2):
                    load_w(wo[:, hh * 8:(hh + 1) * 8, :], wo_v[:, hh * 8:(hh + 1) * 8, :], B_wo[hh])
                P.dma("sp", scw[:], scw_d, writes=[B_sm], slot=B_sm)
                for i in range(3):
                    P.dma("sp", h16[:, i, :], h16_d[i].partition_broadcast(128), writes=[B_sm], slot=B_sm)
                P.op("act", lambda e: e.activation(abc[:], h16[:, 1, :], AF.Exp), reads=[B_sm], writes=[B_sm])
                P.op("dve", lambda e: e.tensor_scalar_mul(abc[:], abc[:], -1.0), reads=[B_sm], writes=[B_sm])
                for c in range(12):
                    P.op("pool", lambda e, c=c: e.tensor_tensor(
                        dg4[:, c * 4:(c + 1) * 4, :],
                        identb[:].unsqueeze(1).broadcast_to([128, 4, 128]),
                        scw[:, c, 0:4].unsqueeze(2).broadcast_to([128, 4, 128]), ALU.mult),
                        reads=[B_sm, Bc], writes=[B_dg4])
                P.op("dve", lambda e: e.memset(xb[:, :, 0:3], 0.0), writes=[B_xb])
                P.op("dve", lambda e: e.memset(prev[:], 0.0), writes=[B_prev])
                P.op("dve", lambda e: e.memset(prevb[:], 0.0), writes=[B_prevb])
                def prologue(it_):
                    s_ = it_ % 2
                    P.dma("sp", ht[s_][:], Hv[:, :, it_ * T:(it_ + 1) * T], reads=[B_H[it_]], writes=[B_ht[s_]], slot=B_ht[s_])
                    make_xn(ht[s_], B_ht[s_], V_NMIX + 0, xn2[s_], B_xn2[s_], sqt, B_sq, rstd2[s_], B_rstd2[s_], pb[6], Bpb[6])

                prologue(0)
                for it in range(nt):
                    s = it % 2
                    t0 = it * T
                    hs = ht[s]
                    xn = xn2[s]
                    B_xn = B_xn2[s]
                    P.dma("sp", ut[:], Uv[:, :, t0:t0 + T], reads=[B_U[it]], writes=[B_ut], slot=B_ut)
                    if it > 0:
                        P.op("dve", lambda e: e.tensor_copy(xb[:, :, 0:3], xb[:, :, T:T + 3]),
                             reads=[B_xb], writes=[B_xb])
                    for c in range(8):
                        jb = c % 2
                        P.op("pe", [lambda e, k=k, c=c, jb=jb, xn=xn: e.matmul(
                            pb[jb][:, 0:T], wz[:, k, sl(c)], xn[:, k, :], start=(k == 0), stop=(k == 7))
                            for k in range(8)], reads=[B_xn] + B_wz, writes=[Bpb[jb]])
                        P.op("act", lambda e, c=c, jb=jb: e.activation(sz[:, c, :], pb[jb][:, 0:T], AF.Silu),
                             reads=[Bpb[jb]], writes=[B_sz])
                    for c in range(12):
                        jb = c % 2
                        P.op("pe", [lambda e, k=k, c=c, jb=jb, xn=xn: e.matmul(
                            pb[jb][:, 0:T], wz[:, k, 1024 + c * 128:1024 + (c + 1) * 128], xn[:, k, :],
                            start=(k == 0), stop=(k == 7))
                            for k in range(8)], reads=[B_xn] + B_wz, writes=[Bpb[jb]])
                        P.op("act", lambda e, c=c, jb=jb: e.copy(xb[:, c, 3:3 + T], pb[jb][:, 0:T]),
                             reads=[Bpb[jb]], writes=[B_xb])
                    for c in range(12):
                        mb = 2 + c % 2
                        P.op("pe", [lambda e, c=c, k=k, mb=mb: e.matmul(
                            pb[mb][:, 0:T], dg4[:, c * 4 + k, :], xb[:, c, k:k + T], start=(k == 0), stop=(k == 3))
                            for k in range(4)], reads=[B_xb, B_dg4], writes=[Bpb[mb]])
                        if c < 8:
                            P.op("act", lambda e, c=c, mb=mb: e.activation(
                                xsf[:, c, :], pb[mb][:, 0:T], AF.Silu, bias=scw[:, c, 4:5], scale=1.0),
                                reads=[Bpb[mb], B_sm], writes=[B_xsf[c]])
                            P.op("pool", lambda e, c=c: e.tensor_copy(xsb[:, c, :], xsf[:, c, :]),
                                 reads=[B_xsf[c]], writes=[B_xsb[c]])
                        else:
                            P.op("act", lambda e, c=c, mb=mb: e.activation(
                                bcb[:, c - 8, :], pb[mb][:, 0:T], AF.Silu, bias=scw[:, c, 4:5], scale=1.0),
                                reads=[Bpb[mb], B_sm], writes=[B_bc])
                    if it + 1 < nt:
                        prologue(it + 1)
                    for cch in range(NB if STG >= 2 else 0):
                        csl = slice(cch * 128, (cch + 1) * 128)
                        P.op("pe", [lambda e, k=k, csl=csl, xn=xn: e.matmul(
                            pb[5][:, 256:272], xn[:, k, csl], wz[:, k, 2560:2576], start=(k == 0), stop=(k == 7))
                            for k in range(8)], reads=[B_xn] + B_wz, writes=[Bpb[5]])
                        P.op("dve", lambda e: e.tensor_tensor(dtt[:], pb[5][:, 256:272], h16[:, 0, :], ALU.add),
                             reads=[Bpb[5], B_sm], writes=[B_dt])
                        P.op("act", lambda e: e.activation(dtt[:], dtt[:], AF.Exp), reads=[B_dt], writes=[B_dt])
                        P.op("act", lambda e: e.activation(dtt[:], dtt[:], AF.Ln, bias=onec, scale=1.0),
                             reads=[B_dt, Bc], writes=[B_dt])
                        P.op("dve", lambda e: e.tensor_tensor(adt[:], dtt[:], abc[:], ALU.mult),
                             reads=[B_dt, B_sm], writes=[B_dt])
                        P.op("pe", [lambda e: e.matmul(pb[5][:, 272:288], triu[:], adt[:], start=True, stop=True),
                                    lambda e: e.matmul(pb[5][:, 288:304], ones_f[:], adt[:], start=True, stop=True)],
                             reads=[B_dt, Bc], writes=[Bpb[5]])
                        P.op("dve", lambda e: e.tensor_copy(acs[:], pb[5][:, 272:288]), reads=[Bpb[5]], writes=[B_co])
                        P.op("dve", lambda e: e.tensor_copy(ala[:], pb[5][:, 288:304]), reads=[Bpb[5]], writes=[B_co])
                        P.op("act", lambda e: e.activation(cdec[:], ala[:], AF.Exp), reads=[B_co], writes=[B_co])
                        P.op("dve", lambda e: e.tensor_tensor(coef[:], ala[:], acs[:], ALU.subtract),
                             reads=[B_co], writes=[B_co])
                        P.op("act", lambda e: e.activation(coef[:], coef[:], AF.Exp), reads=[B_co], writes=[B_co])
                        P.op("dve", lambda e: e.tensor_tensor(coef[:], coef[:], dtt[:], ALU.mult),
                             reads=[B_co, B_dt], writes=[B_co])
                        if STG < 3:
                            continue
                        pbt = pb[4][:].bitcast(BF16)
                        P.op("pe", [lambda e, c=c, csl=csl: e.transpose(pbt[:, sl(c)], xsb[:, c, csl], identb[:])
                                    for c in range(8)], reads=B_xsb + [Bc], writes=[Bpb[4]])
                        pbt3 = pbt.rearrange("p (h d) -> p h d", h=16)
                        P.op("dve", lambda e: e.tensor_tensor(
                            xdt[:], pbt3, dtt[:].unsqueeze(2).broadcast_to([128, 16, 64]), ALU.mult),
                            reads=[Bpb[4], B_dt], writes=[B_xdt])
                        P.op("dve", lambda e: e.tensor_tensor(
                            xdd[:], pbt3, coef[:].unsqueeze(2).broadcast_to([128, 16, 64]), ALU.mult),
                            reads=[Bpb[4], B_co], writes=[B_xdt])
                        P.op("pe", [lambda e, g=g, csl=csl: e.transpose(pbt[:, sl(g)], bcb[:, g, csl], identb[:])
                                    for g in range(2)], reads=[B_bc, Bc], writes=[Bpb[4]])
                        P.op("act", lambda e: e.copy(btm[:], pbt[:, 0:256].rearrange("p (g n) -> p g n", g=2)),
                             reads=[Bpb[4]], writes=[B_btm])
                        P.op("pe", [lambda e, g=g, csl=csl: e.matmul(
                            pb[5][:, sl(g)], bcb[:, g, csl], bcb[:, 2 + g, csl], start=True, stop=True)
                            for g in range(2)], reads=[B_bc], writes=[Bpb[5]])
                        P.op("dve", lambda e: e.tensor_tensor(
                            cbm[:], pb[5][:, 0:256].rearrange("p (g n) -> p g n", g=2),
                            triu[:].unsqueeze(1).broadcast_to([128, 2, 128]), ALU.mult),
                            reads=[Bpb[5], Bc], writes=[B_cbm])
                        if STG < 4:
                            continue
                        rbk = [(0, 1), (6, 7)]
                        for g in range(2):
                            hsl = slice(g * 8, (g + 1) * 8)
                            P.op("pool", lambda e, hsl=hsl, g=g: e.tensor_tensor(
                                R2[g][:], triu[:].unsqueeze(1).broadcast_to([128, 8, 128]),
                                adt[:, hsl].unsqueeze(2).broadcast_to([128, 8, 128]), ALU.mult),
                                reads=[B_dt, Bc], writes=[B_R2[g]])
                            P.op("pe", [lambda e, h2=h2, g=g: e.matmul(
                                pb[rbk[g][h2 // 2]][:, (h2 % 2) * 256:(h2 % 2) * 256 + 256], ones_f[:],
                                R2[g][:, h2 * 2:(h2 + 1) * 2, :], start=True, stop=True)
                                for h2 in range(4)], reads=[B_R2[g], Bc], writes=[Bpb[rbk[g][0]], Bpb[rbk[g][1]]])
                        for g in range(2):
                            hsl = slice(g * 8, (g + 1) * 8)
                            for hh in range(2):
                                h4 = slice(hh * 4, (hh + 1) * 4)
                                a4 = slice(g * 8 + hh * 4, g * 8 + hh * 4 + 4)
                                bk = rbk[g][hh]
                                rb = pb[bk][:].rearrange("p (h l) -> p h l", h=4)
                                P.op("dve", lambda e, h4=h4, a4=a4, rb=rb: e.tensor_tensor(
                                    dif[:, h4, :], rb, acs[:, a4].unsqueeze(2).broadcast_to([128, 4, 128]),
                                    ALU.subtract), reads=[Bpb[bk], B_co], writes=[B_dif])
                                P.op("act", lambda e, h4=h4, rb=rb: e.activation(erow[:, h4, :], rb, AF.Exp),
                                     reads=[Bpb[bk]], writes=[B_er])
                            if SUB != 1:
                                P.op("act", lambda e: e.activation(dif[:], dif[:], AF.Exp), reads=[B_dif], writes=[B_dif])
                            P.op("dve", lambda e, g=g: e.scalar_tensor_tensor(
                                MT[:], dif[:], 1.0, cbm[:, g, :].unsqueeze(1).broadcast_to([128, 8, 128]),
                                ALU.min, ALU.mult), reads=[B_dif, B_cbm], writes=[B_MT])
                            P.op("pool", lambda e, g=g, csl=csl: e.tensor_tensor(
                                Cs[:], erow[:], bcb[:, 2 + g, csl].unsqueeze(1).broadcast_to([128, 8, 128]),
                                ALU.mult), reads=[B_er, B_bc], writes=[B_Cs])
                            for hh in range(8 if STG >= 5 else 0):
                                h = g * 8 + hh
                                cch_out = h // 2
                                half = h % 2
                                bank = pb[2 + cch_out // 4]
                                col = (cch_out % 4) * 128
                                P.op("pe", [
                                    lambda e, h=h, hh=hh, half=half, bank=bank, col=col: e.matmul(
                                        bank[half * 64:(half + 1) * 64, col:col + 128], xdt[:, h, :], MT[:, hh, :],
                                        start=True, stop=False),
                                    lambda e, h=h, hh=hh, half=half, bank=bank, col=col: e.matmul(
                                        bank[half * 64:(half + 1) * 64, col:col + 128], prevb[:, h, :], Cs[:, hh, :],
                                        start=False, stop=True)],
                                    reads=[B_xdt, B_MT, B_prevb, B_Cs], writes=[Bpb[2 + cch_out // 4]])
                            if SUB != 2:
                              P.op("pe", lambda e, g=g: e.matmul(
                                pb[4 + g][:], btm[:, g, :], xdd[:, g * 8:(g + 1) * 8, :], start=True, stop=True),
                                reads=[B_btm, B_xdt], writes=[Bpb[4 + g]])
                        for g in range(2 if SUB != 2 else 0):
                            hsl = slice(g * 8, (g + 1) * 8)
                            P.op("dve", lambda e, hsl=hsl: e.tensor_tensor(
                                prev[:, hsl, :], prev[:, hsl, :],
                                cdec[:, hsl].unsqueeze(2).broadcast_to([128, 8, 64]), ALU.mult),
                                reads=[B_co, B_prev], writes=[B_prev])
                            P.op("dve", lambda e, hsl=hsl, g=g: e.tensor_tensor(
                                prev[:, hsl, :], prev[:, hsl, :],
                                pb[4 + g][:].rearrange("p (h d) -> p h d", h=8), ALU.add),
                                reads=[Bpb[4 + g], B_prev], writes=[B_prev])
                        P.op("act", lambda e: e.copy(prevb[:], prev[:]), reads=[B_prev], writes=[B_prevb])
                        for c in range(8):
                            bank = pb[2 + c // 4]
                            col = (c % 4) * 128
                            P.op("dve", lambda e, c=c, bank=bank, col=col, csl=csl: e.scalar_tensor_tensor(
                                yg[:, c, csl], xsf[:, c, csl], vec[:, V_SSD, c:c + 1], bank[:, col:col + 128],
                                ALU.mult, ALU.add), reads=[Bpb[2 + c // 4], B_xsf[c], Bc], writes=[B_yg])
                    P.op("pool", lambda e: e.tensor_tensor(yg[:], yg[:], sz[:], ALU.mult),
                         reads=[B_sz, B_yg], writes=[B_yg])
                    for g in range(2):
                        rms_rstd(yg, B_yg, 4, ones512, sqt, B_sq, pb[6], Bpb[6], grs[g][:], B_grs[g], k0=g * 4)
                    for c in range(8):
                        P.op("dve", lambda e, c=c: e.scalar_tensor_tensor(
                            yn[:, c, :], yg[:, c, :], vec[:, V_SSN, c:c + 1], grs[c // 4][:], ALU.mult, ALU.mult),
                            reads=[B_yg, Bc, B_grs[c // 4]], writes=[B_yn])
                    for m in range(8):
                        mb = m % 2
                        P.op("pe", [lambda e, j=j, m=m, mb=mb: e.matmul(
                            pb[mb][:, 0:T], wo[:, j, sl(m)], (ut[:, j, :] if j < 8 else yn[:, j - 8, :]),
                            start=(j == 0), stop=(j == 15))
                            for j in range(16)], reads=[B_ut, B_yn] + B_wo, writes=[Bpb[mb]])
                        P.op("dve", lambda e, m=m, mb=mb, hs=hs: e.tensor_tensor(
                            hs[:, m, :], hs[:, m, :], pb[mb][:, 0:T], ALU.add),
                            reads=[Bpb[mb], B_ht[s]], writes=[B_ht[s]])
                    P.dma("sp", Hv[:, :, t0:t0 + T], hs[:], reads=[B_ht[s]], writes=[B_H[it]], slot=B_ht[s])
                P.end_phase()

        def att_phase():
            with ExitStack() as pes:
                sb = lambda name, shape, dt: pes.enter_context(nc.sbuf_tensor(uniq(name), shape, dt))
                wq = sb("wq", [128, 8, 1536], BF16)
                wo = sb("wo", [128, 8, D], BF16)
                bqk = sb("bqk", [128, 10], F32)
                bvb = sb("bvb", [128, 256], F32)
                snk = sb("snk", [128, 16], F32)
                mskb = sb("mskb", [128, 2, 256], BF16)
                ht = [sb(f"ht{i}", [128, 8, T], F32) for i in range(2)]
                xn2 = [sb(f"xn{i}", [128, 8, T], BF16) for i in range(2)]
                sqt = sb("sqt", [128, 2, T], BF16)
                rstd2 = [sb(f"rstd{i}", [128, T], F32) for i in range(2)]
                rope = sb("rope", [128, 2, T], F32)
                qf = [sb(f"qf{i}", [128, T], F32) for i in range(2)]
                qb = [sb(f"qb{i}", [128, T], BF16) for i in range(2)]
                t1 = [sb(f"t1{i}", [128, T], F32) for i in range(2)]
                qr = sb("qr", [128, 8, T], BF16)
                kr = sb("kr", [128, 2, 128 + T], BF16)
                vt = sb("vt", [128, 1 + NB, 256], BF16)
                ee = [sb(f"ee{i}", [128, 4, 256], F32) for i in range(2)]
                pp = [sb(f"pp{i}", [128, 4, 256], BF16) for i in range(2)]
                pT = [sb(f"pT{i}", [128, 8, 128], BF16) for i in range(2)]
                mx = [sb(f"mx{i}", [128, 4], F32) for i in range(2)]
                nmx = [sb(f"nmx{i}", [128, 4], F32) for i in range(2)]
                rs = [sb(f"rs{i}", [128, 4], F32) for i in range(2)]
                es_ = [sb(f"es_{i}", [128, 4], F32) for i in range(2)]
                oT = sb("oT", [128, 8, T], BF16)
                B_wq = P.bufs(8, "wq")
                B_wo = P.buf("wo")
                B_sm_ = P.buf("small")
                B_ht = P.bufs(2, "ht")
                B_xn2 = P.bufs(2, "xn")
                B_sq = P.bufs(2, "sq")
                B_rstd2 = P.bufs(2, "rstd")
                B_rope = P.buf("rope")
                B_qf = P.bufs(2, "qf")
                B_qb = P.bufs(2, "qb")
                B_t1 = P.bufs(2, "t1")
                B_qr = P.buf("qr")
                B_kr = P.buf("kr")
                B_vt = P.buf("vt")
                B_s = P.bufs(2, "sm")
                B_e = P.bufs(2, "ee")
                B_p = P.bufs(2, "pp")
                B_pT = P.bufs(2, "pT")
                B_st = P.bufs(2, "stats")
                B_oT = P.buf("oT")
                for k in range(8):
                    load_w(wq[:, k, :], wqkv_d[sl(k), :], B_wq[k])
                load_w(wo[:], wo_d.rearrange("(k p) m -> p k m", p=128), B_wo)
                P.dma("sp", bqk[:], bqk_d, writes=[B_sm_], slot=B_sm_)
                P.dma("sp", bvb[:], bv_d.partition_broadcast(128), writes=[B_sm_], slot=B_sm_)
                P.dma("sp", snk[:], h16_d[2].partition_broadcast(128), writes=[B_sm_], slot=B_sm_)
                P.dma("pool", mskb[:], msk_d.rearrange("a p s -> p a s"), writes=[B_sm_], slot=B_sm_)
                P.op("dve", lambda e: e.memset(kr[:, :, 0:128], 0.0), writes=[B_kr])
                P.op("dve", lambda e: e.memset(vt[:, 0, :], 0.0), writes=[B_vt])
                def prologue(it_):
                    s_ = it_ % 2
                    P.dma("sp", ht[s_][:], Hv[:, :, it_ * T:(it_ + 1) * T], reads=[B_H[it_]], writes=[B_ht[s_]], slot=B_ht[s_])
                    make_xn(ht[s_], B_ht[s_], V_NMIX + 1, xn2[s_], B_xn2[s_], sqt, B_sq, rstd2[s_], B_rstd2[s_], pb[6], Bpb[6])

                prologue(0)
                for it in range(nt):
                    s = it % 2
                    t0 = it * T
                    hs = ht[s]
                    xn = xn2[s]
                    B_xn = B_xn2[s]
                    P.dma("sp", rope[:], rope_d[:, :, t0:t0 + T].rearrange("a p s -> p a s"),
                          writes=[B_rope], slot=B_rope)
                    if it > 0:
                        P.op("dve", lambda e: e.tensor_copy(kr[:, :, 0:128], kr[:, :, T:T + 128]),
                             reads=[B_kr], writes=[B_kr])
                        P.op("dve", lambda e: e.tensor_copy(vt[:, 0, :], vt[:, NB, :]), reads=[B_vt], writes=[B_vt])
                    for c in range(10):
                        jb = c % 2
                        P.op("pe", [lambda e, k=k, c=c, jb=jb, xn=xn: e.matmul(
                            pb[jb][:, 0:T], wq[:, k, sl(c)], xn[:, k, :], start=(k == 0), stop=(k == 7))
                            for k in range(8)], reads=[B_xn] + B_wq, writes=[Bpb[jb]])
                        P.op("act", lambda e, c=c, jb=jb: e.activation(
                            qf[jb][:], pb[jb][:, 0:T], AF.Identity, bias=bqk[:, c:c + 1], scale=1.0),
                            reads=[Bpb[jb], B_sm_], writes=[B_qf[jb]])
                        P.op("pe", lambda e, jb=jb: e.matmul(pb[2 + jb][:, 0:T], pswapf[:], qf[jb][:], start=True, stop=True),
                             reads=[B_qf[jb], Bc], writes=[Bpb[2 + jb]])
                        P.op("dve", lambda e, jb=jb: e.tensor_tensor(t1[jb][:], qf[jb][:], rope[:, 0, :], ALU.mult),
                             reads=[B_qf[jb], B_rope], writes=[B_t1[jb]])
                        P.op("dve", lambda e, jb=jb: e.tensor_tensor(qf[jb][:], pb[2 + jb][:, 0:T], rope[:, 1, :], ALU.mult),
                             reads=[Bpb[2 + jb], B_rope, B_qf[jb]], writes=[B_qf[jb]])
                        dst = qr[:, c, :] if c < 8 else kr[:, c - 8, 128:128 + T]
                        P.op("dve", lambda e, jb=jb, dst=dst: e.tensor_tensor(dst, t1[jb][:], qf[jb][:], ALU.add),
                             reads=[B_t1[jb], B_qf[jb]], writes=[B_qr if c < 8 else B_kr])
                    for b in range(NB):
                        jb = b % 2
                        P.op("pe", [lambda e, k=k, b=b, jb=jb, xn=xn: e.matmul(
                            pb[jb][:, 0:256], xn[:, k, sl(b)], wq[:, k, 1280:1536], start=(k == 0), stop=(k == 7))
                            for k in range(8)], reads=[B_xn] + B_wq, writes=[Bpb[jb]])
                        P.op("dve", lambda e, b=b, jb=jb: e.tensor_tensor(vt[:, 1 + b, :], pb[jb][:, 0:256], bvb[:], ALU.add),
                             reads=[Bpb[jb], B_sm_], writes=[B_vt])
                    if it + 1 < nt:
                        prologue(it + 1)
                    def mk_group(b, kv, a):
                        gblk = it * NB + b
                        mi = 0 if gblk == 0 else 1
                        half = kv % 2
                        hp = slice(half * 64, (half + 1) * 64)
                        kc = kv // 2
                        qc0 = (kv // 2) * 4
                        S0, S1, PTb, Ob = 4 * a, 4 * a + 1, 4 * a + 2, 4 * a + 3
                        ee_, pp_, pT_ = ee[a], pp[a], pT[a]
                        mx_, nmx_, rs_, es2 = mx[a], nmx[a], rs[a], es_[a]
                        Be, Bp, BpT, Bst = B_e[a], B_p[a], B_pT[a], B_st[a]
                        sg4 = snk[:, kv * 4:(kv + 1) * 4]
                        pbt = pb[PTb][:].bitcast(BF16)
                        G = {}

                        def pe_s():
                            fns = []
                            for i in range(4):
                                dst = pb[S0 + i // 2][:, (i % 2) * 256:(i % 2) * 256 + 256]
                                fns.append(lambda e, i=i, dst=dst: e.matmul(
                                    dst, qr[hp, qc0 + i, sl(b)], kr[hp, kc, b * 128:b * 128 + 256],
                                    start=True, stop=False))
                                fns.append(lambda e, dst=dst: e.matmul(
                                    dst, identb[:], mskb[:, mi, :], start=False, stop=True))
                            P.op("pe", fns, reads=[B_qr, B_kr, B_sm_, Bc], writes=[Bpb[S0], Bpb[S1]])

                        def d1():
                            for hh in range(2):
                                P.op("dve", lambda e, hh=hh: e.tensor_reduce(
                                    mx_[:, hh * 2:hh * 2 + 2], pb[S0 + hh][:].rearrange("p (h s) -> p h s", h=2),
                                    AX.X, ALU.max), reads=[Bpb[S0 + hh]], writes=[Bst])
                            P.op("dve", lambda e: e.scalar_tensor_tensor(mx_[:], mx_[:], 0.125, sg4, ALU.mult, ALU.max),
                                 reads=[Bst, B_sm_], writes=[Bst])
                            P.op("dve", lambda e: e.tensor_scalar_mul(nmx_[:], mx_[:], -1.0), reads=[Bst], writes=[Bst])
                            P.op("dve", lambda e: e.tensor_tensor(es2[:], sg4, mx_[:], ALU.subtract),
                                 reads=[Bst, B_sm_], writes=[Bst])
                            P.op("dve", lambda e: e.memset(rs_[:], 0.0), writes=[Bst])

                        def a1():
                            for i in range(4):
                                src = pb[S0 + i // 2][:, (i % 2) * 256:(i % 2) * 256 + 256]
                                P.op("act", lambda e, i=i, src=src: e.activation(
                                    ee_[:, i, :], src, AF.Exp, bias=nmx_[:, i:i + 1], scale=0.125,
                                    accum_out=rs_[:, i:i + 1]), reads=[Bpb[S0 + i // 2], Bst], writes=[Be, Bst])
                            P.op("act", lambda e: e.activation(es2[:], es2[:], AF.Exp), reads=[Bst], writes=[Bst])

                        def d2():
                            P.op("dve", lambda e: e.tensor_tensor(rs_[:], rs_[:], es2[:], ALU.add), reads=[Bst], writes=[Bst])
                            P.op("dve", lambda e: e.reciprocal(rs_[:], rs_[:]), reads=[Bst], writes=[Bst])
                            P.op("dve", lambda e: e.tensor_tensor(
                                pp_[:], ee_[:], rs_[:].unsqueeze(2).broadcast_to([128, 4, 256]), ALU.mult),
                                reads=[Be, Bst], writes=[Bp])

                        def pe_t():
                            P.op("pe", [lambda e, i=i, kb=kb: e.transpose(
                                pbt[:, sl(i * 2 + kb)], pp_[:, i, sl(kb)], identb[:])
                                for i in range(4) for kb in range(2)], reads=[Bp, Bc], writes=[Bpb[PTb]])

                        def a2():
                            P.op("act", lambda e: e.copy(pT_[:], pbt.rearrange("p (a q) -> p a q", a=8)),
                                 reads=[Bpb[PTb]], writes=[BpT])

                        def pe_pv():
                            P.op("pe", [lambda e, i=i, kb=kb: e.matmul(
                                pb[Ob][hp, i * 128:(i + 1) * 128], vt[:, b + kb, kv * 64:(kv + 1) * 64],
                                pT_[:, i * 2 + kb, :], start=(kb == 0), stop=(kb == 1))
                                for i in range(4) for kb in range(2)], reads=[B_vt, BpT], writes=[Bpb[Ob]])

                        def a3():
                            P.op("act", lambda e: e.copy(
                                oT[hp, qc0:qc0 + 4, sl(b)], pb[Ob][hp, :].rearrange("p (i q) -> p i q", i=4)),
                                reads=[Bpb[Ob]], writes=[B_oT])
                        G.update(pe_s=pe_s, d1=d1, a1=a1, d2=d2, pe_t=pe_t, a2=a2, pe_pv=pe_pv, a3=a3)
                        return G

                    groups = [mk_group(b_, kv_, gi % 2) for gi, (b_, kv_) in
                              enumerate([(b_, kv_) for b_ in range(NB) for kv_ in range(4)])]
                    ng_ = len(groups)
                    groups[0]["pe_s"]()
                    for c_ in range(ng_ + 1):
                        cur = groups[c_] if c_ < ng_ else None
                        prv = groups[c_ - 1] if c_ > 0 else None
                        if cur:
                            cur["d1"]()
                        if c_ + 1 < ng_:
                            groups[c_ + 1]["pe_s"]()
                        if cur:
                            cur["a1"]()
                        if prv:
                            prv["d2"]()
                            prv["pe_t"]()
                            prv["a2"]()
                            prv["pe_pv"]()
                            prv["a3"]()
                    for m in range(8):
                        mb = m % 2
                        P.op("pe", [lambda e, j=j, m=m, mb=mb: e.matmul(
                            pb[mb][:, 0:T], wo[:, j, sl(m)], oT[:, j, :], start=(j == 0), stop=(j == 7))
                            for j in range(8)], reads=[B_oT, B_wo], writes=[Bpb[mb]])
                        P.op("dve", lambda e, m=m, mb=mb, hs=hs: e.scalar_tensor_tensor(
                            hs[:, m, :], pb[mb][:, 0:T], vec[:, V_BO, m:m + 1], hs[:, m, :], ALU.add, ALU.add),
                            reads=[Bpb[mb], B_ht[s], Bc], writes=[B_ht[s]])
                    P.dma("sp", Hv[:, :, t0:t0 + T], hs[:], reads=[B_ht[s]], writes=[B_H[it]], slot=B_ht[s])
                P.end_phase()

        last = phases[-1]
        for ph in phases:
            if ph == "ffn1_0":
                ffn_phase(0, 0, True, False, False)
            elif ph == "conv":
                conv_phase()
            elif ph == "ssd":
                ssd_phase()
            elif ph == "ffn2_0":
                ffn_phase(1, 0, False, True, False)
            elif ph == "ffn1_1":
                ffn_phase(2, 1, False, False, False)
            elif ph == "att":
                att_phase()
            elif ph == "ffn2_1":
                ffn_phase(3, 1, False, True, True)
        P.barrier()
        P.emit()
    return nc


Q_LOWER = [0, 1, 2, 3, 8, 9, 10, 11]
Q_UPPER = [4, 5, 6, 7, 12, 13, 14, 15]
HEAD_ORDER = [h for c in range(8) for h in (Q_LOWER[c], Q_UPPER[c])]


def _pk(v):
    v = np.asarray(v, np.float32)
    return np.array(v.reshape(-1, 128).T, dtype=np.float32, order='C', copy=True)


def prep_shared(inp, S=SEQ):
    f = lambda a: np.array(a, dtype=np.float32, order='C', copy=True)
    sh = {}
    sh["ffn_win"] = f(np.stack([inp["ffn1_w_in"][0], inp["ffn2_w_in"][0], inp["ffn1_w_in"][1], inp["ffn2_w_in"][1]]))
    sh["ffn_wout"] = f(np.stack([inp["ffn1_w_out"][0], inp["ffn2_w_out"][0], inp["ffn1_w_out"][1], inp["ffn2_w_out"][1]]))
    vec = np.zeros((128, 20, 8), np.float32)
    for li in range(2):
        vec[:, 0 + li] = _pk(inp["norm_ffn1"][li])
        vec[:, 2 + li] = _pk(inp["norm_mix"][li])
        vec[:, 4 + li] = _pk(inp["norm_ffn2"][li])
        vec[:, 6 + li] = _pk(inp["ple_norm"][li])
    vec[:, 8] = _pk(inp["final_norm"])
    vec[:, 9] = _pk(inp["conv_dw_b"][0])
    vec[:, 10] = _pk(inp["conv_ln_g"][0])
    vec[:, 11] = _pk(inp["conv_ln_b"][0])
    vec[:, 12] = _pk(np.repeat(np.asarray(inp["ssm_d"][0], np.float32), 64))
    vec[:, 13] = _pk(inp["ssm_norm"][0])
    vec[:, 14] = _pk(inp["att_b_o"][0])
    sh["vecs"] = vec
    sh["ple_wg"] = f(inp["ple_gate_w"])
    sh["ple_wp"] = f(inp["ple_proj_w"])
    sh["hyb_win"] = f(inp["hyb_w_in"][0])
    sh["hyb_wout"] = f(inp["hyb_w_out"][0])
    cw = np.asarray(inp["conv_dw_w"][0], np.float32)
    sh["cw"] = f(cw.T.reshape(8, 128, CW).transpose(1, 0, 2))
    scw = np.asarray(inp["ssm_conv_w"][0], np.float32)
    scb = np.asarray(inp["ssm_conv_b"][0], np.float32)
    sc = np.concatenate([scw, scb[None]], 0)
    sh["scw"] = f(sc.T.reshape(12, 128, 5).transpose(1, 0, 2))
    sinks = np.asarray(inp["att_sinks"][0], np.float32)
    sg_order = [HEAD_ORDER[((kv // 2) * 4 + i) * 2 + kv % 2] for kv in range(4) for i in range(4)]
    sh["h16"] = f(np.stack([inp["ssm_dt_bias"][0], inp["ssm_a_log"][0], sinks[sg_order]]))
    wqkv = np.asarray(inp["att_w_qkv"][0], np.float32)
    bqkv = np.asarray(inp["att_b_qkv"][0], np.float32)
    qcols = np.concatenate([np.arange(h * 64, (h + 1) * 64) for h in HEAD_ORDER])
    cols = np.concatenate([qcols, np.arange(1024, 1536)])
    sh["wqkv"] = f(wqkv[:, cols])
    bp = bqkv[cols]
    sh["bqk"] = _pk(bp[:1280])
    sh["bv"] = f(bp[1280:])
    sh["wo"] = f(np.asarray(inp["att_w_o"][0], np.float32)[qcols, :])
    inv = (np.float32(10000.0) ** (-np.arange(0, 64, 2, dtype=np.float32) / np.float32(64))).astype(np.float32)
    ang = (np.arange(S, dtype=np.float32)[:, None] * inv[None, :]).astype(np.float32)
    cos, sin = np.cos(ang).astype(np.float32), np.sin(ang).astype(np.float32)
    prt = np.arange(128)
    CC = cos.T[prt % 32]
    sgn = np.where((prt % 64) < 32, -1.0, 1.0).astype(np.float32)
    SSn = sin.T[prt % 32] * sgn[:, None]
    sh["rope"] = f(np.stack([CC, SSn]))
    cst = np.zeros((5, 128, 128), np.float32)
    cst[0] = np.eye(128)
    cst[1] = np.triu(np.ones((128, 128)))
    sw = np.zeros((128, 128), np.float32)
    for m in range(128):
        sw[(m + 32) % 64 + (m // 64) * 64, m] = 1.0
    cst[2] = sw
    sh["cst"] = cst
    q = np.arange(128)[:, None]
    sp = np.arange(256)[None, :]
    valid = np.where(sp < 128, sp > q, (sp - 128) <= q)
    m1 = np.where(valid, 0.0, NEG).astype(np.float32)
    m0 = np.where(valid & (sp >= 128), 0.0, NEG).astype(np.float32)
    sh["msk"] = f(np.stack([m0, m1]))
    return sh


_CACHE = {}


def kernel(**inputs):
    x = np.asarray(inputs["x"], np.float32)
    p = np.asarray(inputs["p"], np.float32)
    B, S, _ = x.shape
    sh = prep_shared(inputs, S)
    key = ("full", S)
    if key not in _CACHE:
        _CACHE[key] = build_program(S)
    nc = _CACHE[key]
    in_maps = []
    for b in range(B):
        m = dict(sh)
        m["x"] = np.array(x[b], dtype=np.float32, order='C', copy=True)
        m["p"] = np.array(p[:, b], dtype=np.float32, order='C', copy=True)
        in_maps.append(m)
    res = run_bass_kernel_spmd(nc, in_maps, core_ids=list(range(B)))
    return np.stack([np.asarray(r["y"], np.float32) for r in res.results], 0)
```
